# Optimizing a Trainium2 kernel written in Bass

```python
import math
import jax, jax.numpy as jnp
from jax import lax
import numpy as np

D_MODEL = 1024
BATCH = 4
SEQ = 4096
DEPTH = 2

MEM_LEN = 256
EPS = 1e-6
NEG = -1e30
BLK = 128

DIL_HEADS = 8
DIL_HEAD_DIM = 64
DIL_PATTERNS = ((128, 1), (512, 4), (2048, 16))
RET_HEADS = 4
RET_KEY_DIM = 64
RET_VAL_DIM = 128
ROPE_BASE = 10000.0
DIFF_HEADS = 4
DIFF_HEAD_DIM = 64
SSM_D_INNER = 512
SSM_HEAD_DIM = 64
SSM_HEADS = SSM_D_INNER // SSM_HEAD_DIM
SSM_GROUPS = 2
SSM_STATE = 128
SSM_CONV = 4
N_BRANCH = 4
BRANCH_WIDTH = 512
X_HEADS = 4
X_HEAD_DIM = 128
FFN_HIDDEN = -(-8 * D_MODEL // (3 * 256)) * 256

A_QKV = DIL_HEADS * DIL_HEAD_DIM
B_QK = RET_HEADS * RET_KEY_DIM
B_VG = RET_HEADS * RET_VAL_DIM
C_QK = DIFF_HEADS * 2 * DIFF_HEAD_DIM
C_V = DIFF_HEADS * 2 * DIFF_HEAD_DIM
D_XBC = SSM_D_INNER + 2 * SSM_GROUPS * SSM_STATE
IN_SPLITS = (A_QKV, A_QKV, A_QKV, B_QK, B_QK, B_VG, B_VG, C_QK, C_QK, C_V,
             SSM_D_INNER, D_XBC, SSM_HEADS, N_BRANCH * D_MODEL)
IN_WIDTH = sum(IN_SPLITS)

kernel_name = 'hybrid_gated_parallel_mixer_trunk'


def rmsnorm(x, g):
    xf = x.astype(jnp.float32)
    y = xf * lax.rsqrt(jnp.mean(xf * xf, axis=-1, keepdims=True) + EPS)
    return (y * g.astype(jnp.float32)).astype(x.dtype)


def split_cols(h, sizes):
    offs = np.cumsum(sizes)[:-1].tolist()
    return jnp.split(h, offs, axis=-1)


def band_attention(q, k, v, w):
    *lead, L, hd = q.shape
    nb = L // BLK
    qb = q.reshape(*lead, nb, BLK, hd)

    def windows(t):
        tb = t.reshape(*lead, nb, BLK, t.shape[-1])
        tp = jnp.pad(tb, [(0, 0)] * len(lead) + [(1, 0), (0, 0), (0, 0)])
        return jnp.concatenate([tp[..., :-1, :, :], tp[..., 1:, :, :]], axis=-2)

    kw, vw = windows(k), windows(v)
    s = jnp.einsum('...nqd,...nkd->...nqk', qb, kw).astype(jnp.float32) * (hd ** -0.5)
    blk = jnp.arange(nb)[:, None, None]
    qpos = blk * BLK + jnp.arange(BLK)[None, :, None]
    kpos = (blk - 1) * BLK + jnp.arange(2 * BLK)[None, None, :]
    dist = qpos - kpos
    mask = (dist >= 0) & (dist <= w) & (kpos >= 0)
    s = jnp.where(mask, s, NEG)
    m = jnp.max(s, axis=-1, keepdims=True)
    p = jnp.exp(s - m)
    den = jnp.sum(p, axis=-1, keepdims=True)
    o = jnp.einsum('...nqk,...nkd->...nqd', (p / den).astype(v.dtype), vw)
    lse = (m + jnp.log(den))[..., 0]
    return o.reshape(*lead, L, hd), lse.reshape(*lead, L)


def dilated_attention(q, k, v):
    Bsz, S, H, hd = q.shape
    outs, lses = [], []
    for window, dil in DIL_PATTERNS:
        L = S // dil
        Lp = -(-L // BLK) * BLK

        def strided(t):
            t = t.reshape(Bsz, L, dil, H, hd).transpose(0, 3, 2, 1, 4)
            return jnp.pad(t, ((0, 0), (0, 0), (0, 0), (0, Lp - L), (0, 0)))

        o, lse = band_attention(strided(q), strided(k), strided(v), window // dil)
        outs.append(o[..., :L, :].transpose(0, 3, 2, 1, 4).reshape(Bsz, S, H, hd))
        lses.append(lse[..., :L].transpose(0, 3, 2, 1).reshape(Bsz, S, H))
    wts = jax.nn.softmax(jnp.stack(lses, axis=0), axis=0)
    o = jnp.sum(wts[..., None] * jnp.stack(outs, axis=0).astype(jnp.float32), axis=0)
    return o.reshape(Bsz, S, H * hd).astype(q.dtype)


def rotary(t, pos):
    half = t.shape[-1] // 2
    inv_freq = ROPE_BASE ** (-jnp.arange(half, dtype=jnp.float32) / half)
    ang = pos[:, None] * inv_freq[None, :]
    cos, sin = jnp.cos(ang)[:, None, :], jnp.sin(ang)[:, None, :]
    t1, t2 = t[..., :half], t[..., half:]
    return jnp.concatenate([t1 * cos - t2 * sin, t1 * sin + t2 * cos], axis=-1)


def retention(q, k, v, g, gn_g):
    Bsz, S, H, dk = q.shape
    dv = RET_VAL_DIM
    f32 = jnp.float32
    pos = jnp.arange(S, dtype=f32)
    q = rotary(q.astype(f32), pos)
    k = rotary(k.astype(f32), pos) * (dk ** -0.5)
    nc = S // BLK
    qc = q.reshape(Bsz, nc, BLK, H, dk)
    kc = k.reshape(Bsz, nc, BLK, H, dk)
    vc = v.astype(f32).reshape(Bsz, nc, BLK, H, dv)
    log_gamma = jnp.log1p(-jnp.exp2(-5.0 - jnp.arange(H, dtype=f32)))
    idx = jnp.arange(BLK, dtype=f32)
    rel = idx[:, None] - idx[None, :]
    decay = jnp.where(rel >= 0, jnp.exp(log_gamma[:, None, None] * jnp.maximum(rel, 0.0)), 0.0)
    s = jnp.einsum('bnihd,bnjhd->bnhij', qc, kc) * decay
    o_intra = jnp.einsum('bnhij,bnjhe->bnihe', s, vc)
    k_end = kc * jnp.exp((BLK - 1 - idx)[:, None] * log_gamma[None, :])[..., None]
    kv = jnp.einsum('bnjhd,bnjhe->nbhde', k_end, vc)
    chunk_decay = jnp.exp(log_gamma * BLK)[:, None, None]

    def step(state, kv_n):
        return state * chunk_decay + kv_n, state

    _, r_prev = lax.scan(step, jnp.zeros((Bsz, H, dk, dv), f32), kv)
    q_in = qc * jnp.exp((idx + 1.0)[:, None] * log_gamma[None, :])[..., None]
    o_cross = jnp.einsum('bnihd,nbhde->bnihe', q_in, r_prev)
    o = (o_intra + o_cross).reshape(Bsz, S, H, dv)
    mu = jnp.mean(o, axis=-1, keepdims=True)
    var = jnp.mean(jnp.square(o - mu), axis=-1, keepdims=True)
    o = ((o - mu) * lax.rsqrt(var + EPS)).reshape(Bsz, S, H * dv) * gn_g.astype(f32)
    return (jax.nn.silu(g.astype(f32)) * o).astype(g.dtype)


def diff_attention(q, k, v, lam_params, subln_g, lam_init):
    Bsz, S, H, _, d = q.shape
    lp = lam_params.astype(jnp.float32)
    lam = jnp.exp(jnp.sum(lp[0] * lp[1])) - jnp.exp(jnp.sum(lp[2] * lp[3])) + lam_init
    nb = S // BLK
    qb = q.reshape(Bsz, nb, BLK, H, 2, d).transpose(1, 0, 3, 4, 2, 5)
    kt = k.transpose(0, 2, 3, 1, 4)
    vt = v.transpose(0, 2, 1, 3)
    kpos = jnp.arange(S)

    def block(args):
        qi, i = args
        s = jnp.einsum('bhcqd,bhckd->bhcqk', qi, kt).astype(jnp.float32) * (d ** -0.5)
        qpos = i * BLK + jnp.arange(BLK)
        s = jnp.where(kpos[None, :] <= qpos[:, None], s, NEG)
        a = jax.nn.softmax(s, axis=-1)
        attn = a[:, :, 0] - lam * a[:, :, 1]
        return jnp.einsum('bhqk,bhke->bhqe', attn.astype(vt.dtype), vt)

    o = lax.map(block, (qb, jnp.arange(nb)))
    o = o.transpose(1, 0, 3, 2, 4).reshape(Bsz, S, H, 2 * d)
    o = rmsnorm(o, subln_g) * (1.0 - lam_init)
    return o.reshape(Bsz, S, H * 2 * d)


def mamba2_ssd(z, xbc, dt, conv_w, conv_b, dt_bias, A_log, D_skip, norm_g):
    Bsz, S, C = xbc.shape
    f32 = jnp.float32
    G, Hg, P, N = SSM_GROUPS, SSM_HEADS // SSM_GROUPS, SSM_HEAD_DIM, SSM_STATE
    xbc = lax.conv_general_dilated(xbc, conv_w[:, None, :], window_strides=(1,),
                                   padding=[(SSM_CONV - 1, 0)],
                                   dimension_numbers=('NWC', 'WIO', 'NWC'),
                                   feature_group_count=C) + conv_b
    xbc = jax.nn.silu(xbc.astype(f32))
    xs, Bm, Cm = split_cols(xbc, (SSM_D_INNER, G * N, G * N))
    dt = jax.nn.softplus(dt.astype(f32) + dt_bias.astype(f32))
    A = -jnp.exp(A_log.astype(f32)).reshape(G, Hg)
    nc = S // BLK
    xc = xs.reshape(Bsz, nc, BLK, G, Hg, P)
    Bc = Bm.reshape(Bsz, nc, BLK, G, N)
    Cc = Cm.reshape(Bsz, nc, BLK, G, N)
    dtc = dt.reshape(Bsz, nc, BLK, G, Hg)
    a_cum = jnp.cumsum(dtc * A, axis=2)
    seg = a_cum[:, :, :, None] - a_cum[:, :, None, :]
    causal = jnp.tril(jnp.ones((BLK, BLK), dtype=bool))[:, :, None, None]
    Lmat = jnp.exp(jnp.where(causal, seg, NEG))
    cb = jnp.einsum('bcign,bcjgn->bcijg', Cc, Bc)
    xdt = xc * dtc[..., None]
    y_diag = jnp.einsum('bcijg,bcijgh,bcjghp->bcighp', cb, Lmat, xdt)
    decay_end = jnp.exp(a_cum[:, :, -1:] - a_cum)
    states = jnp.einsum('bcjgn,bcjgh,bcjghp->cbghpn', Bc, decay_end, xdt)
    chunk_decay = jnp.exp(a_cum[:, :, -1]).transpose(1, 0, 2, 3)

    def step(h, inp):
        st, dec = inp
        return h * dec[..., None, None] + st, h

    _, h_prev = lax.scan(step, jnp.zeros((Bsz, G, Hg, P, N), f32), (states, chunk_decay))
    y_off = jnp.einsum('bcign,cbghpn,bcigh->bcighp', Cc, h_prev, jnp.exp(a_cum))
    y = y_diag + y_off + xc * D_skip.astype(f32).reshape(G, Hg)[:, :, None]
    y = y.reshape(Bsz, S, SSM_D_INNER)
    y = rmsnorm(y * jax.nn.silu(z.astype(f32)), norm_g)
    return y.astype(z.dtype)


def hybrid_mixer(h, w_in, ret_gn_g, diff_lambda, diff_subln_g, conv_w, conv_b, dt_bias,
                 A_log, D_skip, ssm_norm_g, w_branch, w_out, lam_init):
    Bsz, S, _ = h.shape
    proj = h @ w_in
    (aq, ak, av, bq, bk, bv, bg, cq, ck, cv, dz, dxbc, ddt, gates) = split_cols(proj, IN_SPLITS)
    ash = (Bsz, S, DIL_HEADS, DIL_HEAD_DIM)
    y_a = dilated_attention(aq.reshape(ash), ak.reshape(ash), av.reshape(ash))
    bsh = (Bsz, S, RET_HEADS, RET_KEY_DIM)
    y_b = retention(bq.reshape(bsh), bk.reshape(bsh), bv, bg, ret_gn_g)
    csh = (Bsz, S, DIFF_HEADS, 2, DIFF_HEAD_DIM)
    y_c = diff_attention(cq.reshape(csh), ck.reshape(csh),
                         cv.reshape(Bsz, S, DIFF_HEADS, 2 * DIFF_HEAD_DIM),
                         diff_lambda, diff_subln_g, lam_init)
    y_d = mamba2_ssd(dz, dxbc, ddt, conv_w, conv_b, dt_bias, A_log, D_skip, ssm_norm_g)
    branches = jnp.stack([y_a, y_b, y_c, y_d], axis=2)
    up = jnp.einsum('bsim,imd->bsid', branches, w_branch)
    gate = jax.nn.sigmoid(gates.reshape(Bsz, S, N_BRANCH, D_MODEL))
    return jnp.sum(gate * up, axis=2) @ w_out


def cross_attention(h, m, w_q, w_kv, w_o):
    Bsz, S, _ = h.shape
    M = m.shape[1]
    q = (h @ w_q).reshape(Bsz, S, X_HEADS, X_HEAD_DIM)
    kv = (m @ w_kv).reshape(Bsz, M, 2, X_HEADS, X_HEAD_DIM)
    k, v = kv[:, :, 0], kv[:, :, 1]
    s = jnp.einsum('bshd,bmhd->bhsm', q, k).astype(jnp.float32) * (X_HEAD_DIM ** -0.5)
    a = jax.nn.softmax(s, axis=-1)
    o = jnp.einsum('bhsm,bmhd->bshd', a.astype(v.dtype), v).reshape(Bsz, S, X_HEADS * X_HEAD_DIM)
    return o @ w_o


def swiglu(h, w_in, w_out):
    a, b = jnp.split(h @ w_in, 2, axis=-1)
    return (jax.nn.silu(a) * b) @ w_out


def setup_inputs(seed: int = 0) -> dict:
    key = jax.random.key(seed)
    ks = jax.random.split(key, 24)
    nrm = jax.random.normal
    L = DEPTH

    def gain(k, n):
        return 1.0 + 0.05 * nrm(k, (L, n), jnp.float32)

    u = jax.random.uniform(ks[9], (L, SSM_HEADS), jnp.float32)
    dt0 = jnp.exp(u * (math.log(0.1) - math.log(0.001)) + math.log(0.001))
    return {
        'x': nrm(ks[0], (BATCH, SEQ, D_MODEL), jnp.float32),
        'mem': nrm(ks[1], (BATCH, MEM_LEN, D_MODEL), jnp.float32),
        'norm_mix_g': gain(ks[2], D_MODEL),
        'w_in': nrm(ks[3], (L, D_MODEL, IN_WIDTH), jnp.float32) * D_MODEL ** -0.5,
        'ret_gn_g': gain(ks[4], RET_HEADS * RET_VAL_DIM),
        'diff_lambda': 0.1 * nrm(ks[5], (L, 4, DIFF_HEAD_DIM), jnp.float32),
        'diff_subln_g': gain(ks[6], 2 * DIFF_HEAD_DIM),
        'ssm_conv_w': nrm(ks[7], (L, SSM_CONV, D_XBC), jnp.float32) * SSM_CONV ** -0.5,
        'ssm_conv_b': 0.02 * nrm(ks[8], (L, D_XBC), jnp.float32),
        'ssm_dt_bias': dt0 + jnp.log(-jnp.expm1(-dt0)),
        'ssm_A_log': jnp.log(jax.random.uniform(ks[10], (L, SSM_HEADS), jnp.float32, 1.0, 16.0)),
        'ssm_D': 1.0 + 0.1 * nrm(ks[11], (L, SSM_HEADS), jnp.float32),
        'ssm_norm_g': gain(ks[12], SSM_D_INNER),
        'w_branch': nrm(ks[13], (L, N_BRANCH, BRANCH_WIDTH, D_MODEL), jnp.float32) * BRANCH_WIDTH ** -0.5,
        'w_mix_out': nrm(ks[14], (L, D_MODEL, D_MODEL), jnp.float32) * D_MODEL ** -0.5,
        'norm_x_g': gain(ks[15], D_MODEL),
        'norm_mem_g': gain(ks[16], D_MODEL),
        'w_xq': nrm(ks[17], (L, D_MODEL, X_HEADS * X_HEAD_DIM), jnp.float32) * D_MODEL ** -0.5,
        'w_xkv': nrm(ks[18], (L, D_MODEL, 2 * X_HEADS * X_HEAD_DIM), jnp.float32) * D_MODEL ** -0.5,
        'w_xo': nrm(ks[19], (L, X_HEADS * X_HEAD_DIM, D_MODEL), jnp.float32) * (X_HEADS * X_HEAD_DIM) ** -0.5,
        'norm_ffn_g': gain(ks[20], D_MODEL),
        'w_ffn_in': nrm(ks[21], (L, D_MODEL, 2 * FFN_HIDDEN), jnp.float32) * D_MODEL ** -0.5,
        'w_ffn_out': nrm(ks[22], (L, FFN_HIDDEN, D_MODEL), jnp.float32) * FFN_HIDDEN ** -0.5,
        'norm_f_g': 1.0 + 0.05 * nrm(ks[23], (D_MODEL,), jnp.float32),
    }


def reference(x, mem, norm_mix_g, w_in, ret_gn_g, diff_lambda, diff_subln_g, ssm_conv_w,
              ssm_conv_b, ssm_dt_bias, ssm_A_log, ssm_D, ssm_norm_g, w_branch, w_mix_out,
              norm_x_g, norm_mem_g, w_xq, w_xkv, w_xo, norm_ffn_g, w_ffn_in, w_ffn_out,
              norm_f_g):
    for l in range(DEPTH):
        lam_init = 0.8 - 0.6 * math.exp(-0.3 * l)
        h = rmsnorm(x, norm_mix_g[l])
        x = x + hybrid_mixer(h, w_in[l], ret_gn_g[l], diff_lambda[l], diff_subln_g[l],
                             ssm_conv_w[l], ssm_conv_b[l], ssm_dt_bias[l], ssm_A_log[l],
                             ssm_D[l], ssm_norm_g[l], w_branch[l], w_mix_out[l], lam_init)
        m = rmsnorm(mem, norm_mem_g[l])
        x = x + cross_attention(rmsnorm(x, norm_x_g[l]), m, w_xq[l], w_xkv[l], w_xo[l])
        x = x + swiglu(rmsnorm(x, norm_ffn_g[l]), w_ffn_in[l], w_ffn_out[l])
    return rmsnorm(x, norm_f_g)
```

```python
import math
from contextlib import ExitStack

import numpy as np
import concourse.bass as bass
import concourse.mybir as mybir
from concourse.bass_utils import run_bass_kernel_spmd

F32 = mybir.dt.float32
BF16 = mybir.dt.bfloat16
AF = mybir.ActivationFunctionType
ALU = mybir.AluOpType

D = 1024
DC = 8
DEPTH = 2
MEM = 256
EPS = 1e-6
FFN = 2816
FC = 22
IN_W = 10248
O_AQ, O_AK, O_AV = 0, 512, 1024
O_BQ, O_BK, O_BV, O_BG = 1536, 1792, 2048, 2560
O_CQ, O_CK, O_CV = 3072, 3584, 4096
O_DZ, O_DX, O_DDT, O_G = 4608, 5120, 6144, 6152


class Buf:
    __slots__ = ("name", "w", "r")

    def __init__(self, name):
        self.name = name
        self.w = None
        self.r = []


class Em:
    NRING = 24

    def __init__(self, nc):
        self.nc = nc
        self.eng = {"pe": nc.tensor, "act": nc.scalar, "dve": nc.vector, "pool": nc.gpsimd, "sp": nc.sync}
        self.sem = {}
        self.cnt = {}
        for e in ("pe", "act", "dve", "pool"):
            self.sem[e] = nc.alloc_semaphore("s_" + e)
            self.cnt[e] = 0
        self.known = {e: {} for e in self.eng}
        self.ring = {}
        self.ring_use = {}
        self.ring_next = {}
        for q in ("sp", "pool", "act"):
            self.ring[q] = [nc.alloc_semaphore("d_%s_%d" % (q, i)) for i in range(self.NRING)]
            self.ring_use[q] = [0] * self.NRING
            self.ring_next[q] = 0
        self.dma_tokens = []
        self.nbuf = 0

    def buf(self, name=None):
        self.nbuf += 1
        return Buf(name or ("b%d" % self.nbuf))

    def _wait(self, engine, tok):
        if tok is None:
            return
        key, sem, val = tok
        if key == engine and engine == "pe":
            return
        kn = self.known[engine]
        if kn.get(key, 0) >= val:
            return
        self.eng[engine].wait_ge(sem, val)
        kn[key] = val

    def _deps(self, engine, reads, writes):
        for b in reads:
            self._wait(engine, b.w)
        for b in writes:
            self._wait(engine, b.w)
            for t in b.r:
                self._wait(engine, t)

    def _commit(self, tok, reads, writes):
        for b in reads:
            b.r.append(tok)
            if len(b.r) > 12:
                last = {}
                for t in b.r:
                    if t[0] not in last or last[t[0]][2] < t[2]:
                        last[t[0]] = t
                b.r = list(last.values())
        for b in writes:
            b.w = tok
            b.r = []

    def op(self, engine, fn, reads=(), writes=()):
        self._deps(engine, reads, writes)
        inst = fn(self.eng[engine])
        self.cnt[engine] += 1
        inst.then_inc(self.sem[engine], 1)
        tok = (engine, self.sem[engine], self.cnt[engine])
        self._commit(tok, reads, writes)
        return tok

    def dma(self, q, out, in_, reads=(), writes=()):
        i = self.ring_next[q]
        self.ring_next[q] = (i + 1) % self.NRING
        sem = self.ring[q][i]
        key = ("dma", q, i)
        prior = 16 * self.ring_use[q][i]
        if prior:
            self._wait(q, (key, sem, prior))
        self._deps(q, reads, writes)
        self.eng[q].dma_start(out=out, in_=in_).then_inc(sem, 16)
        self.ring_use[q][i] += 1
        tok = (key, sem, 16 * self.ring_use[q][i])
        self._commit(tok, reads, writes)
        self.dma_tokens.append(tok)
        if len(self.dma_tokens) > 3 * self.NRING:
            self.dma_tokens = self.dma_tokens[-3 * self.NRING:]
        return tok

    def barrier(self):
        toks = [(e, self.sem[e], self.cnt[e]) for e in ("pe", "act", "dve", "pool") if self.cnt[e]]
        for q in self.ring:
            for i in range(self.NRING):
                if self.ring_use[q][i]:
                    toks.append((("dma", q, i), self.ring[q][i], 16 * self.ring_use[q][i]))
        for e in self.eng:
            for t in toks:
                if t[0] == e and e == "pe":
                    continue
                self._wait(e, t)

    def finish(self, toks):
        for t in toks:
            self._wait("sp", t)


def _ap(t):
    return t if isinstance(t, bass.AP) else t.ap()


class Ctx:
    def __init__(self, S, depth=DEPTH):
        self.S = S
        self.NT = S // 512
        self.NB = S // 128
        self.depth = depth
        nc = self.nc = bass.Bass("TRN2", target_bir_lowering=False)
        self.em = Em(nc)
        self.es = ExitStack()
        dr = self.dram = {}

        def din(name, shape, dt=F32):
            dr[name] = nc.dram_tensor(name, list(shape), dt, kind="ExternalInput").ap()

        din("xT", [D, S])
        din("memT", [D, MEM])
        din("w_in", [depth, D, IN_W])
        din("w_branch", [depth, 2048, D])
        din("w_mix_out", [depth, D, D])
        din("w_xq", [depth, D, 512])
        din("w_xkv", [depth, D, 1024])
        din("w_xo", [depth, 512, D])
        din("w_ffn_in", [depth, D, 2 * FFN])
        din("w_ffn_out", [depth, FFN, D])
        din("pvec", [128, PV_N])
        din("rvec", [128, RV_N])
        din("cst_f32", [128, CF_N])
        din("rot", [64, 2, S])
        dr["out"] = nc.dram_tensor("out", [D, S], F32, kind="ExternalOutput").ap()
        dr["xs"] = nc.dram_tensor("xs", [D, S], F32, kind="Internal").ap()
        dr["ybr"] = nc.dram_tensor("ybr", [2048, S], BF16, kind="Internal").ap()
        dr["gat"] = nc.dram_tensor("gat", [4096, S], BF16, kind="Internal").ap()

    def sb(self, name, shape, dt):
        return self.es.enter_context(self.nc.sbuf_tensor(name, list(shape), dt))

    _uid = 0

    def tl(self, st, name, shape, dt):
        Ctx._uid += 1
        return st.enter_context(self.nc.sbuf_tensor("%s_%d" % (name, Ctx._uid), list(shape), dt))


def _pv_layout():
    off = {}
    n = 0
    for l in range(DEPTH):
        for nm, w in (("norm_mix_g", 8), ("norm_x_g", 8), ("norm_mem_g", 8), ("norm_ffn_g", 8),
                      ("ret_gn_g", 4), ("diff_subln_g", 1), ("conv_w", 32), ("conv_b", 8)):
            off[(nm, l)] = n
            n += w
    off[("norm_f_g", 0)] = n
    n += 8
    return off, n


PV_OFF, PV_N = _pv_layout()


def _rv_layout():
    off = {}
    n = 0
    for l in range(DEPTH):
        for nm, w in (("dt_bias", 8), ("A_log", 8), ("D", 8), ("ssm_norm_g", 512), ("diff_lambda", 256)):
            off[(nm, l)] = n
            n += w
    return off, n


RV_OFF, RV_N = _rv_layout()

CF_IDENT, CF_TRIU, CF_TRIL, CF_DECT, CF_KEND, CF_QDEC, CF_SL = 0, 128, 256, 384, 896, 900, 1412
CF_N = 1540


class Builder(Ctx):
    def __init__(self, S, depth=DEPTH):
        super().__init__(S, depth)
        nc, em = self.nc, self.em
        self.pv = self.sb("pv", [128, PV_N], F32)
        self.rv = self.sb("rv", [128, RV_N], F32)
        self.cf = self.sb("cf", [128, CF_N], F32)
        self.ones_bf = self.sb("ones_bf", [128, 128], BF16)
        self.ones_f = self.sb("ones_f", [128, 128], F32)
        self.ident_bf = self.sb("ident_bf", [128, 128], BF16)
        self.triU_bf = self.sb("triU_bf", [128, 128], BF16)
        self.triL_bf = self.sb("triL_bf", [128, 128], BF16)
        self.b_const = em.buf("const")
        self.ps = []
        self.psb = []
        for i in range(8):
            self.ps.append(self.es.enter_context(nc.psum_tensor("ps%d" % i, [128, 512], F32)))
            self.psb.append(em.buf("ps%d" % i))
        self.stg = [self.sb("stg%d" % i, [128, 1024], F32) for i in range(2)]
        self.stgb = [em.buf("stg%d" % i) for i in range(2)]
        self.stg_i = 0
        d = self.dram
        bc = self.b_const
        em.dma("sp", self.pv[:], d["pvec"], writes=[bc])
        em.dma("sp", self.rv[:], d["rvec"], writes=[bc])
        em.dma("sp", self.cf[:], d["cst_f32"], writes=[bc])
        em.op("dve", lambda e: e.memset(self.ones_bf[:], 1.0), writes=[bc])
        em.op("dve", lambda e: e.memset(self.ones_f[:], 1.0), writes=[bc])
        self.eps_c = self.sb("eps_c", [128, 1], F32)
        em.op("dve", lambda e: e.memset(self.eps_c[:], EPS), writes=[bc])
        em.op("dve", lambda e: e.tensor_copy(out=self.ident_bf[:], in_=self.cf[:, CF_IDENT:CF_IDENT + 128]), reads=[bc], writes=[bc])
        em.op("dve", lambda e: e.tensor_copy(out=self.triU_bf[:], in_=self.cf[:, CF_TRIU:CF_TRIU + 128]), reads=[bc], writes=[bc])
        em.op("dve", lambda e: e.tensor_copy(out=self.triL_bf[:], in_=self.cf[:, CF_TRIL:CF_TRIL + 128]), reads=[bc], writes=[bc])
        em.barrier()

    def pvcol(self, name, l, j=0):
        o = PV_OFF[(name, l)] + j
        return self.pv[:, o:o + 1]

    def load_w(self, dst, src, nrows, ncols, wbuf):
        em = self.em
        c0 = 0
        while c0 < ncols:
            w = min(1024, ncols - c0)
            i = self.stg_i
            self.stg_i = (i + 1) % 2
            st, sbf = self.stg[i], self.stgb[i]
            em.dma("sp", st[0:nrows, 0:w], src[:, c0:c0 + w], writes=[sbf])
            o, ww, cc = dst, w, c0
            em.op("pool", lambda e, st=st, o=o, ww=ww, cc=cc: e.tensor_copy(out=o[:, cc:cc + ww], in_=st[0:nrows, 0:ww]),
                  reads=[sbf], writes=[wbuf])
            c0 += w

    def norm_tile(self, xt, ht, gname, l, sq, xb, hb, width=512, pi=0):
        em = self.em
        ps, psb = self.ps[pi], self.psb[pi]
        bsq = self.b_sq
        for c in range(DC):
            em.op("act", lambda e, c=c: e.activation(out=sq[:, c % 2, 0:width], in_=xt[:, c, 0:width], func=AF.Square),
                  reads=[xb], writes=[bsq[c % 2]])
            em.op("pe", lambda e, c=c: e.matmul(ps[:, 0:width], lhsT=self.ones_bf[:], rhs=sq[:, c % 2, 0:width],
                                                start=(c == 0), stop=(c == DC - 1)),
                  reads=[bsq[c % 2], self.b_const], writes=[psb])
        rs = self.rstd
        em.op("act", lambda e: e.activation(out=rs[:, 0:width], in_=ps[:, 0:width], func=AF.Sqrt, scale=1.0 / D, bias=self.eps_t[:, 0:1]),
              reads=[psb, self.b_const], writes=[self.b_rstd])
        em.op("dve", lambda e: e.reciprocal(out=rs[:, 0:width], in_=rs[:, 0:width]), reads=[self.b_rstd], writes=[self.b_rstd])
        for c in range(DC):
            em.op("dve", lambda e, c=c: e.scalar_tensor_tensor(out=ht[:, c, 0:width], in0=xt[:, c, 0:width],
                                                                scalar=self.pvcol(gname, l, c), in1=rs[:, 0:width],
                                                                op0=ALU.mult, op1=ALU.mult),
                  reads=[xb, self.b_rstd, self.b_const], writes=[hb])

    def alloc_norm_scratch(self, st):
        em = self.em
        self.sqt = self.tl(st, "sqt", [128, 2, 512], BF16)
        self.rstd = self.tl(st, "rstd", [128, 512], F32)
        self.eps_t = self.tl(st, "eps_t", [128, 1], F32)
        self.b_sq = [em.buf("sq0"), em.buf("sq1")]
        self.b_rstd = em.buf("rstd")
        em.op("dve", lambda e: e.memset(self.eps_t[:], EPS), writes=[self.b_const])

    def xtile_ap(self, name, t, width=512):
        return self.dram[name].rearrange("(c p) s -> p c s", p=128)[:, :, t * width:(t + 1) * width]

    def phase_copy_in(self):
        em = self.em
        b = self.b_xs = em.buf("xs")
        for c in range(DC):
            em.dma("sp", self.dram["xs"][c * 128:(c + 1) * 128, :], self.dram["xT"][c * 128:(c + 1) * 128, :], writes=[b])
        em.barrier()

    def phase_ffn(self, l):
        em, nc = self.em, self.nc
        with ExitStack() as st:
            self.alloc_norm_scratch(st)
            w1 = self.tl(st, "w1", [128, 8, 2 * FFN], BF16)
            w2 = self.tl(st, "w2", [128, FC, D], BF16)
            xt1 = self.tl(st, "xt0", [128, 8, 512], F32)
            xt = [xt1, xt1]
            ht = self.tl(st, "ht", [128, 8, 512], BF16)
            u = self.tl(st, "u", [128, FC, 512], BF16)
            sa = self.tl(st, "sa", [128, 2, 512], BF16)
            bw1 = [em.buf() for _ in range(8)]
            bw2 = [em.buf() for _ in range(FC)]
            bx0 = em.buf()
            bx = [bx0, bx0]
            bh, bu, bsa = em.buf(), em.buf(), [em.buf(), em.buf()]
            for k in range(8):
                self.load_w(w1[:, k, :], self.dram["w_ffn_in"][l, k * 128:(k + 1) * 128, :], 128, 2 * FFN, bw1[k])
            for f in range(FC):
                self.load_w(w2[:, f, :], self.dram["w_ffn_out"][l, f * 128:(f + 1) * 128, :], 128, D, bw2[f])
            for t in range(self.NT):
                cur = 0
                em.dma("sp", xt[cur][:], self.xtile_ap("xs", t), reads=[self.b_xs], writes=[bx[cur]])
                self.norm_tile(xt[cur], ht, "norm_ffn_g", l, self.sqt, bx[cur], bh)
                for f in range(FC):
                    pa, pb = 1 + (f % 2) * 2, 2 + (f % 2) * 2
                    for k in range(8):
                        em.op("pe", lambda e, k=k, f=f, pa=pa: e.matmul(self.ps[pa][:], lhsT=w1[:, k, f * 128:(f + 1) * 128], rhs=ht[:, k, :],
                                                                    start=(k == 0), stop=(k == 7)),
                              reads=[bw1[k], bh], writes=[self.psb[pa]])
                    for k in range(8):
                        em.op("pe", lambda e, k=k, f=f, pb=pb: e.matmul(self.ps[pb][:], lhsT=w1[:, k, FFN + f * 128:FFN + (f + 1) * 128], rhs=ht[:, k, :],
                                                                    start=(k == 0), stop=(k == 7)),
                              reads=[bw1[k], bh], writes=[self.psb[pb]])
                    si = f % 2
                    em.op("act", lambda e, pa=pa, si=si: e.activation(out=sa[:, si, :], in_=self.ps[pa][:], func=AF.Silu),
                          reads=[self.psb[pa]], writes=[bsa[si]])
                    em.op("dve", lambda e, pb=pb, si=si, f=f: e.tensor_tensor(out=u[:, f, :], in0=self.ps[pb][:], in1=sa[:, si, :], op=ALU.mult),
                          reads=[self.psb[pb], bsa[si]], writes=[bu])
                for c in range(DC):
                    po = 5 + (c % 2)
                    for f in range(FC):
                        em.op("pe", lambda e, c=c, f=f, po=po: e.matmul(self.ps[po][:], lhsT=w2[:, f, c * 128:(c + 1) * 128], rhs=u[:, f, :],
                                                                    start=(f == 0), stop=(f == FC - 1)),
                              reads=[bw2[f], bu], writes=[self.psb[po]])
                    em.op("dve", lambda e, c=c, po=po, cur=cur: e.tensor_tensor(out=xt[cur][:, c, :], in0=self.ps[po][:], in1=xt[cur][:, c, :], op=ALU.add),
                          reads=[self.psb[po], bx[cur]], writes=[bx[cur]])
                em.dma("sp", self.xtile_ap("xs", t), xt[cur][:], reads=[bx[cur]], writes=[self.b_xs])
            em.barrier()

    def phase_final(self):
        em, nc = self.em, self.nc
        toks = []
        with ExitStack() as st:
            self.alloc_norm_scratch(st)
            xt = self.tl(st, "xt", [128, 8, 512], F32)
            ot = self.tl(st, "ot", [128, 8, 512], F32)
            bx, bo = em.buf(), em.buf()
            b_out = em.buf("out")
            for t in range(self.NT):
                em.dma("sp", xt[:], self.xtile_ap("xs", t), reads=[self.b_xs], writes=[bx])
                self.norm_tile(xt, ot, "norm_f_g", 0, self.sqt, bx, bo)
                toks.append(em.dma("sp", self.xtile_ap("out", t), ot[:], reads=[bo], writes=[b_out]))
            em.barrier()
        return toks


def _chunked(v, nch):
    return np.ascontiguousarray(np.asarray(v, np.float32).reshape(nch, 128).T)


def make_tables(inp, S):
    pv = np.zeros((128, PV_N), np.float32)
    rv = np.zeros((128, RV_N), np.float32)
    for l in range(DEPTH):
        pv[:, PV_OFF[("norm_mix_g", l)]:][:, :8] = _chunked(inp["norm_mix_g"][l], 8)
        pv[:, PV_OFF[("norm_x_g", l)]:][:, :8] = _chunked(inp["norm_x_g"][l], 8)
        pv[:, PV_OFF[("norm_mem_g", l)]:][:, :8] = _chunked(inp["norm_mem_g"][l], 8)
        pv[:, PV_OFF[("norm_ffn_g", l)]:][:, :8] = _chunked(inp["norm_ffn_g"][l], 8)
        pv[:, PV_OFF[("ret_gn_g", l)]:][:, :4] = _chunked(inp["ret_gn_g"][l], 4)
        pv[:, PV_OFF[("diff_subln_g", l)]:][:, :1] = _chunked(inp["diff_subln_g"][l], 1)
        cw = np.asarray(inp["ssm_conv_w"][l], np.float32)
        o = PV_OFF[("conv_w", l)]
        for c in range(8):
            for k in range(4):
                pv[:, o + c * 4 + k] = cw[k, c * 128:(c + 1) * 128]
        pv[:, PV_OFF[("conv_b", l)]:][:, :8] = _chunked(inp["ssm_conv_b"][l], 8)
        for nm, key, w in (("dt_bias", "ssm_dt_bias", 8), ("A_log", "ssm_A_log", 8), ("D", "ssm_D", 8),
                           ("ssm_norm_g", "ssm_norm_g", 512)):
            o = RV_OFF[(nm, l)]
            rv[:, o:o + w] = np.asarray(inp[key][l], np.float32).reshape(1, w)
        o = RV_OFF[("diff_lambda", l)]
        rv[:, o:o + 256] = np.asarray(inp["diff_lambda"][l], np.float32).reshape(1, 256)
    pv[:, PV_OFF[("norm_f_g", 0)]:][:, :8] = _chunked(inp["norm_f_g"], 8)
    cf = np.zeros((128, CF_N), np.float32)
    idx = np.arange(128)
    cf[:, CF_IDENT:CF_IDENT + 128] = np.eye(128, dtype=np.float32)
    cf[:, CF_TRIU:CF_TRIU + 128] = (idx[:, None] <= idx[None, :])
    cf[:, CF_TRIL:CF_TRIL + 128] = (idx[:, None] >= idx[None, :])
    cf[:, CF_SL:CF_SL + 128] = (idx[:, None] > idx[None, :])
    lg = np.log1p(-np.exp2(-5.0 - np.arange(4, dtype=np.float32))).astype(np.float32)
    for h in range(4):
        rel = (idx[None, :] - idx[:, None]).astype(np.float32)
        dec = np.where(rel >= 0, np.exp(lg[h] * np.maximum(rel, 0.0)), 0.0)
        cf[:, CF_DECT + h * 128:CF_DECT + (h + 1) * 128] = dec
        cf[:, CF_KEND + h] = np.exp((127 - idx) * lg[h])
        cf[:, CF_QDEC + h * 128:CF_QDEC + (h + 1) * 128] = np.exp((idx + 1.0) * lg[h])[None, :]
    half = 32
    inv_freq = (10000.0 ** (-np.arange(half, dtype=np.float32) / half)).astype(np.float32)
    pos = np.arange(S, dtype=np.float32)
    ang = (pos[None, :] * inv_freq[:, None]).astype(np.float32)
    cos, sin = np.cos(ang).astype(np.float32), np.sin(ang).astype(np.float32)
    rot = np.zeros((64, 2, S), np.float32)
    rot[:32, 0] = cos
    rot[32:, 0] = cos
    rot[:32, 1] = -sin
    rot[32:, 1] = sin
    return pv, rv, cf, rot


def make_in_map(inp, b, S):
    pv, rv, cf, rot = make_tables(inp, S)
    f = lambda a: np.ascontiguousarray(np.asarray(a, np.float32))
    return {
        "xT": f(np.asarray(inp["x"][b]).T[:, :S]),
        "memT": f(np.asarray(inp["mem"][b]).T),
        "w_in": f(inp["w_in"]),
        "w_branch": f(np.asarray(inp["w_branch"]).reshape(DEPTH, 2048, D)),
        "w_mix_out": f(inp["w_mix_out"]),
        "w_xq": f(inp["w_xq"]),
        "w_xkv": f(inp["w_xkv"]),
        "w_xo": f(inp["w_xo"]),
        "w_ffn_in": f(inp["w_ffn_in"]),
        "w_ffn_out": f(inp["w_ffn_out"]),
        "pvec": pv, "rvec": rv, "cst_f32": cf, "rot": rot,
    }


def _phase_xattn(self, l):
    em, nc = self.em, self.nc
    ps, psb = self.ps, self.psb
    with ExitStack() as st:
        self.alloc_norm_scratch(st)
        wq = self.tl(st, "wq", [128, 8, 512], BF16)
        wkv = self.tl(st, "wkv", [128, 8, 1024], BF16)
        wo = self.tl(st, "wo", [128, 4, D], BF16)
        mt = self.tl(st, "mt", [128, 8, MEM], F32)
        mh = self.tl(st, "mh", [128, 8, MEM], BF16)
        kT = self.tl(st, "kT", [128, 4, MEM], BF16)
        V = self.tl(st, "V", [128, 2, 512], BF16)
        xt = self.tl(st, "xt", [128, 8, 512], F32)
        ht = self.tl(st, "ht", [128, 8, 512], BF16)
        qh = self.tl(st, "qh", [128, 512], BF16)
        PT = self.tl(st, "PT", [128, 2, 512], BF16)
        rden = self.tl(st, "rden", [128, 512], F32)
        o = self.tl(st, "o", [128, 4, 512], BF16)
        bwq, bwkv, bwo = em.buf(), em.buf(), em.buf()
        bmt, bmh, bk, bv = em.buf(), em.buf(), em.buf(), em.buf()
        bx, bh, bq, bp, brd, bo = em.buf(), em.buf(), em.buf(), [em.buf(), em.buf()], em.buf(), em.buf()
        for k in range(8):
            self.load_w(wq[:, k, :], self.dram["w_xq"][l, k * 128:(k + 1) * 128, :], 128, 512, bwq)
            self.load_w(wkv[:, k, :], self.dram["w_xkv"][l, k * 128:(k + 1) * 128, :], 128, 1024, bwkv)
        for h in range(4):
            self.load_w(wo[:, h, :], self.dram["w_xo"][l, h * 128:(h + 1) * 128, :], 128, D, bwo)
        em.dma("sp", mt[:], self.dram["memT"].rearrange("(c p) s -> p c s", p=128), writes=[bmt])
        self.norm_tile(mt, mh, "norm_mem_g", l, self.sqt, bmt, bmh, width=MEM)
        for h in range(4):
            for k in range(8):
                em.op("pe", lambda e, k=k, h=h: e.matmul(ps[1][:, 0:MEM], lhsT=wkv[:, k, h * 128:(h + 1) * 128], rhs=mh[:, k, :],
                                                         start=(k == 0), stop=(k == 7)), reads=[bwkv, bmh], writes=[psb[1]])
            em.op("act", lambda e, h=h: e.copy(out=kT[:, h, :], in_=ps[1][:, 0:MEM]), reads=[psb[1]], writes=[bk])
        for mb in range(2):
            for k in range(8):
                em.op("pe", lambda e, k=k, mb=mb: e.matmul(ps[2][:], lhsT=mh[:, k, mb * 128:(mb + 1) * 128], rhs=wkv[:, k, 512:1024],
                                                           start=(k == 0), stop=(k == 7)), reads=[bwkv, bmh], writes=[psb[2]])
            em.op("act", lambda e, mb=mb: e.copy(out=V[:, mb, :], in_=ps[2][:]), reads=[psb[2]], writes=[bv])
        sc = 128.0 ** -0.5
        for t in range(self.NT):
            em.dma("sp", xt[:], self.xtile_ap("xs", t), reads=[self.b_xs], writes=[bx])
            self.norm_tile(xt, ht, "norm_x_g", l, self.sqt, bx, bh)
            for h in range(4):
                for k in range(8):
                    em.op("pe", lambda e, k=k, h=h: e.matmul(ps[1][:], lhsT=wq[:, k, h * 128:(h + 1) * 128], rhs=ht[:, k, :],
                                                             start=(k == 0), stop=(k == 7)), reads=[bwq, bh], writes=[psb[1]])
                em.op("act", lambda e: e.copy(out=qh[:], in_=ps[1][:]), reads=[psb[1]], writes=[bq])
                for mb in range(2):
                    em.op("pe", lambda e, h=h, mb=mb: e.matmul(ps[2 + mb][:], lhsT=kT[:, h, mb * 128:(mb + 1) * 128], rhs=qh[:],
                                                               start=True, stop=True), reads=[bk, bq], writes=[psb[2 + mb]])
                    em.op("act", lambda e, mb=mb: e.activation(out=PT[:, mb, :], in_=ps[2 + mb][:], func=AF.Exp, scale=sc),
                          reads=[psb[2 + mb]], writes=[bp[mb]])
                for mb in range(2):
                    em.op("pe", lambda e, h=h, mb=mb: e.matmul(ps[4][:], lhsT=V[:, mb, h * 128:(h + 1) * 128], rhs=PT[:, mb, :],
                                                               start=(mb == 0), stop=(mb == 1)), reads=[bv, bp[mb]], writes=[psb[4]])
                for mb in range(2):
                    em.op("pe", lambda e, mb=mb: e.matmul(ps[5][:], lhsT=self.ones_bf[:], rhs=PT[:, mb, :],
                                                          start=(mb == 0), stop=(mb == 1)), reads=[self.b_const, bp[mb]], writes=[psb[5]])
                em.op("dve", lambda e: e.reciprocal(out=rden[:], in_=ps[5][:]), reads=[psb[5]], writes=[brd])
                em.op("dve", lambda e, h=h: e.tensor_tensor(out=o[:, h, :], in0=ps[4][:], in1=rden[:], op=ALU.mult),
                      reads=[psb[4], brd], writes=[bo])
            for c in range(DC):
                po = 6 + (c % 2)
                for h in range(4):
                    em.op("pe", lambda e, c=c, h=h, po=po: e.matmul(ps[po][:], lhsT=wo[:, h, c * 128:(c + 1) * 128], rhs=o[:, h, :],
                                                                    start=(h == 0), stop=(h == 3)), reads=[bwo, bo], writes=[psb[po]])
                em.op("dve", lambda e, c=c, po=po: e.tensor_tensor(out=xt[:, c, :], in0=ps[po][:], in1=xt[:, c, :], op=ALU.add),
                      reads=[psb[po], bx], writes=[bx])
            em.dma("sp", self.xtile_ap("xs", t), xt[:], reads=[bx], writes=[self.b_xs])
        em.barrier()


Builder.phase_xattn = _phase_xattn


def _mix_begin(self, l, st):
    em, nc = self.em, self.nc
    self.hT = self.tl(st, "hT", [128, 8, self.S], BF16)
    self.b_hT = em.buf("hT")
    with ExitStack() as s2:
        self.alloc_norm_scratch(s2)
        xt = self.tl(s2, "xt", [128, 8, 512], F32)
        bx = em.buf()
        for t in range(self.NT):
            em.dma("sp", xt[:], self.xtile_ap("xs", t), reads=[self.b_xs], writes=[bx])
            self.norm_tile(xt, self.hT[:, :, t * 512:(t + 1) * 512], "norm_mix_g", l, self.sqt, bx, self.b_hT)
        em.barrier()


def _load_wk(self, dst, l, col0, n, wbuf):
    em = self.em
    i = self.stg_i
    self.stg_i = (i + 1) % 2
    stg, sbf = self.stg[i], self.stgb[i]
    src = self.dram["w_in"][l].rearrange("(k p) c -> p k c", p=128)[:, :, col0:col0 + n]
    sv = stg[:, 0:8 * n].rearrange("p (k c) -> p k c", k=8)
    em.dma("sp", sv, src, writes=[sbf])
    em.op("pool", lambda e: e.tensor_copy(out=dst, in_=sv), reads=[sbf], writes=[wbuf])


def _proj_fm(self, w, n, wbuf, pi, evac):
    em = self.em
    for t in range(self.NT):
        p = pi[t % len(pi)]
        for k in range(8):
            em.op("pe", lambda e, k=k, t=t, p=p: e.matmul(self.ps[p][0:n, :], lhsT=w[:, k, 0:n], rhs=self.hT[:, k, t * 512:(t + 1) * 512],
                                                          start=(k == 0), stop=(k == 7)), reads=[wbuf, self.b_hT], writes=[self.psb[p]])
        evac(t, p)


def _proj_tm(self, w, n, wbuf, tok_ap_fn, nblk, pi, evac):
    em = self.em
    for b in range(nblk):
        p = pi[b % len(pi)]
        for k in range(8):
            em.op("pe", lambda e, k=k, b=b, p=p: e.matmul(self.ps[p][:, 0:n], lhsT=tok_ap_fn(k, b), rhs=w[:, k, 0:n],
                                                          start=(k == 0), stop=(k == 7)), reads=[wbuf, self.b_hT], writes=[self.psb[p]])
        evac(b, p)


def _mixer_C(self, l):
    em, nc = self.em, self.nc
    ps, psb = self.ps, self.psb
    S, NB, NT = self.S, self.NB, self.NT
    lam_init = 0.8 - 0.6 * math.exp(-0.3 * l)
    with ExitStack() as st:
        wq = self.tl(st, "wq", [128, 8, 128], BF16)
        wk = self.tl(st, "wk", [128, 8, 128], BF16)
        wv = self.tl(st, "wv", [128, 8, 128], BF16)
        qT = self.tl(st, "qT", [128, S], BF16)
        kT = self.tl(st, "kT", [128, S], BF16)
        V = self.tl(st, "V", [128, NB, 128], BF16)
        PT = self.tl(st, "PT", [128, 2, 512], BF16)
        r1 = self.tl(st, "r1", [128, 512], F32)
        r2 = self.tl(st, "r2", [128, 512], F32)
        t1 = self.tl(st, "t1", [128, 512], F32)
        t2 = self.tl(st, "t2", [128, 512], F32)
        yo = self.tl(st, "yo", [128, 2, 512], BF16)
        lam = self.tl(st, "lam", [128, 8], F32)
        ltmp = self.tl(st, "ltmp", [128, 64], F32)
        bwq, bwk, bwv, bq, bk, bv = [em.buf() for _ in range(6)]
        bp = [em.buf(), em.buf()]
        br1, br2, bt1, bt2, blam = [em.buf() for _ in range(5)]
        byo = [em.buf(), em.buf()]
        b_y = self.b_ybr
        o = RV_OFF[("diff_lambda", l)]
        for i in range(2):
            em.op("dve", lambda e, i=i: e.tensor_tensor(out=ltmp[:], in0=self.rv[:, o + 128 * i:o + 128 * i + 64],
                                                         in1=self.rv[:, o + 128 * i + 64:o + 128 * i + 128], op=ALU.mult),
                  reads=[self.b_const], writes=[blam])
            em.op("dve", lambda e, i=i: e.reduce_sum(out=lam[:, i:i + 1], in_=ltmp[:], axis=mybir.AxisListType.X), reads=[blam], writes=[blam])
        em.op("act", lambda e: e.activation(out=lam[:, 2:4], in_=lam[:, 0:2], func=AF.Exp), reads=[blam], writes=[blam])
        em.op("dve", lambda e: e.tensor_tensor(out=lam[:, 4:5], in0=lam[:, 3:4], in1=lam[:, 2:3], op=ALU.subtract), reads=[blam], writes=[blam])
        em.op("dve", lambda e: e.tensor_scalar(out=lam[:, 5:6], in0=lam[:, 4:5], scalar1=-lam_init, scalar2=None, op0=ALU.add), reads=[blam], writes=[blam])
        em.op("dve", lambda e: e.memset(lam[:, 6:7], EPS / (1.0 - lam_init) ** 2), writes=[blam])
        sc = 64.0 ** -0.5
        for h in range(4):
            self.load_wk(wq[:], l, O_CQ + h * 128, 128, bwq)
            self.load_wk(wk[:], l, O_CK + h * 128, 128, bwk)
            self.load_wk(wv[:], l, O_CV + h * 128, 128, bwv)
            self.proj_fm(wq, 128, bwq, [6, 7], lambda t, p: em.op("act", lambda e: e.copy(out=qT[:, t * 512:(t + 1) * 512], in_=ps[p][:]), reads=[psb[p]], writes=[bq]))
            self.proj_fm(wk, 128, bwk, [6, 7], lambda t, p: em.op("dve", lambda e: e.tensor_copy(out=kT[:, t * 512:(t + 1) * 512], in_=ps[p][:]), reads=[psb[p]], writes=[bk]))
            self.proj_tm(wv, 128, bwv, lambda k, b: self.hT[:, k, b * 128:(b + 1) * 128], NB, [6, 7],
                         lambda b, p: em.op("act", lambda e: e.copy(out=V[:, b, :], in_=ps[p][:, 0:128]), reads=[psb[p]], writes=[bv]))
            for t in range(NT):
                nj = 4 * t + 4
                step = 0
                for c in range(2):
                    lo, hi = c * 64, (c + 1) * 64
                    for j in range(nj):
                        a = max(0, j - 4 * t)
                        q0 = a * 128
                        s = step % 2
                        step += 1
                        em.op("pe", lambda e, j=j, s=s, q0=q0, lo=lo, hi=hi, t=t: e.matmul(
                            ps[s][:, q0:512], lhsT=kT[lo:hi, j * 128:(j + 1) * 128], rhs=qT[lo:hi, t * 512 + q0:(t + 1) * 512],
                            start=True, stop=True), reads=[bk, bq], writes=[psb[s]])
                        em.op("act", lambda e, s=s, q0=q0: e.activation(out=PT[:, s, q0:512], in_=ps[s][:, q0:512], func=AF.Exp, scale=sc),
                              reads=[psb[s]], writes=[bp[s]])
                        if j >= 4 * t:
                            em.op("pool", lambda e, s=s, q0=q0: e.tensor_tensor(out=PT[:, s, q0:q0 + 128], in0=PT[:, s, q0:q0 + 128],
                                                                               in1=self.triU_bf[:], op=ALU.mult),
                                  reads=[bp[s], self.b_const], writes=[bp[s]])
                        em.op("pe", lambda e, j=j, s=s, q0=q0, c=c, nj=nj: e.matmul(
                            ps[2 + c][:, q0:512], lhsT=V[:, j, :], rhs=PT[:, s, q0:512], start=(j == 0), stop=(j == nj - 1)),
                            reads=[bv, bp[s]], writes=[psb[2 + c]])
                        em.op("pe", lambda e, j=j, s=s, q0=q0, c=c, nj=nj: e.matmul(
                            ps[4 + c][:, q0:512], lhsT=self.ones_bf[:], rhs=PT[:, s, q0:512], start=(j == 0), stop=(j == nj - 1)),
                            reads=[self.b_const, bp[s]], writes=[psb[4 + c]])
                em.op("dve", lambda e: e.reciprocal(out=r1[:], in_=ps[4][:]), reads=[psb[4]], writes=[br1])
                em.op("dve", lambda e: e.reciprocal(out=r2[:], in_=ps[5][:]), reads=[psb[5]], writes=[br2])
                em.op("dve", lambda e: e.tensor_tensor(out=t1[:], in0=ps[2][:], in1=r1[:], op=ALU.mult), reads=[psb[2], br1], writes=[bt1])
                em.op("dve", lambda e: e.tensor_tensor(out=t2[:], in0=ps[3][:], in1=r2[:], op=ALU.mult), reads=[psb[3], br2], writes=[bt2])
                em.op("dve", lambda e: e.scalar_tensor_tensor(out=t1[:], in0=t2[:], scalar=lam[:, 5:6], in1=t1[:], op0=ALU.mult, op1=ALU.add),
                      reads=[bt1, bt2, blam], writes=[bt1])
                em.op("act", lambda e: e.activation(out=t2[:], in_=t1[:], func=AF.Square), reads=[bt1], writes=[bt2])
                em.op("pe", lambda e: e.matmul(ps[6][:], lhsT=self.ones_f[:], rhs=t2[:], start=True, stop=True),
                      reads=[self.b_const, bt2], writes=[psb[6]])
                em.op("act", lambda e: e.activation(out=r1[:], in_=ps[6][:], func=AF.Sqrt, scale=1.0 / (128.0 * (1.0 - lam_init) ** 2), bias=lam[:, 6:7]),
                      reads=[psb[6], blam], writes=[br1])
                em.op("dve", lambda e: e.reciprocal(out=r1[:], in_=r1[:]), reads=[br1], writes=[br1])
                yi = t % 2
                em.op("dve", lambda e, yi=yi: e.scalar_tensor_tensor(out=yo[:, yi, :], in0=t1[:], scalar=self.pvcol("diff_subln_g", l), in1=r1[:],
                                                                     op0=ALU.mult, op1=ALU.mult), reads=[bt1, br1, self.b_const], writes=[byo[yi]])
                r0 = 1024 + h * 128
                em.dma("sp", self.dram["ybr"][r0:r0 + 128, t * 512:(t + 1) * 512], yo[:, yi, :], reads=[byo[yi]], writes=[b_y])
        em.barrier()


Builder.mix_begin = _mix_begin
Builder.load_wk = _load_wk
Builder.proj_fm = _proj_fm
Builder.proj_tm = _proj_tm
Builder.mixer_C = _mixer_C


def _gates(self, l):
    em = self.em
    ps, psb = self.ps, self.psb
    with ExitStack() as st:
        wg = [self.tl(st, "wg", [128, 8, 128], BF16) for _ in range(2)]
        bwg = [em.buf(), em.buf()]
        go = [self.tl(st, "go", [128, 512], BF16) for _ in range(2)]
        bgo = [em.buf(), em.buf()]
        b_g = self.b_gat
        n = 0
        for ic in range(32):
            w, bw = wg[ic % 2], bwg[ic % 2]
            self.load_wk(w[:], l, O_G + ic * 128, 128, bw)

            def evac(t, p, ic=ic):
                nonlocal n
                g, bg = go[n % 2], bgo[n % 2]
                n += 1
                em.op("act", lambda e: e.activation(out=g[:], in_=ps[p][:], func=AF.Sigmoid), reads=[psb[p]], writes=[bg])
                em.dma("sp", self.dram["gat"][ic * 128:(ic + 1) * 128, t * 512:(t + 1) * 512], g[:], reads=[bg], writes=[b_g])
            self.proj_fm(w, 128, bw, [0, 1, 2, 3], evac)
        em.barrier()


def _phase_merge(self, l):
    em, nc = self.em, self.nc
    ps, psb = self.ps, self.psb
    with ExitStack() as st:
        wb = self.tl(st, "wb", [128, 16, D], BF16)
        wo = self.tl(st, "wo", [128, 8, D], BF16)
        y = self.tl(st, "y", [128, 16, 512], BF16)
        g = self.tl(st, "g", [128, 32, 512], BF16)
        xt = self.tl(st, "xt", [128, 8, 512], F32)
        mg = self.tl(st, "mg", [128, 8, 512], BF16)
        acc = self.tl(st, "acc", [128, 512], F32)
        tmp = self.tl(st, "tmp", [128, 2, 512], F32)
        bwb = [em.buf() for _ in range(16)]
        bwo = [em.buf() for _ in range(8)]
        by, bg, bx, bmg, bacc = em.buf(), em.buf(), em.buf(), em.buf(), em.buf()
        btmp = [em.buf(), em.buf()]
        for r in range(16):
            self.load_w(wb[:, r, :], self.dram["w_branch"][l, r * 128:(r + 1) * 128, :], 128, D, bwb[r])
        for c in range(8):
            self.load_w(wo[:, c, :], self.dram["w_mix_out"][l, c * 128:(c + 1) * 128, :], 128, D, bwo[c])
        for t in range(self.NT):
            em.dma("sp", y[:], self.dram["ybr"].rearrange("(r p) s -> p r s", p=128)[:, :, t * 512:(t + 1) * 512], reads=[self.b_ybr], writes=[by])
            em.dma("sp", g[:], self.dram["gat"].rearrange("(r p) s -> p r s", p=128)[:, :, t * 512:(t + 1) * 512], reads=[self.b_gat], writes=[bg])
            em.dma("sp", xt[:], self.xtile_ap("xs", t), reads=[self.b_xs], writes=[bx])
            n = 0
            for c in range(8):
                for i in range(4):
                    p = n % 4
                    n += 1
                    for m in range(4):
                        em.op("pe", lambda e, p=p, i=i, m=m, c=c: e.matmul(ps[p][:], lhsT=wb[:, i * 4 + m, c * 128:(c + 1) * 128], rhs=y[:, i * 4 + m, :],
                                                                           start=(m == 0), stop=(m == 3)), reads=[bwb[i * 4 + m], by], writes=[psb[p]])
                    if i == 0:
                        em.op("dve", lambda e, p=p, c=c: e.tensor_tensor(out=acc[:], in0=ps[p][:], in1=g[:, c, :], op=ALU.mult),
                              reads=[psb[p], bg], writes=[bacc])
                    else:
                        ti = i % 2
                        em.op("dve", lambda e, p=p, c=c, i=i, ti=ti: e.tensor_tensor(out=tmp[:, ti, :], in0=ps[p][:], in1=g[:, i * 8 + c, :], op=ALU.mult),
                              reads=[psb[p], bg], writes=[btmp[ti]])
                        if i < 3:
                            em.op("pool", lambda e, ti=ti: e.tensor_tensor(out=acc[:], in0=acc[:], in1=tmp[:, ti, :], op=ALU.add),
                                  reads=[bacc, btmp[ti]], writes=[bacc])
                        else:
                            em.op("pool", lambda e, ti=ti, c=c: e.tensor_tensor(out=mg[:, c, :], in0=acc[:], in1=tmp[:, ti, :], op=ALU.add),
                                  reads=[bacc, btmp[ti]], writes=[bmg])
            for c2 in range(8):
                po = 4 + (c2 % 2)
                for c in range(8):
                    em.op("pe", lambda e, c=c, c2=c2, po=po: e.matmul(ps[po][:], lhsT=wo[:, c, c2 * 128:(c2 + 1) * 128], rhs=mg[:, c, :],
                                                                      start=(c == 0), stop=(c == 7)), reads=[bwo[c], bmg], writes=[psb[po]])
                em.op("dve", lambda e, c2=c2, po=po: e.tensor_tensor(out=xt[:, c2, :], in0=ps[po][:], in1=xt[:, c2, :], op=ALU.add),
                      reads=[psb[po], bx], writes=[bx])
            em.dma("sp", self.xtile_ap("xs", t), xt[:], reads=[bx], writes=[self.b_xs])
        em.barrier()


Builder.gates = _gates
Builder.phase_merge = _phase_merge


def _mixer_A(self, l):
    em, nc = self.em, self.nc
    ps, psb = self.ps, self.psb
    S, NB, NT = self.S, self.NB, self.NT
    with ExitStack() as st:
        wq = self.tl(st, "wq", [128, 8, 64], BF16)
        wk = self.tl(st, "wk", [128, 8, 64], BF16)
        wv = self.tl(st, "wv", [128, 8, 64], BF16)
        qT = self.tl(st, "qT", [64, S], BF16)
        kT = self.tl(st, "kT", [64, S], BF16)
        V = self.tl(st, "V", [128, NB, 64], BF16)
        PT = self.tl(st, "PT", [128, 2, 256], BF16)
        msk = self.tl(st, "msk", [128, 256], BF16)
        accN = self.tl(st, "accN", [64, S], F32)
        accD = self.tl(st, "accD", [64, S], F32)
        yo = self.tl(st, "yo", [64, 2, 512], BF16)
        bwq, bwk, bwv, bq, bk, bv, bm, baN, baD = [em.buf() for _ in range(9)]
        bp = [em.buf(), em.buf()]
        byo = [em.buf(), em.buf()]
        em.op("dve", lambda e: e.tensor_copy(out=msk[:, 0:128], in_=self.triU_bf[:]), reads=[self.b_const], writes=[bm])
        em.op("dve", lambda e: e.tensor_copy(out=msk[:, 128:256], in_=self.triL_bf[:]), reads=[self.b_const], writes=[bm])
        sc = 64.0 ** -0.5
        for h in range(8):
            self.load_wk(wq[:], l, O_AQ + h * 64, 64, bwq)
            self.load_wk(wk[:], l, O_AK + h * 64, 64, bwk)
            self.load_wk(wv[:], l, O_AV + h * 64, 64, bwv)
            self.proj_fm(wq, 64, bwq, [6, 7], lambda t, p: em.op("act", lambda e: e.copy(out=qT[:, t * 512:(t + 1) * 512], in_=ps[p][0:64, :]), reads=[psb[p]], writes=[bq]))
            self.proj_fm(wk, 64, bwk, [6, 7], lambda t, p: em.op("dve", lambda e: e.tensor_copy(out=kT[:, t * 512:(t + 1) * 512], in_=ps[p][0:64, :]), reads=[psb[p]], writes=[bk]))
            for pi_, dil in enumerate((1, 4, 16)):
                L = S // dil
                nbs = L // 128
                assert nbs >= 1 and nbs * 128 * dil == S

                def tok(r, n, cnt=128, dil=dil):
                    s0 = r + dil * n * 128
                    return slice(s0, s0 + dil * (cnt - 1) + 1, dil)
                self.proj_tm(wv, 64, bwv, lambda k, b: self.hT[:, k, tok(b // nbs, b % nbs)], NB, [6, 7],
                             lambda b, p: em.op("act", lambda e: e.copy(out=V[:, b, :], in_=ps[p][:, 0:64]), reads=[psb[p]], writes=[bv]))
                grp = min(4, nbs)
                step = 0
                gi = 0
                for r in range(dil):
                    for n0 in range(0, nbs, grp):
                        pn, pd = 2 + (gi % 2), 4 + (gi % 2)
                        gi += 1
                        for n in range(n0, n0 + grp):
                            s = step % 2
                            step += 1
                            pb = r * nbs + n
                            qs = (n - n0) * 128
                            w = 256 if n > 0 else 128
                            em.op("pe", lambda e, s=s, r=r, n=n: e.matmul(ps[s][:, 0:128], lhsT=kT[:, tok(r, n)], rhs=qT[:, tok(r, n)], start=True, stop=True),
                                  reads=[bk, bq], writes=[psb[s]])
                            if n > 0:
                                em.op("pe", lambda e, s=s, r=r, n=n: e.matmul(ps[s][:, 128:256], lhsT=kT[:, tok(r, n - 1)], rhs=qT[:, tok(r, n)], start=True, stop=True),
                                      reads=[bk, bq], writes=[psb[s]])
                            em.op("act", lambda e, s=s, w=w: e.activation(out=PT[:, s, 0:w], in_=ps[s][:, 0:w], func=AF.Exp, scale=sc),
                                  reads=[psb[s]], writes=[bp[s]])
                            em.op("pool", lambda e, s=s, w=w: e.tensor_tensor(out=PT[:, s, 0:w], in0=PT[:, s, 0:w], in1=msk[:, 0:w], op=ALU.mult),
                                  reads=[bp[s], bm], writes=[bp[s]])
                            last = (n == 0)
                            em.op("pe", lambda e, s=s, pb=pb, pn=pn, qs=qs, last=last: e.matmul(ps[pn][0:64, qs:qs + 128], lhsT=V[:, pb, :], rhs=PT[:, s, 0:128], start=True, stop=last),
                                  reads=[bv, bp[s]], writes=[psb[pn]])
                            if n > 0:
                                em.op("pe", lambda e, s=s, pb=pb, pn=pn, qs=qs: e.matmul(ps[pn][0:64, qs:qs + 128], lhsT=V[:, pb - 1, :], rhs=PT[:, s, 128:256], start=False, stop=True),
                                      reads=[bv, bp[s]], writes=[psb[pn]])
                            em.op("pe", lambda e, s=s, pd=pd, qs=qs, last=last: e.matmul(ps[pd][0:64, qs:qs + 128], lhsT=self.ones_bf[:, 0:64], rhs=PT[:, s, 0:128], start=True, stop=last),
                                  reads=[self.b_const, bp[s]], writes=[psb[pd]])
                            if n > 0:
                                em.op("pe", lambda e, s=s, pd=pd, qs=qs: e.matmul(ps[pd][0:64, qs:qs + 128], lhsT=self.ones_bf[:, 0:64], rhs=PT[:, s, 128:256], start=False, stop=True),
                                      reads=[self.b_const, bp[s]], writes=[psb[pd]])
                        tsl = tok(r, n0, grp * 128)
                        gw = grp * 128
                        if pi_ == 0:
                            em.op("dve", lambda e, pn=pn, tsl=tsl, gw=gw: e.tensor_copy(out=accN[:, tsl], in_=ps[pn][0:64, 0:gw]), reads=[psb[pn]], writes=[baN])
                            em.op("act", lambda e, pd=pd, tsl=tsl, gw=gw: e.copy(out=accD[:, tsl], in_=ps[pd][0:64, 0:gw]), reads=[psb[pd]], writes=[baD])
                        else:
                            em.op("dve", lambda e, pn=pn, tsl=tsl, gw=gw: e.tensor_tensor(out=accN[:, tsl], in0=ps[pn][0:64, 0:gw], in1=accN[:, tsl], op=ALU.add),
                                  reads=[psb[pn], baN], writes=[baN])
                            em.op("dve", lambda e, pd=pd, tsl=tsl, gw=gw: e.tensor_tensor(out=accD[:, tsl], in0=ps[pd][0:64, 0:gw], in1=accD[:, tsl], op=ALU.add),
                                  reads=[psb[pd], baD], writes=[baD])
            for t in range(NT):
                sl = slice(t * 512, (t + 1) * 512)
                yi = t % 2
                em.op("dve", lambda e, sl=sl: e.reciprocal(out=accD[:, sl], in_=accD[:, sl]), reads=[baD], writes=[baD])
                em.op("dve", lambda e, sl=sl, yi=yi: e.tensor_tensor(out=yo[:, yi, :], in0=accN[:, sl], in1=accD[:, sl], op=ALU.mult),
                      reads=[baN, baD], writes=[byo[yi]])
                em.dma("sp", self.dram["ybr"][h * 64:(h + 1) * 64, sl], yo[:, yi, :], reads=[byo[yi]], writes=[self.b_ybr])
        em.barrier()


Builder.mixer_A = _mixer_A


def _bcast_mid(ap2d, n):
    a = ap2d.ap
    return bass.AP(ap2d.tensor, ap2d.offset, [list(a[0]), [0, n], list(a[1])])


def _mixer_B(self, l):
    em, nc = self.em, self.nc
    ps, psb = self.ps, self.psb
    S, NB, NT = self.S, self.NB, self.NT
    lg = [math.log1p(-2.0 ** (-5.0 - h)) for h in range(4)]
    with ExitStack() as st:
        wq = self.tl(st, "wq", [128, 8, 64], BF16)
        wqs = self.tl(st, "wqs", [128, 8, 64], BF16)
        wk = self.tl(st, "wk", [128, 8, 64], BF16)
        wks = self.tl(st, "wks", [128, 8, 64], BF16)
        wv = self.tl(st, "wv", [128, 8, 128], BF16)
        wg = self.tl(st, "wg", [128, 8, 128], BF16)
        rot = self.tl(st, "rot", [64, 2, S], F32)
        qr = self.tl(st, "qr", [64, S], BF16)
        qi = self.tl(st, "qi", [64, S], BF16)
        kr = self.tl(st, "kr", [64, S], BF16)
        V = self.tl(st, "V", [128, NB, 128], BF16)
        gs = self.tl(st, "gs", [128, S], BF16)
        ta = self.tl(st, "ta", [64, 512], F32)
        tb = self.tl(st, "tb", [64, 512], F32)
        Sm = self.tl(st, "Sm", [128, 2, 128], BF16)
        kend = self.tl(st, "kend", [128, 2, 64], BF16)
        R = self.tl(st, "R", [64, 128], F32)
        Rb = self.tl(st, "Rb", [64, 128], BF16)
        ot = self.tl(st, "ot", [128, 512], F32)
        cen = self.tl(st, "cen", [128, 512], F32)
        sq = self.tl(st, "sq", [128, 512], F32)
        rs = self.tl(st, "rs", [128, 512], F32)
        yo = self.tl(st, "yo", [128, 2, 512], BF16)
        epst = self.tl(st, "epst", [128, 1], F32)
        (bwq, bwqs, bwk, bwks, bwv, bwg, brot, bqr, bqi, bkr, bv, bgs, bta, btb, bR, bRb, bot, bcen, bsq, brs) = [em.buf() for _ in range(20)]
        bSm = [em.buf(), em.buf()]
        bke = [em.buf(), em.buf()]
        byo = [em.buf(), em.buf()]
        em.dma("sp", rot[:], self.dram["rot"], writes=[brot])
        em.op("dve", lambda e: e.memset(epst[:], EPS), writes=[brs])
        psT = [ps[4].bitcast(BF16)]
        for h in range(4):
            self.load_wk(wq[:], l, O_BQ + h * 64, 64, bwq)
            self.load_wk(wqs[:, :, 0:32], l, O_BQ + h * 64 + 32, 32, bwqs)
            self.load_wk(wqs[:, :, 32:64], l, O_BQ + h * 64, 32, bwqs)
            self.load_wk(wk[:], l, O_BK + h * 64, 64, bwk)
            self.load_wk(wks[:, :, 0:32], l, O_BK + h * 64 + 32, 32, bwks)
            self.load_wk(wks[:, :, 32:64], l, O_BK + h * 64, 32, bwks)
            self.load_wk(wv[:], l, O_BV + h * 128, 128, bwv)
            self.load_wk(wg[:], l, O_BG + h * 128, 128, bwg)
            qdec = self.cf[0:64, CF_QDEC + h * 128:CF_QDEC + (h + 1) * 128]
            for (wa, wb_, ba, bb, dst, bdst) in ((wq, wqs, bwq, bwqs, qr, bqr), (wk, wks, bwk, bwks, kr, bkr)):
                for t in range(NT):
                    sl = slice(t * 512, (t + 1) * 512)
                    for (w_, bw_, p) in ((wa, ba, 6), (wb_, bb, 7)):
                        for k in range(8):
                            em.op("pe", lambda e, k=k, w_=w_, p=p, sl=sl: e.matmul(ps[p][0:64, :], lhsT=w_[:, k, :], rhs=self.hT[:, k, sl],
                                                                                 start=(k == 0), stop=(k == 7)), reads=[bw_, self.b_hT], writes=[psb[p]])
                    em.op("dve", lambda e, sl=sl: e.tensor_tensor(out=ta[:], in0=ps[6][0:64, :], in1=rot[:, 0, sl], op=ALU.mult), reads=[psb[6], brot], writes=[bta])
                    em.op("dve", lambda e, sl=sl: e.tensor_tensor(out=tb[:], in0=ps[7][0:64, :], in1=rot[:, 1, sl], op=ALU.mult), reads=[psb[7], brot], writes=[btb])
                    em.op("pool", lambda e, sl=sl, dst=dst: e.tensor_tensor(out=dst[:, sl], in0=ta[:], in1=tb[:], op=ALU.add), reads=[bta, btb], writes=[bdst])
                    if dst is qr:
                        em.op("pool", lambda e, sl=sl: e.tensor_tensor(out=qi[:, sl].rearrange("p (a b) -> p a b", a=4), in0=qr[:, sl].rearrange("p (a b) -> p a b", a=4),
                                                                       in1=_bcast_mid(qdec, 4), op=ALU.mult), reads=[bqr, self.b_const], writes=[bqi])
            self.proj_tm(wv, 128, bwv, lambda k, b: self.hT[:, k, b * 128:(b + 1) * 128], NB, [6, 7],
                         lambda b, p: em.op("act", lambda e: e.copy(out=V[:, b, :], in_=ps[p][:, 0:128]), reads=[psb[p]], writes=[bv]))
            self.proj_fm(wg, 128, bwg, [6, 7], lambda t, p: em.op("act", lambda e: e.activation(out=gs[:, t * 512:(t + 1) * 512], in_=ps[p][:], func=AF.Silu), reads=[psb[p]], writes=[bgs]))
            decT = self.cf[:, CF_DECT + h * 128:CF_DECT + (h + 1) * 128]
            cdec = math.exp(128.0 * lg[h])
            for n in range(NB):
                c0 = n * 128
                s = n % 2
                qs = (n % 4) * 128
                em.op("pe", lambda e, c0=c0: e.transpose(out=psT[0][:, 0:64], in_=kr[:, c0:c0 + 128], identity=self.ident_bf[0:64, 0:64]),
                      reads=[bkr, self.b_const], writes=[psb[4]])
                em.op("dve", lambda e, s=s: e.tensor_scalar(out=kend[:, s, :], in0=psT[0][:, 0:64], scalar1=self.cf[:, CF_KEND + h:CF_KEND + h + 1], scalar2=0.125,
                                                           op0=ALU.mult, op1=ALU.mult), reads=[psb[4], self.b_const], writes=[bke[s]])
                em.op("pe", lambda e, s=s, c0=c0: e.matmul(ps[s][:, 0:128], lhsT=kr[:, c0:c0 + 128], rhs=qr[:, c0:c0 + 128], start=True, stop=True),
                      reads=[bkr, bqr], writes=[psb[s]])
                em.op("dve", lambda e, s=s: e.scalar_tensor_tensor(out=Sm[:, s, :], in0=ps[s][:, 0:128], scalar=0.125, in1=decT, op0=ALU.mult, op1=ALU.mult),
                      reads=[psb[s], self.b_const], writes=[bSm[s]])
                em.op("pe", lambda e, s=s, n=n, qs=qs: e.matmul(ps[2][:, qs:qs + 128], lhsT=V[:, n, :], rhs=Sm[:, s, :], start=True, stop=(n == 0)),
                      reads=[bv, bSm[s]], writes=[psb[2]])
                if n > 0:
                    em.op("pe", lambda e, c0=c0, qs=qs: e.matmul(ps[2][:, qs:qs + 128], lhsT=Rb[:], rhs=qi[:, c0:c0 + 128], start=False, stop=True),
                          reads=[bRb, bqi], writes=[psb[2]])
                if n < NB - 1:
                    em.op("pe", lambda e, s=s, n=n: e.matmul(ps[3][0:64, 0:128], lhsT=kend[:, s, :], rhs=V[:, n, :], start=True, stop=True),
                          reads=[bke[s], bv], writes=[psb[3]])
                    if n == 0:
                        em.op("dve", lambda e: e.tensor_copy(out=R[:], in_=ps[3][0:64, 0:128]), reads=[psb[3]], writes=[bR])
                    else:
                        em.op("dve", lambda e: e.scalar_tensor_tensor(out=R[:], in0=R[:], scalar=cdec, in1=ps[3][0:64, 0:128], op0=ALU.mult, op1=ALU.add),
                              reads=[psb[3], bR], writes=[bR])
                    em.op("act", lambda e: e.copy(out=Rb[:], in_=R[:]), reads=[bR], writes=[bRb])
                if n % 4 == 3:
                    t = n // 4
                    sl = slice(t * 512, (t + 1) * 512)
                    em.op("act", lambda e: e.copy(out=ot[:], in_=ps[2][:]), reads=[psb[2]], writes=[bot])
                    em.op("pe", lambda e: e.matmul(ps[5][:], lhsT=self.ones_f[:], rhs=ot[:], start=True, stop=True), reads=[self.b_const, bot], writes=[psb[5]])
                    em.op("dve", lambda e: e.scalar_tensor_tensor(out=cen[:], in0=ps[5][:], scalar=-1.0 / 128, in1=ot[:], op0=ALU.mult, op1=ALU.add),
                          reads=[psb[5], bot], writes=[bcen])
                    em.op("act", lambda e: e.activation(out=sq[:], in_=cen[:], func=AF.Square), reads=[bcen], writes=[bsq])
                    em.op("pe", lambda e: e.matmul(ps[5][:], lhsT=self.ones_f[:], rhs=sq[:], start=True, stop=True), reads=[self.b_const, bsq], writes=[psb[5]])
                    em.op("act", lambda e: e.activation(out=rs[:], in_=ps[5][:], func=AF.Sqrt, scale=1.0 / 128, bias=epst[:, 0:1]), reads=[psb[5], brs], writes=[brs])
                    em.op("dve", lambda e: e.reciprocal(out=rs[:], in_=rs[:]), reads=[brs], writes=[brs])
                    em.op("dve", lambda e: e.scalar_tensor_tensor(out=cen[:], in0=cen[:], scalar=self.pvcol("ret_gn_g", l, h), in1=rs[:], op0=ALU.mult, op1=ALU.mult),
                          reads=[bcen, brs, self.b_const], writes=[bcen])
                    yi = t % 2
                    em.op("pool", lambda e, yi=yi, sl=sl: e.tensor_tensor(out=yo[:, yi, :], in0=cen[:], in1=gs[:, sl], op=ALU.mult), reads=[bcen, bgs], writes=[byo[yi]])
                    r0 = 512 + h * 128
                    em.dma("sp", self.dram["ybr"][r0:r0 + 128, sl], yo[:, yi, :], reads=[byo[yi]], writes=[self.b_ybr])
        em.barrier()


Builder.mixer_B = _mixer_B


def _bcast_last(ap2d, n):
    a = ap2d.ap
    return bass.AP(ap2d.tensor, ap2d.offset, [list(a[0]), list(a[1]), [0, n]])


def _mixer_D(self, l):
    em, nc = self.em, self.nc
    ps, psb = self.ps, self.psb
    S, NB, NT = self.S, self.NB, self.NT
    X = mybir.AxisListType.X
    bc = self.b_const
    with ExitStack() as st:
        x_tm = self.tl(st, "x_tm", [128, NB, 512], BF16)
        B_tm = self.tl(st, "B_tm", [128, NB, 256], BF16)
        BT = self.tl(st, "BT", [128, 2, S], BF16)
        CT = self.tl(st, "CT", [128, 2, S], BF16)
        bxtm, bBtm, bBT, bCT = [em.buf() for _ in range(4)]
        psT = ps[7].bitcast(BF16)
        with ExitStack() as s1:
            w = [self.tl(s1, "w", [128, 8, 128], BF16) for _ in range(2)]
            bw = [em.buf(), em.buf()]
            praw = self.tl(s1, "praw", [128, 3 + S], F32)
            cacc = self.tl(s1, "cacc", [128, 512], F32)
            cact = self.tl(s1, "cact", [128, 512], BF16)
            bpraw, bcacc, bcact = [em.buf() for _ in range(3)]
            em.op("dve", lambda e: e.memset(praw[:, 0:3], 0.0), writes=[bpraw])
            for c in range(8):
                ww, bww = w[c % 2], bw[c % 2]
                self.load_wk(ww[:], l, O_DX + c * 128, 128, bww)
                self.proj_fm(ww, 128, bww, [0, 1], lambda t, p: em.op("act", lambda e: e.copy(out=praw[:, 3 + t * 512:3 + (t + 1) * 512], in_=ps[p][:]),
                                                                      reads=[psb[p]], writes=[bpraw]))
                cw = PV_OFF[("conv_w", l)] + c * 4
                for t in range(NT):
                    t0 = t * 512
                    em.op("dve", lambda e, t0=t0: e.tensor_scalar(out=cacc[:], in0=praw[:, t0 + 3:t0 + 515], scalar1=self.pv[:, cw + 3:cw + 4], scalar2=None, op0=ALU.mult),
                          reads=[bpraw, bc], writes=[bcacc])
                    for k in range(3):
                        em.op("dve", lambda e, t0=t0, k=k: e.scalar_tensor_tensor(out=cacc[:], in0=praw[:, t0 + k:t0 + k + 512], scalar=self.pv[:, cw + k:cw + k + 1], in1=cacc[:],
                                                                                op0=ALU.mult, op1=ALU.add), reads=[bpraw, bcacc, bc], writes=[bcacc])
                    if c < 4:
                        dst, bd = cact[:], bcact
                    elif c < 6:
                        dst, bd = BT[:, c - 4, t0:t0 + 512], bBT
                    else:
                        dst, bd = CT[:, c - 6, t0:t0 + 512], bCT
                    em.op("act", lambda e, dst=dst: e.activation(out=dst, in_=cacc[:], func=AF.Silu, bias=self.pvcol("conv_b", l, c), scale=1.0),
                          reads=[bcacc, bc], writes=[bd])
                    if c < 6:
                        for q in range(4):
                            in_ap = cact[:, q * 128:(q + 1) * 128] if c < 4 else BT[:, c - 4, t0 + q * 128:t0 + (q + 1) * 128]
                            em.op("pe", lambda e, q=q, in_ap=in_ap: e.transpose(out=psT[:, q * 128:(q + 1) * 128], in_=in_ap, identity=self.ident_bf[:]),
                                  reads=[bd, bc], writes=[psb[7]])
                        if c < 4:
                            em.op("dve", lambda e, t=t, c=c: e.tensor_copy(out=x_tm[:, 4 * t:4 * t + 4, c * 128:(c + 1) * 128], in_=psT[:, 0:512].rearrange("p (a b) -> p a b", a=4)),
                                  reads=[psb[7]], writes=[bxtm])
                        else:
                            em.op("dve", lambda e, t=t, c=c: e.tensor_copy(out=B_tm[:, 4 * t:4 * t + 4, (c - 4) * 128:(c - 3) * 128], in_=psT[:, 0:512].rearrange("p (a b) -> p a b", a=4)),
                                  reads=[psb[7]], writes=[bBtm])
            em.barrier()
        with ExitStack() as s2:
            wz = self.tl(s2, "wz", [128, 8, 512], BF16)
            wdt = self.tl(s2, "wdt", [128, 8, 8], BF16)
            sm = self.tl(s2, "sm", [128, 64], F32)
            expA = self.tl(s2, "expA", [128, 8], F32)
            xdt = self.tl(s2, "xdt", [128, 512], BF16)
            xdd = self.tl(s2, "xdd", [128, 512], BF16)
            cbm = self.tl(s2, "cbm", [128, 256], BF16)
            lh = self.tl(s2, "lh", [128, 2, 128], F32)
            Lx = self.tl(s2, "Lx", [128, 2, 128], BF16)
            Mx = self.tl(s2, "Mx", [128, 2, 128], BF16)
            H = self.tl(s2, "H", [128, 512], F32)
            Hb = self.tl(s2, "Hb", [128, 512], BF16)
            t1 = self.tl(s2, "t1", [128, 512], F32)
            t2 = self.tl(s2, "t2", [128, 512], F32)
            zs = self.tl(s2, "zs", [128, 512], F32)
            yn = self.tl(s2, "yn", [128, 512], BF16)
            ysg = self.tl(s2, "ysg", [128, 4, 512], BF16)
            (bwz, bwdt, bsm, bexpA, bxdt, bxdd, bcbm, bH, bHb, bt1, bt2, bzs, byn, bysg) = [em.buf() for _ in range(14)]
            blh, bLx, bMx = [em.buf(), em.buf()], [em.buf(), em.buf()], [em.buf(), em.buf()]
            for k in range(8):
                self.load_w(wz[:, k, :], self.dram["w_in"][l, k * 128:(k + 1) * 128, O_DZ:O_DZ + 512], 128, 512, bwz)
            self.load_wk(wdt[:], l, O_DDT, 8, bwdt)
            oA, oB, oD, oG = RV_OFF[("A_log", l)], RV_OFF[("dt_bias", l)], RV_OFF[("D", l)], RV_OFF[("ssm_norm_g", l)]
            em.op("act", lambda e: e.activation(out=expA[:], in_=self.rv[:, oA:oA + 8], func=AF.Exp), reads=[bc], writes=[bexpA])
            triU_f = self.cf[:, CF_TRIU:CF_TRIU + 128]
            SL_f = self.cf[:, CF_SL:CF_SL + 128]
            for n in range(NB):
                c0 = n * 128
                for k in range(8):
                    em.op("pe", lambda e, k=k, c0=c0: e.matmul(ps[6][:, 0:8], lhsT=self.hT[:, k, c0:c0 + 128], rhs=wdt[:, k, :], start=(k == 0), stop=(k == 7)),
                          reads=[bwdt, self.b_hT], writes=[psb[6]])
                em.op("dve", lambda e: e.tensor_tensor(out=sm[:, 0:8], in0=ps[6][:, 0:8], in1=self.rv[:, oB:oB + 8], op=ALU.add), reads=[psb[6], bc], writes=[bsm])
                em.op("act", lambda e: e.activation(out=sm[:, 0:8], in_=sm[:, 0:8], func=AF.Exp), reads=[bsm], writes=[bsm])
                em.op("act", lambda e: e.activation(out=sm[:, 0:8], in_=sm[:, 0:8], func=AF.Ln, bias=self.ones_f[:, 0:1], scale=1.0), reads=[bsm, bc], writes=[bsm])
                em.op("dve", lambda e: e.scalar_tensor_tensor(out=sm[:, 8:16], in0=sm[:, 0:8], scalar=-1.0, in1=expA[:], op0=ALU.mult, op1=ALU.mult),
                      reads=[bsm, bexpA], writes=[bsm])
                em.op("pe", lambda e: e.matmul(ps[2][:, 256:264], lhsT=triU_f, rhs=sm[:, 8:16], start=True, stop=True), reads=[bc, bsm], writes=[psb[2]])
                em.op("pe", lambda e: e.matmul(ps[2][:, 264:272], lhsT=self.ones_f[:], rhs=sm[:, 8:16], start=True, stop=True), reads=[bc, bsm], writes=[psb[2]])
                em.op("dve", lambda e: e.tensor_copy(out=sm[:, 16:32], in_=ps[2][:, 256:272]), reads=[psb[2]], writes=[bsm])
                em.op("dve", lambda e: e.tensor_tensor(out=sm[:, 40:48], in0=sm[:, 24:32], in1=sm[:, 16:24], op=ALU.subtract), reads=[bsm], writes=[bsm])
                em.op("act", lambda e: e.activation(out=sm[:, 32:40], in_=sm[:, 16:24], func=AF.Exp), reads=[bsm], writes=[bsm])
                em.op("act", lambda e: e.activation(out=sm[:, 40:48], in_=sm[:, 40:48], func=AF.Exp), reads=[bsm], writes=[bsm])
                em.op("act", lambda e: e.activation(out=sm[:, 48:56], in_=sm[:, 24:32], func=AF.Exp), reads=[bsm], writes=[bsm])
                xv = x_tm[:, n, :].rearrange("p (h d) -> p h d", h=8)
                em.op("dve", lambda e, xv=xv: e.tensor_tensor(out=xdt[:].rearrange("p (h d) -> p h d", h=8), in0=xv, in1=_bcast_last(sm[:, 0:8], 64), op=ALU.mult),
                      reads=[bxtm, bsm], writes=[bxdt])
                em.op("pool", lambda e: e.tensor_tensor(out=xdd[:].rearrange("p (h d) -> p h d", h=8), in0=xdt[:].rearrange("p (h d) -> p h d", h=8),
                                                        in1=_bcast_last(sm[:, 40:48], 64), op=ALU.mult), reads=[bxdt, bsm], writes=[bxdd])
                for g in range(2):
                    em.op("pe", lambda e, g=g, c0=c0: e.matmul(ps[2][:, g * 128:(g + 1) * 128], lhsT=BT[:, g, c0:c0 + 128], rhs=CT[:, g, c0:c0 + 128], start=True, stop=True),
                          reads=[bBT, bCT], writes=[psb[2]])
                em.op("dve", lambda e: e.tensor_tensor(out=cbm[:].rearrange("p (g i) -> p g i", g=2), in0=ps[2][:, 0:256].rearrange("p (g i) -> p g i", g=2),
                                                       in1=_bcast_mid(triU_f, 2), op=ALU.mult), reads=[psb[2], bc], writes=[bcbm])
                if n > 0:
                    for g in range(2):
                        em.op("pe", lambda e, g=g, c0=c0: e.matmul(ps[4][:, g * 256:(g + 1) * 256], lhsT=CT[:, g, c0:c0 + 128], rhs=Hb[:, g * 256:(g + 1) * 256], start=True, stop=True),
                              reads=[bCT, bHb], writes=[psb[4]])
                for h in range(8):
                    s = h % 2
                    g = h // 4
                    em.op("dve", lambda e, h=h, s=s: e.tensor_scalar(out=lh[:, s, :], in0=SL_f, scalar1=sm[:, 8 + h:9 + h], scalar2=None, op0=ALU.mult),
                          reads=[bc, bsm], writes=[blh[s]])
                    em.op("pe", lambda e, s=s: e.matmul(ps[s][:, 0:128], lhsT=lh[:, s, :], rhs=triU_f, start=True, stop=True), reads=[blh[s], bc], writes=[psb[s]])
                    em.op("act", lambda e, s=s: e.activation(out=Lx[:, s, :], in_=ps[s][:, 0:128], func=AF.Exp), reads=[psb[s]], writes=[bLx[s]])
                    em.op("pool", lambda e, s=s, g=g: e.tensor_tensor(out=Mx[:, s, :], in0=Lx[:, s, :], in1=cbm[:, g * 128:(g + 1) * 128], op=ALU.mult),
                          reads=[bLx[s], bcbm], writes=[bMx[s]])
                    em.op("pe", lambda e, s=s, h=h: e.matmul(ps[3][:, h * 64:(h + 1) * 64], lhsT=Mx[:, s, :], rhs=xdt[:, h * 64:(h + 1) * 64], start=True, stop=True),
                          reads=[bMx[s], bxdt], writes=[psb[3]])
                if n > 0:
                    em.op("dve", lambda e: e.tensor_tensor(out=t1[:].rearrange("p (h d) -> p h d", h=8), in0=ps[4][:].rearrange("p (h d) -> p h d", h=8),
                                                           in1=_bcast_last(sm[:, 32:40], 64), op=ALU.mult), reads=[psb[4], bsm], writes=[bt1])
                    em.op("dve", lambda e: e.tensor_tensor(out=t1[:], in0=ps[3][:], in1=t1[:], op=ALU.add), reads=[psb[3], bt1], writes=[bt1])
                else:
                    em.op("dve", lambda e: e.tensor_copy(out=t1[:], in_=ps[3][:]), reads=[psb[3]], writes=[bt1])
                em.op("pool", lambda e, xv=xv: e.tensor_tensor(out=t2[:].rearrange("p (h d) -> p h d", h=8), in0=xv, in1=_bcast_last(self.rv[:, oD:oD + 8], 64), op=ALU.mult),
                      reads=[bxtm, bc], writes=[bt2])
                em.op("pool", lambda e: e.tensor_tensor(out=t1[:], in0=t1[:], in1=t2[:], op=ALU.add), reads=[bt1, bt2], writes=[bt1])
                if n < NB - 1:
                    for g in range(2):
                        em.op("pe", lambda e, g=g, n=n: e.matmul(ps[5][:, g * 256:(g + 1) * 256], lhsT=B_tm[:, n, g * 128:(g + 1) * 128], rhs=xdd[:, g * 256:(g + 1) * 256], start=True, stop=True),
                              reads=[bBtm, bxdd], writes=[psb[5]])
                    if n == 0:
                        em.op("dve", lambda e: e.tensor_copy(out=H[:], in_=ps[5][:]), reads=[psb[5]], writes=[bH])
                    else:
                        em.op("pool", lambda e: e.tensor_tensor(out=H[:].rearrange("p (h d) -> p h d", h=8), in0=H[:].rearrange("p (h d) -> p h d", h=8),
                                                                in1=_bcast_last(sm[:, 48:56], 64), op=ALU.mult), reads=[bH, bsm], writes=[bH])
                        em.op("dve", lambda e: e.tensor_tensor(out=H[:], in0=ps[5][:], in1=H[:], op=ALU.add), reads=[psb[5], bH], writes=[bH])
                    em.op("act", lambda e: e.copy(out=Hb[:], in_=H[:]), reads=[bH], writes=[bHb])
                for k in range(8):
                    em.op("pe", lambda e, k=k, c0=c0: e.matmul(ps[6][:], lhsT=self.hT[:, k, c0:c0 + 128], rhs=wz[:, k, :], start=(k == 0), stop=(k == 7)),
                          reads=[bwz, self.b_hT], writes=[psb[6]])
                em.op("act", lambda e: e.activation(out=zs[:], in_=ps[6][:], func=AF.Silu), reads=[psb[6]], writes=[bzs])
                em.op("dve", lambda e: e.tensor_tensor(out=t1[:], in0=t1[:], in1=zs[:], op=ALU.mult), reads=[bt1, bzs], writes=[bt1])
                em.op("act", lambda e: e.activation(out=t2[:], in_=t1[:], func=AF.Square), reads=[bt1], writes=[bt2])
                em.op("dve", lambda e: e.reduce_sum(out=sm[:, 56:57], in_=t2[:], axis=X), reads=[bt2], writes=[bsm])
                em.op("act", lambda e: e.activation(out=sm[:, 57:58], in_=sm[:, 56:57], func=AF.Sqrt, scale=1.0 / 512, bias=self.eps_c[:, 0:1]), reads=[bsm, bc], writes=[bsm])
                em.op("dve", lambda e: e.reciprocal(out=sm[:, 57:58], in_=sm[:, 57:58]), reads=[bsm], writes=[bsm])
                em.op("dve", lambda e: e.scalar_tensor_tensor(out=yn[:], in0=t1[:], scalar=sm[:, 57:58], in1=self.rv[:, oG:oG + 512], op0=ALU.mult, op1=ALU.mult),
                      reads=[bt1, bsm, bc], writes=[byn])
                qn = n % 4
                for c in range(4):
                    em.op("pe", lambda e, c=c: e.transpose(out=psT[:, c * 128:(c + 1) * 128], in_=yn[:, c * 128:(c + 1) * 128], identity=self.ident_bf[:]),
                          reads=[byn, bc], writes=[psb[7]])
                em.op("act", lambda e, qn=qn: e.copy(out=ysg[:, :, qn * 128:(qn + 1) * 128], in_=psT[:, 0:512].rearrange("p (a b) -> p a b", a=4)), reads=[psb[7]], writes=[bysg])
                if qn == 3:
                    t = n // 4
                    em.dma("sp", self.dram["ybr"][1536:2048, t * 512:(t + 1) * 512].rearrange("(c p) s -> p c s", p=128), ysg[:], reads=[bysg], writes=[self.b_ybr])
            em.barrier()


Builder.mixer_D = _mixer_D


def build_program(S=4096):
    b = Builder(S)
    em = b.em
    b.phase_copy_in()
    b.b_ybr = em.buf("ybr")
    b.b_gat = em.buf("gat")
    for l in range(DEPTH):
        with ExitStack() as st:
            b.mix_begin(l, st)
            b.gates(l)
            b.mixer_C(l)
            b.mixer_A(l)
            b.mixer_B(l)
            b.mixer_D(l)
        em.barrier()
        b.phase_merge(l)
        b.phase_xattn(l)
        b.phase_ffn(l)
    toks = b.phase_final()
    em.finish(toks)
    return b


_CACHE = {}


def kernel(**inputs):
    S = 4096
    if "b" not in _CACHE:
        _CACHE["b"] = build_program(S)
    b = _CACHE["b"]
    in_maps = [make_in_map(inputs, c % 4, S) for c in range(8)]
    res = run_bass_kernel_spmd(b.nc, in_maps, core_ids=list(range(8)))
    out = np.stack([np.asarray(res.results[c]["out"]).T for c in range(4)], axis=0)
    return np.ascontiguousarray(out.astype(np.float32))
```

```python
import math
from contextlib import ExitStack

import numpy as np
import concourse.bass as bass
import concourse.mybir as mybir
from concourse.bass_utils import run_bass_kernel_spmd

F32 = mybir.dt.float32
BF16 = mybir.dt.bfloat16
AF = mybir.ActivationFunctionType
ALU = mybir.AluOpType

D = 1024
DC = 8
DEPTH = 2
MEM = 256
EPS = 1e-6
FFN = 2816
FC = 22
IN_W = 10248
O_AQ, O_AK, O_AV = 0, 512, 1024
O_BQ, O_BK, O_BV, O_BG = 1536, 1792, 2048, 2560
O_CQ, O_CK, O_CV = 3072, 3584, 4096
O_DZ, O_DX, O_DDT, O_G = 4608, 5120, 6144, 6152


class Buf:
    __slots__ = ("name", "w", "r")

    def __init__(self, name):
        self.name = name
        self.w = None
        self.r = []


class Em:
    NRING = 24

    def __init__(self, nc):
        self.nc = nc
        self.eng = {"pe": nc.tensor, "act": nc.scalar, "dve": nc.vector, "pool": nc.gpsimd, "sp": nc.sync}
        self.sem = {}
        self.cnt = {}
        for e in ("pe", "act", "dve", "pool"):
            self.sem[e] = nc.alloc_semaphore("s_" + e)
            self.cnt[e] = 0
        self.known = {e: {} for e in self.eng}
        self.ring = {}
        self.ring_use = {}
        self.ring_next = {}
        for q in ("sp", "pool", "act"):
            self.ring[q] = [nc.alloc_semaphore("d_%s_%d" % (q, i)) for i in range(self.NRING)]
            self.ring_use[q] = [0] * self.NRING
            self.ring_next[q] = 0
        self.dma_tokens = []
        self.nbuf = 0

    def buf(self, name=None):
        self.nbuf += 1
        return Buf(name or ("b%d" % self.nbuf))

    def _wait(self, engine, tok):
        if tok is None:
            return
        key, sem, val = tok
        if key == engine and engine == "pe":
            return
        kn = self.known[engine]
        if kn.get(key, 0) >= val:
            return
        self.eng[engine].wait_ge(sem, val)
        kn[key] = val

    def _deps(self, engine, reads, writes):
        for b in reads:
            self._wait(engine, b.w)
        for b in writes:
            self._wait(engine, b.w)
            for t in b.r:
                self._wait(engine, t)

    def _commit(self, tok, reads, writes):
        for b in reads:
            b.r.append(tok)
            if len(b.r) > 12:
                last = {}
                for t in b.r:
                    if t[0] not in last or last[t[0]][2] < t[2]:
                        last[t[0]] = t
                b.r = list(last.values())
        for b in writes:
            b.w = tok
            b.r = []

    def op(self, engine, fn, reads=(), writes=()):
        self._deps(engine, reads, writes)
        inst = fn(self.eng[engine])
        self.cnt[engine] += 1
        inst.then_inc(self.sem[engine], 1)
        tok = (engine, self.sem[engine], self.cnt[engine])
        self._commit(tok, reads, writes)
        return tok

    def dma(self, q, out, in_, reads=(), writes=()):
        i = self.ring_next[q]
        self.ring_next[q] = (i + 1) % self.NRING
        sem = self.ring[q][i]
        key = ("dma", q, i)
        prior = 16 * self.ring_use[q][i]
        if prior:
            self._wait(q, (key, sem, prior))
        self._deps(q, reads, writes)
        self.eng[q].dma_start(out=out, in_=in_).then_inc(sem, 16)
        self.ring_use[q][i] += 1
        tok = (key, sem, 16 * self.ring_use[q][i])
        self._commit(tok, reads, writes)
        self.dma_tokens.append(tok)
        if len(self.dma_tokens) > 3 * self.NRING:
            self.dma_tokens = self.dma_tokens[-3 * self.NRING:]
        return tok

    def barrier(self):
        toks = [(e, self.sem[e], self.cnt[e]) for e in ("pe", "act", "dve", "pool") if self.cnt[e]]
        for q in self.ring:
            for i in range(self.NRING):
                if self.ring_use[q][i]:
                    toks.append((("dma", q, i), self.ring[q][i], 16 * self.ring_use[q][i]))
        for e in self.eng:
            for t in toks:
                if t[0] == e and e == "pe":
                    continue
                self._wait(e, t)

    def finish(self, toks):
        for t in toks:
            self._wait("sp", t)


def _ap(t):
    return t if isinstance(t, bass.AP) else t.ap()


class Ctx:
    def __init__(self, S, depth=DEPTH):
        self.S = S
        self.NT = S // 512
        self.NB = S // 128
        self.depth = depth
        nc = self.nc = bass.Bass("TRN2", target_bir_lowering=False)
        self.em = Em(nc)
        self.es = ExitStack()
        dr = self.dram = {}

        def din(name, shape, dt=F32):
            dr[name] = nc.dram_tensor(name, list(shape), dt, kind="ExternalInput").ap()

        din("xT", [D, S])
        din("memT", [D, MEM])
        din("w_in", [depth, D, IN_W])
        din("w_branch", [depth, 2048, D])
        din("w_mix_out", [depth, D, D])
        din("w_xq", [depth, D, 512])
        din("w_xkv", [depth, D, 1024])
        din("w_xo", [depth, 512, D])
        din("w_ffn_in", [depth, D, 2 * FFN])
        din("w_ffn_out", [depth, FFN, D])
        din("pvec", [128, PV_N])
        din("rvec", [128, RV_N])
        din("cst_f32", [128, CF_N])
        din("rot", [64, 2, S])
        dr["out"] = nc.dram_tensor("out", [D, S], F32, kind="ExternalOutput").ap()
        dr["xs"] = nc.dram_tensor("xs", [D, S], F32, kind="Internal").ap()
        dr["ybr"] = nc.dram_tensor("ybr", [2048, S], BF16, kind="Internal").ap()
        dr["gat"] = nc.dram_tensor("gat", [4096, S], BF16, kind="Internal").ap()

    def sb(self, name, shape, dt):
        return self.es.enter_context(self.nc.sbuf_tensor(name, list(shape), dt))

    _uid = 0

    def tl(self, st, name, shape, dt):
        Ctx._uid += 1
        return st.enter_context(self.nc.sbuf_tensor("%s_%d" % (name, Ctx._uid), list(shape), dt))


def _pv_layout():
    off = {}
    n = 0
    for l in range(DEPTH):
        for nm, w in (("norm_mix_g", 8), ("norm_x_g", 8), ("norm_mem_g", 8), ("norm_ffn_g", 8),
                      ("ret_gn_g", 4), ("diff_subln_g", 1), ("conv_w", 32), ("conv_b", 8)):
            off[(nm, l)] = n
            n += w
    off[("norm_f_g", 0)] = n
    n += 8
    return off, n


PV_OFF, PV_N = _pv_layout()


def _rv_layout():
    off = {}
    n = 0
    for l in range(DEPTH):
        for nm, w in (("dt_bias", 8), ("A_log", 8), ("D", 8), ("ssm_norm_g", 512), ("diff_lambda", 256)):
            off[(nm, l)] = n
            n += w
    return off, n


RV_OFF, RV_N = _rv_layout()

CF_IDENT, CF_TRIU, CF_TRIL, CF_DECT, CF_KEND, CF_QDEC, CF_SL = 0, 128, 256, 384, 896, 900, 1412
CF_N = 1540


class Builder(Ctx):
    def __init__(self, S, depth=DEPTH):
        super().__init__(S, depth)
        nc, em = self.nc, self.em
        self.pv = self.sb("pv", [128, PV_N], F32)
        self.rv = self.sb("rv", [128, RV_N], F32)
        self.cf = self.sb("cf", [128, CF_N], F32)
        self.ones_bf = self.sb("ones_bf", [128, 128], BF16)
        self.ones_f = self.sb("ones_f", [128, 128], F32)
        self.ident_bf = self.sb("ident_bf", [128, 128], BF16)
        self.triU_bf = self.sb("triU_bf", [128, 128], BF16)
        self.triL_bf = self.sb("triL_bf", [128, 128], BF16)
        self.b_const = em.buf("const")
        self.ps = []
        self.psb = []
        for i in range(8):
            self.ps.append(self.es.enter_context(nc.psum_tensor("ps%d" % i, [128, 512], F32)))
            self.psb.append(em.buf("ps%d" % i))
        self.stg = [self.sb("stg%d" % i, [128, 1024], F32) for i in range(2)]
        self.stgb = [em.buf("stg%d" % i) for i in range(2)]
        self.stg_i = 0
        d = self.dram
        bc = self.b_const
        em.dma("sp", self.pv[:], d["pvec"], writes=[bc])
        em.dma("sp", self.rv[:], d["rvec"], writes=[bc])
        em.dma("sp", self.cf[:], d["cst_f32"], writes=[bc])
        em.op("dve", lambda e: e.memset(self.ones_bf[:], 1.0), writes=[bc])
        em.op("dve", lambda e: e.memset(self.ones_f[:], 1.0), writes=[bc])
        self.eps_c = self.sb("eps_c", [128, 1], F32)
        em.op("dve", lambda e: e.memset(self.eps_c[:], EPS), writes=[bc])
        em.op("dve", lambda e: e.tensor_copy(out=self.ident_bf[:], in_=self.cf[:, CF_IDENT:CF_IDENT + 128]), reads=[bc], writes=[bc])
        em.op("dve", lambda e: e.tensor_copy(out=self.triU_bf[:], in_=self.cf[:, CF_TRIU:CF_TRIU + 128]), reads=[bc], writes=[bc])
        em.op("dve", lambda e: e.tensor_copy(out=self.triL_bf[:], in_=self.cf[:, CF_TRIL:CF_TRIL + 128]), reads=[bc], writes=[bc])
        em.barrier()

    def pvcol(self, name, l, j=0):
        o = PV_OFF[(name, l)] + j
        return self.pv[:, o:o + 1]

    def load_w(self, dst, src, nrows, ncols, wbuf):
        em = self.em
        c0 = 0
        while c0 < ncols:
            w = min(1024, ncols - c0)
            i = self.stg_i
            self.stg_i = (i + 1) % 2
            st, sbf = self.stg[i], self.stgb[i]
            em.dma("sp", st[0:nrows, 0:w], src[:, c0:c0 + w], writes=[sbf])
            o, ww, cc = dst, w, c0
            em.op("pool", lambda e, st=st, o=o, ww=ww, cc=cc: e.tensor_copy(out=o[:, cc:cc + ww], in_=st[0:nrows, 0:ww]),
                  reads=[sbf], writes=[wbuf])
            c0 += w

    def norm_tile(self, xt, ht, gname, l, sq, xb, hb, width=512, pi=0):
        em = self.em
        ps, psb = self.ps[pi], self.psb[pi]
        bsq = self.b_sq
        for c in range(DC):
            em.op("act", lambda e, c=c: e.activation(out=sq[:, c % 2, 0:width], in_=xt[:, c, 0:width], func=AF.Square),
                  reads=[xb], writes=[bsq[c % 2]])
            em.op("pe", lambda e, c=c: e.matmul(ps[:, 0:width], lhsT=self.ones_bf[:], rhs=sq[:, c % 2, 0:width],
                                                start=(c == 0), stop=(c == DC - 1)),
                  reads=[bsq[c % 2], self.b_const], writes=[psb])
        rs = self.rstd
        em.op("act", lambda e: e.activation(out=rs[:, 0:width], in_=ps[:, 0:width], func=AF.Sqrt, scale=1.0 / D, bias=self.eps_t[:, 0:1]),
              reads=[psb, self.b_const], writes=[self.b_rstd])
        em.op("dve", lambda e: e.reciprocal(out=rs[:, 0:width], in_=rs[:, 0:width]), reads=[self.b_rstd], writes=[self.b_rstd])
        for c in range(DC):
            em.op("dve", lambda e, c=c: e.scalar_tensor_tensor(out=ht[:, c, 0:width], in0=xt[:, c, 0:width],
                                                                scalar=self.pvcol(gname, l, c), in1=rs[:, 0:width],
                                                                op0=ALU.mult, op1=ALU.mult),
                  reads=[xb, self.b_rstd, self.b_const], writes=[hb])

    def alloc_norm_scratch(self, st):
        em = self.em
        self.sqt = self.tl(st, "sqt", [128, 2, 512], BF16)
        self.rstd = self.tl(st, "rstd", [128, 512], F32)
        self.eps_t = self.tl(st, "eps_t", [128, 1], F32)
        self.b_sq = [em.buf("sq0"), em.buf("sq1")]
        self.b_rstd = em.buf("rstd")
        em.op("dve", lambda e: e.memset(self.eps_t[:], EPS), writes=[self.b_const])

    def xtile_ap(self, name, t, width=512):
        return self.dram[name].rearrange("(c p) s -> p c s", p=128)[:, :, t * width:(t + 1) * width]

    def phase_copy_in(self):
        em = self.em
        b = self.b_xs = em.buf("xs")
        for c in range(DC):
            em.dma("sp", self.dram["xs"][c * 128:(c + 1) * 128, :], self.dram["xT"][c * 128:(c + 1) * 128, :], writes=[b])
        em.barrier()

    def phase_ffn(self, l):
        em, nc = self.em, self.nc
        with ExitStack() as st:
            self.alloc_norm_scratch(st)
            w1 = self.tl(st, "w1", [128, 8, 2 * FFN], BF16)
            w2 = self.tl(st, "w2", [128, FC, D], BF16)
            xt1 = self.tl(st, "xt0", [128, 8, 512], F32)
            xt = [xt1, xt1]
            ht = self.tl(st, "ht", [128, 8, 512], BF16)
            u = self.tl(st, "u", [128, FC, 512], BF16)
            sa = self.tl(st, "sa", [128, 2, 512], BF16)
            bw1 = [em.buf() for _ in range(8)]
            bw2 = [em.buf() for _ in range(FC)]
            bx0 = em.buf()
            bx = [bx0, bx0]
            bh, bu, bsa = em.buf(), em.buf(), [em.buf(), em.buf()]
            for k in range(8):
                self.load_w(w1[:, k, :], self.dram["w_ffn_in"][l, k * 128:(k + 1) * 128, :], 128, 2 * FFN, bw1[k])
            for f in range(FC):
                self.load_w(w2[:, f, :], self.dram["w_ffn_out"][l, f * 128:(f + 1) * 128, :], 128, D, bw2[f])
            for t in range(self.NT):
                cur = 0
                em.dma("sp", xt[cur][:], self.xtile_ap("xs", t), reads=[self.b_xs], writes=[bx[cur]])
                self.norm_tile(xt[cur], ht, "norm_ffn_g", l, self.sqt, bx[cur], bh)
                for f in range(FC):
                    pa, pb = 1 + (f % 2) * 2, 2 + (f % 2) * 2
                    for k in range(8):
                        em.op("pe", lambda e, k=k, f=f, pa=pa: e.matmul(self.ps[pa][:], lhsT=w1[:, k, f * 128:(f + 1) * 128], rhs=ht[:, k, :],
                                                                    start=(k == 0), stop=(k == 7)),
                              reads=[bw1[k], bh], writes=[self.psb[pa]])
                    for k in range(8):
                        em.op("pe", lambda e, k=k, f=f, pb=pb: e.matmul(self.ps[pb][:], lhsT=w1[:, k, FFN + f * 128:FFN + (f + 1) * 128], rhs=ht[:, k, :],
                                                                    start=(k == 0), stop=(k == 7)),
                              reads=[bw1[k], bh], writes=[self.psb[pb]])
                    si = f % 2
                    em.op("act", lambda e, pa=pa, si=si: e.activation(out=sa[:, si, :], in_=self.ps[pa][:], func=AF.Silu),
                          reads=[self.psb[pa]], writes=[bsa[si]])
                    em.op("dve", lambda e, pb=pb, si=si, f=f: e.tensor_tensor(out=u[:, f, :], in0=self.ps[pb][:], in1=sa[:, si, :], op=ALU.mult),
                          reads=[self.psb[pb], bsa[si]], writes=[bu])
                for c in range(DC):
                    po = 5 + (c % 2)
                    for f in range(FC):
                        em.op("pe", lambda e, c=c, f=f, po=po: e.matmul(self.ps[po][:], lhsT=w2[:, f, c * 128:(c + 1) * 128], rhs=u[:, f, :],
                                                                    start=(f == 0), stop=(f == FC - 1)),
                              reads=[bw2[f], bu], writes=[self.psb[po]])
                    em.op("dve", lambda e, c=c, po=po, cur=cur: e.tensor_tensor(out=xt[cur][:, c, :], in0=self.ps[po][:], in1=xt[cur][:, c, :], op=ALU.add),
                          reads=[self.psb[po], bx[cur]], writes=[bx[cur]])
                em.dma("sp", self.xtile_ap("xs", t), xt[cur][:], reads=[bx[cur]], writes=[self.b_xs])
            em.barrier()

    def phase_final(self):
        em, nc = self.em, self.nc
        toks = []
        with ExitStack() as st:
            self.alloc_norm_scratch(st)
            xt = self.tl(st, "xt", [128, 8, 512], F32)
            ot = self.tl(st, "ot", [128, 8, 512], F32)
            bx, bo = em.buf(), em.buf()
            b_out = em.buf("out")
            for t in range(self.NT):
                em.dma("sp", xt[:], self.xtile_ap("xs", t), reads=[self.b_xs], writes=[bx])
                self.norm_tile(xt, ot, "norm_f_g", 0, self.sqt, bx, bo)
                toks.append(em.dma("sp", self.xtile_ap("out", t), ot[:], reads=[bo], writes=[b_out]))
            em.barrier()
        return toks


def _chunked(v, nch):
    return np.ascontiguousarray(np.asarray(v, np.float32).reshape(nch, 128).T)


def make_tables(inp, S):
    pv = np.zeros((128, PV_N), np.float32)
    rv = np.zeros((128, RV_N), np.float32)
    for l in range(DEPTH):
        pv[:, PV_OFF[("norm_mix_g", l)]:][:, :8] = _chunked(inp["norm_mix_g"][l], 8)
        pv[:, PV_OFF[("norm_x_g", l)]:][:, :8] = _chunked(inp["norm_x_g"][l], 8)
        pv[:, PV_OFF[("norm_mem_g", l)]:][:, :8] = _chunked(inp["norm_mem_g"][l], 8)
        pv[:, PV_OFF[("norm_ffn_g", l)]:][:, :8] = _chunked(inp["norm_ffn_g"][l], 8)
        pv[:, PV_OFF[("ret_gn_g", l)]:][:, :4] = _chunked(inp["ret_gn_g"][l], 4)
        pv[:, PV_OFF[("diff_subln_g", l)]:][:, :1] = _chunked(inp["diff_subln_g"][l], 1)
        cw = np.asarray(inp["ssm_conv_w"][l], np.float32)
        o = PV_OFF[("conv_w", l)]
        for c in range(8):
            for k in range(4):
                pv[:, o + c * 4 + k] = cw[k, c * 128:(c + 1) * 128]
        pv[:, PV_OFF[("conv_b", l)]:][:, :8] = _chunked(inp["ssm_conv_b"][l], 8)
        for nm, key, w in (("dt_bias", "ssm_dt_bias", 8), ("A_log", "ssm_A_log", 8), ("D", "ssm_D", 8),
                           ("ssm_norm_g", "ssm_norm_g", 512)):
            o = RV_OFF[(nm, l)]
            rv[:, o:o + w] = np.asarray(inp[key][l], np.float32).reshape(1, w)
        o = RV_OFF[("diff_lambda", l)]
        rv[:, o:o + 256] = np.asarray(inp["diff_lambda"][l], np.float32).reshape(1, 256)
    pv[:, PV_OFF[("norm_f_g", 0)]:][:, :8] = _chunked(inp["norm_f_g"], 8)
    cf = np.zeros((128, CF_N), np.float32)
    idx = np.arange(128)
    cf[:, CF_IDENT:CF_IDENT + 128] = np.eye(128, dtype=np.float32)
    cf[:, CF_TRIU:CF_TRIU + 128] = (idx[:, None] <= idx[None, :])
    cf[:, CF_TRIL:CF_TRIL + 128] = (idx[:, None] >= idx[None, :])
    cf[:, CF_SL:CF_SL + 128] = (idx[:, None] > idx[None, :])
    lg = np.log1p(-np.exp2(-5.0 - np.arange(4, dtype=np.float32))).astype(np.float32)
    for h in range(4):
        rel = (idx[None, :] - idx[:, None]).astype(np.float32)
        dec = np.where(rel >= 0, np.exp(lg[h] * np.maximum(rel, 0.0)), 0.0)
        cf[:, CF_DECT + h * 128:CF_DECT + (h + 1) * 128] = dec
        cf[:, CF_KEND + h] = np.exp((127 - idx) * lg[h])
        cf[:, CF_QDEC + h * 128:CF_QDEC + (h + 1) * 128] = np.exp((idx + 1.0) * lg[h])[None, :]
    half = 32
    inv_freq = (10000.0 ** (-np.arange(half, dtype=np.float32) / half)).astype(np.float32)
    pos = np.arange(S, dtype=np.float32)
    ang = (pos[None, :] * inv_freq[:, None]).astype(np.float32)
    cos, sin = np.cos(ang).astype(np.float32), np.sin(ang).astype(np.float32)
    rot = np.zeros((64, 2, S), np.float32)
    rot[:32, 0] = cos
    rot[32:, 0] = cos
    rot[:32, 1] = -sin
    rot[32:, 1] = sin
    return pv, rv, cf, rot


def make_in_map(inp, b, S):
    pv, rv, cf, rot = make_tables(inp, S)
    f = lambda a: np.ascontiguousarray(np.asarray(a, np.float32))
    return {
        "xT": f(np.asarray(inp["x"][b]).T[:, :S]),
        "memT": f(np.asarray(inp["mem"][b]).T),
        "w_in": f(inp["w_in"]),
        "w_branch": f(np.asarray(inp["w_branch"]).reshape(DEPTH, 2048, D)),
        "w_mix_out": f(inp["w_mix_out"]),
        "w_xq": f(inp["w_xq"]),
        "w_xkv": f(inp["w_xkv"]),
        "w_xo": f(inp["w_xo"]),
        "w_ffn_in": f(inp["w_ffn_in"]),
        "w_ffn_out": f(inp["w_ffn_out"]),
        "pvec": pv, "rvec": rv, "cst_f32": cf, "rot": rot,
    }


def _phase_xattn(self, l):
    em, nc = self.em, self.nc
    ps, psb = self.ps, self.psb
    with ExitStack() as st:
        self.alloc_norm_scratch(st)
        wq = self.tl(st, "wq", [128, 8, 512], BF16)
        wkv = self.tl(st, "wkv", [128, 8, 1024], BF16)
        wo = self.tl(st, "wo", [128, 4, D], BF16)
        mt = self.tl(st, "mt", [128, 8, MEM], F32)
        mh = self.tl(st, "mh", [128, 8, MEM], BF16)
        kT = self.tl(st, "kT", [128, 4, MEM], BF16)
        V = self.tl(st, "V", [128, 2, 512], BF16)
        xt = self.tl(st, "xt", [128, 8, 512], F32)
        ht = self.tl(st, "ht", [128, 8, 512], BF16)
        qh = self.tl(st, "qh", [128, 512], BF16)
        PT = self.tl(st, "PT", [128, 2, 512], BF16)
        rden = self.tl(st, "rden", [128, 512], F32)
        o = self.tl(st, "o", [128, 4, 512], BF16)
        bwq, bwkv, bwo = em.buf(), em.buf(), em.buf()
        bmt, bmh, bk, bv = em.buf(), em.buf(), em.buf(), em.buf()
        bx, bh, bq, bp, brd, bo = em.buf(), em.buf(), em.buf(), [em.buf(), em.buf()], em.buf(), em.buf()
        for k in range(8):
            self.load_w(wq[:, k, :], self.dram["w_xq"][l, k * 128:(k + 1) * 128, :], 128, 512, bwq)
            self.load_w(wkv[:, k, :], self.dram["w_xkv"][l, k * 128:(k + 1) * 128, :], 128, 1024, bwkv)
        for h in range(4):
            self.load_w(wo[:, h, :], self.dram["w_xo"][l, h * 128:(h + 1) * 128, :], 128, D, bwo)
        em.dma("sp", mt[:], self.dram["memT"].rearrange("(c p) s -> p c s", p=128), writes=[bmt])
        self.norm_tile(mt, mh, "norm_mem_g", l, self.sqt, bmt, bmh, width=MEM)
        for h in range(4):
            for k in range(8):
                em.op("pe", lambda e, k=k, h=h: e.matmul(ps[1][:, 0:MEM], lhsT=wkv[:, k, h * 128:(h + 1) * 128], rhs=mh[:, k, :],
                                                         start=(k == 0), stop=(k == 7)), reads=[bwkv, bmh], writes=[psb[1]])
            em.op("act", lambda e, h=h: e.copy(out=kT[:, h, :], in_=ps[1][:, 0:MEM]), reads=[psb[1]], writes=[bk])
        for mb in range(2):
            for k in range(8):
                em.op("pe", lambda e, k=k, mb=mb: e.matmul(ps[2][:], lhsT=mh[:, k, mb * 128:(mb + 1) * 128], rhs=wkv[:, k, 512:1024],
                                                           start=(k == 0), stop=(k == 7)), reads=[bwkv, bmh], writes=[psb[2]])
            em.op("act", lambda e, mb=mb: e.copy(out=V[:, mb, :], in_=ps[2][:]), reads=[psb[2]], writes=[bv])
        sc = 128.0 ** -0.5
        for t in range(self.NT):
            em.dma("sp", xt[:], self.xtile_ap("xs", t), reads=[self.b_xs], writes=[bx])
            self.norm_tile(xt, ht, "norm_x_g", l, self.sqt, bx, bh)
            for h in range(4):
                for k in range(8):
                    em.op("pe", lambda e, k=k, h=h: e.matmul(ps[1][:], lhsT=wq[:, k, h * 128:(h + 1) * 128], rhs=ht[:, k, :],
                                                             start=(k == 0), stop=(k == 7)), reads=[bwq, bh], writes=[psb[1]])
                em.op("act", lambda e: e.copy(out=qh[:], in_=ps[1][:]), reads=[psb[1]], writes=[bq])
                for mb in range(2):
                    em.op("pe", lambda e, h=h, mb=mb: e.matmul(ps[2 + mb][:], lhsT=kT[:, h, mb * 128:(mb + 1) * 128], rhs=qh[:],
                                                               start=True, stop=True), reads=[bk, bq], writes=[psb[2 + mb]])
                    em.op("act", lambda e, mb=mb: e.activation(out=PT[:, mb, :], in_=ps[2 + mb][:], func=AF.Exp, scale=sc),
                          reads=[psb[2 + mb]], writes=[bp[mb]])
                for mb in range(2):
                    em.op("pe", lambda e, h=h, mb=mb: e.matmul(ps[4][:], lhsT=V[:, mb, h * 128:(h + 1) * 128], rhs=PT[:, mb, :],
                                                               start=(mb == 0), stop=(mb == 1)), reads=[bv, bp[mb]], writes=[psb[4]])
                for mb in range(2):
                    em.op("pe", lambda e, mb=mb: e.matmul(ps[5][:], lhsT=self.ones_bf[:], rhs=PT[:, mb, :],
                                                          start=(mb == 0), stop=(mb == 1)), reads=[self.b_const, bp[mb]], writes=[psb[5]])
                em.op("dve", lambda e: e.reciprocal(out=rden[:], in_=ps[5][:]), reads=[psb[5]], writes=[brd])
                em.op("dve", lambda e, h=h: e.tensor_tensor(out=o[:, h, :], in0=ps[4][:], in1=rden[:], op=ALU.mult),
                      reads=[psb[4], brd], writes=[bo])
            for c in range(DC):
                po = 6 + (c % 2)
                for h in range(4):
                    em.op("pe", lambda e, c=c, h=h, po=po: e.matmul(ps[po][:], lhsT=wo[:, h, c * 128:(c + 1) * 128], rhs=o[:, h, :],
                                                                    start=(h == 0), stop=(h == 3)), reads=[bwo, bo], writes=[psb[po]])
                em.op("dve", lambda e, c=c, po=po: e.tensor_tensor(out=xt[:, c, :], in0=ps[po][:], in1=xt[:, c, :], op=ALU.add),
                      reads=[psb[po], bx], writes=[bx])
            em.dma("sp", self.xtile_ap("xs", t), xt[:], reads=[bx], writes=[self.b_xs])
        em.barrier()


Builder.phase_xattn = _phase_xattn


def _mix_begin(self, l, st):
    em, nc = self.em, self.nc
    self.hT = self.tl(st, "hT", [128, 8, self.S], BF16)
    self.b_hT = em.buf("hT")
    with ExitStack() as s2:
        self.alloc_norm_scratch(s2)
        xt = self.tl(s2, "xt", [128, 8, 512], F32)
        bx = em.buf()
        for t in range(self.NT):
            em.dma("sp", xt[:], self.xtile_ap("xs", t), reads=[self.b_xs], writes=[bx])
            self.norm_tile(xt, self.hT[:, :, t * 512:(t + 1) * 512], "norm_mix_g", l, self.sqt, bx, self.b_hT)
        em.barrier()


def _load_wk(self, dst, l, col0, n, wbuf):
    em = self.em
    i = self.stg_i
    self.stg_i = (i + 1) % 2
    stg, sbf = self.stg[i], self.stgb[i]
    src = self.dram["w_in"][l].rearrange("(k p) c -> p k c", p=128)[:, :, col0:col0 + n]
    sv = stg[:, 0:8 * n].rearrange("p (k c) -> p k c", k=8)
    em.dma("sp", sv, src, writes=[sbf])
    em.op("pool", lambda e: e.tensor_copy(out=dst, in_=sv), reads=[sbf], writes=[wbuf])


def _proj_fm(self, w, n, wbuf, pi, evac):
    em = self.em
    for t in range(self.NT):
        p = pi[t % len(pi)]
        for k in range(8):
            em.op("pe", lambda e, k=k, t=t, p=p: e.matmul(self.ps[p][0:n, :], lhsT=w[:, k, 0:n], rhs=self.hT[:, k, t * 512:(t + 1) * 512],
                                                          start=(k == 0), stop=(k == 7)), reads=[wbuf, self.b_hT], writes=[self.psb[p]])
        evac(t, p)


def _proj_tm(self, w, n, wbuf, tok_ap_fn, nblk, pi, evac):
    em = self.em
    for b in range(nblk):
        p = pi[b % len(pi)]
        for k in range(8):
            em.op("pe", lambda e, k=k, b=b, p=p: e.matmul(self.ps[p][:, 0:n], lhsT=tok_ap_fn(k, b), rhs=w[:, k, 0:n],
                                                          start=(k == 0), stop=(k == 7)), reads=[wbuf, self.b_hT], writes=[self.psb[p]])
        evac(b, p)


def _mixer_C(self, l):
    em, nc = self.em, self.nc
    ps, psb = self.ps, self.psb
    S, NB, NT = self.S, self.NB, self.NT
    lam_init = 0.8 - 0.6 * math.exp(-0.3 * l)
    with ExitStack() as st:
        wq = self.tl(st, "wq", [128, 8, 128], BF16)
        wk = self.tl(st, "wk", [128, 8, 128], BF16)
        wv = self.tl(st, "wv", [128, 8, 128], BF16)
        qT = self.tl(st, "qT", [128, S], BF16)
        kT = self.tl(st, "kT", [128, 2, S], BF16)
        V = self.tl(st, "V", [128, NB, 128], BF16)
        PT = self.tl(st, "PT", [128, 4, 512], BF16)
        r1 = self.tl(st, "r1", [128, 512], F32)
        r2 = self.tl(st, "r2", [128, 512], F32)
        t1 = self.tl(st, "t1", [128, 512], F32)
        t2 = self.tl(st, "t2", [128, 512], F32)
        yo = self.tl(st, "yo", [128, 2, 512], BF16)
        lam = self.tl(st, "lam", [128, 8], F32)
        ltmp = self.tl(st, "ltmp", [128, 64], F32)
        bwq, bwk, bwv, bq, bk, bv = [em.buf() for _ in range(6)]
        bp = [em.buf() for _ in range(4)]
        br1, br2, bt1, bt2, blam = [em.buf() for _ in range(5)]
        byo = [em.buf(), em.buf()]
        b_y = self.b_ybr
        o = RV_OFF[("diff_lambda", l)]
        for i in range(2):
            em.op("dve", lambda e, i=i: e.tensor_tensor(out=ltmp[:], in0=self.rv[:, o + 128 * i:o + 128 * i + 64],
                                                         in1=self.rv[:, o + 128 * i + 64:o + 128 * i + 128], op=ALU.mult),
                  reads=[self.b_const], writes=[blam])
            em.op("dve", lambda e, i=i: e.reduce_sum(out=lam[:, i:i + 1], in_=ltmp[:], axis=mybir.AxisListType.X), reads=[blam], writes=[blam])
        em.op("act", lambda e: e.activation(out=lam[:, 2:4], in_=lam[:, 0:2], func=AF.Exp), reads=[blam], writes=[blam])
        em.op("dve", lambda e: e.tensor_tensor(out=lam[:, 4:5], in0=lam[:, 3:4], in1=lam[:, 2:3], op=ALU.subtract), reads=[blam], writes=[blam])
        em.op("dve", lambda e: e.tensor_scalar(out=lam[:, 5:6], in0=lam[:, 4:5], scalar1=-lam_init, scalar2=None, op0=ALU.add), reads=[blam], writes=[blam])
        em.op("dve", lambda e: e.memset(lam[:, 6:7], EPS / (1.0 - lam_init) ** 2), writes=[blam])
        sc = 64.0 ** -0.5
        em.op("pool", lambda e: e.memset(kT[64:128, 0, :], 0.0), writes=[bk])
        em.op("pool", lambda e: e.memset(kT[0:64, 1, :], 0.0), writes=[bk])
        for h in range(4):
            self.load_wk(wq[:], l, O_CQ + h * 128, 128, bwq)
            self.load_wk(wk[:], l, O_CK + h * 128, 128, bwk)
            self.load_wk(wv[:], l, O_CV + h * 128, 128, bwv)
            self.proj_fm(wq, 128, bwq, [6, 7], lambda t, p: em.op("act", lambda e: e.copy(out=qT[:, t * 512:(t + 1) * 512], in_=ps[p][:]), reads=[psb[p]], writes=[bq]))
            def evk(t, p):
                em.op("dve", lambda e: e.tensor_copy(out=kT[0:64, 0, t * 512:(t + 1) * 512], in_=ps[p][0:64, :]), reads=[psb[p]], writes=[bk])
                em.op("dve", lambda e: e.tensor_copy(out=kT[64:128, 1, t * 512:(t + 1) * 512], in_=ps[p][64:128, :]), reads=[psb[p]], writes=[bk])
            self.proj_fm(wk, 128, bwk, [6, 7], evk)
            self.proj_tm(wv, 128, bwv, lambda k, b: self.hT[:, k, b * 128:(b + 1) * 128], NB, [6, 7],
                         lambda b, p: em.op("act", lambda e: e.copy(out=V[:, b, :], in_=ps[p][:, 0:128]), reads=[psb[p]], writes=[bv]))
            LAG = 3
            items = []
            for t in range(NT):
                nj = 4 * t + 4
                for c in range(2):
                    for j in range(nj):
                        items.append((t, c, j, nj))
            SB = [0, 1, 6, 7]

            def stage1(i):
                t, c, j, nj = items[i]
                lo, hi = c * 64, (c + 1) * 64
                q0 = max(0, j - 4 * t) * 128
                s = i % 4
                sp = SB[s]
                em.op("pe", lambda e: e.matmul(ps[sp][:, q0:512], lhsT=kT[:, c, j * 128:(j + 1) * 128], rhs=qT[:, t * 512 + q0:(t + 1) * 512],
                                               start=True, stop=True), reads=[bk, bq], writes=[psb[sp]])
                em.op("act", lambda e: e.activation(out=PT[:, s, q0:512], in_=ps[sp][:, q0:512], func=AF.Exp, scale=sc), reads=[psb[sp]], writes=[bp[s]])
                if j >= 4 * t:
                    em.op("pool", lambda e: e.tensor_tensor(out=PT[:, s, q0:q0 + 128], in0=PT[:, s, q0:q0 + 128], in1=self.triU_bf[:], op=ALU.mult),
                          reads=[bp[s], self.b_const], writes=[bp[s]])

            def stage2(i):
                t, c, j, nj = items[i]
                q0 = max(0, j - 4 * t) * 128
                s = i % 4
                em.op("pe", lambda e: e.matmul(ps[2 + c][:, q0:512], lhsT=V[:, j, :], rhs=PT[:, s, q0:512], start=(j == 0), stop=(j == nj - 1)),
                      reads=[bv, bp[s]], writes=[psb[2 + c]])
                em.op("pe", lambda e: e.matmul(ps[4 + c][:, q0:512], lhsT=self.ones_bf[:], rhs=PT[:, s, q0:512], start=(j == 0), stop=(j == nj - 1)),
                      reads=[self.b_const, bp[s]], writes=[psb[4 + c]])
                if c == 1 and j == nj - 1:
                    epilogue(t)

            def epilogue(t):
                em.op("dve", lambda e: e.reciprocal(out=r1[:], in_=ps[4][:]), reads=[psb[4]], writes=[br1])
                em.op("dve", lambda e: e.reciprocal(out=r2[:], in_=ps[5][:]), reads=[psb[5]], writes=[br2])
                em.op("dve", lambda e: e.tensor_tensor(out=t1[:], in0=ps[2][:], in1=r1[:], op=ALU.mult), reads=[psb[2], br1], writes=[bt1])
                em.op("dve", lambda e: e.tensor_tensor(out=t2[:], in0=ps[3][:], in1=r2[:], op=ALU.mult), reads=[psb[3], br2], writes=[bt2])
                em.op("dve", lambda e: e.scalar_tensor_tensor(out=t1[:], in0=t2[:], scalar=lam[:, 5:6], in1=t1[:], op0=ALU.mult, op1=ALU.add),
                      reads=[bt1, bt2, blam], writes=[bt1])
                em.op("act", lambda e: e.activation(out=t2[:], in_=t1[:], func=AF.Square), reads=[bt1], writes=[bt2])
                em.op("pe", lambda e: e.matmul(ps[4][:], lhsT=self.ones_f[:], rhs=t2[:], start=True, stop=True),
                      reads=[self.b_const, bt2], writes=[psb[4]])
                em.op("act", lambda e: e.activation(out=r1[:], in_=ps[4][:], func=AF.Sqrt, scale=1.0 / (128.0 * (1.0 - lam_init) ** 2), bias=lam[:, 6:7]),
                      reads=[psb[4], blam], writes=[br1])
                em.op("dve", lambda e: e.reciprocal(out=r1[:], in_=r1[:]), reads=[br1], writes=[br1])
                yi = t % 2
                em.op("dve", lambda e: e.scalar_tensor_tensor(out=yo[:, yi, :], in0=t1[:], scalar=self.pvcol("diff_subln_g", l), in1=r1[:],
                                                              op0=ALU.mult, op1=ALU.mult), reads=[bt1, br1, self.b_const], writes=[byo[yi]])
                r0 = 1024 + h * 128
                em.dma("sp", self.dram["ybr"][r0:r0 + 128, t * 512:(t + 1) * 512], yo[:, yi, :], reads=[byo[yi]], writes=[b_y])

            for i in range(len(items) + LAG):
                if i < len(items):
                    stage1(i)
                if i - LAG >= 0:
                    stage2(i - LAG)
        em.barrier()


Builder.mix_begin = _mix_begin
Builder.load_wk = _load_wk
Builder.proj_fm = _proj_fm
Builder.proj_tm = _proj_tm
Builder.mixer_C = _mixer_C


def _gates(self, l):
    em = self.em
    ps, psb = self.ps, self.psb
    with ExitStack() as st:
        wg = [self.tl(st, "wg", [128, 8, 128], BF16) for _ in range(2)]
        bwg = [em.buf(), em.buf()]
        go = [self.tl(st, "go", [128, 512], BF16) for _ in range(2)]
        bgo = [em.buf(), em.buf()]
        b_g = self.b_gat
        n = 0
        for ic in range(32):
            w, bw = wg[ic % 2], bwg[ic % 2]
            self.load_wk(w[:], l, O_G + ic * 128, 128, bw)

            def evac(t, p, ic=ic):
                nonlocal n
                g, bg = go[n % 2], bgo[n % 2]
                n += 1
                em.op("act", lambda e: e.activation(out=g[:], in_=ps[p][:], func=AF.Sigmoid), reads=[psb[p]], writes=[bg])
                em.dma("sp", self.dram["gat"][ic * 128:(ic + 1) * 128, t * 512:(t + 1) * 512], g[:], reads=[bg], writes=[b_g])
            self.proj_fm(w, 128, bw, [0, 1, 2, 3], evac)
        em.barrier()


def _phase_merge(self, l):
    em, nc = self.em, self.nc
    ps, psb = self.ps, self.psb
    with ExitStack() as st:
        wb = self.tl(st, "wb", [128, 16, D], BF16)
        wo = self.tl(st, "wo", [128, 8, D], BF16)
        y = self.tl(st, "y", [128, 16, 512], BF16)
        g = self.tl(st, "g", [128, 32, 512], BF16)
        xt = self.tl(st, "xt", [128, 8, 512], F32)
        mg = self.tl(st, "mg", [128, 8, 512], BF16)
        acc = self.tl(st, "acc", [128, 512], F32)
        tmp = self.tl(st, "tmp", [128, 2, 512], F32)
        bwb = [em.buf() for _ in range(16)]
        bwo = [em.buf() for _ in range(8)]
        by, bg, bx, bmg, bacc = em.buf(), em.buf(), em.buf(), em.buf(), em.buf()
        btmp = [em.buf(), em.buf()]
        for r in range(16):
            self.load_w(wb[:, r, :], self.dram["w_branch"][l, r * 128:(r + 1) * 128, :], 128, D, bwb[r])
        for c in range(8):
            self.load_w(wo[:, c, :], self.dram["w_mix_out"][l, c * 128:(c + 1) * 128, :], 128, D, bwo[c])
        for t in range(self.NT):
            em.dma("sp", y[:], self.dram["ybr"].rearrange("(r p) s -> p r s", p=128)[:, :, t * 512:(t + 1) * 512], reads=[self.b_ybr], writes=[by])
            em.dma("sp", g[:], self.dram["gat"].rearrange("(r p) s -> p r s", p=128)[:, :, t * 512:(t + 1) * 512], reads=[self.b_gat], writes=[bg])
            em.dma("sp", xt[:], self.xtile_ap("xs", t), reads=[self.b_xs], writes=[bx])
            n = 0
            for c in range(8):
                for i in range(4):
                    p = n % 4
                    n += 1
                    for m in range(4):
                        em.op("pe", lambda e, p=p, i=i, m=m, c=c: e.matmul(ps[p][:], lhsT=wb[:, i * 4 + m, c * 128:(c + 1) * 128], rhs=y[:, i * 4 + m, :],
                                                                           start=(m == 0), stop=(m == 3)), reads=[bwb[i * 4 + m], by], writes=[psb[p]])
                    if i == 0:
                        em.op("dve", lambda e, p=p, c=c: e.tensor_tensor(out=acc[:], in0=ps[p][:], in1=g[:, c, :], op=ALU.mult),
                              reads=[psb[p], bg], writes=[bacc])
                    else:
                        ti = i % 2
                        em.op("dve", lambda e, p=p, c=c, i=i, ti=ti: e.tensor_tensor(out=tmp[:, ti, :], in0=ps[p][:], in1=g[:, i * 8 + c, :], op=ALU.mult),
                              reads=[psb[p], bg], writes=[btmp[ti]])
                        if i < 3:
                            em.op("pool", lambda e, ti=ti: e.tensor_tensor(out=acc[:], in0=acc[:], in1=tmp[:, ti, :], op=ALU.add),
                                  reads=[bacc, btmp[ti]], writes=[bacc])
                        else:
                            em.op("pool", lambda e, ti=ti, c=c: e.tensor_tensor(out=mg[:, c, :], in0=acc[:], in1=tmp[:, ti, :], op=ALU.add),
                                  reads=[bacc, btmp[ti]], writes=[bmg])
            for c2 in range(8):
                po = 4 + (c2 % 2)
                for c in range(8):
                    em.op("pe", lambda e, c=c, c2=c2, po=po: e.matmul(ps[po][:], lhsT=wo[:, c, c2 * 128:(c2 + 1) * 128], rhs=mg[:, c, :],
                                                                      start=(c == 0), stop=(c == 7)), reads=[bwo[c], bmg], writes=[psb[po]])
                em.op("dve", lambda e, c2=c2, po=po: e.tensor_tensor(out=xt[:, c2, :], in0=ps[po][:], in1=xt[:, c2, :], op=ALU.add),
                      reads=[psb[po], bx], writes=[bx])
            em.dma("sp", self.xtile_ap("xs", t), xt[:], reads=[bx], writes=[self.b_xs])
        em.barrier()


Builder.gates = _gates
Builder.phase_merge = _phase_merge


def _mixer_A(self, l):
    em, nc = self.em, self.nc
    ps, psb = self.ps, self.psb
    S, NB, NT = self.S, self.NB, self.NT
    with ExitStack() as st:
        wq = self.tl(st, "wq", [128, 8, 64], BF16)
        wk = self.tl(st, "wk", [128, 8, 64], BF16)
        wv = self.tl(st, "wv", [128, 8, 64], BF16)
        qT = self.tl(st, "qT", [64, S], BF16)
        kT = self.tl(st, "kT", [64, S], BF16)
        V = self.tl(st, "V", [128, NB, 64], BF16)
        PT = self.tl(st, "PT", [128, 2, 256], BF16)
        msk = self.tl(st, "msk", [128, 256], BF16)
        accN = self.tl(st, "accN", [64, S], F32)
        accD = self.tl(st, "accD", [64, S], F32)
        yo = self.tl(st, "yo", [64, 2, 512], BF16)
        bwq, bwk, bwv, bq, bk, bv, bm, baN, baD = [em.buf() for _ in range(9)]
        bp = [em.buf(), em.buf()]
        byo = [em.buf(), em.buf()]
        em.op("dve", lambda e: e.tensor_copy(out=msk[:, 0:128], in_=self.triU_bf[:]), reads=[self.b_const], writes=[bm])
        em.op("dve", lambda e: e.tensor_copy(out=msk[:, 128:256], in_=self.triL_bf[:]), reads=[self.b_const], writes=[bm])
        sc = 64.0 ** -0.5
        for h in range(8):
            self.load_wk(wq[:], l, O_AQ + h * 64, 64, bwq)
            self.load_wk(wk[:], l, O_AK + h * 64, 64, bwk)
            self.load_wk(wv[:], l, O_AV + h * 64, 64, bwv)
            self.proj_fm(wq, 64, bwq, [6, 7], lambda t, p: em.op("act", lambda e: e.copy(out=qT[:, t * 512:(t + 1) * 512], in_=ps[p][0:64, :]), reads=[psb[p]], writes=[bq]))
            self.proj_fm(wk, 64, bwk, [6, 7], lambda t, p: em.op("dve", lambda e: e.tensor_copy(out=kT[:, t * 512:(t + 1) * 512], in_=ps[p][0:64, :]), reads=[psb[p]], writes=[bk]))
            for pi_, dil in enumerate((1, 4, 16)):
                L = S // dil
                nbs = L // 128
                assert nbs >= 1 and nbs * 128 * dil == S

                def tok(r, n, cnt=128, dil=dil):
                    s0 = r + dil * n * 128
                    return slice(s0, s0 + dil * (cnt - 1) + 1, dil)
                self.proj_tm(wv, 64, bwv, lambda k, b: self.hT[:, k, tok(b // nbs, b % nbs)], NB, [6, 7],
                             lambda b, p: em.op("act", lambda e: e.copy(out=V[:, b, :], in_=ps[p][:, 0:64]), reads=[psb[p]], writes=[bv]))
                grp = min(4, nbs)
                step = 0
                gi = 0
                for r in range(dil):
                    for n0 in range(0, nbs, grp):
                        pn, pd = 2 + (gi % 2), 4 + (gi % 2)
                        gi += 1
                        for n in range(n0, n0 + grp):
                            s = step % 2
                            step += 1
                            pb = r * nbs + n
                            qs = (n - n0) * 128
                            w = 256 if n > 0 else 128
                            em.op("pe", lambda e, s=s, r=r, n=n: e.matmul(ps[s][:, 0:128], lhsT=kT[:, tok(r, n)], rhs=qT[:, tok(r, n)], start=True, stop=True),
                                  reads=[bk, bq], writes=[psb[s]])
                            if n > 0:
                                em.op("pe", lambda e, s=s, r=r, n=n: e.matmul(ps[s][:, 128:256], lhsT=kT[:, tok(r, n - 1)], rhs=qT[:, tok(r, n)], start=True, stop=True),
                                      reads=[bk, bq], writes=[psb[s]])
                            em.op("act", lambda e, s=s, w=w: e.activation(out=PT[:, s, 0:w], in_=ps[s][:, 0:w], func=AF.Exp, scale=sc),
                                  reads=[psb[s]], writes=[bp[s]])
                            em.op("pool", lambda e, s=s, w=w: e.tensor_tensor(out=PT[:, s, 0:w], in0=PT[:, s, 0:w], in1=msk[:, 0:w], op=ALU.mult),
                                  reads=[bp[s], bm], writes=[bp[s]])
                            last = (n == 0)
                            em.op("pe", lambda e, s=s, pb=pb, pn=pn, qs=qs, last=last: e.matmul(ps[pn][0:64, qs:qs + 128], lhsT=V[:, pb, :], rhs=PT[:, s, 0:128], start=True, stop=last),
                                  reads=[bv, bp[s]], writes=[psb[pn]])
                            if n > 0:
                                em.op("pe", lambda e, s=s, pb=pb, pn=pn, qs=qs: e.matmul(ps[pn][0:64, qs:qs + 128], lhsT=V[:, pb - 1, :], rhs=PT[:, s, 128:256], start=False, stop=True),
                                      reads=[bv, bp[s]], writes=[psb[pn]])
                            em.op("pe", lambda e, s=s, pd=pd, qs=qs, last=last: e.matmul(ps[pd][0:64, qs:qs + 128], lhsT=self.ones_bf[:, 0:64], rhs=PT[:, s, 0:128], start=True, stop=last),
                                  reads=[self.b_const, bp[s]], writes=[psb[pd]])
                            if n > 0:
                                em.op("pe", lambda e, s=s, pd=pd, qs=qs: e.matmul(ps[pd][0:64, qs:qs + 128], lhsT=self.ones_bf[:, 0:64], rhs=PT[:, s, 128:256], start=False, stop=True),
                                      reads=[self.b_const, bp[s]], writes=[psb[pd]])
                        tsl = tok(r, n0, grp * 128)
                        gw = grp * 128
                        if pi_ == 0:
                            em.op("dve", lambda e, pn=pn, tsl=tsl, gw=gw: e.tensor_copy(out=accN[:, tsl], in_=ps[pn][0:64, 0:gw]), reads=[psb[pn]], writes=[baN])
                            em.op("act", lambda e, pd=pd, tsl=tsl, gw=gw: e.copy(out=accD[:, tsl], in_=ps[pd][0:64, 0:gw]), reads=[psb[pd]], writes=[baD])
                        else:
                            em.op("dve", lambda e, pn=pn, tsl=tsl, gw=gw: e.tensor_tensor(out=accN[:, tsl], in0=ps[pn][0:64, 0:gw], in1=accN[:, tsl], op=ALU.add),
                                  reads=[psb[pn], baN], writes=[baN])
                            em.op("dve", lambda e, pd=pd, tsl=tsl, gw=gw: e.tensor_tensor(out=accD[:, tsl], in0=ps[pd][0:64, 0:gw], in1=accD[:, tsl], op=ALU.add),
                                  reads=[psb[pd], baD], writes=[baD])
            for t in range(NT):
                sl = slice(t * 512, (t + 1) * 512)
                yi = t % 2
                em.op("dve", lambda e, sl=sl: e.reciprocal(out=accD[:, sl], in_=accD[:, sl]), reads=[baD], writes=[baD])
                em.op("dve", lambda e, sl=sl, yi=yi: e.tensor_tensor(out=yo[:, yi, :], in0=accN[:, sl], in1=accD[:, sl], op=ALU.mult),
                      reads=[baN, baD], writes=[byo[yi]])
                em.dma("sp", self.dram["ybr"][h * 64:(h + 1) * 64, sl], yo[:, yi, :], reads=[byo[yi]], writes=[self.b_ybr])
        em.barrier()


Builder.mixer_A = _mixer_A


def _bcast_mid(ap2d, n):
    a = ap2d.ap
    return bass.AP(ap2d.tensor, ap2d.offset, [list(a[0]), [0, n], list(a[1])])


def _mixer_B(self, l):
    em, nc = self.em, self.nc
    ps, psb = self.ps, self.psb
    S, NB, NT = self.S, self.NB, self.NT
    lg = [math.log1p(-2.0 ** (-5.0 - h)) for h in range(4)]
    with ExitStack() as st:
        wq = self.tl(st, "wq", [128, 8, 64], BF16)
        wqs = self.tl(st, "wqs", [128, 8, 64], BF16)
        wk = self.tl(st, "wk", [128, 8, 64], BF16)
        wks = self.tl(st, "wks", [128, 8, 64], BF16)
        wv = self.tl(st, "wv", [128, 8, 128], BF16)
        wg = self.tl(st, "wg", [128, 8, 128], BF16)
        rot = self.tl(st, "rot", [64, 2, S], F32)
        qr = self.tl(st, "qr", [64, S], BF16)
        qi = self.tl(st, "qi", [64, S], BF16)
        kr = self.tl(st, "kr", [64, S], BF16)
        V = self.tl(st, "V", [128, NB, 128], BF16)
        gs = self.tl(st, "gs", [128, S], BF16)
        ta = self.tl(st, "ta", [64, 512], F32)
        tb = self.tl(st, "tb", [64, 512], F32)
        Sm = self.tl(st, "Sm", [128, 2, 128], BF16)
        kend = self.tl(st, "kend", [128, 2, 64], BF16)
        R = self.tl(st, "R", [64, 128], F32)
        Rb = self.tl(st, "Rb", [64, 128], BF16)
        ot = self.tl(st, "ot", [128, 512], F32)
        cen = self.tl(st, "cen", [128, 512], F32)
        sq = self.tl(st, "sq", [128, 512], F32)
        rs = self.tl(st, "rs", [128, 512], F32)
        yo = self.tl(st, "yo", [128, 2, 512], BF16)
        epst = self.tl(st, "epst", [128, 1], F32)
        (bwq, bwqs, bwk, bwks, bwv, bwg, brot, bqr, bqi, bkr, bv, bgs, bta, btb, bR, bRb, bot, bcen, bsq, brs) = [em.buf() for _ in range(20)]
        bSm = [em.buf(), em.buf()]
        bke = [em.buf(), em.buf()]
        byo = [em.buf(), em.buf()]
        em.dma("sp", rot[:], self.dram["rot"], writes=[brot])
        em.op("dve", lambda e: e.memset(epst[:], EPS), writes=[brs])
        psT = [ps[4].bitcast(BF16)]
        for h in range(4):
            self.load_wk(wq[:], l, O_BQ + h * 64, 64, bwq)
            self.load_wk(wqs[:, :, 0:32], l, O_BQ + h * 64 + 32, 32, bwqs)
            self.load_wk(wqs[:, :, 32:64], l, O_BQ + h * 64, 32, bwqs)
            self.load_wk(wk[:], l, O_BK + h * 64, 64, bwk)
            self.load_wk(wks[:, :, 0:32], l, O_BK + h * 64 + 32, 32, bwks)
            self.load_wk(wks[:, :, 32:64], l, O_BK + h * 64, 32, bwks)
            self.load_wk(wv[:], l, O_BV + h * 128, 128, bwv)
            self.load_wk(wg[:], l, O_BG + h * 128, 128, bwg)
            qdec = self.cf[0:64, CF_QDEC + h * 128:CF_QDEC + (h + 1) * 128]
            for (wa, wb_, ba, bb, dst, bdst) in ((wq, wqs, bwq, bwqs, qr, bqr), (wk, wks, bwk, bwks, kr, bkr)):
                for t in range(NT):
                    sl = slice(t * 512, (t + 1) * 512)
                    for (w_, bw_, p) in ((wa, ba, 6), (wb_, bb, 7)):
                        for k in range(8):
                            em.op("pe", lambda e, k=k, w_=w_, p=p, sl=sl: e.matmul(ps[p][0:64, :], lhsT=w_[:, k, :], rhs=self.hT[:, k, sl],
                                                                                 start=(k == 0), stop=(k == 7)), reads=[bw_, self.b_hT], writes=[psb[p]])
                    em.op("dve", lambda e, sl=sl: e.tensor_tensor(out=ta[:], in0=ps[6][0:64, :], in1=rot[:, 0, sl], op=ALU.mult), reads=[psb[6], brot], writes=[bta])
                    em.op("dve", lambda e, sl=sl: e.tensor_tensor(out=tb[:], in0=ps[7][0:64, :], in1=rot[:, 1, sl], op=ALU.mult), reads=[psb[7], brot], writes=[btb])
                    em.op("pool", lambda e, sl=sl, dst=dst: e.tensor_tensor(out=dst[:, sl], in0=ta[:], in1=tb[:], op=ALU.add), reads=[bta, btb], writes=[bdst])
                    if dst is qr:
                        em.op("pool", lambda e, sl=sl: e.tensor_tensor(out=qi[:, sl].rearrange("p (a b) -> p a b", a=4), in0=qr[:, sl].rearrange("p (a b) -> p a b", a=4),
                                                                       in1=_bcast_mid(qdec, 4), op=ALU.mult), reads=[bqr, self.b_const], writes=[bqi])
            self.proj_tm(wv, 128, bwv, lambda k, b: self.hT[:, k, b * 128:(b + 1) * 128], NB, [6, 7],
                         lambda b, p: em.op("act", lambda e: e.copy(out=V[:, b, :], in_=ps[p][:, 0:128]), reads=[psb[p]], writes=[bv]))
            self.proj_fm(wg, 128, bwg, [6, 7], lambda t, p: em.op("act", lambda e: e.activation(out=gs[:, t * 512:(t + 1) * 512], in_=ps[p][:], func=AF.Silu), reads=[psb[p]], writes=[bgs]))
            decT = self.cf[:, CF_DECT + h * 128:CF_DECT + (h + 1) * 128]
            cdec = math.exp(128.0 * lg[h])
            for n in range(NB):
                c0 = n * 128
                s = n % 2
                qs = (n % 4) * 128
                em.op("pe", lambda e, c0=c0: e.transpose(out=psT[0][:, 0:64], in_=kr[:, c0:c0 + 128], identity=self.ident_bf[0:64, 0:64]),
                      reads=[bkr, self.b_const], writes=[psb[4]])
                em.op("dve", lambda e, s=s: e.tensor_scalar(out=kend[:, s, :], in0=psT[0][:, 0:64], scalar1=self.cf[:, CF_KEND + h:CF_KEND + h + 1], scalar2=0.125,
                                                           op0=ALU.mult, op1=ALU.mult), reads=[psb[4], self.b_const], writes=[bke[s]])
                em.op("pe", lambda e, s=s, c0=c0: e.matmul(ps[s][:, 0:128], lhsT=kr[:, c0:c0 + 128], rhs=qr[:, c0:c0 + 128], start=True, stop=True),
                      reads=[bkr, bqr], writes=[psb[s]])
                em.op("dve", lambda e, s=s: e.scalar_tensor_tensor(out=Sm[:, s, :], in0=ps[s][:, 0:128], scalar=0.125, in1=decT, op0=ALU.mult, op1=ALU.mult),
                      reads=[psb[s], self.b_const], writes=[bSm[s]])
                em.op("pe", lambda e, s=s, n=n, qs=qs: e.matmul(ps[2][:, qs:qs + 128], lhsT=V[:, n, :], rhs=Sm[:, s, :], start=True, stop=(n == 0)),
                      reads=[bv, bSm[s]], writes=[psb[2]])
                if n > 0:
                    em.op("pe", lambda e, c0=c0, qs=qs: e.matmul(ps[2][:, qs:qs + 128], lhsT=Rb[:], rhs=qi[:, c0:c0 + 128], start=False, stop=True),
                          reads=[bRb, bqi], writes=[psb[2]])
                if n < NB - 1:
                    em.op("pe", lambda e, s=s, n=n: e.matmul(ps[3][0:64, 0:128], lhsT=kend[:, s, :], rhs=V[:, n, :], start=True, stop=True),
                          reads=[bke[s], bv], writes=[psb[3]])
                    if n == 0:
                        em.op("dve", lambda e: e.tensor_copy(out=R[:], in_=ps[3][0:64, 0:128]), reads=[psb[3]], writes=[bR])
                    else:
                        em.op("dve", lambda e: e.scalar_tensor_tensor(out=R[:], in0=R[:], scalar=cdec, in1=ps[3][0:64, 0:128], op0=ALU.mult, op1=ALU.add),
                              reads=[psb[3], bR], writes=[bR])
                    em.op("act", lambda e: e.copy(out=Rb[:], in_=R[:]), reads=[bR], writes=[bRb])
                if n % 4 == 3:
                    t = n // 4
                    sl = slice(t * 512, (t + 1) * 512)
                    em.op("act", lambda e: e.copy(out=ot[:], in_=ps[2][:]), reads=[psb[2]], writes=[bot])
                    em.op("pe", lambda e: e.matmul(ps[5][:], lhsT=self.ones_f[:], rhs=ot[:], start=True, stop=True), reads=[self.b_const, bot], writes=[psb[5]])
                    em.op("dve", lambda e: e.scalar_tensor_tensor(out=cen[:], in0=ps[5][:], scalar=-1.0 / 128, in1=ot[:], op0=ALU.mult, op1=ALU.add),
                          reads=[psb[5], bot], writes=[bcen])
                    em.op("act", lambda e: e.activation(out=sq[:], in_=cen[:], func=AF.Square), reads=[bcen], writes=[bsq])
                    em.op("pe", lambda e: e.matmul(ps[5][:], lhsT=self.ones_f[:], rhs=sq[:], start=True, stop=True), reads=[self.b_const, bsq], writes=[psb[5]])
                    em.op("act", lambda e: e.activation(out=rs[:], in_=ps[5][:], func=AF.Sqrt, scale=1.0 / 128, bias=epst[:, 0:1]), reads=[psb[5], brs], writes=[brs])
                    em.op("dve", lambda e: e.reciprocal(out=rs[:], in_=rs[:]), reads=[brs], writes=[brs])
                    em.op("dve", lambda e: e.scalar_tensor_tensor(out=cen[:], in0=cen[:], scalar=self.pvcol("ret_gn_g", l, h), in1=rs[:], op0=ALU.mult, op1=ALU.mult),
                          reads=[bcen, brs, self.b_const], writes=[bcen])
                    yi = t % 2
                    em.op("pool", lambda e, yi=yi, sl=sl: e.tensor_tensor(out=yo[:, yi, :], in0=cen[:], in1=gs[:, sl], op=ALU.mult), reads=[bcen, bgs], writes=[byo[yi]])
                    r0 = 512 + h * 128
                    em.dma("sp", self.dram["ybr"][r0:r0 + 128, sl], yo[:, yi, :], reads=[byo[yi]], writes=[self.b_ybr])
        em.barrier()


Builder.mixer_B = _mixer_B


def _bcast_last(ap2d, n):
    a = ap2d.ap
    return bass.AP(ap2d.tensor, ap2d.offset, [list(a[0]), list(a[1]), [0, n]])


def _mixer_D(self, l):
    em, nc = self.em, self.nc
    ps, psb = self.ps, self.psb
    S, NB, NT = self.S, self.NB, self.NT
    X = mybir.AxisListType.X
    bc = self.b_const
    with ExitStack() as st:
        x_tm = self.tl(st, "x_tm", [128, NB, 512], BF16)
        B_tm = self.tl(st, "B_tm", [128, NB, 256], BF16)
        BT = self.tl(st, "BT", [128, 2, S], BF16)
        CT = self.tl(st, "CT", [128, 2, S], BF16)
        bxtm, bBtm, bBT, bCT = [em.buf() for _ in range(4)]
        psT = ps[7].bitcast(BF16)
        with ExitStack() as s1:
            w = [self.tl(s1, "w", [128, 8, 128], BF16) for _ in range(2)]
            bw = [em.buf(), em.buf()]
            praw = self.tl(s1, "praw", [128, 3 + S], F32)
            cacc = self.tl(s1, "cacc", [128, 512], F32)
            cact = self.tl(s1, "cact", [128, 512], BF16)
            bpraw, bcacc, bcact = [em.buf() for _ in range(3)]
            em.op("dve", lambda e: e.memset(praw[:, 0:3], 0.0), writes=[bpraw])
            for c in range(8):
                ww, bww = w[c % 2], bw[c % 2]
                self.load_wk(ww[:], l, O_DX + c * 128, 128, bww)
                self.proj_fm(ww, 128, bww, [0, 1], lambda t, p: em.op("act", lambda e: e.copy(out=praw[:, 3 + t * 512:3 + (t + 1) * 512], in_=ps[p][:]),
                                                                      reads=[psb[p]], writes=[bpraw]))
                cw = PV_OFF[("conv_w", l)] + c * 4
                for t in range(NT):
                    t0 = t * 512
                    em.op("dve", lambda e, t0=t0: e.tensor_scalar(out=cacc[:], in0=praw[:, t0 + 3:t0 + 515], scalar1=self.pv[:, cw + 3:cw + 4], scalar2=None, op0=ALU.mult),
                          reads=[bpraw, bc], writes=[bcacc])
                    for k in range(3):
                        em.op("dve", lambda e, t0=t0, k=k: e.scalar_tensor_tensor(out=cacc[:], in0=praw[:, t0 + k:t0 + k + 512], scalar=self.pv[:, cw + k:cw + k + 1], in1=cacc[:],
                                                                                op0=ALU.mult, op1=ALU.add), reads=[bpraw, bcacc, bc], writes=[bcacc])
                    if c < 4:
                        dst, bd = cact[:], bcact
                    elif c < 6:
                        dst, bd = BT[:, c - 4, t0:t0 + 512], bBT
                    else:
                        dst, bd = CT[:, c - 6, t0:t0 + 512], bCT
                    em.op("act", lambda e, dst=dst: e.activation(out=dst, in_=cacc[:], func=AF.Silu, bias=self.pvcol("conv_b", l, c), scale=1.0),
                          reads=[bcacc, bc], writes=[bd])
                    if c < 6:
                        for q in range(4):
                            in_ap = cact[:, q * 128:(q + 1) * 128] if c < 4 else BT[:, c - 4, t0 + q * 128:t0 + (q + 1) * 128]
                            em.op("pe", lambda e, q=q, in_ap=in_ap: e.transpose(out=psT[:, q * 128:(q + 1) * 128], in_=in_ap, identity=self.ident_bf[:]),
                                  reads=[bd, bc], writes=[psb[7]])
                        if c < 4:
                            em.op("dve", lambda e, t=t, c=c: e.tensor_copy(out=x_tm[:, 4 * t:4 * t + 4, c * 128:(c + 1) * 128], in_=psT[:, 0:512].rearrange("p (a b) -> p a b", a=4)),
                                  reads=[psb[7]], writes=[bxtm])
                        else:
                            em.op("dve", lambda e, t=t, c=c: e.tensor_copy(out=B_tm[:, 4 * t:4 * t + 4, (c - 4) * 128:(c - 3) * 128], in_=psT[:, 0:512].rearrange("p (a b) -> p a b", a=4)),
                                  reads=[psb[7]], writes=[bBtm])
            em.barrier()
        with ExitStack() as s2:
            wz = self.tl(s2, "wz", [128, 8, 512], BF16)
            wdt = self.tl(s2, "wdt", [128, 8, 8], BF16)
            sm = self.tl(s2, "sm", [128, 64], F32)
            expA = self.tl(s2, "expA", [128, 8], F32)
            xdt = self.tl(s2, "xdt", [128, 512], BF16)
            xdd = self.tl(s2, "xdd", [128, 512], BF16)
            cbm = self.tl(s2, "cbm", [128, 256], BF16)
            lh = self.tl(s2, "lh", [128, 2, 128], F32)
            Lx = self.tl(s2, "Lx", [128, 2, 128], BF16)
            Mx = self.tl(s2, "Mx", [128, 2, 128], BF16)
            H = self.tl(s2, "H", [128, 512], F32)
            Hb = self.tl(s2, "Hb", [128, 512], BF16)
            t1 = self.tl(s2, "t1", [128, 512], F32)
            t2 = self.tl(s2, "t2", [128, 512], F32)
            zs = self.tl(s2, "zs", [128, 512], F32)
            yn = self.tl(s2, "yn", [128, 512], BF16)
            ysg = self.tl(s2, "ysg", [128, 4, 512], BF16)
            (bwz, bwdt, bsm, bexpA, bxdt, bxdd, bcbm, bH, bHb, bt1, bt2, bzs, byn, bysg) = [em.buf() for _ in range(14)]
            blh, bLx, bMx = [em.buf(), em.buf()], [em.buf(), em.buf()], [em.buf(), em.buf()]
            for k in range(8):
                self.load_w(wz[:, k, :], self.dram["w_in"][l, k * 128:(k + 1) * 128, O_DZ:O_DZ + 512], 128, 512, bwz)
            self.load_wk(wdt[:], l, O_DDT, 8, bwdt)
            oA, oB, oD, oG = RV_OFF[("A_log", l)], RV_OFF[("dt_bias", l)], RV_OFF[("D", l)], RV_OFF[("ssm_norm_g", l)]
            em.op("act", lambda e: e.activation(out=expA[:], in_=self.rv[:, oA:oA + 8], func=AF.Exp), reads=[bc], writes=[bexpA])
            triU_f = self.cf[:, CF_TRIU:CF_TRIU + 128]
            SL_f = self.cf[:, CF_SL:CF_SL + 128]
            for n in range(NB):
                c0 = n * 128
                for k in range(8):
                    em.op("pe", lambda e, k=k, c0=c0: e.matmul(ps[6][:, 0:8], lhsT=self.hT[:, k, c0:c0 + 128], rhs=wdt[:, k, :], start=(k == 0), stop=(k == 7)),
                          reads=[bwdt, self.b_hT], writes=[psb[6]])
                em.op("dve", lambda e: e.tensor_tensor(out=sm[:, 0:8], in0=ps[6][:, 0:8], in1=self.rv[:, oB:oB + 8], op=ALU.add), reads=[psb[6], bc], writes=[bsm])
                em.op("act", lambda e: e.activation(out=sm[:, 0:8], in_=sm[:, 0:8], func=AF.Exp), reads=[bsm], writes=[bsm])
                em.op("act", lambda e: e.activation(out=sm[:, 0:8], in_=sm[:, 0:8], func=AF.Ln, bias=self.ones_f[:, 0:1], scale=1.0), reads=[bsm, bc], writes=[bsm])
                em.op("dve", lambda e: e.scalar_tensor_tensor(out=sm[:, 8:16], in0=sm[:, 0:8], scalar=-1.0, in1=expA[:], op0=ALU.mult, op1=ALU.mult),
                      reads=[bsm, bexpA], writes=[bsm])
                em.op("pe", lambda e: e.matmul(ps[2][:, 256:264], lhsT=triU_f, rhs=sm[:, 8:16], start=True, stop=True), reads=[bc, bsm], writes=[psb[2]])
                em.op("pe", lambda e: e.matmul(ps[2][:, 264:272], lhsT=self.ones_f[:], rhs=sm[:, 8:16], start=True, stop=True), reads=[bc, bsm], writes=[psb[2]])
                em.op("dve", lambda e: e.tensor_copy(out=sm[:, 16:32], in_=ps[2][:, 256:272]), reads=[psb[2]], writes=[bsm])
                em.op("dve", lambda e: e.tensor_tensor(out=sm[:, 40:48], in0=sm[:, 24:32], in1=sm[:, 16:24], op=ALU.subtract), reads=[bsm], writes=[bsm])
                em.op("act", lambda e: e.activation(out=sm[:, 32:40], in_=sm[:, 16:24], func=AF.Exp), reads=[bsm], writes=[bsm])
                em.op("act", lambda e: e.activation(out=sm[:, 40:48], in_=sm[:, 40:48], func=AF.Exp), reads=[bsm], writes=[bsm])
                em.op("act", lambda e: e.activation(out=sm[:, 48:56], in_=sm[:, 24:32], func=AF.Exp), reads=[bsm], writes=[bsm])
                xv = x_tm[:, n, :].rearrange("p (h d) -> p h d", h=8)
                em.op("dve", lambda e, xv=xv: e.tensor_tensor(out=xdt[:].rearrange("p (h d) -> p h d", h=8), in0=xv, in1=_bcast_last(sm[:, 0:8], 64), op=ALU.mult),
                      reads=[bxtm, bsm], writes=[bxdt])
                em.op("pool", lambda e: e.tensor_tensor(out=xdd[:].rearrange("p (h d) -> p h d", h=8), in0=xdt[:].rearrange("p (h d) -> p h d", h=8),
                                                        in1=_bcast_last(sm[:, 40:48], 64), op=ALU.mult), reads=[bxdt, bsm], writes=[bxdd])
                for g in range(2):
                    em.op("pe", lambda e, g=g, c0=c0: e.matmul(ps[2][:, g * 128:(g + 1) * 128], lhsT=BT[:, g, c0:c0 + 128], rhs=CT[:, g, c0:c0 + 128], start=True, stop=True),
                          reads=[bBT, bCT], writes=[psb[2]])
                em.op("dve", lambda e: e.tensor_tensor(out=cbm[:].rearrange("p (g i) -> p g i", g=2), in0=ps[2][:, 0:256].rearrange("p (g i) -> p g i", g=2),
                                                       in1=_bcast_mid(triU_f, 2), op=ALU.mult), reads=[psb[2], bc], writes=[bcbm])
                if n > 0:
                    for g in range(2):
                        em.op("pe", lambda e, g=g, c0=c0: e.matmul(ps[4][:, g * 256:(g + 1) * 256], lhsT=CT[:, g, c0:c0 + 128], rhs=Hb[:, g * 256:(g + 1) * 256], start=True, stop=True),
                              reads=[bCT, bHb], writes=[psb[4]])
                for h in range(8):
                    s = h % 2
                    g = h // 4
                    em.op("dve", lambda e, h=h, s=s: e.tensor_scalar(out=lh[:, s, :], in0=SL_f, scalar1=sm[:, 8 + h:9 + h], scalar2=None, op0=ALU.mult),
                          reads=[bc, bsm], writes=[blh[s]])
                    em.op("pe", lambda e, s=s: e.matmul(ps[s][:, 0:128], lhsT=lh[:, s, :], rhs=triU_f, start=True, stop=True), reads=[blh[s], bc], writes=[psb[s]])
                    em.op("act", lambda e, s=s: e.activation(out=Lx[:, s, :], in_=ps[s][:, 0:128], func=AF.Exp), reads=[psb[s]], writes=[bLx[s]])
                    em.op("pool", lambda e, s=s, g=g: e.tensor_tensor(out=Mx[:, s, :], in0=Lx[:, s, :], in1=cbm[:, g * 128:(g + 1) * 128], op=ALU.mult),
                          reads=[bLx[s], bcbm], writes=[bMx[s]])
                    em.op("pe", lambda e, s=s, h=h: e.matmul(ps[3][:, h * 64:(h + 1) * 64], lhsT=Mx[:, s, :], rhs=xdt[:, h * 64:(h + 1) * 64], start=True, stop=True),
                          reads=[bMx[s], bxdt], writes=[psb[3]])
                if n > 0:
                    em.op("dve", lambda e: e.tensor_tensor(out=t1[:].rearrange("p (h d) -> p h d", h=8), in0=ps[4][:].rearrange("p (h d) -> p h d", h=8),
                                                           in1=_bcast_last(sm[:, 32:40], 64), op=ALU.mult), reads=[psb[4], bsm], writes=[bt1])
                    em.op("dve", lambda e: e.tensor_tensor(out=t1[:], in0=ps[3][:], in1=t1[:], op=ALU.add), reads=[psb[3], bt1], writes=[bt1])
                else:
                    em.op("dve", lambda e: e.tensor_copy(out=t1[:], in_=ps[3][:]), reads=[psb[3]], writes=[bt1])
                em.op("pool", lambda e, xv=xv: e.tensor_tensor(out=t2[:].rearrange("p (h d) -> p h d", h=8), in0=xv, in1=_bcast_last(self.rv[:, oD:oD + 8], 64), op=ALU.mult),
                      reads=[bxtm, bc], writes=[bt2])
                em.op("pool", lambda e: e.tensor_tensor(out=t1[:], in0=t1[:], in1=t2[:], op=ALU.add), reads=[bt1, bt2], writes=[bt1])
                if n < NB - 1:
                    for g in range(2):
                        em.op("pe", lambda e, g=g, n=n: e.matmul(ps[5][:, g * 256:(g + 1) * 256], lhsT=B_tm[:, n, g * 128:(g + 1) * 128], rhs=xdd[:, g * 256:(g + 1) * 256], start=True, stop=True),
                              reads=[bBtm, bxdd], writes=[psb[5]])
                    if n == 0:
                        em.op("dve", lambda e: e.tensor_copy(out=H[:], in_=ps[5][:]), reads=[psb[5]], writes=[bH])
                    else:
                        em.op("pool", lambda e: e.tensor_tensor(out=H[:].rearrange("p (h d) -> p h d", h=8), in0=H[:].rearrange("p (h d) -> p h d", h=8),
                                                                in1=_bcast_last(sm[:, 48:56], 64), op=ALU.mult), reads=[bH, bsm], writes=[bH])
                        em.op("dve", lambda e: e.tensor_tensor(out=H[:], in0=ps[5][:], in1=H[:], op=ALU.add), reads=[psb[5], bH], writes=[bH])
                    em.op("act", lambda e: e.copy(out=Hb[:], in_=H[:]), reads=[bH], writes=[bHb])
                for k in range(8):
                    em.op("pe", lambda e, k=k, c0=c0: e.matmul(ps[6][:], lhsT=self.hT[:, k, c0:c0 + 128], rhs=wz[:, k, :], start=(k == 0), stop=(k == 7)),
                          reads=[bwz, self.b_hT], writes=[psb[6]])
                em.op("act", lambda e: e.activation(out=zs[:], in_=ps[6][:], func=AF.Silu), reads=[psb[6]], writes=[bzs])
                em.op("dve", lambda e: e.tensor_tensor(out=t1[:], in0=t1[:], in1=zs[:], op=ALU.mult), reads=[bt1, bzs], writes=[bt1])
                em.op("act", lambda e: e.activation(out=t2[:], in_=t1[:], func=AF.Square), reads=[bt1], writes=[bt2])
                em.op("dve", lambda e: e.reduce_sum(out=sm[:, 56:57], in_=t2[:], axis=X), reads=[bt2], writes=[bsm])
                em.op("act", lambda e: e.activation(out=sm[:, 57:58], in_=sm[:, 56:57], func=AF.Sqrt, scale=1.0 / 512, bias=self.eps_c[:, 0:1]), reads=[bsm, bc], writes=[bsm])
                em.op("dve", lambda e: e.reciprocal(out=sm[:, 57:58], in_=sm[:, 57:58]), reads=[bsm], writes=[bsm])
                em.op("dve", lambda e: e.scalar_tensor_tensor(out=yn[:], in0=t1[:], scalar=sm[:, 57:58], in1=self.rv[:, oG:oG + 512], op0=ALU.mult, op1=ALU.mult),
                      reads=[bt1, bsm, bc], writes=[byn])
                qn = n % 4
                for c in range(4):
                    em.op("pe", lambda e, c=c: e.transpose(out=psT[:, c * 128:(c + 1) * 128], in_=yn[:, c * 128:(c + 1) * 128], identity=self.ident_bf[:]),
                          reads=[byn, bc], writes=[psb[7]])
                em.op("act", lambda e, qn=qn: e.copy(out=ysg[:, :, qn * 128:(qn + 1) * 128], in_=psT[:, 0:512].rearrange("p (a b) -> p a b", a=4)), reads=[psb[7]], writes=[bysg])
                if qn == 3:
                    t = n // 4
                    em.dma("sp", self.dram["ybr"][1536:2048, t * 512:(t + 1) * 512].rearrange("(c p) s -> p c s", p=128), ysg[:], reads=[bysg], writes=[self.b_ybr])
            em.barrier()


Builder.mixer_D = _mixer_D


def build_program(S=4096):
    b = Builder(S)
    em = b.em
    b.phase_copy_in()
    b.b_ybr = em.buf("ybr")
    b.b_gat = em.buf("gat")
    for l in range(DEPTH):
        with ExitStack() as st:
            b.mix_begin(l, st)
            b.gates(l)
            b.mixer_C(l)
            b.mixer_A(l)
            b.mixer_B(l)
            b.mixer_D(l)
        em.barrier()
        b.phase_merge(l)
        b.phase_xattn(l)
        b.phase_ffn(l)
    toks = b.phase_final()
    em.finish(toks)
    return b


_CACHE = {}


def kernel(**inputs):
    S = 4096
    if "b" not in _CACHE:
        _CACHE["b"] = build_program(S)
    b = _CACHE["b"]
    in_maps = [make_in_map(inputs, c % 4, S) for c in range(8)]
    res = run_bass_kernel_spmd(b.nc, in_maps, core_ids=list(range(8)))
    out = np.stack([np.asarray(res.results[c]["out"]).T for c in range(4)], axis=0)
    return np.ascontiguousarray(out.astype(np.float32))
```

```python
import math
from contextlib import ExitStack

import numpy as np
import concourse.bass as bass
import concourse.mybir as mybir
from concourse.bass_utils import run_bass_kernel_spmd

F32 = mybir.dt.float32
BF16 = mybir.dt.bfloat16
AF = mybir.ActivationFunctionType
ALU = mybir.AluOpType

D = 1024
DC = 8
DEPTH = 2
MEM = 256
EPS = 1e-6
FFN = 2816
FC = 22
IN_W = 10248
O_AQ, O_AK, O_AV = 0, 512, 1024
O_BQ, O_BK, O_BV, O_BG = 1536, 1792, 2048, 2560
O_CQ, O_CK, O_CV = 3072, 3584, 4096
O_DZ, O_DX, O_DDT, O_G = 4608, 5120, 6144, 6152


class Buf:
    __slots__ = ("name", "w", "r")

    def __init__(self, name):
        self.name = name
        self.w = None
        self.r = []


class Em:
    NRING = 24

    def __init__(self, nc):
        self.nc = nc
        self.eng = {"pe": nc.tensor, "act": nc.scalar, "dve": nc.vector, "pool": nc.gpsimd, "sp": nc.sync}
        self.sem = {}
        self.cnt = {}
        for e in ("pe", "act", "dve", "pool"):
            self.sem[e] = nc.alloc_semaphore("s_" + e)
            self.cnt[e] = 0
        self.known = {e: {} for e in self.eng}
        self.ring = {}
        self.ring_use = {}
        self.ring_next = {}
        for q in ("sp", "pool", "act"):
            self.ring[q] = [nc.alloc_semaphore("d_%s_%d" % (q, i)) for i in range(self.NRING)]
            self.ring_use[q] = [0] * self.NRING
            self.ring_next[q] = 0
        self.dma_tokens = []
        self.nbuf = 0

    def buf(self, name=None):
        self.nbuf += 1
        return Buf(name or ("b%d" % self.nbuf))

    def _wait(self, engine, tok):
        if tok is None:
            return
        key, sem, val = tok
        if key == engine and engine == "pe":
            return
        kn = self.known[engine]
        if kn.get(key, 0) >= val:
            return
        self.eng[engine].wait_ge(sem, val)
        kn[key] = val

    def _deps(self, engine, reads, writes):
        for b in reads:
            self._wait(engine, b.w)
        for b in writes:
            self._wait(engine, b.w)
            for t in b.r:
                self._wait(engine, t)

    def _commit(self, tok, reads, writes):
        for b in reads:
            b.r.append(tok)
            if len(b.r) > 12:
                last = {}
                for t in b.r:
                    if t[0] not in last or last[t[0]][2] < t[2]:
                        last[t[0]] = t
                b.r = list(last.values())
        for b in writes:
            b.w = tok
            b.r = []

    def op(self, engine, fn, reads=(), writes=()):
        self._deps(engine, reads, writes)
        inst = fn(self.eng[engine])
        self.cnt[engine] += 1
        inst.then_inc(self.sem[engine], 1)
        tok = (engine, self.sem[engine], self.cnt[engine])
        self._commit(tok, reads, writes)
        return tok

    def dma(self, q, out, in_, reads=(), writes=()):
        i = self.ring_next[q]
        self.ring_next[q] = (i + 1) % self.NRING
        sem = self.ring[q][i]
        key = ("dma", q, i)
        prior = 16 * self.ring_use[q][i]
        if prior:
            self._wait(q, (key, sem, prior))
        self._deps(q, reads, writes)
        self.eng[q].dma_start(out=out, in_=in_).then_inc(sem, 16)
        self.ring_use[q][i] += 1
        tok = (key, sem, 16 * self.ring_use[q][i])
        self._commit(tok, reads, writes)
        self.dma_tokens.append(tok)
        if len(self.dma_tokens) > 3 * self.NRING:
            self.dma_tokens = self.dma_tokens[-3 * self.NRING:]
        return tok

    def barrier(self):
        toks = [(e, self.sem[e], self.cnt[e]) for e in ("pe", "act", "dve", "pool") if self.cnt[e]]
        for q in self.ring:
            for i in range(self.NRING):
                if self.ring_use[q][i]:
                    toks.append((("dma", q, i), self.ring[q][i], 16 * self.ring_use[q][i]))
        for e in self.eng:
            for t in toks:
                if t[0] == e and e == "pe":
                    continue
                self._wait(e, t)

    def finish(self, toks):
        for t in toks:
            self._wait("sp", t)


def _ap(t):
    return t if isinstance(t, bass.AP) else t.ap()


class Ctx:
    def __init__(self, S, depth=DEPTH):
        self.S = S
        self.NT = S // 512
        self.NB = S // 128
        self.depth = depth
        nc = self.nc = bass.Bass("TRN2", target_bir_lowering=False)
        self.em = Em(nc)
        self.es = ExitStack()
        dr = self.dram = {}

        def din(name, shape, dt=F32):
            dr[name] = nc.dram_tensor(name, list(shape), dt, kind="ExternalInput").ap()

        din("xT", [D, S])
        din("memT", [D, MEM])
        din("w_in", [depth, D, IN_W])
        din("w_branch", [depth, 2048, D])
        din("w_mix_out", [depth, D, D])
        din("w_xq", [depth, D, 512])
        din("w_xkv", [depth, D, 1024])
        din("w_xo", [depth, 512, D])
        din("w_ffn_in", [depth, D, 2 * FFN])
        din("w_ffn_out", [depth, FFN, D])
        din("pvec", [128, PV_N])
        din("rvec", [128, RV_N])
        din("cst_f32", [128, CF_N])
        din("rot", [64, 2, S])
        dr["out"] = nc.dram_tensor("out", [D, S], F32, kind="ExternalOutput").ap()
        dr["xs"] = nc.dram_tensor("xs", [D, S], F32, kind="Internal").ap()
        dr["ybr"] = nc.dram_tensor("ybr", [2048, S], BF16, kind="Internal").ap()
        dr["gat"] = nc.dram_tensor("gat", [4096, S], BF16, kind="Internal").ap()

    def sb(self, name, shape, dt):
        return self.es.enter_context(self.nc.sbuf_tensor(name, list(shape), dt))

    _uid = 0

    def tl(self, st, name, shape, dt):
        Ctx._uid += 1
        return st.enter_context(self.nc.sbuf_tensor("%s_%d" % (name, Ctx._uid), list(shape), dt))


def _pv_layout():
    off = {}
    n = 0
    for l in range(DEPTH):
        for nm, w in (("norm_mix_g", 8), ("norm_x_g", 8), ("norm_mem_g", 8), ("norm_ffn_g", 8),
                      ("ret_gn_g", 4), ("diff_subln_g", 1), ("conv_w", 32), ("conv_b", 8)):
            off[(nm, l)] = n
            n += w
    off[("norm_f_g", 0)] = n
    n += 8
    return off, n


PV_OFF, PV_N = _pv_layout()


def _rv_layout():
    off = {}
    n = 0
    for l in range(DEPTH):
        for nm, w in (("dt_bias", 8), ("A_log", 8), ("D", 8), ("ssm_norm_g", 512), ("diff_lambda", 256)):
            off[(nm, l)] = n
            n += w
    return off, n


RV_OFF, RV_N = _rv_layout()

CF_IDENT, CF_TRIU, CF_TRIL, CF_DECT, CF_KEND, CF_QDEC, CF_SL = 0, 128, 256, 384, 896, 900, 1412
CF_N = 1540


class Builder(Ctx):
    def __init__(self, S, depth=DEPTH):
        super().__init__(S, depth)
        nc, em = self.nc, self.em
        self.pv = self.sb("pv", [128, PV_N], F32)
        self.rv = self.sb("rv", [128, RV_N], F32)
        self.cf = self.sb("cf", [128, CF_N], F32)
        self.ones_bf = self.sb("ones_bf", [128, 128], BF16)
        self.ones_f = self.sb("ones_f", [128, 128], F32)
        self.ident_bf = self.sb("ident_bf", [128, 128], BF16)
        self.triU_bf = self.sb("triU_bf", [128, 128], BF16)
        self.triL_bf = self.sb("triL_bf", [128, 128], BF16)
        self.b_const = em.buf("const")
        self.ps = []
        self.psb = []
        for i in range(8):
            self.ps.append(self.es.enter_context(nc.psum_tensor("ps%d" % i, [128, 512], F32)))
            self.psb.append(em.buf("ps%d" % i))
        self.stg = [self.sb("stg%d" % i, [128, 1024], F32) for i in range(2)]
        self.stgb = [em.buf("stg%d" % i) for i in range(2)]
        self.stg_i = 0
        d = self.dram
        bc = self.b_const
        em.dma("sp", self.pv[:], d["pvec"], writes=[bc])
        em.dma("sp", self.rv[:], d["rvec"], writes=[bc])
        em.dma("sp", self.cf[:], d["cst_f32"], writes=[bc])
        em.op("dve", lambda e: e.memset(self.ones_bf[:], 1.0), writes=[bc])
        em.op("dve", lambda e: e.memset(self.ones_f[:], 1.0), writes=[bc])
        self.eps_c = self.sb("eps_c", [128, 1], F32)
        em.op("dve", lambda e: e.memset(self.eps_c[:], EPS), writes=[bc])
        em.op("dve", lambda e: e.tensor_copy(out=self.ident_bf[:], in_=self.cf[:, CF_IDENT:CF_IDENT + 128]), reads=[bc], writes=[bc])
        em.op("dve", lambda e: e.tensor_copy(out=self.triU_bf[:], in_=self.cf[:, CF_TRIU:CF_TRIU + 128]), reads=[bc], writes=[bc])
        em.op("dve", lambda e: e.tensor_copy(out=self.triL_bf[:], in_=self.cf[:, CF_TRIL:CF_TRIL + 128]), reads=[bc], writes=[bc])
        em.barrier()

    def pvcol(self, name, l, j=0):
        o = PV_OFF[(name, l)] + j
        return self.pv[:, o:o + 1]

    def load_w(self, dst, src, nrows, ncols, wbuf):
        em = self.em
        c0 = 0
        while c0 < ncols:
            w = min(1024, ncols - c0)
            i = self.stg_i
            self.stg_i = (i + 1) % 2
            st, sbf = self.stg[i], self.stgb[i]
            em.dma("sp", st[0:nrows, 0:w], src[:, c0:c0 + w], writes=[sbf])
            o, ww, cc = dst, w, c0
            em.op("pool", lambda e, st=st, o=o, ww=ww, cc=cc: e.tensor_copy(out=o[:, cc:cc + ww], in_=st[0:nrows, 0:ww]),
                  reads=[sbf], writes=[wbuf])
            c0 += w

    def norm_tile(self, xt, ht, gname, l, sq, xb, hb, width=512, pi=0):
        em = self.em
        ps, psb = self.ps[pi], self.psb[pi]
        bsq = self.b_sq
        for c in range(DC):
            em.op("act", lambda e, c=c: e.activation(out=sq[:, c % 2, 0:width], in_=xt[:, c, 0:width], func=AF.Square),
                  reads=[xb], writes=[bsq[c % 2]])
            em.op("pe", lambda e, c=c: e.matmul(ps[:, 0:width], lhsT=self.ones_bf[:], rhs=sq[:, c % 2, 0:width],
                                                start=(c == 0), stop=(c == DC - 1)),
                  reads=[bsq[c % 2], self.b_const], writes=[psb])
        rs = self.rstd
        em.op("act", lambda e: e.activation(out=rs[:, 0:width], in_=ps[:, 0:width], func=AF.Sqrt, scale=1.0 / D, bias=self.eps_t[:, 0:1]),
              reads=[psb, self.b_const], writes=[self.b_rstd])
        em.op("dve", lambda e: e.reciprocal(out=rs[:, 0:width], in_=rs[:, 0:width]), reads=[self.b_rstd], writes=[self.b_rstd])
        for c in range(DC):
            em.op("dve", lambda e, c=c: e.scalar_tensor_tensor(out=ht[:, c, 0:width], in0=xt[:, c, 0:width],
                                                                scalar=self.pvcol(gname, l, c), in1=rs[:, 0:width],
                                                                op0=ALU.mult, op1=ALU.mult),
                  reads=[xb, self.b_rstd, self.b_const], writes=[hb])

    def alloc_norm_scratch(self, st):
        em = self.em
        self.sqt = self.tl(st, "sqt", [128, 2, 512], BF16)
        self.rstd = self.tl(st, "rstd", [128, 512], F32)
        self.eps_t = self.tl(st, "eps_t", [128, 1], F32)
        self.b_sq = [em.buf("sq0"), em.buf("sq1")]
        self.b_rstd = em.buf("rstd")
        em.op("dve", lambda e: e.memset(self.eps_t[:], EPS), writes=[self.b_const])

    def xtile_ap(self, name, t, width=512):
        return self.dram[name].rearrange("(c p) s -> p c s", p=128)[:, :, t * width:(t + 1) * width]

    def phase_copy_in(self):
        em = self.em
        b = self.b_xs = em.buf("xs")
        for c in range(DC):
            em.dma("sp", self.dram["xs"][c * 128:(c + 1) * 128, :], self.dram["xT"][c * 128:(c + 1) * 128, :], writes=[b])
        em.barrier()

    def phase_ffn(self, l):
        em, nc = self.em, self.nc
        with ExitStack() as st:
            self.alloc_norm_scratch(st)
            w1 = self.tl(st, "w1", [128, 8, 2 * FFN], BF16)
            w2 = self.tl(st, "w2", [128, FC, D], BF16)
            xt1 = self.tl(st, "xt0", [128, 8, 512], F32)
            xt = [xt1, xt1]
            ht = self.tl(st, "ht", [128, 8, 512], BF16)
            u = self.tl(st, "u", [128, FC, 512], BF16)
            sa = self.tl(st, "sa", [128, 2, 512], BF16)
            bw1 = [em.buf() for _ in range(8)]
            bw2 = [em.buf() for _ in range(FC)]
            bx0 = em.buf()
            bx = [bx0, bx0]
            bh, bu, bsa = em.buf(), em.buf(), [em.buf(), em.buf()]
            for k in range(8):
                self.load_w(w1[:, k, :], self.dram["w_ffn_in"][l, k * 128:(k + 1) * 128, :], 128, 2 * FFN, bw1[k])
            for f in range(FC):
                self.load_w(w2[:, f, :], self.dram["w_ffn_out"][l, f * 128:(f + 1) * 128, :], 128, D, bw2[f])
            for t in range(self.NT):
                cur = 0
                em.dma("sp", xt[cur][:], self.xtile_ap("xs", t), reads=[self.b_xs], writes=[bx[cur]])
                self.norm_tile(xt[cur], ht, "norm_ffn_g", l, self.sqt, bx[cur], bh)
                for f in range(FC):
                    pa, pb = 1 + (f % 2) * 2, 2 + (f % 2) * 2
                    for k in range(8):
                        em.op("pe", lambda e, k=k, f=f, pa=pa: e.matmul(self.ps[pa][:], lhsT=w1[:, k, f * 128:(f + 1) * 128], rhs=ht[:, k, :],
                                                                    start=(k == 0), stop=(k == 7)),
                              reads=[bw1[k], bh], writes=[self.psb[pa]])
                    for k in range(8):
                        em.op("pe", lambda e, k=k, f=f, pb=pb: e.matmul(self.ps[pb][:], lhsT=w1[:, k, FFN + f * 128:FFN + (f + 1) * 128], rhs=ht[:, k, :],
                                                                    start=(k == 0), stop=(k == 7)),
                              reads=[bw1[k], bh], writes=[self.psb[pb]])
                    si = f % 2
                    em.op("act", lambda e, pa=pa, si=si: e.activation(out=sa[:, si, :], in_=self.ps[pa][:], func=AF.Silu),
                          reads=[self.psb[pa]], writes=[bsa[si]])
                    em.op("dve", lambda e, pb=pb, si=si, f=f: e.tensor_tensor(out=u[:, f, :], in0=self.ps[pb][:], in1=sa[:, si, :], op=ALU.mult),
                          reads=[self.psb[pb], bsa[si]], writes=[bu])
                for c in range(DC):
                    po = 5 + (c % 2)
                    for f in range(FC):
                        em.op("pe", lambda e, c=c, f=f, po=po: e.matmul(self.ps[po][:], lhsT=w2[:, f, c * 128:(c + 1) * 128], rhs=u[:, f, :],
                                                                    start=(f == 0), stop=(f == FC - 1)),
                              reads=[bw2[f], bu], writes=[self.psb[po]])
                    em.op("dve", lambda e, c=c, po=po, cur=cur: e.tensor_tensor(out=xt[cur][:, c, :], in0=self.ps[po][:], in1=xt[cur][:, c, :], op=ALU.add),
                          reads=[self.psb[po], bx[cur]], writes=[bx[cur]])
                em.dma("sp", self.xtile_ap("xs", t), xt[cur][:], reads=[bx[cur]], writes=[self.b_xs])
            em.barrier()

    def phase_final(self):
        em, nc = self.em, self.nc
        toks = []
        with ExitStack() as st:
            self.alloc_norm_scratch(st)
            xt = self.tl(st, "xt", [128, 8, 512], F32)
            ot = self.tl(st, "ot", [128, 8, 512], F32)
            bx, bo = em.buf(), em.buf()
            b_out = em.buf("out")
            for t in range(self.NT):
                em.dma("sp", xt[:], self.xtile_ap("xs", t), reads=[self.b_xs], writes=[bx])
                self.norm_tile(xt, ot, "norm_f_g", 0, self.sqt, bx, bo)
                toks.append(em.dma("sp", self.xtile_ap("out", t), ot[:], reads=[bo], writes=[b_out]))
            em.barrier()
        return toks


def _chunked(v, nch):
    return np.ascontiguousarray(np.asarray(v, np.float32).reshape(nch, 128).T)


def make_tables(inp, S):
    pv = np.zeros((128, PV_N), np.float32)
    rv = np.zeros((128, RV_N), np.float32)
    for l in range(DEPTH):
        pv[:, PV_OFF[("norm_mix_g", l)]:][:, :8] = _chunked(inp["norm_mix_g"][l], 8)
        pv[:, PV_OFF[("norm_x_g", l)]:][:, :8] = _chunked(inp["norm_x_g"][l], 8)
        pv[:, PV_OFF[("norm_mem_g", l)]:][:, :8] = _chunked(inp["norm_mem_g"][l], 8)
        pv[:, PV_OFF[("norm_ffn_g", l)]:][:, :8] = _chunked(inp["norm_ffn_g"][l], 8)
        pv[:, PV_OFF[("ret_gn_g", l)]:][:, :4] = _chunked(inp["ret_gn_g"][l], 4)
        pv[:, PV_OFF[("diff_subln_g", l)]:][:, :1] = _chunked(inp["diff_subln_g"][l], 1)
        cw = np.asarray(inp["ssm_conv_w"][l], np.float32)
        o = PV_OFF[("conv_w", l)]
        for c in range(8):
            for k in range(4):
                pv[:, o + c * 4 + k] = cw[k, c * 128:(c + 1) * 128]
        pv[:, PV_OFF[("conv_b", l)]:][:, :8] = _chunked(inp["ssm_conv_b"][l], 8)
        for nm, key, w in (("dt_bias", "ssm_dt_bias", 8), ("A_log", "ssm_A_log", 8), ("D", "ssm_D", 8),
                           ("ssm_norm_g", "ssm_norm_g", 512)):
            o = RV_OFF[(nm, l)]
            rv[:, o:o + w] = np.asarray(inp[key][l], np.float32).reshape(1, w)
        o = RV_OFF[("diff_lambda", l)]
        rv[:, o:o + 256] = np.asarray(inp["diff_lambda"][l], np.float32).reshape(1, 256)
    pv[:, PV_OFF[("norm_f_g", 0)]:][:, :8] = _chunked(inp["norm_f_g"], 8)
    cf = np.zeros((128, CF_N), np.float32)
    idx = np.arange(128)
    cf[:, CF_IDENT:CF_IDENT + 128] = np.eye(128, dtype=np.float32)
    cf[:, CF_TRIU:CF_TRIU + 128] = (idx[:, None] <= idx[None, :])
    cf[:, CF_TRIL:CF_TRIL + 128] = (idx[:, None] >= idx[None, :])
    cf[:, CF_SL:CF_SL + 128] = (idx[:, None] > idx[None, :])
    lg = np.log1p(-np.exp2(-5.0 - np.arange(4, dtype=np.float32))).astype(np.float32)
    for h in range(4):
        rel = (idx[None, :] - idx[:, None]).astype(np.float32)
        dec = np.where(rel >= 0, np.exp(lg[h] * np.maximum(rel, 0.0)), 0.0)
        cf[:, CF_DECT + h * 128:CF_DECT + (h + 1) * 128] = dec
        cf[:, CF_KEND + h] = np.exp((127 - idx) * lg[h])
        cf[:, CF_QDEC + h * 128:CF_QDEC + (h + 1) * 128] = np.exp((idx + 1.0) * lg[h])[None, :]
    half = 32
    inv_freq = (10000.0 ** (-np.arange(half, dtype=np.float32) / half)).astype(np.float32)
    pos = np.arange(S, dtype=np.float32)
    ang = (pos[None, :] * inv_freq[:, None]).astype(np.float32)
    cos, sin = np.cos(ang).astype(np.float32), np.sin(ang).astype(np.float32)
    rot = np.zeros((64, 2, S), np.float32)
    rot[:32, 0] = cos
    rot[32:, 0] = cos
    rot[:32, 1] = -sin
    rot[32:, 1] = sin
    return pv, rv, cf, rot


def make_in_map(inp, b, S):
    pv, rv, cf, rot = make_tables(inp, S)
    f = lambda a: np.ascontiguousarray(np.asarray(a, np.float32))
    return {
        "xT": f(np.asarray(inp["x"][b]).T[:, :S]),
        "memT": f(np.asarray(inp["mem"][b]).T),
        "w_in": f(inp["w_in"]),
        "w_branch": f(np.asarray(inp["w_branch"]).reshape(DEPTH, 2048, D)),
        "w_mix_out": f(inp["w_mix_out"]),
        "w_xq": f(inp["w_xq"]),
        "w_xkv": f(inp["w_xkv"]),
        "w_xo": f(inp["w_xo"]),
        "w_ffn_in": f(inp["w_ffn_in"]),
        "w_ffn_out": f(inp["w_ffn_out"]),
        "pvec": pv, "rvec": rv, "cst_f32": cf, "rot": rot,
    }


def _phase_xattn(self, l):
    em, nc = self.em, self.nc
    ps, psb = self.ps, self.psb
    with ExitStack() as st:
        self.alloc_norm_scratch(st)
        wq = self.tl(st, "wq", [128, 8, 512], BF16)
        wkv = self.tl(st, "wkv", [128, 8, 1024], BF16)
        wo = self.tl(st, "wo", [128, 4, D], BF16)
        mt = self.tl(st, "mt", [128, 8, MEM], F32)
        mh = self.tl(st, "mh", [128, 8, MEM], BF16)
        kT = self.tl(st, "kT", [128, 4, MEM], BF16)
        V = self.tl(st, "V", [128, 2, 512], BF16)
        xt = self.tl(st, "xt", [128, 8, 512], F32)
        ht = self.tl(st, "ht", [128, 8, 512], BF16)
        qh = self.tl(st, "qh", [128, 512], BF16)
        PT = self.tl(st, "PT", [128, 2, 512], BF16)
        rden = self.tl(st, "rden", [128, 512], F32)
        o = self.tl(st, "o", [128, 4, 512], BF16)
        bwq, bwkv, bwo = em.buf(), em.buf(), em.buf()
        bmt, bmh, bk, bv = em.buf(), em.buf(), em.buf(), em.buf()
        bx, bh, bq, bp, brd, bo = em.buf(), em.buf(), em.buf(), [em.buf(), em.buf()], em.buf(), em.buf()
        for k in range(8):
            self.load_w(wq[:, k, :], self.dram["w_xq"][l, k * 128:(k + 1) * 128, :], 128, 512, bwq)
            self.load_w(wkv[:, k, :], self.dram["w_xkv"][l, k * 128:(k + 1) * 128, :], 128, 1024, bwkv)
        for h in range(4):
            self.load_w(wo[:, h, :], self.dram["w_xo"][l, h * 128:(h + 1) * 128, :], 128, D, bwo)
        em.dma("sp", mt[:], self.dram["memT"].rearrange("(c p) s -> p c s", p=128), writes=[bmt])
        self.norm_tile(mt, mh, "norm_mem_g", l, self.sqt, bmt, bmh, width=MEM)
        for h in range(4):
            for k in range(8):
                em.op("pe", lambda e, k=k, h=h: e.matmul(ps[1][:, 0:MEM], lhsT=wkv[:, k, h * 128:(h + 1) * 128], rhs=mh[:, k, :],
                                                         start=(k == 0), stop=(k == 7)), reads=[bwkv, bmh], writes=[psb[1]])
            em.op("act", lambda e, h=h: e.copy(out=kT[:, h, :], in_=ps[1][:, 0:MEM]), reads=[psb[1]], writes=[bk])
        for mb in range(2):
            for k in range(8):
                em.op("pe", lambda e, k=k, mb=mb: e.matmul(ps[2][:], lhsT=mh[:, k, mb * 128:(mb + 1) * 128], rhs=wkv[:, k, 512:1024],
                                                           start=(k == 0), stop=(k == 7)), reads=[bwkv, bmh], writes=[psb[2]])
            em.op("act", lambda e, mb=mb: e.copy(out=V[:, mb, :], in_=ps[2][:]), reads=[psb[2]], writes=[bv])
        sc = 128.0 ** -0.5
        for t in range(self.NT):
            em.dma("sp", xt[:], self.xtile_ap("xs", t), reads=[self.b_xs], writes=[bx])
            self.norm_tile(xt, ht, "norm_x_g", l, self.sqt, bx, bh)
            for h in range(4):
                for k in range(8):
                    em.op("pe", lambda e, k=k, h=h: e.matmul(ps[1][:], lhsT=wq[:, k, h * 128:(h + 1) * 128], rhs=ht[:, k, :],
                                                             start=(k == 0), stop=(k == 7)), reads=[bwq, bh], writes=[psb[1]])
                em.op("act", lambda e: e.copy(out=qh[:], in_=ps[1][:]), reads=[psb[1]], writes=[bq])
                for mb in range(2):
                    em.op("pe", lambda e, h=h, mb=mb: e.matmul(ps[2 + mb][:], lhsT=kT[:, h, mb * 128:(mb + 1) * 128], rhs=qh[:],
                                                               start=True, stop=True), reads=[bk, bq], writes=[psb[2 + mb]])
                    em.op("act", lambda e, mb=mb: e.activation(out=PT[:, mb, :], in_=ps[2 + mb][:], func=AF.Exp, scale=sc),
                          reads=[psb[2 + mb]], writes=[bp[mb]])
                for mb in range(2):
                    em.op("pe", lambda e, h=h, mb=mb: e.matmul(ps[4][:], lhsT=V[:, mb, h * 128:(h + 1) * 128], rhs=PT[:, mb, :],
                                                               start=(mb == 0), stop=(mb == 1)), reads=[bv, bp[mb]], writes=[psb[4]])
                for mb in range(2):
                    em.op("pe", lambda e, mb=mb: e.matmul(ps[5][:], lhsT=self.ones_bf[:], rhs=PT[:, mb, :],
                                                          start=(mb == 0), stop=(mb == 1)), reads=[self.b_const, bp[mb]], writes=[psb[5]])
                em.op("dve", lambda e: e.reciprocal(out=rden[:], in_=ps[5][:]), reads=[psb[5]], writes=[brd])
                em.op("dve", lambda e, h=h: e.tensor_tensor(out=o[:, h, :], in0=ps[4][:], in1=rden[:], op=ALU.mult),
                      reads=[psb[4], brd], writes=[bo])
            for c in range(DC):
                po = 6 + (c % 2)
                for h in range(4):
                    em.op("pe", lambda e, c=c, h=h, po=po: e.matmul(ps[po][:], lhsT=wo[:, h, c * 128:(c + 1) * 128], rhs=o[:, h, :],
                                                                    start=(h == 0), stop=(h == 3)), reads=[bwo, bo], writes=[psb[po]])
                em.op("dve", lambda e, c=c, po=po: e.tensor_tensor(out=xt[:, c, :], in0=ps[po][:], in1=xt[:, c, :], op=ALU.add),
                      reads=[psb[po], bx], writes=[bx])
            em.dma("sp", self.xtile_ap("xs", t), xt[:], reads=[bx], writes=[self.b_xs])
        em.barrier()


Builder.phase_xattn = _phase_xattn


def _mix_begin(self, l, st):
    em, nc = self.em, self.nc
    self.hT = self.tl(st, "hT", [128, 8, self.S], BF16)
    self.b_hT = em.buf("hT")
    with ExitStack() as s2:
        self.alloc_norm_scratch(s2)
        xt = self.tl(s2, "xt", [128, 8, 512], F32)
        bx = em.buf()
        for t in range(self.NT):
            em.dma("sp", xt[:], self.xtile_ap("xs", t), reads=[self.b_xs], writes=[bx])
            self.norm_tile(xt, self.hT[:, :, t * 512:(t + 1) * 512], "norm_mix_g", l, self.sqt, bx, self.b_hT)
        em.barrier()


def _load_wk(self, dst, l, col0, n, wbuf):
    em = self.em
    i = self.stg_i
    self.stg_i = (i + 1) % 2
    stg, sbf = self.stg[i], self.stgb[i]
    src = self.dram["w_in"][l].rearrange("(k p) c -> p k c", p=128)[:, :, col0:col0 + n]
    sv = stg[:, 0:8 * n].rearrange("p (k c) -> p k c", k=8)
    em.dma("sp", sv, src, writes=[sbf])
    em.op("pool", lambda e: e.tensor_copy(out=dst, in_=sv), reads=[sbf], writes=[wbuf])


def _proj_fm(self, w, n, wbuf, pi, evac):
    em = self.em
    for t in range(self.NT):
        p = pi[t % len(pi)]
        for k in range(8):
            em.op("pe", lambda e, k=k, t=t, p=p: e.matmul(self.ps[p][0:n, :], lhsT=w[:, k, 0:n], rhs=self.hT[:, k, t * 512:(t + 1) * 512],
                                                          start=(k == 0), stop=(k == 7)), reads=[wbuf, self.b_hT], writes=[self.psb[p]])
        evac(t, p)


def _proj_tm(self, w, n, wbuf, tok_ap_fn, nblk, pi, evac):
    em = self.em
    for b in range(nblk):
        p = pi[b % len(pi)]
        for k in range(8):
            em.op("pe", lambda e, k=k, b=b, p=p: e.matmul(self.ps[p][:, 0:n], lhsT=tok_ap_fn(k, b), rhs=w[:, k, 0:n],
                                                          start=(k == 0), stop=(k == 7)), reads=[wbuf, self.b_hT], writes=[self.psb[p]])
        evac(b, p)


def _mixer_C(self, l):
    em, nc = self.em, self.nc
    ps, psb = self.ps, self.psb
    S, NB, NT = self.S, self.NB, self.NT
    lam_init = 0.8 - 0.6 * math.exp(-0.3 * l)
    with ExitStack() as st:
        wq = self.tl(st, "wq", [128, 8, 128], BF16)
        wk = self.tl(st, "wk", [128, 8, 128], BF16)
        wv = self.tl(st, "wv", [128, 8, 128], BF16)
        qT = self.tl(st, "qT", [128, S], BF16)
        kT = self.tl(st, "kT", [128, 2, S], BF16)
        V = self.tl(st, "V", [128, NB, 128], BF16)
        PT = self.tl(st, "PT", [128, 4, 512], BF16)
        r1 = self.tl(st, "r1", [128, 512], F32)
        r2 = self.tl(st, "r2", [128, 512], F32)
        t1 = self.tl(st, "t1", [128, 512], F32)
        t2 = self.tl(st, "t2", [128, 512], F32)
        yo = self.tl(st, "yo", [128, 2, 512], BF16)
        lam = self.tl(st, "lam", [128, 8], F32)
        ltmp = self.tl(st, "ltmp", [128, 64], F32)
        bwq, bwk, bwv, bq, bk, bv = [em.buf() for _ in range(6)]
        bp = [em.buf() for _ in range(4)]
        br1, br2, bt1, bt2, blam = [em.buf() for _ in range(5)]
        byo = [em.buf(), em.buf()]
        b_y = self.b_ybr
        o = RV_OFF[("diff_lambda", l)]
        for i in range(2):
            em.op("dve", lambda e, i=i: e.tensor_tensor(out=ltmp[:], in0=self.rv[:, o + 128 * i:o + 128 * i + 64],
                                                         in1=self.rv[:, o + 128 * i + 64:o + 128 * i + 128], op=ALU.mult),
                  reads=[self.b_const], writes=[blam])
            em.op("dve", lambda e, i=i: e.reduce_sum(out=lam[:, i:i + 1], in_=ltmp[:], axis=mybir.AxisListType.X), reads=[blam], writes=[blam])
        em.op("act", lambda e: e.activation(out=lam[:, 2:4], in_=lam[:, 0:2], func=AF.Exp), reads=[blam], writes=[blam])
        em.op("dve", lambda e: e.tensor_tensor(out=lam[:, 4:5], in0=lam[:, 3:4], in1=lam[:, 2:3], op=ALU.subtract), reads=[blam], writes=[blam])
        em.op("dve", lambda e: e.tensor_scalar(out=lam[:, 5:6], in0=lam[:, 4:5], scalar1=-lam_init, scalar2=None, op0=ALU.add), reads=[blam], writes=[blam])
        em.op("dve", lambda e: e.memset(lam[:, 6:7], EPS / (1.0 - lam_init) ** 2), writes=[blam])
        sc = 64.0 ** -0.5
        em.op("pool", lambda e: e.memset(kT[64:128, 0, :], 0.0), writes=[bk])
        em.op("pool", lambda e: e.memset(kT[0:64, 1, :], 0.0), writes=[bk])
        for h in range(4):
            self.load_wk(wq[:], l, O_CQ + h * 128, 128, bwq)
            self.load_wk(wk[:], l, O_CK + h * 128, 128, bwk)
            self.load_wk(wv[:], l, O_CV + h * 128, 128, bwv)
            self.proj_fm(wq, 128, bwq, [6, 7], lambda t, p: em.op("act", lambda e: e.copy(out=qT[:, t * 512:(t + 1) * 512], in_=ps[p][:]), reads=[psb[p]], writes=[bq]))
            def evk(t, p):
                em.op("dve", lambda e: e.tensor_copy(out=kT[0:64, 0, t * 512:(t + 1) * 512], in_=ps[p][0:64, :]), reads=[psb[p]], writes=[bk])
                em.op("dve", lambda e: e.tensor_copy(out=kT[64:128, 1, t * 512:(t + 1) * 512], in_=ps[p][64:128, :]), reads=[psb[p]], writes=[bk])
            self.proj_fm(wk, 128, bwk, [6, 7], evk)
            self.proj_tm(wv, 128, bwv, lambda k, b: self.hT[:, k, b * 128:(b + 1) * 128], NB, [6, 7],
                         lambda b, p: em.op("act", lambda e: e.copy(out=V[:, b, :], in_=ps[p][:, 0:128]), reads=[psb[p]], writes=[bv]))
            LAG = 3
            items = []
            for t in range(NT):
                nj = 4 * t + 4
                for c in range(2):
                    for j in range(nj):
                        items.append((t, c, j, nj))
            SB = [0, 1, 6, 7]

            def stage1(i):
                t, c, j, nj = items[i]
                lo, hi = c * 64, (c + 1) * 64
                q0 = max(0, j - 4 * t) * 128
                s = i % 4
                sp = SB[s]
                em.op("pe", lambda e: e.matmul(ps[sp][:, q0:512], lhsT=kT[:, c, j * 128:(j + 1) * 128], rhs=qT[:, t * 512 + q0:(t + 1) * 512],
                                               start=True, stop=True), reads=[bk, bq], writes=[psb[sp]])
                em.op("act", lambda e: e.activation(out=PT[:, s, q0:512], in_=ps[sp][:, q0:512], func=AF.Exp, scale=sc), reads=[psb[sp]], writes=[bp[s]])
                if j >= 4 * t:
                    em.op("pool", lambda e: e.tensor_tensor(out=PT[:, s, q0:q0 + 128], in0=PT[:, s, q0:q0 + 128], in1=self.triU_bf[:], op=ALU.mult),
                          reads=[bp[s], self.b_const], writes=[bp[s]])

            def stage2(i):
                t, c, j, nj = items[i]
                q0 = max(0, j - 4 * t) * 128
                s = i % 4
                em.op("pe", lambda e: e.matmul(ps[2 + c][:, q0:512], lhsT=V[:, j, :], rhs=PT[:, s, q0:512], start=(j == 0), stop=(j == nj - 1)),
                      reads=[bv, bp[s]], writes=[psb[2 + c]])
                em.op("pe", lambda e: e.matmul(ps[4 + c][:, q0:512], lhsT=self.ones_bf[:], rhs=PT[:, s, q0:512], start=(j == 0), stop=(j == nj - 1)),
                      reads=[self.b_const, bp[s]], writes=[psb[4 + c]])
                if c == 1 and j == nj - 1:
                    epilogue(t)

            def epilogue(t):
                em.op("dve", lambda e: e.reciprocal(out=r1[:], in_=ps[4][:]), reads=[psb[4]], writes=[br1])
                em.op("dve", lambda e: e.reciprocal(out=r2[:], in_=ps[5][:]), reads=[psb[5]], writes=[br2])
                em.op("dve", lambda e: e.tensor_tensor(out=t1[:], in0=ps[2][:], in1=r1[:], op=ALU.mult), reads=[psb[2], br1], writes=[bt1])
                em.op("dve", lambda e: e.tensor_tensor(out=t2[:], in0=ps[3][:], in1=r2[:], op=ALU.mult), reads=[psb[3], br2], writes=[bt2])
                em.op("dve", lambda e: e.scalar_tensor_tensor(out=t1[:], in0=t2[:], scalar=lam[:, 5:6], in1=t1[:], op0=ALU.mult, op1=ALU.add),
                      reads=[bt1, bt2, blam], writes=[bt1])
                em.op("act", lambda e: e.activation(out=t2[:], in_=t1[:], func=AF.Square), reads=[bt1], writes=[bt2])
                em.op("pe", lambda e: e.matmul(ps[4][:], lhsT=self.ones_f[:], rhs=t2[:], start=True, stop=True),
                      reads=[self.b_const, bt2], writes=[psb[4]])
                em.op("act", lambda e: e.activation(out=r1[:], in_=ps[4][:], func=AF.Sqrt, scale=1.0 / (128.0 * (1.0 - lam_init) ** 2), bias=lam[:, 6:7]),
                      reads=[psb[4], blam], writes=[br1])
                em.op("dve", lambda e: e.reciprocal(out=r1[:], in_=r1[:]), reads=[br1], writes=[br1])
                yi = t % 2
                em.op("dve", lambda e: e.scalar_tensor_tensor(out=yo[:, yi, :], in0=t1[:], scalar=self.pvcol("diff_subln_g", l), in1=r1[:],
                                                              op0=ALU.mult, op1=ALU.mult), reads=[bt1, br1, self.b_const], writes=[byo[yi]])
                r0 = 1024 + h * 128
                em.dma("sp", self.dram["ybr"][r0:r0 + 128, t * 512:(t + 1) * 512], yo[:, yi, :], reads=[byo[yi]], writes=[b_y])

            for i in range(len(items) + LAG):
                if i < len(items):
                    stage1(i)
                if i - LAG >= 0:
                    stage2(i - LAG)
        em.barrier()


Builder.mix_begin = _mix_begin
Builder.load_wk = _load_wk
Builder.proj_fm = _proj_fm
Builder.proj_tm = _proj_tm
Builder.mixer_C = _mixer_C


def _gates(self, l):
    em = self.em
    ps, psb = self.ps, self.psb
    with ExitStack() as st:
        wg = [self.tl(st, "wg", [128, 8, 128], BF16) for _ in range(2)]
        bwg = [em.buf(), em.buf()]
        go = [self.tl(st, "go", [128, 512], BF16) for _ in range(2)]
        bgo = [em.buf(), em.buf()]
        b_g = self.b_gat
        n = 0
        for ic in range(32):
            w, bw = wg[ic % 2], bwg[ic % 2]
            self.load_wk(w[:], l, O_G + ic * 128, 128, bw)

            def evac(t, p, ic=ic):
                nonlocal n
                g, bg = go[n % 2], bgo[n % 2]
                n += 1
                em.op("act", lambda e: e.activation(out=g[:], in_=ps[p][:], func=AF.Sigmoid), reads=[psb[p]], writes=[bg])
                em.dma("sp", self.dram["gat"][ic * 128:(ic + 1) * 128, t * 512:(t + 1) * 512], g[:], reads=[bg], writes=[b_g])
            self.proj_fm(w, 128, bw, [0, 1, 2, 3], evac)
        em.barrier()


def _phase_merge(self, l):
    em, nc = self.em, self.nc
    ps, psb = self.ps, self.psb
    with ExitStack() as st:
        wb = self.tl(st, "wb", [128, 16, D], BF16)
        wo = self.tl(st, "wo", [128, 8, D], BF16)
        y = self.tl(st, "y", [128, 16, 512], BF16)
        g = self.tl(st, "g", [128, 32, 512], BF16)
        xt = self.tl(st, "xt", [128, 8, 512], F32)
        mg = self.tl(st, "mg", [128, 8, 512], BF16)
        acc = self.tl(st, "acc", [128, 512], F32)
        tmp = self.tl(st, "tmp", [128, 2, 512], F32)
        bwb = [em.buf() for _ in range(16)]
        bwo = [em.buf() for _ in range(8)]
        by, bg, bx, bmg, bacc = em.buf(), em.buf(), em.buf(), em.buf(), em.buf()
        btmp = [em.buf(), em.buf()]
        for r in range(16):
            self.load_w(wb[:, r, :], self.dram["w_branch"][l, r * 128:(r + 1) * 128, :], 128, D, bwb[r])
        for c in range(8):
            self.load_w(wo[:, c, :], self.dram["w_mix_out"][l, c * 128:(c + 1) * 128, :], 128, D, bwo[c])
        for t in range(self.NT):
            em.dma("sp", y[:], self.dram["ybr"].rearrange("(r p) s -> p r s", p=128)[:, :, t * 512:(t + 1) * 512], reads=[self.b_ybr], writes=[by])
            em.dma("sp", g[:], self.dram["gat"].rearrange("(r p) s -> p r s", p=128)[:, :, t * 512:(t + 1) * 512], reads=[self.b_gat], writes=[bg])
            em.dma("sp", xt[:], self.xtile_ap("xs", t), reads=[self.b_xs], writes=[bx])
            n = 0
            for c in range(8):
                for i in range(4):
                    p = n % 4
                    n += 1
                    for m in range(4):
                        em.op("pe", lambda e, p=p, i=i, m=m, c=c: e.matmul(ps[p][:], lhsT=wb[:, i * 4 + m, c * 128:(c + 1) * 128], rhs=y[:, i * 4 + m, :],
                                                                           start=(m == 0), stop=(m == 3)), reads=[bwb[i * 4 + m], by], writes=[psb[p]])
                    if i == 0:
                        em.op("dve", lambda e, p=p, c=c: e.tensor_tensor(out=acc[:], in0=ps[p][:], in1=g[:, c, :], op=ALU.mult),
                              reads=[psb[p], bg], writes=[bacc])
                    else:
                        ti = i % 2
                        em.op("dve", lambda e, p=p, c=c, i=i, ti=ti: e.tensor_tensor(out=tmp[:, ti, :], in0=ps[p][:], in1=g[:, i * 8 + c, :], op=ALU.mult),
                              reads=[psb[p], bg], writes=[btmp[ti]])
                        if i < 3:
                            em.op("pool", lambda e, ti=ti: e.tensor_tensor(out=acc[:], in0=acc[:], in1=tmp[:, ti, :], op=ALU.add),
                                  reads=[bacc, btmp[ti]], writes=[bacc])
                        else:
                            em.op("pool", lambda e, ti=ti, c=c: e.tensor_tensor(out=mg[:, c, :], in0=acc[:], in1=tmp[:, ti, :], op=ALU.add),
                                  reads=[bacc, btmp[ti]], writes=[bmg])
            for c2 in range(8):
                po = 4 + (c2 % 2)
                for c in range(8):
                    em.op("pe", lambda e, c=c, c2=c2, po=po: e.matmul(ps[po][:], lhsT=wo[:, c, c2 * 128:(c2 + 1) * 128], rhs=mg[:, c, :],
                                                                      start=(c == 0), stop=(c == 7)), reads=[bwo[c], bmg], writes=[psb[po]])
                em.op("dve", lambda e, c2=c2, po=po: e.tensor_tensor(out=xt[:, c2, :], in0=ps[po][:], in1=xt[:, c2, :], op=ALU.add),
                      reads=[psb[po], bx], writes=[bx])
            em.dma("sp", self.xtile_ap("xs", t), xt[:], reads=[bx], writes=[self.b_xs])
        em.barrier()


Builder.gates = _gates
Builder.phase_merge = _phase_merge


def _mixer_A(self, l):
    em, nc = self.em, self.nc
    ps, psb = self.ps, self.psb
    S, NB, NT = self.S, self.NB, self.NT
    with ExitStack() as st:
        wq = self.tl(st, "wq", [128, 8, 128], BF16)
        wk = self.tl(st, "wk", [128, 8, 128], BF16)
        wv = self.tl(st, "wv", [128, 8, 128], BF16)
        qT = self.tl(st, "qT", [128, S], BF16)
        kT = self.tl(st, "kT", [128, S], BF16)
        V = self.tl(st, "V", [128, NB, 128], BF16)
        PT = self.tl(st, "PT", [128, 4, 256], BF16)
        msk = self.tl(st, "msk", [128, 256], BF16)
        accN = self.tl(st, "accN", [128, S], F32)
        accD = self.tl(st, "accD", [128, S], F32)
        yo = self.tl(st, "yo", [64, 2, 512], BF16)
        bwq, bwk, bwv, bq, bk, bv, bm, baN, baD = [em.buf() for _ in range(9)]
        bp = [em.buf() for _ in range(4)]
        byo = [em.buf(), em.buf()]
        em.op("dve", lambda e: e.tensor_copy(out=msk[:, 0:128], in_=self.triU_bf[:]), reads=[self.b_const], writes=[bm])
        em.op("dve", lambda e: e.tensor_copy(out=msk[:, 128:256], in_=self.triL_bf[:]), reads=[self.b_const], writes=[bm])
        sc = 64.0 ** -0.5
        for (wt_, bw_) in ((wq, bwq), (wk, bwk), (wv, bwv)):
            em.op("pool", lambda e, wt_=wt_: e.memset(wt_[:, :, 64:128], 0.0), writes=[bw_])
        for h in range(8):
            self.load_wk(wq[:, :, 0:64], l, O_AQ + h * 64, 64, bwq)
            self.load_wk(wk[:, :, 0:64], l, O_AK + h * 64, 64, bwk)
            self.load_wk(wv[:, :, 0:64], l, O_AV + h * 64, 64, bwv)
            self.proj_fm(wq, 128, bwq, [6, 7], lambda t, p: em.op("act", lambda e: e.copy(out=qT[:, t * 512:(t + 1) * 512], in_=ps[p][:, :]), reads=[psb[p]], writes=[bq]))
            self.proj_fm(wk, 128, bwk, [6, 7], lambda t, p: em.op("dve", lambda e: e.tensor_copy(out=kT[:, t * 512:(t + 1) * 512], in_=ps[p][:, :]), reads=[psb[p]], writes=[bk]))
            for pi_, dil in enumerate((1, 4, 16)):
                L = S // dil
                nbs = L // 128
                assert nbs >= 1 and nbs * 128 * dil == S

                def tok(r, n, cnt=128, dil=dil):
                    s0 = r + dil * n * 128
                    return slice(s0, s0 + dil * (cnt - 1) + 1, dil)
                self.proj_tm(wv, 128, bwv, lambda k, b: self.hT[:, k, tok(b // nbs, b % nbs)], NB, [6, 7],
                             lambda b, p: em.op("act", lambda e: e.copy(out=V[:, b, :], in_=ps[p][:, 0:128]), reads=[psb[p]], writes=[bv]))
                grp = min(4, nbs)
                LAG = 2
                SBF = [0, 1, 6]
                blocks = [(r, n) for r in range(dil) for n in range(nbs)]

                def stage1(i, dil=dil, nbs=nbs, tok=tok):
                    r, n = blocks[i]
                    s = i % 3
                    sp = SBF[s]
                    w = 256 if n > 0 else 128
                    em.op("pe", lambda e: e.matmul(ps[sp][:, 0:128], lhsT=kT[:, tok(r, n)], rhs=qT[:, tok(r, n)], start=True, stop=True),
                          reads=[bk, bq], writes=[psb[sp]])
                    if n > 0:
                        em.op("pe", lambda e: e.matmul(ps[sp][:, 128:256], lhsT=kT[:, tok(r, n - 1)], rhs=qT[:, tok(r, n)], start=True, stop=True),
                              reads=[bk, bq], writes=[psb[sp]])
                    em.op("act", lambda e: e.activation(out=PT[:, s, 0:w], in_=ps[sp][:, 0:w], func=AF.Exp, scale=sc), reads=[psb[sp]], writes=[bp[s]])
                    em.op("pool", lambda e: e.tensor_tensor(out=PT[:, s, 0:w], in0=PT[:, s, 0:w], in1=msk[:, 0:w], op=ALU.mult),
                          reads=[bp[s], bm], writes=[bp[s]])

                def stage2(i, dil=dil, nbs=nbs, tok=tok, pi_=pi_, grp=grp):
                    r, n = blocks[i]
                    s = i % 3
                    gidx = i // grp
                    pn = 2 + (gidx % 2)
                    pd = 4 + (gidx % 2)
                    n0 = (n // grp) * grp
                    qs = (n - n0) * 128
                    pb = r * nbs + n
                    last = (n == 0)
                    em.op("pe", lambda e: e.matmul(ps[pn][:, qs:qs + 128], lhsT=V[:, pb, :], rhs=PT[:, s, 0:128], start=True, stop=last),
                          reads=[bv, bp[s]], writes=[psb[pn]])
                    if n > 0:
                        em.op("pe", lambda e: e.matmul(ps[pn][:, qs:qs + 128], lhsT=V[:, pb - 1, :], rhs=PT[:, s, 128:256], start=False, stop=True),
                              reads=[bv, bp[s]], writes=[psb[pn]])
                    em.op("pe", lambda e: e.matmul(ps[pd][:, qs:qs + 128], lhsT=self.ones_bf[:], rhs=PT[:, s, 0:128], start=True, stop=last),
                          reads=[self.b_const, bp[s]], writes=[psb[pd]])
                    if n > 0:
                        em.op("pe", lambda e: e.matmul(ps[pd][:, qs:qs + 128], lhsT=self.ones_bf[:], rhs=PT[:, s, 128:256], start=False, stop=True),
                              reads=[self.b_const, bp[s]], writes=[psb[pd]])
                    if n == n0 + grp - 1:
                        tsl = tok(r, n0, grp * 128)
                        gw = grp * 128
                        if pi_ == 0:
                            em.op("dve", lambda e: e.tensor_copy(out=accN[:, tsl], in_=ps[pn][:, 0:gw]), reads=[psb[pn]], writes=[baN])
                            em.op("act", lambda e: e.copy(out=accD[:, tsl], in_=ps[pd][:, 0:gw]), reads=[psb[pd]], writes=[baD])
                        else:
                            em.op("dve", lambda e: e.tensor_tensor(out=accN[:, tsl], in0=ps[pn][:, 0:gw], in1=accN[:, tsl], op=ALU.add),
                                  reads=[psb[pn], baN], writes=[baN])
                            em.op("dve", lambda e: e.tensor_tensor(out=accD[:, tsl], in0=ps[pd][:, 0:gw], in1=accD[:, tsl], op=ALU.add),
                                  reads=[psb[pd], baD], writes=[baD])

                for i in range(len(blocks) + LAG):
                    if i < len(blocks):
                        stage1(i)
                    if i - LAG >= 0:
                        stage2(i - LAG)
            for t in range(NT):
                sl = slice(t * 512, (t + 1) * 512)
                yi = t % 2
                em.op("dve", lambda e, sl=sl: e.reciprocal(out=accD[0:64, sl], in_=accD[0:64, sl]), reads=[baD], writes=[baD])
                em.op("dve", lambda e, sl=sl, yi=yi: e.tensor_tensor(out=yo[:, yi, :], in0=accN[0:64, sl], in1=accD[0:64, sl], op=ALU.mult),
                      reads=[baN, baD], writes=[byo[yi]])
                em.dma("sp", self.dram["ybr"][h * 64:(h + 1) * 64, sl], yo[:, yi, :], reads=[byo[yi]], writes=[self.b_ybr])
        em.barrier()


Builder.mixer_A = _mixer_A


def _bcast_mid(ap2d, n):
    a = ap2d.ap
    return bass.AP(ap2d.tensor, ap2d.offset, [list(a[0]), [0, n], list(a[1])])


def _mixer_B(self, l):
    em, nc = self.em, self.nc
    ps, psb = self.ps, self.psb
    S, NB, NT = self.S, self.NB, self.NT
    lg = [math.log1p(-2.0 ** (-5.0 - h)) for h in range(4)]
    with ExitStack() as st:
        wq = self.tl(st, "wq", [128, 8, 64], BF16)
        wqs = self.tl(st, "wqs", [128, 8, 64], BF16)
        wk = self.tl(st, "wk", [128, 8, 64], BF16)
        wks = self.tl(st, "wks", [128, 8, 64], BF16)
        wv = self.tl(st, "wv", [128, 8, 128], BF16)
        wg = self.tl(st, "wg", [128, 8, 128], BF16)
        rot = self.tl(st, "rot", [64, 2, S], F32)
        qr = self.tl(st, "qr", [64, S], BF16)
        qi = self.tl(st, "qi", [64, S], BF16)
        kr = self.tl(st, "kr", [64, S], BF16)
        V = self.tl(st, "V", [128, NB, 128], BF16)
        gs = self.tl(st, "gs", [128, S], BF16)
        ta = self.tl(st, "ta", [64, 512], F32)
        tb = self.tl(st, "tb", [64, 512], F32)
        Sm = self.tl(st, "Sm", [128, 2, 128], BF16)
        kend = self.tl(st, "kend", [128, 2, 64], BF16)
        R = self.tl(st, "R", [64, 128], F32)
        Rb = self.tl(st, "Rb", [64, 128], BF16)
        ot = self.tl(st, "ot", [128, 512], F32)
        cen = self.tl(st, "cen", [128, 512], F32)
        sq = self.tl(st, "sq", [128, 512], F32)
        rs = self.tl(st, "rs", [128, 512], F32)
        yo = self.tl(st, "yo", [128, 2, 512], BF16)
        epst = self.tl(st, "epst", [128, 1], F32)
        (bwq, bwqs, bwk, bwks, bwv, bwg, brot, bqr, bqi, bkr, bv, bgs, bta, btb, bR, bRb, bot, bcen, bsq, brs) = [em.buf() for _ in range(20)]
        bSm = [em.buf(), em.buf()]
        bke = [em.buf(), em.buf()]
        byo = [em.buf(), em.buf()]
        em.dma("sp", rot[:], self.dram["rot"], writes=[brot])
        em.op("dve", lambda e: e.memset(epst[:], EPS), writes=[brs])
        psT = [ps[4].bitcast(BF16)]
        for h in range(4):
            self.load_wk(wq[:], l, O_BQ + h * 64, 64, bwq)
            self.load_wk(wqs[:, :, 0:32], l, O_BQ + h * 64 + 32, 32, bwqs)
            self.load_wk(wqs[:, :, 32:64], l, O_BQ + h * 64, 32, bwqs)
            self.load_wk(wk[:], l, O_BK + h * 64, 64, bwk)
            self.load_wk(wks[:, :, 0:32], l, O_BK + h * 64 + 32, 32, bwks)
            self.load_wk(wks[:, :, 32:64], l, O_BK + h * 64, 32, bwks)
            self.load_wk(wv[:], l, O_BV + h * 128, 128, bwv)
            self.load_wk(wg[:], l, O_BG + h * 128, 128, bwg)
            qdec = self.cf[0:64, CF_QDEC + h * 128:CF_QDEC + (h + 1) * 128]
            for (wa, wb_, ba, bb, dst, bdst) in ((wq, wqs, bwq, bwqs, qr, bqr), (wk, wks, bwk, bwks, kr, bkr)):
                for t in range(NT):
                    sl = slice(t * 512, (t + 1) * 512)
                    for (w_, bw_, p) in ((wa, ba, 6), (wb_, bb, 7)):
                        for k in range(8):
                            em.op("pe", lambda e, k=k, w_=w_, p=p, sl=sl: e.matmul(ps[p][0:64, :], lhsT=w_[:, k, :], rhs=self.hT[:, k, sl],
                                                                                 start=(k == 0), stop=(k == 7)), reads=[bw_, self.b_hT], writes=[psb[p]])
                    em.op("dve", lambda e, sl=sl: e.tensor_tensor(out=ta[:], in0=ps[6][0:64, :], in1=rot[:, 0, sl], op=ALU.mult), reads=[psb[6], brot], writes=[bta])
                    em.op("dve", lambda e, sl=sl: e.tensor_tensor(out=tb[:], in0=ps[7][0:64, :], in1=rot[:, 1, sl], op=ALU.mult), reads=[psb[7], brot], writes=[btb])
                    em.op("pool", lambda e, sl=sl, dst=dst: e.tensor_tensor(out=dst[:, sl], in0=ta[:], in1=tb[:], op=ALU.add), reads=[bta, btb], writes=[bdst])
                    if dst is qr:
                        em.op("pool", lambda e, sl=sl: e.tensor_tensor(out=qi[:, sl].rearrange("p (a b) -> p a b", a=4), in0=qr[:, sl].rearrange("p (a b) -> p a b", a=4),
                                                                       in1=_bcast_mid(qdec, 4), op=ALU.mult), reads=[bqr, self.b_const], writes=[bqi])
            self.proj_tm(wv, 128, bwv, lambda k, b: self.hT[:, k, b * 128:(b + 1) * 128], NB, [6, 7],
                         lambda b, p: em.op("act", lambda e: e.copy(out=V[:, b, :], in_=ps[p][:, 0:128]), reads=[psb[p]], writes=[bv]))
            self.proj_fm(wg, 128, bwg, [6, 7], lambda t, p: em.op("act", lambda e: e.activation(out=gs[:, t * 512:(t + 1) * 512], in_=ps[p][:], func=AF.Silu), reads=[psb[p]], writes=[bgs]))
            decT = self.cf[:, CF_DECT + h * 128:CF_DECT + (h + 1) * 128]
            cdec = math.exp(128.0 * lg[h])
            for n in range(NB):
                c0 = n * 128
                s = n % 2
                qs = (n % 4) * 128
                em.op("pe", lambda e, c0=c0: e.transpose(out=psT[0][:, 0:64], in_=kr[:, c0:c0 + 128], identity=self.ident_bf[0:64, 0:64]),
                      reads=[bkr, self.b_const], writes=[psb[4]])
                em.op("dve", lambda e, s=s: e.tensor_scalar(out=kend[:, s, :], in0=psT[0][:, 0:64], scalar1=self.cf[:, CF_KEND + h:CF_KEND + h + 1], scalar2=0.125,
                                                           op0=ALU.mult, op1=ALU.mult), reads=[psb[4], self.b_const], writes=[bke[s]])
                em.op("pe", lambda e, s=s, c0=c0: e.matmul(ps[s][:, 0:128], lhsT=kr[:, c0:c0 + 128], rhs=qr[:, c0:c0 + 128], start=True, stop=True),
                      reads=[bkr, bqr], writes=[psb[s]])
                em.op("dve", lambda e, s=s: e.scalar_tensor_tensor(out=Sm[:, s, :], in0=ps[s][:, 0:128], scalar=0.125, in1=decT, op0=ALU.mult, op1=ALU.mult),
                      reads=[psb[s], self.b_const], writes=[bSm[s]])
                em.op("pe", lambda e, s=s, n=n, qs=qs: e.matmul(ps[2][:, qs:qs + 128], lhsT=V[:, n, :], rhs=Sm[:, s, :], start=True, stop=(n == 0)),
                      reads=[bv, bSm[s]], writes=[psb[2]])
                if n > 0:
                    em.op("pe", lambda e, c0=c0, qs=qs: e.matmul(ps[2][:, qs:qs + 128], lhsT=Rb[:], rhs=qi[:, c0:c0 + 128], start=False, stop=True),
                          reads=[bRb, bqi], writes=[psb[2]])
                if n < NB - 1:
                    em.op("pe", lambda e, s=s, n=n: e.matmul(ps[3][0:64, 0:128], lhsT=kend[:, s, :], rhs=V[:, n, :], start=True, stop=True),
                          reads=[bke[s], bv], writes=[psb[3]])
                    if n == 0:
                        em.op("dve", lambda e: e.tensor_copy(out=R[:], in_=ps[3][0:64, 0:128]), reads=[psb[3]], writes=[bR])
                    else:
                        em.op("dve", lambda e: e.scalar_tensor_tensor(out=R[:], in0=R[:], scalar=cdec, in1=ps[3][0:64, 0:128], op0=ALU.mult, op1=ALU.add),
                              reads=[psb[3], bR], writes=[bR])
                    em.op("act", lambda e: e.copy(out=Rb[:], in_=R[:]), reads=[bR], writes=[bRb])
                if n % 4 == 3:
                    t = n // 4
                    sl = slice(t * 512, (t + 1) * 512)
                    em.op("act", lambda e: e.copy(out=ot[:], in_=ps[2][:]), reads=[psb[2]], writes=[bot])
                    em.op("pe", lambda e: e.matmul(ps[5][:], lhsT=self.ones_f[:], rhs=ot[:], start=True, stop=True), reads=[self.b_const, bot], writes=[psb[5]])
                    em.op("dve", lambda e: e.scalar_tensor_tensor(out=cen[:], in0=ps[5][:], scalar=-1.0 / 128, in1=ot[:], op0=ALU.mult, op1=ALU.add),
                          reads=[psb[5], bot], writes=[bcen])
                    em.op("act", lambda e: e.activation(out=sq[:], in_=cen[:], func=AF.Square), reads=[bcen], writes=[bsq])
                    em.op("pe", lambda e: e.matmul(ps[5][:], lhsT=self.ones_f[:], rhs=sq[:], start=True, stop=True), reads=[self.b_const, bsq], writes=[psb[5]])
                    em.op("act", lambda e: e.activation(out=rs[:], in_=ps[5][:], func=AF.Sqrt, scale=1.0 / 128, bias=epst[:, 0:1]), reads=[psb[5], brs], writes=[brs])
                    em.op("dve", lambda e: e.reciprocal(out=rs[:], in_=rs[:]), reads=[brs], writes=[brs])
                    em.op("dve", lambda e: e.scalar_tensor_tensor(out=cen[:], in0=cen[:], scalar=self.pvcol("ret_gn_g", l, h), in1=rs[:], op0=ALU.mult, op1=ALU.mult),
                          reads=[bcen, brs, self.b_const], writes=[bcen])
                    yi = t % 2
                    em.op("pool", lambda e, yi=yi, sl=sl: e.tensor_tensor(out=yo[:, yi, :], in0=cen[:], in1=gs[:, sl], op=ALU.mult), reads=[bcen, bgs], writes=[byo[yi]])
                    r0 = 512 + h * 128
                    em.dma("sp", self.dram["ybr"][r0:r0 + 128, sl], yo[:, yi, :], reads=[byo[yi]], writes=[self.b_ybr])
        em.barrier()


Builder.mixer_B = _mixer_B


def _bcast_last(ap2d, n):
    a = ap2d.ap
    return bass.AP(ap2d.tensor, ap2d.offset, [list(a[0]), list(a[1]), [0, n]])


def _mixer_D(self, l):
    em, nc = self.em, self.nc
    ps, psb = self.ps, self.psb
    S, NB, NT = self.S, self.NB, self.NT
    X = mybir.AxisListType.X
    bc = self.b_const
    with ExitStack() as st:
        x_tm = self.tl(st, "x_tm", [128, NB, 512], BF16)
        B_tm = self.tl(st, "B_tm", [128, NB, 256], BF16)
        BT = self.tl(st, "BT", [128, 2, S], BF16)
        CT = self.tl(st, "CT", [128, 2, S], BF16)
        bxtm, bBtm, bBT, bCT = [em.buf() for _ in range(4)]
        psT = ps[7].bitcast(BF16)
        with ExitStack() as s1:
            w = [self.tl(s1, "w", [128, 8, 128], BF16) for _ in range(2)]
            bw = [em.buf(), em.buf()]
            praw = self.tl(s1, "praw", [128, 3 + S], F32)
            cacc = self.tl(s1, "cacc", [128, 512], F32)
            cact = self.tl(s1, "cact", [128, 512], BF16)
            bpraw, bcacc, bcact = [em.buf() for _ in range(3)]
            em.op("dve", lambda e: e.memset(praw[:, 0:3], 0.0), writes=[bpraw])
            for c in range(8):
                ww, bww = w[c % 2], bw[c % 2]
                self.load_wk(ww[:], l, O_DX + c * 128, 128, bww)
                self.proj_fm(ww, 128, bww, [0, 1], lambda t, p: em.op("act", lambda e: e.copy(out=praw[:, 3 + t * 512:3 + (t + 1) * 512], in_=ps[p][:]),
                                                                      reads=[psb[p]], writes=[bpraw]))
                cw = PV_OFF[("conv_w", l)] + c * 4
                for t in range(NT):
                    t0 = t * 512
                    em.op("dve", lambda e, t0=t0: e.tensor_scalar(out=cacc[:], in0=praw[:, t0 + 3:t0 + 515], scalar1=self.pv[:, cw + 3:cw + 4], scalar2=None, op0=ALU.mult),
                          reads=[bpraw, bc], writes=[bcacc])
                    for k in range(3):
                        em.op("dve", lambda e, t0=t0, k=k: e.scalar_tensor_tensor(out=cacc[:], in0=praw[:, t0 + k:t0 + k + 512], scalar=self.pv[:, cw + k:cw + k + 1], in1=cacc[:],
                                                                                op0=ALU.mult, op1=ALU.add), reads=[bpraw, bcacc, bc], writes=[bcacc])
                    if c < 4:
                        dst, bd = cact[:], bcact
                    elif c < 6:
                        dst, bd = BT[:, c - 4, t0:t0 + 512], bBT
                    else:
                        dst, bd = CT[:, c - 6, t0:t0 + 512], bCT
                    em.op("act", lambda e, dst=dst: e.activation(out=dst, in_=cacc[:], func=AF.Silu, bias=self.pvcol("conv_b", l, c), scale=1.0),
                          reads=[bcacc, bc], writes=[bd])
                    if c < 6:
                        for q in range(4):
                            in_ap = cact[:, q * 128:(q + 1) * 128] if c < 4 else BT[:, c - 4, t0 + q * 128:t0 + (q + 1) * 128]
                            em.op("pe", lambda e, q=q, in_ap=in_ap: e.transpose(out=psT[:, q * 128:(q + 1) * 128], in_=in_ap, identity=self.ident_bf[:]),
                                  reads=[bd, bc], writes=[psb[7]])
                        if c < 4:
                            em.op("dve", lambda e, t=t, c=c: e.tensor_copy(out=x_tm[:, 4 * t:4 * t + 4, c * 128:(c + 1) * 128], in_=psT[:, 0:512].rearrange("p (a b) -> p a b", a=4)),
                                  reads=[psb[7]], writes=[bxtm])
                        else:
                            em.op("dve", lambda e, t=t, c=c: e.tensor_copy(out=B_tm[:, 4 * t:4 * t + 4, (c - 4) * 128:(c - 3) * 128], in_=psT[:, 0:512].rearrange("p (a b) -> p a b", a=4)),
                                  reads=[psb[7]], writes=[bBtm])
            em.barrier()
        with ExitStack() as s2:
            wz = self.tl(s2, "wz", [128, 8, 512], BF16)
            wdt = self.tl(s2, "wdt", [128, 8, 8], BF16)
            sm = self.tl(s2, "sm", [128, 64], F32)
            expA = self.tl(s2, "expA", [128, 8], F32)
            xdt = self.tl(s2, "xdt", [128, 512], BF16)
            xdd = self.tl(s2, "xdd", [128, 512], BF16)
            cbm = self.tl(s2, "cbm", [128, 256], BF16)
            lh = self.tl(s2, "lh", [128, 2, 128], F32)
            Lx = self.tl(s2, "Lx", [128, 2, 128], BF16)
            Mx = self.tl(s2, "Mx", [128, 2, 128], BF16)
            H = self.tl(s2, "H", [128, 512], F32)
            Hb = self.tl(s2, "Hb", [128, 512], BF16)
            t1 = self.tl(s2, "t1", [128, 512], F32)
            t2 = self.tl(s2, "t2", [128, 512], F32)
            zs = self.tl(s2, "zs", [128, 512], F32)
            yn = self.tl(s2, "yn", [128, 512], BF16)
            ysg = self.tl(s2, "ysg", [128, 4, 512], BF16)
            (bwz, bwdt, bsm, bexpA, bxdt, bxdd, bcbm, bH, bHb, bt1, bt2, bzs, byn, bysg) = [em.buf() for _ in range(14)]
            blh, bLx, bMx = [em.buf(), em.buf()], [em.buf(), em.buf()], [em.buf(), em.buf()]
            for k in range(8):
                self.load_w(wz[:, k, :], self.dram["w_in"][l, k * 128:(k + 1) * 128, O_DZ:O_DZ + 512], 128, 512, bwz)
            self.load_wk(wdt[:], l, O_DDT, 8, bwdt)
            oA, oB, oD, oG = RV_OFF[("A_log", l)], RV_OFF[("dt_bias", l)], RV_OFF[("D", l)], RV_OFF[("ssm_norm_g", l)]
            em.op("act", lambda e: e.activation(out=expA[:], in_=self.rv[:, oA:oA + 8], func=AF.Exp), reads=[bc], writes=[bexpA])
            triU_f = self.cf[:, CF_TRIU:CF_TRIU + 128]
            SL_f = self.cf[:, CF_SL:CF_SL + 128]
            for n in range(NB):
                c0 = n * 128
                for k in range(8):
                    em.op("pe", lambda e, k=k, c0=c0: e.matmul(ps[6][:, 0:8], lhsT=self.hT[:, k, c0:c0 + 128], rhs=wdt[:, k, :], start=(k == 0), stop=(k == 7)),
                          reads=[bwdt, self.b_hT], writes=[psb[6]])
                em.op("dve", lambda e: e.tensor_tensor(out=sm[:, 0:8], in0=ps[6][:, 0:8], in1=self.rv[:, oB:oB + 8], op=ALU.add), reads=[psb[6], bc], writes=[bsm])
                em.op("act", lambda e: e.activation(out=sm[:, 0:8], in_=sm[:, 0:8], func=AF.Exp), reads=[bsm], writes=[bsm])
                em.op("act", lambda e: e.activation(out=sm[:, 0:8], in_=sm[:, 0:8], func=AF.Ln, bias=self.ones_f[:, 0:1], scale=1.0), reads=[bsm, bc], writes=[bsm])
                em.op("dve", lambda e: e.scalar_tensor_tensor(out=sm[:, 8:16], in0=sm[:, 0:8], scalar=-1.0, in1=expA[:], op0=ALU.mult, op1=ALU.mult),
                      reads=[bsm, bexpA], writes=[bsm])
                em.op("pe", lambda e: e.matmul(ps[2][:, 256:264], lhsT=triU_f, rhs=sm[:, 8:16], start=True, stop=True), reads=[bc, bsm], writes=[psb[2]])
                em.op("pe", lambda e: e.matmul(ps[2][:, 264:272], lhsT=self.ones_f[:], rhs=sm[:, 8:16], start=True, stop=True), reads=[bc, bsm], writes=[psb[2]])
                em.op("dve", lambda e: e.tensor_copy(out=sm[:, 16:32], in_=ps[2][:, 256:272]), reads=[psb[2]], writes=[bsm])
                em.op("dve", lambda e: e.tensor_tensor(out=sm[:, 40:48], in0=sm[:, 24:32], in1=sm[:, 16:24], op=ALU.subtract), reads=[bsm], writes=[bsm])
                em.op("act", lambda e: e.activation(out=sm[:, 32:40], in_=sm[:, 16:24], func=AF.Exp), reads=[bsm], writes=[bsm])
                em.op("act", lambda e: e.activation(out=sm[:, 40:48], in_=sm[:, 40:48], func=AF.Exp), reads=[bsm], writes=[bsm])
                em.op("act", lambda e: e.activation(out=sm[:, 48:56], in_=sm[:, 24:32], func=AF.Exp), reads=[bsm], writes=[bsm])
                xv = x_tm[:, n, :].rearrange("p (h d) -> p h d", h=8)
                em.op("dve", lambda e, xv=xv: e.tensor_tensor(out=xdt[:].rearrange("p (h d) -> p h d", h=8), in0=xv, in1=_bcast_last(sm[:, 0:8], 64), op=ALU.mult),
                      reads=[bxtm, bsm], writes=[bxdt])
                em.op("pool", lambda e: e.tensor_tensor(out=xdd[:].rearrange("p (h d) -> p h d", h=8), in0=xdt[:].rearrange("p (h d) -> p h d", h=8),
                                                        in1=_bcast_last(sm[:, 40:48], 64), op=ALU.mult), reads=[bxdt, bsm], writes=[bxdd])
                for g in range(2):
                    em.op("pe", lambda e, g=g, c0=c0: e.matmul(ps[2][:, g * 128:(g + 1) * 128], lhsT=BT[:, g, c0:c0 + 128], rhs=CT[:, g, c0:c0 + 128], start=True, stop=True),
                          reads=[bBT, bCT], writes=[psb[2]])
                em.op("dve", lambda e: e.tensor_tensor(out=cbm[:].rearrange("p (g i) -> p g i", g=2), in0=ps[2][:, 0:256].rearrange("p (g i) -> p g i", g=2),
                                                       in1=_bcast_mid(triU_f, 2), op=ALU.mult), reads=[psb[2], bc], writes=[bcbm])
                if n > 0:
                    for g in range(2):
                        em.op("pe", lambda e, g=g, c0=c0: e.matmul(ps[4][:, g * 256:(g + 1) * 256], lhsT=CT[:, g, c0:c0 + 128], rhs=Hb[:, g * 256:(g + 1) * 256], start=True, stop=True),
                              reads=[bCT, bHb], writes=[psb[4]])
                for h in range(8):
                    s = h % 2
                    g = h // 4
                    em.op("dve", lambda e, h=h, s=s: e.tensor_scalar(out=lh[:, s, :], in0=SL_f, scalar1=sm[:, 8 + h:9 + h], scalar2=None, op0=ALU.mult),
                          reads=[bc, bsm], writes=[blh[s]])
                    em.op("pe", lambda e, s=s: e.matmul(ps[s][:, 0:128], lhsT=lh[:, s, :], rhs=triU_f, start=True, stop=True), reads=[blh[s], bc], writes=[psb[s]])
                    em.op("act", lambda e, s=s: e.activation(out=Lx[:, s, :], in_=ps[s][:, 0:128], func=AF.Exp), reads=[psb[s]], writes=[bLx[s]])
                    em.op("pool", lambda e, s=s, g=g: e.tensor_tensor(out=Mx[:, s, :], in0=Lx[:, s, :], in1=cbm[:, g * 128:(g + 1) * 128], op=ALU.mult),
                          reads=[bLx[s], bcbm], writes=[bMx[s]])
                    em.op("pe", lambda e, s=s, h=h: e.matmul(ps[3][:, h * 64:(h + 1) * 64], lhsT=Mx[:, s, :], rhs=xdt[:, h * 64:(h + 1) * 64], start=True, stop=True),
                          reads=[bMx[s], bxdt], writes=[psb[3]])
                if n > 0:
                    em.op("dve", lambda e: e.tensor_tensor(out=t1[:].rearrange("p (h d) -> p h d", h=8), in0=ps[4][:].rearrange("p (h d) -> p h d", h=8),
                                                           in1=_bcast_last(sm[:, 32:40], 64), op=ALU.mult), reads=[psb[4], bsm], writes=[bt1])
                    em.op("dve", lambda e: e.tensor_tensor(out=t1[:], in0=ps[3][:], in1=t1[:], op=ALU.add), reads=[psb[3], bt1], writes=[bt1])
                else:
                    em.op("dve", lambda e: e.tensor_copy(out=t1[:], in_=ps[3][:]), reads=[psb[3]], writes=[bt1])
                em.op("pool", lambda e, xv=xv: e.tensor_tensor(out=t2[:].rearrange("p (h d) -> p h d", h=8), in0=xv, in1=_bcast_last(self.rv[:, oD:oD + 8], 64), op=ALU.mult),
                      reads=[bxtm, bc], writes=[bt2])
                em.op("pool", lambda e: e.tensor_tensor(out=t1[:], in0=t1[:], in1=t2[:], op=ALU.add), reads=[bt1, bt2], writes=[bt1])
                if n < NB - 1:
                    for g in range(2):
                        em.op("pe", lambda e, g=g, n=n: e.matmul(ps[5][:, g * 256:(g + 1) * 256], lhsT=B_tm[:, n, g * 128:(g + 1) * 128], rhs=xdd[:, g * 256:(g + 1) * 256], start=True, stop=True),
                              reads=[bBtm, bxdd], writes=[psb[5]])
                    if n == 0:
                        em.op("dve", lambda e: e.tensor_copy(out=H[:], in_=ps[5][:]), reads=[psb[5]], writes=[bH])
                    else:
                        em.op("pool", lambda e: e.tensor_tensor(out=H[:].rearrange("p (h d) -> p h d", h=8), in0=H[:].rearrange("p (h d) -> p h d", h=8),
                                                                in1=_bcast_last(sm[:, 48:56], 64), op=ALU.mult), reads=[bH, bsm], writes=[bH])
                        em.op("dve", lambda e: e.tensor_tensor(out=H[:], in0=ps[5][:], in1=H[:], op=ALU.add), reads=[psb[5], bH], writes=[bH])
                    em.op("act", lambda e: e.copy(out=Hb[:], in_=H[:]), reads=[bH], writes=[bHb])
                for k in range(8):
                    em.op("pe", lambda e, k=k, c0=c0: e.matmul(ps[6][:], lhsT=self.hT[:, k, c0:c0 + 128], rhs=wz[:, k, :], start=(k == 0), stop=(k == 7)),
                          reads=[bwz, self.b_hT], writes=[psb[6]])
                em.op("act", lambda e: e.activation(out=zs[:], in_=ps[6][:], func=AF.Silu), reads=[psb[6]], writes=[bzs])
                em.op("dve", lambda e: e.tensor_tensor(out=t1[:], in0=t1[:], in1=zs[:], op=ALU.mult), reads=[bt1, bzs], writes=[bt1])
                em.op("act", lambda e: e.activation(out=t2[:], in_=t1[:], func=AF.Square), reads=[bt1], writes=[bt2])
                em.op("dve", lambda e: e.reduce_sum(out=sm[:, 56:57], in_=t2[:], axis=X), reads=[bt2], writes=[bsm])
                em.op("act", lambda e: e.activation(out=sm[:, 57:58], in_=sm[:, 56:57], func=AF.Sqrt, scale=1.0 / 512, bias=self.eps_c[:, 0:1]), reads=[bsm, bc], writes=[bsm])
                em.op("dve", lambda e: e.reciprocal(out=sm[:, 57:58], in_=sm[:, 57:58]), reads=[bsm], writes=[bsm])
                em.op("dve", lambda e: e.scalar_tensor_tensor(out=yn[:], in0=t1[:], scalar=sm[:, 57:58], in1=self.rv[:, oG:oG + 512], op0=ALU.mult, op1=ALU.mult),
                      reads=[bt1, bsm, bc], writes=[byn])
                qn = n % 4
                for c in range(4):
                    em.op("pe", lambda e, c=c: e.transpose(out=psT[:, c * 128:(c + 1) * 128], in_=yn[:, c * 128:(c + 1) * 128], identity=self.ident_bf[:]),
                          reads=[byn, bc], writes=[psb[7]])
                em.op("act", lambda e, qn=qn: e.copy(out=ysg[:, :, qn * 128:(qn + 1) * 128], in_=psT[:, 0:512].rearrange("p (a b) -> p a b", a=4)), reads=[psb[7]], writes=[bysg])
                if qn == 3:
                    t = n // 4
                    em.dma("sp", self.dram["ybr"][1536:2048, t * 512:(t + 1) * 512].rearrange("(c p) s -> p c s", p=128), ysg[:], reads=[bysg], writes=[self.b_ybr])
            em.barrier()


Builder.mixer_D = _mixer_D


def build_program(S=4096):
    b = Builder(S)
    em = b.em
    b.phase_copy_in()
    b.b_ybr = em.buf("ybr")
    b.b_gat = em.buf("gat")
    for l in range(DEPTH):
        with ExitStack() as st:
            b.mix_begin(l, st)
            b.gates(l)
            b.mixer_C(l)
            b.mixer_A(l)
            b.mixer_B(l)
            b.mixer_D(l)
        em.barrier()
        b.phase_merge(l)
        b.phase_xattn(l)
        b.phase_ffn(l)
    toks = b.phase_final()
    em.finish(toks)
    return b


_CACHE = {}


def kernel(**inputs):
    S = 4096
    if "b" not in _CACHE:
        _CACHE["b"] = build_program(S)
    b = _CACHE["b"]
    in_maps = [make_in_map(inputs, c % 4, S) for c in range(8)]
    res = run_bass_kernel_spmd(b.nc, in_maps, core_ids=list(range(8)))
    out = np.stack([np.asarray(res.results[c]["out"]).T for c in range(4)], axis=0)
    return np.ascontiguousarray(out.astype(np.float32))
```

```python
import math
from contextlib import ExitStack

import numpy as np
import concourse.bass as bass
import concourse.mybir as mybir
from concourse.bass_utils import run_bass_kernel_spmd

F32 = mybir.dt.float32
BF16 = mybir.dt.bfloat16
AF = mybir.ActivationFunctionType
ALU = mybir.AluOpType

D = 1024
DC = 8
DEPTH = 2
MEM = 256
EPS = 1e-6
FFN = 2816
FC = 22
IN_W = 10248
O_AQ, O_AK, O_AV = 0, 512, 1024
O_BQ, O_BK, O_BV, O_BG = 1536, 1792, 2048, 2560
O_CQ, O_CK, O_CV = 3072, 3584, 4096
O_DZ, O_DX, O_DDT, O_G = 4608, 5120, 6144, 6152


class Buf:
    __slots__ = ("name", "w", "r")

    def __init__(self, name):
        self.name = name
        self.w = None
        self.r = []


class Em:
    NRING = 24

    def __init__(self, nc):
        self.nc = nc
        self.eng = {"pe": nc.tensor, "act": nc.scalar, "dve": nc.vector, "pool": nc.gpsimd, "sp": nc.sync}
        self.sem = {}
        self.cnt = {}
        for e in ("pe", "act", "dve", "pool"):
            self.sem[e] = nc.alloc_semaphore("s_" + e)
            self.cnt[e] = 0
        self.known = {e: {} for e in self.eng}
        self.ring = {}
        self.ring_use = {}
        self.ring_next = {}
        for q in ("sp", "pool", "act"):
            self.ring[q] = [nc.alloc_semaphore("d_%s_%d" % (q, i)) for i in range(self.NRING)]
            self.ring_use[q] = [0] * self.NRING
            self.ring_next[q] = 0
        self.dma_tokens = []
        self.nbuf = 0

    def buf(self, name=None):
        self.nbuf += 1
        return Buf(name or ("b%d" % self.nbuf))

    def _wait(self, engine, tok):
        if tok is None:
            return
        key, sem, val = tok
        if key == engine and engine == "pe":
            return
        kn = self.known[engine]
        if kn.get(key, 0) >= val:
            return
        self.eng[engine].wait_ge(sem, val)
        kn[key] = val

    def _deps(self, engine, reads, writes):
        for b in reads:
            self._wait(engine, b.w)
        for b in writes:
            self._wait(engine, b.w)
            for t in b.r:
                self._wait(engine, t)

    def _commit(self, tok, reads, writes):
        for b in reads:
            b.r.append(tok)
            if len(b.r) > 12:
                last = {}
                for t in b.r:
                    if t[0] not in last or last[t[0]][2] < t[2]:
                        last[t[0]] = t
                b.r = list(last.values())
        for b in writes:
            b.w = tok
            b.r = []

    def op(self, engine, fn, reads=(), writes=()):
        self._deps(engine, reads, writes)
        inst = fn(self.eng[engine])
        self.cnt[engine] += 1
        inst.then_inc(self.sem[engine], 1)
        tok = (engine, self.sem[engine], self.cnt[engine])
        self._commit(tok, reads, writes)
        return tok

    def dma(self, q, out, in_, reads=(), writes=()):
        i = self.ring_next[q]
        self.ring_next[q] = (i + 1) % self.NRING
        sem = self.ring[q][i]
        key = ("dma", q, i)
        prior = 16 * self.ring_use[q][i]
        if prior:
            self._wait(q, (key, sem, prior))
        self._deps(q, reads, writes)
        self.eng[q].dma_start(out=out, in_=in_).then_inc(sem, 16)
        self.ring_use[q][i] += 1
        tok = (key, sem, 16 * self.ring_use[q][i])
        self._commit(tok, reads, writes)
        self.dma_tokens.append(tok)
        if len(self.dma_tokens) > 3 * self.NRING:
            self.dma_tokens = self.dma_tokens[-3 * self.NRING:]
        return tok

    def barrier(self):
        toks = [(e, self.sem[e], self.cnt[e]) for e in ("pe", "act", "dve", "pool") if self.cnt[e]]
        for q in self.ring:
            for i in range(self.NRING):
                if self.ring_use[q][i]:
                    toks.append((("dma", q, i), self.ring[q][i], 16 * self.ring_use[q][i]))
        for e in self.eng:
            for t in toks:
                if t[0] == e and e == "pe":
                    continue
                self._wait(e, t)

    def finish(self, toks):
        for t in toks:
            self._wait("sp", t)


def _ap(t):
    return t if isinstance(t, bass.AP) else t.ap()


class Ctx:
    def __init__(self, S, depth=DEPTH):
        self.S = S
        self.NT = S // 512
        self.NB = S // 128
        self.depth = depth
        nc = self.nc = bass.Bass("TRN2", target_bir_lowering=False)
        self.em = Em(nc)
        self.es = ExitStack()
        dr = self.dram = {}

        def din(name, shape, dt=F32):
            dr[name] = nc.dram_tensor(name, list(shape), dt, kind="ExternalInput").ap()

        din("xT", [D, S])
        din("memT", [D, MEM])
        din("w_in", [depth, D, IN_W])
        din("w_branch", [depth, 2048, D])
        din("w_mix_out", [depth, D, D])
        din("w_xq", [depth, D, 512])
        din("w_xkv", [depth, D, 1024])
        din("w_xo", [depth, 512, D])
        din("w_ffn_in", [depth, D, 2 * FFN])
        din("w_ffn_out", [depth, FFN, D])
        din("pvec", [128, PV_N])
        din("rvec", [128, RV_N])
        din("cst_f32", [128, CF_N])
        din("rot", [64, 2, S])
        dr["out"] = nc.dram_tensor("out", [D, S], F32, kind="ExternalOutput").ap()
        dr["xs"] = nc.dram_tensor("xs", [D, S], F32, kind="Internal").ap()
        dr["ybr"] = nc.dram_tensor("ybr", [2048, S], BF16, kind="Internal").ap()
        dr["gat"] = nc.dram_tensor("gat", [4096, S], BF16, kind="Internal").ap()

    def sb(self, name, shape, dt):
        return self.es.enter_context(self.nc.sbuf_tensor(name, list(shape), dt))

    _uid = 0

    def tl(self, st, name, shape, dt):
        Ctx._uid += 1
        return st.enter_context(self.nc.sbuf_tensor("%s_%d" % (name, Ctx._uid), list(shape), dt))


def _pv_layout():
    off = {}
    n = 0
    for l in range(DEPTH):
        for nm, w in (("norm_mix_g", 8), ("norm_x_g", 8), ("norm_mem_g", 8), ("norm_ffn_g", 8),
                      ("ret_gn_g", 4), ("diff_subln_g", 1), ("conv_w", 32), ("conv_b", 8)):
            off[(nm, l)] = n
            n += w
    off[("norm_f_g", 0)] = n
    n += 8
    return off, n


PV_OFF, PV_N = _pv_layout()


def _rv_layout():
    off = {}
    n = 0
    for l in range(DEPTH):
        for nm, w in (("dt_bias", 8), ("A_log", 8), ("D", 8), ("ssm_norm_g", 512), ("diff_lambda", 256)):
            off[(nm, l)] = n
            n += w
    return off, n


RV_OFF, RV_N = _rv_layout()

CF_IDENT, CF_TRIU, CF_TRIL, CF_DECT, CF_KEND, CF_QDEC, CF_SL = 0, 128, 256, 384, 896, 900, 1412
CF_N = 1540


class Builder(Ctx):
    def __init__(self, S, depth=DEPTH):
        super().__init__(S, depth)
        nc, em = self.nc, self.em
        self.pv = self.sb("pv", [128, PV_N], F32)
        self.rv = self.sb("rv", [128, RV_N], F32)
        self.cf = self.sb("cf", [128, CF_N], F32)
        self.ones_bf = self.sb("ones_bf", [128, 128], BF16)
        self.ones_f = self.sb("ones_f", [128, 128], F32)
        self.ident_bf = self.sb("ident_bf", [128, 128], BF16)
        self.triU_bf = self.sb("triU_bf", [128, 128], BF16)
        self.triL_bf = self.sb("triL_bf", [128, 128], BF16)
        self.b_const = em.buf("const")
        self.ps = []
        self.psb = []
        for i in range(8):
            self.ps.append(self.es.enter_context(nc.psum_tensor("ps%d" % i, [128, 512], F32)))
            self.psb.append(em.buf("ps%d" % i))
        self.stg = [self.sb("stg%d" % i, [128, 1024], F32) for i in range(2)]
        self.stgb = [em.buf("stg%d" % i) for i in range(2)]
        self.stg_i = 0
        d = self.dram
        bc = self.b_const
        em.dma("sp", self.pv[:], d["pvec"], writes=[bc])
        em.dma("sp", self.rv[:], d["rvec"], writes=[bc])
        em.dma("sp", self.cf[:], d["cst_f32"], writes=[bc])
        em.op("dve", lambda e: e.memset(self.ones_bf[:], 1.0), writes=[bc])
        em.op("dve", lambda e: e.memset(self.ones_f[:], 1.0), writes=[bc])
        self.eps_c = self.sb("eps_c", [128, 1], F32)
        em.op("dve", lambda e: e.memset(self.eps_c[:], EPS), writes=[bc])
        em.op("dve", lambda e: e.tensor_copy(out=self.ident_bf[:], in_=self.cf[:, CF_IDENT:CF_IDENT + 128]), reads=[bc], writes=[bc])
        em.op("dve", lambda e: e.tensor_copy(out=self.triU_bf[:], in_=self.cf[:, CF_TRIU:CF_TRIU + 128]), reads=[bc], writes=[bc])
        em.op("dve", lambda e: e.tensor_copy(out=self.triL_bf[:], in_=self.cf[:, CF_TRIL:CF_TRIL + 128]), reads=[bc], writes=[bc])
        em.barrier()

    def pvcol(self, name, l, j=0):
        o = PV_OFF[(name, l)] + j
        return self.pv[:, o:o + 1]

    def load_w(self, dst, src, nrows, ncols, wbuf):
        em = self.em
        c0 = 0
        while c0 < ncols:
            w = min(1024, ncols - c0)
            i = self.stg_i
            self.stg_i = (i + 1) % 2
            st, sbf = self.stg[i], self.stgb[i]
            em.dma("sp", st[0:nrows, 0:w], src[:, c0:c0 + w], writes=[sbf])
            o, ww, cc = dst, w, c0
            em.op("pool", lambda e, st=st, o=o, ww=ww, cc=cc: e.tensor_copy(out=o[:, cc:cc + ww], in_=st[0:nrows, 0:ww]),
                  reads=[sbf], writes=[wbuf])
            c0 += w

    def norm_tile(self, xt, ht, gname, l, sq, xb, hb, width=512, pi=0):
        em = self.em
        ps, psb = self.ps[pi], self.psb[pi]
        bsq = self.b_sq
        for c in range(DC):
            em.op("act", lambda e, c=c: e.activation(out=sq[:, c % 2, 0:width], in_=xt[:, c, 0:width], func=AF.Square),
                  reads=[xb], writes=[bsq[c % 2]])
            em.op("pe", lambda e, c=c: e.matmul(ps[:, 0:width], lhsT=self.ones_bf[:], rhs=sq[:, c % 2, 0:width],
                                                start=(c == 0), stop=(c == DC - 1)),
                  reads=[bsq[c % 2], self.b_const], writes=[psb])
        rs = self.rstd
        em.op("act", lambda e: e.activation(out=rs[:, 0:width], in_=ps[:, 0:width], func=AF.Sqrt, scale=1.0 / D, bias=self.eps_t[:, 0:1]),
              reads=[psb, self.b_const], writes=[self.b_rstd])
        em.op("dve", lambda e: e.reciprocal(out=rs[:, 0:width], in_=rs[:, 0:width]), reads=[self.b_rstd], writes=[self.b_rstd])
        for c in range(DC):
            em.op("dve", lambda e, c=c: e.scalar_tensor_tensor(out=ht[:, c, 0:width], in0=xt[:, c, 0:width],
                                                                scalar=self.pvcol(gname, l, c), in1=rs[:, 0:width],
                                                                op0=ALU.mult, op1=ALU.mult),
                  reads=[xb, self.b_rstd, self.b_const], writes=[hb])

    def alloc_norm_scratch(self, st):
        em = self.em
        self.sqt = self.tl(st, "sqt", [128, 2, 512], BF16)
        self.rstd = self.tl(st, "rstd", [128, 512], F32)
        self.eps_t = self.tl(st, "eps_t", [128, 1], F32)
        self.b_sq = [em.buf("sq0"), em.buf("sq1")]
        self.b_rstd = em.buf("rstd")
        em.op("dve", lambda e: e.memset(self.eps_t[:], EPS), writes=[self.b_const])

    def xtile_ap(self, name, t, width=512):
        return self.dram[name].rearrange("(c p) s -> p c s", p=128)[:, :, t * width:(t + 1) * width]

    def phase_copy_in(self):
        em = self.em
        b = self.b_xs = em.buf("xs")
        for c in range(DC):
            em.dma("sp", self.dram["xs"][c * 128:(c + 1) * 128, :], self.dram["xT"][c * 128:(c + 1) * 128, :], writes=[b])
        em.barrier()

    def phase_ffn(self, l):
        em, nc = self.em, self.nc
        with ExitStack() as st:
            self.alloc_norm_scratch(st)
            w1 = self.tl(st, "w1", [128, 8, 2 * FFN], BF16)
            w2 = self.tl(st, "w2", [128, FC, D], BF16)
            xt1 = self.tl(st, "xt0", [128, 8, 512], F32)
            xt = [xt1, xt1]
            ht = self.tl(st, "ht", [128, 8, 512], BF16)
            u = self.tl(st, "u", [128, FC, 512], BF16)
            sa = self.tl(st, "sa", [128, 2, 512], BF16)
            bw1 = [em.buf() for _ in range(8)]
            bw2 = [em.buf() for _ in range(FC)]
            bx0 = em.buf()
            bx = [bx0, bx0]
            bh, bu, bsa = em.buf(), em.buf(), [em.buf(), em.buf()]
            for k in range(8):
                self.load_w(w1[:, k, :], self.dram["w_ffn_in"][l, k * 128:(k + 1) * 128, :], 128, 2 * FFN, bw1[k])
            for f in range(FC):
                self.load_w(w2[:, f, :], self.dram["w_ffn_out"][l, f * 128:(f + 1) * 128, :], 128, D, bw2[f])
            for t in range(self.NT):
                cur = 0
                em.dma("sp", xt[cur][:], self.xtile_ap("xs", t), reads=[self.b_xs], writes=[bx[cur]])
                self.norm_tile(xt[cur], ht, "norm_ffn_g", l, self.sqt, bx[cur], bh)
                for f in range(FC):
                    pa, pb = 1 + (f % 2) * 2, 2 + (f % 2) * 2
                    for k in range(8):
                        em.op("pe", lambda e, k=k, f=f, pa=pa: e.matmul(self.ps[pa][:], lhsT=w1[:, k, f * 128:(f + 1) * 128], rhs=ht[:, k, :],
                                                                    start=(k == 0), stop=(k == 7)),
                              reads=[bw1[k], bh], writes=[self.psb[pa]])
                    for k in range(8):
                        em.op("pe", lambda e, k=k, f=f, pb=pb: e.matmul(self.ps[pb][:], lhsT=w1[:, k, FFN + f * 128:FFN + (f + 1) * 128], rhs=ht[:, k, :],
                                                                    start=(k == 0), stop=(k == 7)),
                              reads=[bw1[k], bh], writes=[self.psb[pb]])
                    si = f % 2
                    em.op("act", lambda e, pa=pa, si=si: e.activation(out=sa[:, si, :], in_=self.ps[pa][:], func=AF.Silu),
                          reads=[self.psb[pa]], writes=[bsa[si]])
                    em.op("dve", lambda e, pb=pb, si=si, f=f: e.tensor_tensor(out=u[:, f, :], in0=self.ps[pb][:], in1=sa[:, si, :], op=ALU.mult),
                          reads=[self.psb[pb], bsa[si]], writes=[bu])
                for c in range(DC):
                    po = 5 + (c % 2)
                    for f in range(FC):
                        em.op("pe", lambda e, c=c, f=f, po=po: e.matmul(self.ps[po][:], lhsT=w2[:, f, c * 128:(c + 1) * 128], rhs=u[:, f, :],
                                                                    start=(f == 0), stop=(f == FC - 1)),
                              reads=[bw2[f], bu], writes=[self.psb[po]])
                    em.op("dve", lambda e, c=c, po=po, cur=cur: e.tensor_tensor(out=xt[cur][:, c, :], in0=self.ps[po][:], in1=xt[cur][:, c, :], op=ALU.add),
                          reads=[self.psb[po], bx[cur]], writes=[bx[cur]])
                em.dma("sp", self.xtile_ap("xs", t), xt[cur][:], reads=[bx[cur]], writes=[self.b_xs])
            em.barrier()

    def phase_final(self):
        em, nc = self.em, self.nc
        toks = []
        with ExitStack() as st:
            self.alloc_norm_scratch(st)
            xt = self.tl(st, "xt", [128, 8, 512], F32)
            ot = self.tl(st, "ot", [128, 8, 512], F32)
            bx, bo = em.buf(), em.buf()
            b_out = em.buf("out")
            for t in range(self.NT):
                em.dma("sp", xt[:], self.xtile_ap("xs", t), reads=[self.b_xs], writes=[bx])
                self.norm_tile(xt, ot, "norm_f_g", 0, self.sqt, bx, bo)
                toks.append(em.dma("sp", self.xtile_ap("out", t), ot[:], reads=[bo], writes=[b_out]))
            em.barrier()
        return toks


def _chunked(v, nch):
    return np.ascontiguousarray(np.asarray(v, np.float32).reshape(nch, 128).T)


def make_tables(inp, S):
    pv = np.zeros((128, PV_N), np.float32)
    rv = np.zeros((128, RV_N), np.float32)
    for l in range(DEPTH):
        pv[:, PV_OFF[("norm_mix_g", l)]:][:, :8] = _chunked(inp["norm_mix_g"][l], 8)
        pv[:, PV_OFF[("norm_x_g", l)]:][:, :8] = _chunked(inp["norm_x_g"][l], 8)
        pv[:, PV_OFF[("norm_mem_g", l)]:][:, :8] = _chunked(inp["norm_mem_g"][l], 8)
        pv[:, PV_OFF[("norm_ffn_g", l)]:][:, :8] = _chunked(inp["norm_ffn_g"][l], 8)
        pv[:, PV_OFF[("ret_gn_g", l)]:][:, :4] = _chunked(inp["ret_gn_g"][l], 4)
        pv[:, PV_OFF[("diff_subln_g", l)]:][:, :1] = _chunked(inp["diff_subln_g"][l], 1)
        cw = np.asarray(inp["ssm_conv_w"][l], np.float32)
        o = PV_OFF[("conv_w", l)]
        for c in range(8):
            for k in range(4):
                pv[:, o + c * 4 + k] = cw[k, c * 128:(c + 1) * 128]
        pv[:, PV_OFF[("conv_b", l)]:][:, :8] = _chunked(inp["ssm_conv_b"][l], 8)
        for nm, key, w in (("dt_bias", "ssm_dt_bias", 8), ("A_log", "ssm_A_log", 8), ("D", "ssm_D", 8),
                           ("ssm_norm_g", "ssm_norm_g", 512)):
            o = RV_OFF[(nm, l)]
            rv[:, o:o + w] = np.asarray(inp[key][l], np.float32).reshape(1, w)
        o = RV_OFF[("diff_lambda", l)]
        rv[:, o:o + 256] = np.asarray(inp["diff_lambda"][l], np.float32).reshape(1, 256)
    pv[:, PV_OFF[("norm_f_g", 0)]:][:, :8] = _chunked(inp["norm_f_g"], 8)
    cf = np.zeros((128, CF_N), np.float32)
    idx = np.arange(128)
    cf[:, CF_IDENT:CF_IDENT + 128] = np.eye(128, dtype=np.float32)
    cf[:, CF_TRIU:CF_TRIU + 128] = (idx[:, None] <= idx[None, :])
    cf[:, CF_TRIL:CF_TRIL + 128] = (idx[:, None] >= idx[None, :])
    cf[:, CF_SL:CF_SL + 128] = (idx[:, None] > idx[None, :])
    lg = np.log1p(-np.exp2(-5.0 - np.arange(4, dtype=np.float32))).astype(np.float32)
    for h in range(4):
        rel = (idx[None, :] - idx[:, None]).astype(np.float32)
        dec = np.where(rel >= 0, np.exp(lg[h] * np.maximum(rel, 0.0)), 0.0)
        cf[:, CF_DECT + h * 128:CF_DECT + (h + 1) * 128] = dec
        cf[:, CF_KEND + h] = np.exp((127 - idx) * lg[h])
        cf[:, CF_QDEC + h * 128:CF_QDEC + (h + 1) * 128] = np.exp((idx + 1.0) * lg[h])[None, :]
    half = 32
    inv_freq = (10000.0 ** (-np.arange(half, dtype=np.float32) / half)).astype(np.float32)
    pos = np.arange(S, dtype=np.float32)
    ang = (pos[None, :] * inv_freq[:, None]).astype(np.float32)
    cos, sin = np.cos(ang).astype(np.float32), np.sin(ang).astype(np.float32)
    rot = np.zeros((64, 2, S), np.float32)
    rot[:32, 0] = cos
    rot[32:, 0] = cos
    rot[:32, 1] = -sin
    rot[32:, 1] = sin
    return pv, rv, cf, rot


def make_in_map(inp, b, S):
    pv, rv, cf, rot = make_tables(inp, S)
    f = lambda a: np.ascontiguousarray(np.asarray(a, np.float32))
    return {
        "xT": f(np.asarray(inp["x"][b]).T[:, :S]),
        "memT": f(np.asarray(inp["mem"][b]).T),
        "w_in": f(inp["w_in"]),
        "w_branch": f(np.asarray(inp["w_branch"]).reshape(DEPTH, 2048, D)),
        "w_mix_out": f(inp["w_mix_out"]),
        "w_xq": f(inp["w_xq"]),
        "w_xkv": f(inp["w_xkv"]),
        "w_xo": f(inp["w_xo"]),
        "w_ffn_in": f(inp["w_ffn_in"]),
        "w_ffn_out": f(inp["w_ffn_out"]),
        "pvec": pv, "rvec": rv, "cst_f32": cf, "rot": rot,
    }


def _phase_xattn(self, l):
    em, nc = self.em, self.nc
    ps, psb = self.ps, self.psb
    with ExitStack() as st:
        self.alloc_norm_scratch(st)
        wq = self.tl(st, "wq", [128, 8, 512], BF16)
        wkv = self.tl(st, "wkv", [128, 8, 1024], BF16)
        wo = self.tl(st, "wo", [128, 4, D], BF16)
        mt = self.tl(st, "mt", [128, 8, MEM], F32)
        mh = self.tl(st, "mh", [128, 8, MEM], BF16)
        kT = self.tl(st, "kT", [128, 4, MEM], BF16)
        V = self.tl(st, "V", [128, 2, 512], BF16)
        xt = self.tl(st, "xt", [128, 8, 512], F32)
        ht = self.tl(st, "ht", [128, 8, 512], BF16)
        qh = self.tl(st, "qh", [128, 512], BF16)
        PT = self.tl(st, "PT", [128, 2, 512], BF16)
        rden = self.tl(st, "rden", [128, 512], F32)
        o = self.tl(st, "o", [128, 4, 512], BF16)
        bwq, bwkv, bwo = em.buf(), em.buf(), em.buf()
        bmt, bmh, bk, bv = em.buf(), em.buf(), em.buf(), em.buf()
        bx, bh, bq, bp, brd, bo = em.buf(), em.buf(), em.buf(), [em.buf(), em.buf()], em.buf(), em.buf()
        for k in range(8):
            self.load_w(wq[:, k, :], self.dram["w_xq"][l, k * 128:(k + 1) * 128, :], 128, 512, bwq)
            self.load_w(wkv[:, k, :], self.dram["w_xkv"][l, k * 128:(k + 1) * 128, :], 128, 1024, bwkv)
        for h in range(4):
            self.load_w(wo[:, h, :], self.dram["w_xo"][l, h * 128:(h + 1) * 128, :], 128, D, bwo)
        em.dma("sp", mt[:], self.dram["memT"].rearrange("(c p) s -> p c s", p=128), writes=[bmt])
        self.norm_tile(mt, mh, "norm_mem_g", l, self.sqt, bmt, bmh, width=MEM)
        for h in range(4):
            for k in range(8):
                em.op("pe", lambda e, k=k, h=h: e.matmul(ps[1][:, 0:MEM], lhsT=wkv[:, k, h * 128:(h + 1) * 128], rhs=mh[:, k, :],
                                                         start=(k == 0), stop=(k == 7)), reads=[bwkv, bmh], writes=[psb[1]])
            em.op("act", lambda e, h=h: e.copy(out=kT[:, h, :], in_=ps[1][:, 0:MEM]), reads=[psb[1]], writes=[bk])
        for mb in range(2):
            for k in range(8):
                em.op("pe", lambda e, k=k, mb=mb: e.matmul(ps[2][:], lhsT=mh[:, k, mb * 128:(mb + 1) * 128], rhs=wkv[:, k, 512:1024],
                                                           start=(k == 0), stop=(k == 7)), reads=[bwkv, bmh], writes=[psb[2]])
            em.op("act", lambda e, mb=mb: e.copy(out=V[:, mb, :], in_=ps[2][:]), reads=[psb[2]], writes=[bv])
        sc = 128.0 ** -0.5
        for t in range(self.NT):
            em.dma("sp", xt[:], self.xtile_ap("xs", t), reads=[self.b_xs], writes=[bx])
            self.norm_tile(xt, ht, "norm_x_g", l, self.sqt, bx, bh)
            for h in range(4):
                for k in range(8):
                    em.op("pe", lambda e, k=k, h=h: e.matmul(ps[1][:], lhsT=wq[:, k, h * 128:(h + 1) * 128], rhs=ht[:, k, :],
                                                             start=(k == 0), stop=(k == 7)), reads=[bwq, bh], writes=[psb[1]])
                em.op("act", lambda e: e.copy(out=qh[:], in_=ps[1][:]), reads=[psb[1]], writes=[bq])
                for mb in range(2):
                    em.op("pe", lambda e, h=h, mb=mb: e.matmul(ps[2 + mb][:], lhsT=kT[:, h, mb * 128:(mb + 1) * 128], rhs=qh[:],
                                                               start=True, stop=True), reads=[bk, bq], writes=[psb[2 + mb]])
                    em.op("act", lambda e, mb=mb: e.activation(out=PT[:, mb, :], in_=ps[2 + mb][:], func=AF.Exp, scale=sc),
                          reads=[psb[2 + mb]], writes=[bp[mb]])
                for mb in range(2):
                    em.op("pe", lambda e, h=h, mb=mb: e.matmul(ps[4][:], lhsT=V[:, mb, h * 128:(h + 1) * 128], rhs=PT[:, mb, :],
                                                               start=(mb == 0), stop=(mb == 1)), reads=[bv, bp[mb]], writes=[psb[4]])
                for mb in range(2):
                    em.op("pe", lambda e, mb=mb: e.matmul(ps[5][:], lhsT=self.ones_bf[:], rhs=PT[:, mb, :],
                                                          start=(mb == 0), stop=(mb == 1)), reads=[self.b_const, bp[mb]], writes=[psb[5]])
                em.op("dve", lambda e: e.reciprocal(out=rden[:], in_=ps[5][:]), reads=[psb[5]], writes=[brd])
                em.op("dve", lambda e, h=h: e.tensor_tensor(out=o[:, h, :], in0=ps[4][:], in1=rden[:], op=ALU.mult),
                      reads=[psb[4], brd], writes=[bo])
            for c in range(DC):
                po = 6 + (c % 2)
                for h in range(4):
                    em.op("pe", lambda e, c=c, h=h, po=po: e.matmul(ps[po][:], lhsT=wo[:, h, c * 128:(c + 1) * 128], rhs=o[:, h, :],
                                                                    start=(h == 0), stop=(h == 3)), reads=[bwo, bo], writes=[psb[po]])
                em.op("dve", lambda e, c=c, po=po: e.tensor_tensor(out=xt[:, c, :], in0=ps[po][:], in1=xt[:, c, :], op=ALU.add),
                      reads=[psb[po], bx], writes=[bx])
            em.dma("sp", self.xtile_ap("xs", t), xt[:], reads=[bx], writes=[self.b_xs])
        em.barrier()


Builder.phase_xattn = _phase_xattn


def _mix_begin(self, l, st):
    em, nc = self.em, self.nc
    self.hT = self.tl(st, "hT", [128, 8, self.S], BF16)
    self.b_hT = em.buf("hT")
    with ExitStack() as s2:
        self.alloc_norm_scratch(s2)
        xt = self.tl(s2, "xt", [128, 8, 512], F32)
        bx = em.buf()
        for t in range(self.NT):
            em.dma("sp", xt[:], self.xtile_ap("xs", t), reads=[self.b_xs], writes=[bx])
            self.norm_tile(xt, self.hT[:, :, t * 512:(t + 1) * 512], "norm_mix_g", l, self.sqt, bx, self.b_hT)
        em.barrier()


def _load_wk(self, dst, l, col0, n, wbuf):
    em = self.em
    i = self.stg_i
    self.stg_i = (i + 1) % 2
    stg, sbf = self.stg[i], self.stgb[i]
    src = self.dram["w_in"][l].rearrange("(k p) c -> p k c", p=128)[:, :, col0:col0 + n]
    sv = stg[:, 0:8 * n].rearrange("p (k c) -> p k c", k=8)
    em.dma("sp", sv, src, writes=[sbf])
    em.op("pool", lambda e: e.tensor_copy(out=dst, in_=sv), reads=[sbf], writes=[wbuf])


def _proj_fm(self, w, n, wbuf, pi, evac):
    em = self.em
    for t in range(self.NT):
        p = pi[t % len(pi)]
        for k in range(8):
            em.op("pe", lambda e, k=k, t=t, p=p: e.matmul(self.ps[p][0:n, :], lhsT=w[:, k, 0:n], rhs=self.hT[:, k, t * 512:(t + 1) * 512],
                                                          start=(k == 0), stop=(k == 7)), reads=[wbuf, self.b_hT], writes=[self.psb[p]])
        evac(t, p)


def _proj_tm(self, w, n, wbuf, tok_ap_fn, nblk, pi, evac):
    em = self.em
    for b in range(nblk):
        p = pi[b % len(pi)]
        for k in range(8):
            em.op("pe", lambda e, k=k, b=b, p=p: e.matmul(self.ps[p][:, 0:n], lhsT=tok_ap_fn(k, b), rhs=w[:, k, 0:n],
                                                          start=(k == 0), stop=(k == 7)), reads=[wbuf, self.b_hT], writes=[self.psb[p]])
        evac(b, p)


def _mixer_C(self, l):
    em, nc = self.em, self.nc
    ps, psb = self.ps, self.psb
    S, NB, NT = self.S, self.NB, self.NT
    lam_init = 0.8 - 0.6 * math.exp(-0.3 * l)
    with ExitStack() as st:
        wq = self.tl(st, "wq", [128, 8, 128], BF16)
        wk = self.tl(st, "wk", [128, 8, 128], BF16)
        wv = self.tl(st, "wv", [128, 8, 128], BF16)
        qT = self.tl(st, "qT", [128, S], BF16)
        kT = self.tl(st, "kT", [128, 2, S], BF16)
        V = self.tl(st, "V", [128, NB, 128], BF16)
        PT = self.tl(st, "PT", [128, 4, 512], BF16)
        r1 = self.tl(st, "r1", [128, 512], F32)
        r2 = self.tl(st, "r2", [128, 512], F32)
        t1 = self.tl(st, "t1", [128, 512], F32)
        t2 = self.tl(st, "t2", [128, 512], F32)
        yo = self.tl(st, "yo", [128, 2, 512], BF16)
        lam = self.tl(st, "lam", [128, 8], F32)
        ltmp = self.tl(st, "ltmp", [128, 64], F32)
        bwq, bwk, bwv, bq, bk, bv = [em.buf() for _ in range(6)]
        bp = [em.buf() for _ in range(4)]
        br1, br2, bt1, bt2, blam = [em.buf() for _ in range(5)]
        byo = [em.buf(), em.buf()]
        b_y = self.b_ybr
        o = RV_OFF[("diff_lambda", l)]
        for i in range(2):
            em.op("dve", lambda e, i=i: e.tensor_tensor(out=ltmp[:], in0=self.rv[:, o + 128 * i:o + 128 * i + 64],
                                                         in1=self.rv[:, o + 128 * i + 64:o + 128 * i + 128], op=ALU.mult),
                  reads=[self.b_const], writes=[blam])
            em.op("dve", lambda e, i=i: e.reduce_sum(out=lam[:, i:i + 1], in_=ltmp[:], axis=mybir.AxisListType.X), reads=[blam], writes=[blam])
        em.op("act", lambda e: e.activation(out=lam[:, 2:4], in_=lam[:, 0:2], func=AF.Exp), reads=[blam], writes=[blam])
        em.op("dve", lambda e: e.tensor_tensor(out=lam[:, 4:5], in0=lam[:, 3:4], in1=lam[:, 2:3], op=ALU.subtract), reads=[blam], writes=[blam])
        em.op("dve", lambda e: e.tensor_scalar(out=lam[:, 5:6], in0=lam[:, 4:5], scalar1=-lam_init, scalar2=None, op0=ALU.add), reads=[blam], writes=[blam])
        em.op("dve", lambda e: e.memset(lam[:, 6:7], EPS / (1.0 - lam_init) ** 2), writes=[blam])
        sc = 64.0 ** -0.5
        em.op("pool", lambda e: e.memset(kT[64:128, 0, :], 0.0), writes=[bk])
        em.op("pool", lambda e: e.memset(kT[0:64, 1, :], 0.0), writes=[bk])
        for h in range(4):
            self.load_wk(wq[:], l, O_CQ + h * 128, 128, bwq)
            self.load_wk(wk[:], l, O_CK + h * 128, 128, bwk)
            self.load_wk(wv[:], l, O_CV + h * 128, 128, bwv)
            self.proj_fm(wq, 128, bwq, [6, 7], lambda t, p: em.op("act", lambda e: e.copy(out=qT[:, t * 512:(t + 1) * 512], in_=ps[p][:]), reads=[psb[p]], writes=[bq]))
            def evk(t, p):
                em.op("dve", lambda e: e.tensor_copy(out=kT[0:64, 0, t * 512:(t + 1) * 512], in_=ps[p][0:64, :]), reads=[psb[p]], writes=[bk])
                em.op("dve", lambda e: e.tensor_copy(out=kT[64:128, 1, t * 512:(t + 1) * 512], in_=ps[p][64:128, :]), reads=[psb[p]], writes=[bk])
            self.proj_fm(wk, 128, bwk, [6, 7], evk)
            self.proj_tm(wv, 128, bwv, lambda k, b: self.hT[:, k, b * 128:(b + 1) * 128], NB, [6, 7],
                         lambda b, p: em.op("act", lambda e: e.copy(out=V[:, b, :], in_=ps[p][:, 0:128]), reads=[psb[p]], writes=[bv]))
            LAG = 3
            items = []
            for t in range(NT):
                nj = 4 * t + 4
                for c in range(2):
                    for j in range(nj):
                        items.append((t, c, j, nj))
            SB = [0, 1, 6, 7]

            def stage1(i):
                t, c, j, nj = items[i]
                lo, hi = c * 64, (c + 1) * 64
                q0 = max(0, j - 4 * t) * 128
                s = i % 4
                sp = SB[s]
                em.op("pe", lambda e: e.matmul(ps[sp][:, q0:512], lhsT=kT[:, c, j * 128:(j + 1) * 128], rhs=qT[:, t * 512 + q0:(t + 1) * 512],
                                               start=True, stop=True), reads=[bk, bq], writes=[psb[sp]])
                em.op("act", lambda e: e.activation(out=PT[:, s, q0:512], in_=ps[sp][:, q0:512], func=AF.Exp, scale=sc), reads=[psb[sp]], writes=[bp[s]])
                if j >= 4 * t:
                    em.op("pool", lambda e: e.tensor_tensor(out=PT[:, s, q0:q0 + 128], in0=PT[:, s, q0:q0 + 128], in1=self.triU_bf[:], op=ALU.mult),
                          reads=[bp[s], self.b_const], writes=[bp[s]])

            def stage2(i):
                t, c, j, nj = items[i]
                q0 = max(0, j - 4 * t) * 128
                s = i % 4
                em.op("pe", lambda e: e.matmul(ps[2 + c][:, q0:512], lhsT=V[:, j, :], rhs=PT[:, s, q0:512], start=(j == 0), stop=(j == nj - 1)),
                      reads=[bv, bp[s]], writes=[psb[2 + c]])
                em.op("pe", lambda e: e.matmul(ps[4 + c][:, q0:512], lhsT=self.ones_bf[:], rhs=PT[:, s, q0:512], start=(j == 0), stop=(j == nj - 1)),
                      reads=[self.b_const, bp[s]], writes=[psb[4 + c]])
                if c == 1 and j == nj - 1:
                    epilogue(t)

            def epilogue(t):
                em.op("dve", lambda e: e.reciprocal(out=r1[:], in_=ps[4][:]), reads=[psb[4]], writes=[br1])
                em.op("dve", lambda e: e.reciprocal(out=r2[:], in_=ps[5][:]), reads=[psb[5]], writes=[br2])
                em.op("dve", lambda e: e.tensor_tensor(out=t1[:], in0=ps[2][:], in1=r1[:], op=ALU.mult), reads=[psb[2], br1], writes=[bt1])
                em.op("dve", lambda e: e.tensor_tensor(out=t2[:], in0=ps[3][:], in1=r2[:], op=ALU.mult), reads=[psb[3], br2], writes=[bt2])
                em.op("dve", lambda e: e.scalar_tensor_tensor(out=t1[:], in0=t2[:], scalar=lam[:, 5:6], in1=t1[:], op0=ALU.mult, op1=ALU.add),
                      reads=[bt1, bt2, blam], writes=[bt1])
                em.op("act", lambda e: e.activation(out=t2[:], in_=t1[:], func=AF.Square), reads=[bt1], writes=[bt2])
                em.op("pe", lambda e: e.matmul(ps[4][:], lhsT=self.ones_f[:], rhs=t2[:], start=True, stop=True),
                      reads=[self.b_const, bt2], writes=[psb[4]])
                em.op("act", lambda e: e.activation(out=r1[:], in_=ps[4][:], func=AF.Sqrt, scale=1.0 / (128.0 * (1.0 - lam_init) ** 2), bias=lam[:, 6:7]),
                      reads=[psb[4], blam], writes=[br1])
                em.op("dve", lambda e: e.reciprocal(out=r1[:], in_=r1[:]), reads=[br1], writes=[br1])
                yi = t % 2
                em.op("dve", lambda e: e.scalar_tensor_tensor(out=yo[:, yi, :], in0=t1[:], scalar=self.pvcol("diff_subln_g", l), in1=r1[:],
                                                              op0=ALU.mult, op1=ALU.mult), reads=[bt1, br1, self.b_const], writes=[byo[yi]])
                r0 = 1024 + h * 128
                em.dma("sp", self.dram["ybr"][r0:r0 + 128, t * 512:(t + 1) * 512], yo[:, yi, :], reads=[byo[yi]], writes=[b_y])

            for i in range(len(items) + LAG):
                if i < len(items):
                    stage1(i)
                if i - LAG >= 0:
                    stage2(i - LAG)
        em.barrier()


Builder.mix_begin = _mix_begin
Builder.load_wk = _load_wk
Builder.proj_fm = _proj_fm
Builder.proj_tm = _proj_tm
Builder.mixer_C = _mixer_C


def _gates(self, l):
    em = self.em
    ps, psb = self.ps, self.psb
    with ExitStack() as st:
        wg = [self.tl(st, "wg", [128, 8, 128], BF16) for _ in range(2)]
        bwg = [em.buf(), em.buf()]
        go = [self.tl(st, "go", [128, 512], BF16) for _ in range(2)]
        bgo = [em.buf(), em.buf()]
        b_g = self.b_gat
        n = 0
        for ic in range(32):
            w, bw = wg[ic % 2], bwg[ic % 2]
            self.load_wk(w[:], l, O_G + ic * 128, 128, bw)

            def evac(t, p, ic=ic):
                nonlocal n
                g, bg = go[n % 2], bgo[n % 2]
                n += 1
                em.op("act", lambda e: e.activation(out=g[:], in_=ps[p][:], func=AF.Sigmoid), reads=[psb[p]], writes=[bg])
                em.dma("sp", self.dram["gat"][ic * 128:(ic + 1) * 128, t * 512:(t + 1) * 512], g[:], reads=[bg], writes=[b_g])
            self.proj_fm(w, 128, bw, [0, 1, 2, 3], evac)
        em.barrier()


def _phase_merge(self, l):
    em, nc = self.em, self.nc
    ps, psb = self.ps, self.psb
    with ExitStack() as st:
        wb = self.tl(st, "wb", [128, 16, D], BF16)
        wo = self.tl(st, "wo", [128, 8, D], BF16)
        y = self.tl(st, "y", [128, 16, 512], BF16)
        g = self.tl(st, "g", [128, 32, 512], BF16)
        xt = self.tl(st, "xt", [128, 8, 512], F32)
        mg = self.tl(st, "mg", [128, 8, 512], BF16)
        acc = self.tl(st, "acc", [128, 512], F32)
        tmp = self.tl(st, "tmp", [128, 2, 512], F32)
        bwb = [em.buf() for _ in range(16)]
        bwo = [em.buf() for _ in range(8)]
        by, bg, bx, bmg, bacc = em.buf(), em.buf(), em.buf(), em.buf(), em.buf()
        btmp = [em.buf(), em.buf()]
        for r in range(16):
            self.load_w(wb[:, r, :], self.dram["w_branch"][l, r * 128:(r + 1) * 128, :], 128, D, bwb[r])
        for c in range(8):
            self.load_w(wo[:, c, :], self.dram["w_mix_out"][l, c * 128:(c + 1) * 128, :], 128, D, bwo[c])
        for t in range(self.NT):
            em.dma("sp", y[:], self.dram["ybr"].rearrange("(r p) s -> p r s", p=128)[:, :, t * 512:(t + 1) * 512], reads=[self.b_ybr], writes=[by])
            em.dma("sp", g[:], self.dram["gat"].rearrange("(r p) s -> p r s", p=128)[:, :, t * 512:(t + 1) * 512], reads=[self.b_gat], writes=[bg])
            em.dma("sp", xt[:], self.xtile_ap("xs", t), reads=[self.b_xs], writes=[bx])
            n = 0
            for c in range(8):
                for i in range(4):
                    p = n % 4
                    n += 1
                    for m in range(4):
                        em.op("pe", lambda e, p=p, i=i, m=m, c=c: e.matmul(ps[p][:], lhsT=wb[:, i * 4 + m, c * 128:(c + 1) * 128], rhs=y[:, i * 4 + m, :],
                                                                           start=(m == 0), stop=(m == 3)), reads=[bwb[i * 4 + m], by], writes=[psb[p]])
                    if i == 0:
                        em.op("dve", lambda e, p=p, c=c: e.tensor_tensor(out=acc[:], in0=ps[p][:], in1=g[:, c, :], op=ALU.mult),
                              reads=[psb[p], bg], writes=[bacc])
                    else:
                        ti = i % 2
                        em.op("dve", lambda e, p=p, c=c, i=i, ti=ti: e.tensor_tensor(out=tmp[:, ti, :], in0=ps[p][:], in1=g[:, i * 8 + c, :], op=ALU.mult),
                              reads=[psb[p], bg], writes=[btmp[ti]])
                        if i < 3:
                            em.op("pool", lambda e, ti=ti: e.tensor_tensor(out=acc[:], in0=acc[:], in1=tmp[:, ti, :], op=ALU.add),
                                  reads=[bacc, btmp[ti]], writes=[bacc])
                        else:
                            em.op("pool", lambda e, ti=ti, c=c: e.tensor_tensor(out=mg[:, c, :], in0=acc[:], in1=tmp[:, ti, :], op=ALU.add),
                                  reads=[bacc, btmp[ti]], writes=[bmg])
            for c2 in range(8):
                po = 4 + (c2 % 2)
                for c in range(8):
                    em.op("pe", lambda e, c=c, c2=c2, po=po: e.matmul(ps[po][:], lhsT=wo[:, c, c2 * 128:(c2 + 1) * 128], rhs=mg[:, c, :],
                                                                      start=(c == 0), stop=(c == 7)), reads=[bwo[c], bmg], writes=[psb[po]])
                em.op("dve", lambda e, c2=c2, po=po: e.tensor_tensor(out=xt[:, c2, :], in0=ps[po][:], in1=xt[:, c2, :], op=ALU.add),
                      reads=[psb[po], bx], writes=[bx])
            em.dma("sp", self.xtile_ap("xs", t), xt[:], reads=[bx], writes=[self.b_xs])
        em.barrier()


Builder.gates = _gates
Builder.phase_merge = _phase_merge


def _mixer_A(self, l):
    em, nc = self.em, self.nc
    ps, psb = self.ps, self.psb
    S, NB, NT = self.S, self.NB, self.NT
    with ExitStack() as st:
        wq = self.tl(st, "wq", [128, 8, 128], BF16)
        wk = self.tl(st, "wk", [128, 8, 128], BF16)
        wv = self.tl(st, "wv", [128, 8, 128], BF16)
        qT = self.tl(st, "qT", [128, S], BF16)
        kT = self.tl(st, "kT", [128, S], BF16)
        V = self.tl(st, "V", [128, NB, 128], BF16)
        PT = self.tl(st, "PT", [128, 4, 256], BF16)
        msk = self.tl(st, "msk", [128, 256], BF16)
        accN = self.tl(st, "accN", [128, S], F32)
        accD = self.tl(st, "accD", [128, S], F32)
        yo = self.tl(st, "yo", [64, 2, 512], BF16)
        bwq, bwk, bwv, bq, bk, bv, bm, baN, baD = [em.buf() for _ in range(9)]
        bp = [em.buf() for _ in range(4)]
        byo = [em.buf(), em.buf()]
        em.op("dve", lambda e: e.tensor_copy(out=msk[:, 0:128], in_=self.triU_bf[:]), reads=[self.b_const], writes=[bm])
        em.op("dve", lambda e: e.tensor_copy(out=msk[:, 128:256], in_=self.triL_bf[:]), reads=[self.b_const], writes=[bm])
        sc = 64.0 ** -0.5
        for (wt_, bw_) in ((wq, bwq), (wk, bwk), (wv, bwv)):
            em.op("pool", lambda e, wt_=wt_: e.memset(wt_[:, :, 64:128], 0.0), writes=[bw_])
        for h in range(8):
            self.load_wk(wq[:, :, 0:64], l, O_AQ + h * 64, 64, bwq)
            self.load_wk(wk[:, :, 0:64], l, O_AK + h * 64, 64, bwk)
            self.load_wk(wv[:, :, 0:64], l, O_AV + h * 64, 64, bwv)
            self.proj_fm(wq, 128, bwq, [6, 7], lambda t, p: em.op("act", lambda e: e.copy(out=qT[:, t * 512:(t + 1) * 512], in_=ps[p][:, :]), reads=[psb[p]], writes=[bq]))
            self.proj_fm(wk, 128, bwk, [6, 7], lambda t, p: em.op("dve", lambda e: e.tensor_copy(out=kT[:, t * 512:(t + 1) * 512], in_=ps[p][:, :]), reads=[psb[p]], writes=[bk]))
            for pi_, dil in enumerate((1, 4, 16)):
                L = S // dil
                nbs = L // 128
                assert nbs >= 1 and nbs * 128 * dil == S

                def tok(r, n, cnt=128, dil=dil):
                    s0 = r + dil * n * 128
                    return slice(s0, s0 + dil * (cnt - 1) + 1, dil)
                self.proj_tm(wv, 128, bwv, lambda k, b: self.hT[:, k, tok(b // nbs, b % nbs)], NB, [6, 7],
                             lambda b, p: em.op("act", lambda e: e.copy(out=V[:, b, :], in_=ps[p][:, 0:128]), reads=[psb[p]], writes=[bv]))
                grp = min(4, nbs)
                LAG = 2
                SBF = [0, 1, 6]
                blocks = [(r, n) for r in range(dil) for n in range(nbs)]

                def stage1(i, dil=dil, nbs=nbs, tok=tok):
                    r, n = blocks[i]
                    s = i % 3
                    sp = SBF[s]
                    w = 256 if n > 0 else 128
                    em.op("pe", lambda e: e.matmul(ps[sp][:, 0:128], lhsT=kT[:, tok(r, n)], rhs=qT[:, tok(r, n)], start=True, stop=True),
                          reads=[bk, bq], writes=[psb[sp]])
                    if n > 0:
                        em.op("pe", lambda e: e.matmul(ps[sp][:, 128:256], lhsT=kT[:, tok(r, n - 1)], rhs=qT[:, tok(r, n)], start=True, stop=True),
                              reads=[bk, bq], writes=[psb[sp]])
                    em.op("act", lambda e: e.activation(out=PT[:, s, 0:w], in_=ps[sp][:, 0:w], func=AF.Exp, scale=sc), reads=[psb[sp]], writes=[bp[s]])
                    em.op("pool", lambda e: e.tensor_tensor(out=PT[:, s, 0:w], in0=PT[:, s, 0:w], in1=msk[:, 0:w], op=ALU.mult),
                          reads=[bp[s], bm], writes=[bp[s]])

                def stage2(i, dil=dil, nbs=nbs, tok=tok, pi_=pi_, grp=grp):
                    r, n = blocks[i]
                    s = i % 3
                    gidx = i // grp
                    pn = 2 + (gidx % 2)
                    pd = 4 + (gidx % 2)
                    n0 = (n // grp) * grp
                    qs = (n - n0) * 128
                    pb = r * nbs + n
                    last = (n == 0)
                    em.op("pe", lambda e: e.matmul(ps[pn][:, qs:qs + 128], lhsT=V[:, pb, :], rhs=PT[:, s, 0:128], start=True, stop=last),
                          reads=[bv, bp[s]], writes=[psb[pn]])
                    if n > 0:
                        em.op("pe", lambda e: e.matmul(ps[pn][:, qs:qs + 128], lhsT=V[:, pb - 1, :], rhs=PT[:, s, 128:256], start=False, stop=True),
                              reads=[bv, bp[s]], writes=[psb[pn]])
                    em.op("pe", lambda e: e.matmul(ps[pd][:, qs:qs + 128], lhsT=self.ones_bf[:], rhs=PT[:, s, 0:128], start=True, stop=last),
                          reads=[self.b_const, bp[s]], writes=[psb[pd]])
                    if n > 0:
                        em.op("pe", lambda e: e.matmul(ps[pd][:, qs:qs + 128], lhsT=self.ones_bf[:], rhs=PT[:, s, 128:256], start=False, stop=True),
                              reads=[self.b_const, bp[s]], writes=[psb[pd]])
                    if n == n0 + grp - 1:
                        tsl = tok(r, n0, grp * 128)
                        gw = grp * 128
                        if pi_ == 0:
                            em.op("dve", lambda e: e.tensor_copy(out=accN[:, tsl], in_=ps[pn][:, 0:gw]), reads=[psb[pn]], writes=[baN])
                            em.op("act", lambda e: e.copy(out=accD[:, tsl], in_=ps[pd][:, 0:gw]), reads=[psb[pd]], writes=[baD])
                        else:
                            em.op("dve", lambda e: e.tensor_tensor(out=accN[:, tsl], in0=ps[pn][:, 0:gw], in1=accN[:, tsl], op=ALU.add),
                                  reads=[psb[pn], baN], writes=[baN])
                            em.op("dve", lambda e: e.tensor_tensor(out=accD[:, tsl], in0=ps[pd][:, 0:gw], in1=accD[:, tsl], op=ALU.add),
                                  reads=[psb[pd], baD], writes=[baD])

                for i in range(len(blocks) + LAG):
                    if i < len(blocks):
                        stage1(i)
                    if i - LAG >= 0:
                        stage2(i - LAG)
            for t in range(NT):
                sl = slice(t * 512, (t + 1) * 512)
                yi = t % 2
                em.op("dve", lambda e, sl=sl: e.reciprocal(out=accD[0:64, sl], in_=accD[0:64, sl]), reads=[baD], writes=[baD])
                em.op("dve", lambda e, sl=sl, yi=yi: e.tensor_tensor(out=yo[:, yi, :], in0=accN[0:64, sl], in1=accD[0:64, sl], op=ALU.mult),
                      reads=[baN, baD], writes=[byo[yi]])
                em.dma("sp", self.dram["ybr"][h * 64:(h + 1) * 64, sl], yo[:, yi, :], reads=[byo[yi]], writes=[self.b_ybr])
        em.barrier()


Builder.mixer_A = _mixer_A


def _bcast_mid(ap2d, n):
    a = ap2d.ap
    return bass.AP(ap2d.tensor, ap2d.offset, [list(a[0]), [0, n], list(a[1])])


def _mixer_B(self, l):
    em, nc = self.em, self.nc
    ps, psb = self.ps, self.psb
    S, NB, NT = self.S, self.NB, self.NT
    lg = [math.log1p(-2.0 ** (-5.0 - h)) for h in range(4)]
    with ExitStack() as st:
        wq = self.tl(st, "wq", [128, 8, 64], BF16)
        wqs = self.tl(st, "wqs", [128, 8, 64], BF16)
        wk = self.tl(st, "wk", [128, 8, 64], BF16)
        wks = self.tl(st, "wks", [128, 8, 64], BF16)
        wv = self.tl(st, "wv", [128, 8, 128], BF16)
        wg = self.tl(st, "wg", [128, 8, 128], BF16)
        rot = self.tl(st, "rot", [64, 2, S], F32)
        qr = self.tl(st, "qr", [64, S], BF16)
        qi = self.tl(st, "qi", [64, S], BF16)
        kr = self.tl(st, "kr", [64, S], BF16)
        V = self.tl(st, "V", [128, NB, 128], BF16)
        gs = self.tl(st, "gs", [128, S], BF16)
        ta = self.tl(st, "ta", [64, 512], F32)
        tb = self.tl(st, "tb", [64, 512], F32)
        Sm = self.tl(st, "Sm", [128, 2, 128], BF16)
        kend = self.tl(st, "kend", [128, 2, 64], BF16)
        R = self.tl(st, "R", [64, 128], F32)
        Rb = self.tl(st, "Rb", [64, 128], BF16)
        ot = self.tl(st, "ot", [128, 512], F32)
        cen = self.tl(st, "cen", [128, 512], F32)
        sq = self.tl(st, "sq", [128, 512], F32)
        rs = self.tl(st, "rs", [128, 512], F32)
        yo = self.tl(st, "yo", [128, 2, 512], BF16)
        epst = self.tl(st, "epst", [128, 1], F32)
        (bwq, bwqs, bwk, bwks, bwv, bwg, brot, bqr, bqi, bkr, bv, bgs, bta, btb, bR, bRb, bot, bcen, bsq, brs) = [em.buf() for _ in range(20)]
        bSm = [em.buf(), em.buf()]
        bke = [em.buf(), em.buf()]
        byo = [em.buf(), em.buf()]
        em.dma("sp", rot[:], self.dram["rot"], writes=[brot])
        em.op("dve", lambda e: e.memset(epst[:], EPS), writes=[brs])
        psT = [ps[4].bitcast(BF16)]
        for h in range(4):
            self.load_wk(wq[:], l, O_BQ + h * 64, 64, bwq)
            self.load_wk(wqs[:, :, 0:32], l, O_BQ + h * 64 + 32, 32, bwqs)
            self.load_wk(wqs[:, :, 32:64], l, O_BQ + h * 64, 32, bwqs)
            self.load_wk(wk[:], l, O_BK + h * 64, 64, bwk)
            self.load_wk(wks[:, :, 0:32], l, O_BK + h * 64 + 32, 32, bwks)
            self.load_wk(wks[:, :, 32:64], l, O_BK + h * 64, 32, bwks)
            self.load_wk(wv[:], l, O_BV + h * 128, 128, bwv)
            self.load_wk(wg[:], l, O_BG + h * 128, 128, bwg)
            qdec = self.cf[0:64, CF_QDEC + h * 128:CF_QDEC + (h + 1) * 128]
            for (wa, wb_, ba, bb, dst, bdst) in ((wq, wqs, bwq, bwqs, qr, bqr), (wk, wks, bwk, bwks, kr, bkr)):
                for t in range(NT):
                    sl = slice(t * 512, (t + 1) * 512)
                    for (w_, bw_, p) in ((wa, ba, 6), (wb_, bb, 7)):
                        for k in range(8):
                            em.op("pe", lambda e, k=k, w_=w_, p=p, sl=sl: e.matmul(ps[p][0:64, :], lhsT=w_[:, k, :], rhs=self.hT[:, k, sl],
                                                                                 start=(k == 0), stop=(k == 7)), reads=[bw_, self.b_hT], writes=[psb[p]])
                    em.op("dve", lambda e, sl=sl: e.tensor_tensor(out=ta[:], in0=ps[6][0:64, :], in1=rot[:, 0, sl], op=ALU.mult), reads=[psb[6], brot], writes=[bta])
                    em.op("dve", lambda e, sl=sl: e.tensor_tensor(out=tb[:], in0=ps[7][0:64, :], in1=rot[:, 1, sl], op=ALU.mult), reads=[psb[7], brot], writes=[btb])
                    em.op("pool", lambda e, sl=sl, dst=dst: e.tensor_tensor(out=dst[:, sl], in0=ta[:], in1=tb[:], op=ALU.add), reads=[bta, btb], writes=[bdst])
                    if dst is qr:
                        em.op("pool", lambda e, sl=sl: e.tensor_tensor(out=qi[:, sl].rearrange("p (a b) -> p a b", a=4), in0=qr[:, sl].rearrange("p (a b) -> p a b", a=4),
                                                                       in1=_bcast_mid(qdec, 4), op=ALU.mult), reads=[bqr, self.b_const], writes=[bqi])
            self.proj_tm(wv, 128, bwv, lambda k, b: self.hT[:, k, b * 128:(b + 1) * 128], NB, [6, 7],
                         lambda b, p: em.op("act", lambda e: e.copy(out=V[:, b, :], in_=ps[p][:, 0:128]), reads=[psb[p]], writes=[bv]))
            self.proj_fm(wg, 128, bwg, [6, 7], lambda t, p: em.op("act", lambda e: e.activation(out=gs[:, t * 512:(t + 1) * 512], in_=ps[p][:], func=AF.Silu), reads=[psb[p]], writes=[bgs]))
            decT = self.cf[:, CF_DECT + h * 128:CF_DECT + (h + 1) * 128]
            cdec = math.exp(128.0 * lg[h])
            for n in range(NB):
                c0 = n * 128
                s = n % 2
                qs = (n % 4) * 128
                em.op("pe", lambda e, c0=c0: e.transpose(out=psT[0][:, 0:64], in_=kr[:, c0:c0 + 128], identity=self.ident_bf[0:64, 0:64]),
                      reads=[bkr, self.b_const], writes=[psb[4]])
                em.op("dve", lambda e, s=s: e.tensor_scalar(out=kend[:, s, :], in0=psT[0][:, 0:64], scalar1=self.cf[:, CF_KEND + h:CF_KEND + h + 1], scalar2=0.125,
                                                           op0=ALU.mult, op1=ALU.mult), reads=[psb[4], self.b_const], writes=[bke[s]])
                em.op("pe", lambda e, s=s, c0=c0: e.matmul(ps[s][:, 0:128], lhsT=kr[:, c0:c0 + 128], rhs=qr[:, c0:c0 + 128], start=True, stop=True),
                      reads=[bkr, bqr], writes=[psb[s]])
                em.op("dve", lambda e, s=s: e.scalar_tensor_tensor(out=Sm[:, s, :], in0=ps[s][:, 0:128], scalar=0.125, in1=decT, op0=ALU.mult, op1=ALU.mult),
                      reads=[psb[s], self.b_const], writes=[bSm[s]])
                em.op("pe", lambda e, s=s, n=n, qs=qs: e.matmul(ps[2][:, qs:qs + 128], lhsT=V[:, n, :], rhs=Sm[:, s, :], start=True, stop=(n == 0)),
                      reads=[bv, bSm[s]], writes=[psb[2]])
                if n > 0:
                    em.op("pe", lambda e, c0=c0, qs=qs: e.matmul(ps[2][:, qs:qs + 128], lhsT=Rb[:], rhs=qi[:, c0:c0 + 128], start=False, stop=True),
                          reads=[bRb, bqi], writes=[psb[2]])
                if n < NB - 1:
                    em.op("pe", lambda e, s=s, n=n: e.matmul(ps[3][0:64, 0:128], lhsT=kend[:, s, :], rhs=V[:, n, :], start=True, stop=True),
                          reads=[bke[s], bv], writes=[psb[3]])
                    if n == 0:
                        em.op("dve", lambda e: e.tensor_copy(out=R[:], in_=ps[3][0:64, 0:128]), reads=[psb[3]], writes=[bR])
                    else:
                        em.op("dve", lambda e: e.scalar_tensor_tensor(out=R[:], in0=R[:], scalar=cdec, in1=ps[3][0:64, 0:128], op0=ALU.mult, op1=ALU.add),
                              reads=[psb[3], bR], writes=[bR])
                    em.op("act", lambda e: e.copy(out=Rb[:], in_=R[:]), reads=[bR], writes=[bRb])
                if n % 4 == 3:
                    t = n // 4
                    sl = slice(t * 512, (t + 1) * 512)
                    em.op("act", lambda e: e.copy(out=ot[:], in_=ps[2][:]), reads=[psb[2]], writes=[bot])
                    em.op("pe", lambda e: e.matmul(ps[5][:], lhsT=self.ones_f[:], rhs=ot[:], start=True, stop=True), reads=[self.b_const, bot], writes=[psb[5]])
                    em.op("dve", lambda e: e.scalar_tensor_tensor(out=cen[:], in0=ps[5][:], scalar=-1.0 / 128, in1=ot[:], op0=ALU.mult, op1=ALU.add),
                          reads=[psb[5], bot], writes=[bcen])
                    em.op("act", lambda e: e.activation(out=sq[:], in_=cen[:], func=AF.Square), reads=[bcen], writes=[bsq])
                    em.op("pe", lambda e: e.matmul(ps[5][:], lhsT=self.ones_f[:], rhs=sq[:], start=True, stop=True), reads=[self.b_const, bsq], writes=[psb[5]])
                    em.op("act", lambda e: e.activation(out=rs[:], in_=ps[5][:], func=AF.Sqrt, scale=1.0 / 128, bias=epst[:, 0:1]), reads=[psb[5], brs], writes=[brs])
                    em.op("dve", lambda e: e.reciprocal(out=rs[:], in_=rs[:]), reads=[brs], writes=[brs])
                    em.op("dve", lambda e: e.scalar_tensor_tensor(out=cen[:], in0=cen[:], scalar=self.pvcol("ret_gn_g", l, h), in1=rs[:], op0=ALU.mult, op1=ALU.mult),
                          reads=[bcen, brs, self.b_const], writes=[bcen])
                    yi = t % 2
                    em.op("pool", lambda e, yi=yi, sl=sl: e.tensor_tensor(out=yo[:, yi, :], in0=cen[:], in1=gs[:, sl], op=ALU.mult), reads=[bcen, bgs], writes=[byo[yi]])
                    r0 = 512 + h * 128
                    em.dma("sp", self.dram["ybr"][r0:r0 + 128, sl], yo[:, yi, :], reads=[byo[yi]], writes=[self.b_ybr])
        em.barrier()


Builder.mixer_B = _mixer_B


def _bcast_last(ap2d, n):
    a = ap2d.ap
    return bass.AP(ap2d.tensor, ap2d.offset, [list(a[0]), list(a[1]), [0, n]])


def _mixer_D(self, l):
    em, nc = self.em, self.nc
    ps, psb = self.ps, self.psb
    S, NB, NT = self.S, self.NB, self.NT
    X = mybir.AxisListType.X
    bc = self.b_const
    with ExitStack() as st:
        x_tm = self.tl(st, "x_tm", [128, NB, 512], BF16)
        B_tm = self.tl(st, "B_tm", [128, NB, 256], BF16)
        BT = self.tl(st, "BT", [128, 2, S], BF16)
        CT = self.tl(st, "CT", [128, 2, S], BF16)
        bxtm, bBtm, bBT, bCT = [em.buf() for _ in range(4)]
        psT = ps[7].bitcast(BF16)
        with ExitStack() as s1:
            w = [self.tl(s1, "w", [128, 8, 128], BF16) for _ in range(2)]
            bw = [em.buf(), em.buf()]
            praw = self.tl(s1, "praw", [128, 3 + S], F32)
            cacc = self.tl(s1, "cacc", [128, 512], F32)
            cact = self.tl(s1, "cact", [128, 512], BF16)
            bpraw, bcacc, bcact = [em.buf() for _ in range(3)]
            em.op("dve", lambda e: e.memset(praw[:, 0:3], 0.0), writes=[bpraw])
            for c in range(8):
                ww, bww = w[c % 2], bw[c % 2]
                self.load_wk(ww[:], l, O_DX + c * 128, 128, bww)
                self.proj_fm(ww, 128, bww, [0, 1], lambda t, p: em.op("act", lambda e: e.copy(out=praw[:, 3 + t * 512:3 + (t + 1) * 512], in_=ps[p][:]),
                                                                      reads=[psb[p]], writes=[bpraw]))
                cw = PV_OFF[("conv_w", l)] + c * 4
                for t in range(NT):
                    t0 = t * 512
                    em.op("dve", lambda e, t0=t0: e.tensor_scalar(out=cacc[:], in0=praw[:, t0 + 3:t0 + 515], scalar1=self.pv[:, cw + 3:cw + 4], scalar2=None, op0=ALU.mult),
                          reads=[bpraw, bc], writes=[bcacc])
                    for k in range(3):
                        em.op("dve", lambda e, t0=t0, k=k: e.scalar_tensor_tensor(out=cacc[:], in0=praw[:, t0 + k:t0 + k + 512], scalar=self.pv[:, cw + k:cw + k + 1], in1=cacc[:],
                                                                                op0=ALU.mult, op1=ALU.add), reads=[bpraw, bcacc, bc], writes=[bcacc])
                    if c < 4:
                        dst, bd = cact[:], bcact
                    elif c < 6:
                        dst, bd = BT[:, c - 4, t0:t0 + 512], bBT
                    else:
                        dst, bd = CT[:, c - 6, t0:t0 + 512], bCT
                    em.op("act", lambda e, dst=dst: e.activation(out=dst, in_=cacc[:], func=AF.Silu, bias=self.pvcol("conv_b", l, c), scale=1.0),
                          reads=[bcacc, bc], writes=[bd])
                    if c < 6:
                        for q in range(4):
                            in_ap = cact[:, q * 128:(q + 1) * 128] if c < 4 else BT[:, c - 4, t0 + q * 128:t0 + (q + 1) * 128]
                            em.op("pe", lambda e, q=q, in_ap=in_ap: e.transpose(out=psT[:, q * 128:(q + 1) * 128], in_=in_ap, identity=self.ident_bf[:]),
                                  reads=[bd, bc], writes=[psb[7]])
                        if c < 4:
                            em.op("dve", lambda e, t=t, c=c: e.tensor_copy(out=x_tm[:, 4 * t:4 * t + 4, c * 128:(c + 1) * 128], in_=psT[:, 0:512].rearrange("p (a b) -> p a b", a=4)),
                                  reads=[psb[7]], writes=[bxtm])
                        else:
                            em.op("dve", lambda e, t=t, c=c: e.tensor_copy(out=B_tm[:, 4 * t:4 * t + 4, (c - 4) * 128:(c - 3) * 128], in_=psT[:, 0:512].rearrange("p (a b) -> p a b", a=4)),
                                  reads=[psb[7]], writes=[bBtm])
            em.barrier()
        with ExitStack() as s2:
            wz = self.tl(s2, "wz", [128, 8, 512], BF16)
            wdt = self.tl(s2, "wdt", [128, 8, 8], BF16)
            sm = self.tl(s2, "sm", [128, 64], F32)
            expA = self.tl(s2, "expA", [128, 8], F32)
            xdt = self.tl(s2, "xdt", [128, 512], BF16)
            xdd = self.tl(s2, "xdd", [128, 512], BF16)
            cbm = self.tl(s2, "cbm", [128, 256], BF16)
            lh = self.tl(s2, "lh", [128, 2, 128], F32)
            Lx = self.tl(s2, "Lx", [128, 2, 128], BF16)
            Mx = self.tl(s2, "Mx", [128, 2, 128], BF16)
            H = self.tl(s2, "H", [128, 512], F32)
            Hb = self.tl(s2, "Hb", [128, 512], BF16)
            t1 = self.tl(s2, "t1", [128, 512], F32)
            t2 = self.tl(s2, "t2", [128, 512], F32)
            zs = self.tl(s2, "zs", [128, 512], F32)
            yn = self.tl(s2, "yn", [128, 512], BF16)
            ysg = self.tl(s2, "ysg", [128, 4, 512], BF16)
            (bwz, bwdt, bsm, bexpA, bxdt, bxdd, bcbm, bH, bHb, bt1, bt2, bzs, byn, bysg) = [em.buf() for _ in range(14)]
            blh, bLx, bMx = [em.buf(), em.buf()], [em.buf(), em.buf()], [em.buf(), em.buf()]
            for k in range(8):
                self.load_w(wz[:, k, :], self.dram["w_in"][l, k * 128:(k + 1) * 128, O_DZ:O_DZ + 512], 128, 512, bwz)
            self.load_wk(wdt[:], l, O_DDT, 8, bwdt)
            oA, oB, oD, oG = RV_OFF[("A_log", l)], RV_OFF[("dt_bias", l)], RV_OFF[("D", l)], RV_OFF[("ssm_norm_g", l)]
            em.op("act", lambda e: e.activation(out=expA[:], in_=self.rv[:, oA:oA + 8], func=AF.Exp), reads=[bc], writes=[bexpA])
            triU_f = self.cf[:, CF_TRIU:CF_TRIU + 128]
            SL_f = self.cf[:, CF_SL:CF_SL + 128]
            for n in range(NB):
                c0 = n * 128
                for k in range(8):
                    em.op("pe", lambda e, k=k, c0=c0: e.matmul(ps[6][:, 0:8], lhsT=self.hT[:, k, c0:c0 + 128], rhs=wdt[:, k, :], start=(k == 0), stop=(k == 7)),
                          reads=[bwdt, self.b_hT], writes=[psb[6]])
                em.op("dve", lambda e: e.tensor_tensor(out=sm[:, 0:8], in0=ps[6][:, 0:8], in1=self.rv[:, oB:oB + 8], op=ALU.add), reads=[psb[6], bc], writes=[bsm])
                em.op("act", lambda e: e.activation(out=sm[:, 0:8], in_=sm[:, 0:8], func=AF.Exp), reads=[bsm], writes=[bsm])
                em.op("act", lambda e: e.activation(out=sm[:, 0:8], in_=sm[:, 0:8], func=AF.Ln, bias=self.ones_f[:, 0:1], scale=1.0), reads=[bsm, bc], writes=[bsm])
                em.op("dve", lambda e: e.scalar_tensor_tensor(out=sm[:, 8:16], in0=sm[:, 0:8], scalar=-1.0, in1=expA[:], op0=ALU.mult, op1=ALU.mult),
                      reads=[bsm, bexpA], writes=[bsm])
                em.op("pe", lambda e: e.matmul(ps[2][:, 256:264], lhsT=triU_f, rhs=sm[:, 8:16], start=True, stop=True), reads=[bc, bsm], writes=[psb[2]])
                em.op("pe", lambda e: e.matmul(ps[2][:, 264:272], lhsT=self.ones_f[:], rhs=sm[:, 8:16], start=True, stop=True), reads=[bc, bsm], writes=[psb[2]])
                em.op("dve", lambda e: e.tensor_copy(out=sm[:, 16:32], in_=ps[2][:, 256:272]), reads=[psb[2]], writes=[bsm])
                em.op("dve", lambda e: e.tensor_tensor(out=sm[:, 40:48], in0=sm[:, 24:32], in1=sm[:, 16:24], op=ALU.subtract), reads=[bsm], writes=[bsm])
                em.op("act", lambda e: e.activation(out=sm[:, 32:40], in_=sm[:, 16:24], func=AF.Exp), reads=[bsm], writes=[bsm])
                em.op("act", lambda e: e.activation(out=sm[:, 40:48], in_=sm[:, 40:48], func=AF.Exp), reads=[bsm], writes=[bsm])
                em.op("act", lambda e: e.activation(out=sm[:, 48:56], in_=sm[:, 24:32], func=AF.Exp), reads=[bsm], writes=[bsm])
                xv = x_tm[:, n, :].rearrange("p (h d) -> p h d", h=8)
                em.op("dve", lambda e, xv=xv: e.tensor_tensor(out=xdt[:].rearrange("p (h d) -> p h d", h=8), in0=xv, in1=_bcast_last(sm[:, 0:8], 64), op=ALU.mult),
                      reads=[bxtm, bsm], writes=[bxdt])
                em.op("pool", lambda e: e.tensor_tensor(out=xdd[:].rearrange("p (h d) -> p h d", h=8), in0=xdt[:].rearrange("p (h d) -> p h d", h=8),
                                                        in1=_bcast_last(sm[:, 40:48], 64), op=ALU.mult), reads=[bxdt, bsm], writes=[bxdd])
                for g in range(2):
                    em.op("pe", lambda e, g=g, c0=c0: e.matmul(ps[2][:, g * 128:(g + 1) * 128], lhsT=BT[:, g, c0:c0 + 128], rhs=CT[:, g, c0:c0 + 128], start=True, stop=True),
                          reads=[bBT, bCT], writes=[psb[2]])
                em.op("dve", lambda e: e.tensor_tensor(out=cbm[:].rearrange("p (g i) -> p g i", g=2), in0=ps[2][:, 0:256].rearrange("p (g i) -> p g i", g=2),
                                                       in1=_bcast_mid(triU_f, 2), op=ALU.mult), reads=[psb[2], bc], writes=[bcbm])
                if n > 0:
                    for g in range(2):
                        em.op("pe", lambda e, g=g, c0=c0: e.matmul(ps[4][:, g * 256:(g + 1) * 256], lhsT=CT[:, g, c0:c0 + 128], rhs=Hb[:, g * 256:(g + 1) * 256], start=True, stop=True),
                              reads=[bCT, bHb], writes=[psb[4]])
                def d_stage1(h):
                    s = h % 2
                    g = h // 4
                    em.op("dve", lambda e: e.tensor_scalar(out=lh[:, s, :], in0=SL_f, scalar1=sm[:, 8 + h:9 + h], scalar2=None, op0=ALU.mult),
                          reads=[bc, bsm], writes=[blh[s]])
                    em.op("pe", lambda e: e.matmul(ps[s][:, 0:128], lhsT=lh[:, s, :], rhs=triU_f, start=True, stop=True), reads=[blh[s], bc], writes=[psb[s]])
                    em.op("act", lambda e: e.activation(out=Lx[:, s, :], in_=ps[s][:, 0:128], func=AF.Exp), reads=[psb[s]], writes=[bLx[s]])
                    em.op("pool", lambda e: e.tensor_tensor(out=Mx[:, s, :], in0=Lx[:, s, :], in1=cbm[:, g * 128:(g + 1) * 128], op=ALU.mult),
                          reads=[bLx[s], bcbm], writes=[bMx[s]])

                def d_stage2(h):
                    s = h % 2
                    em.op("pe", lambda e: e.matmul(ps[3][:, h * 64:(h + 1) * 64], lhsT=Mx[:, s, :], rhs=xdt[:, h * 64:(h + 1) * 64], start=True, stop=True),
                          reads=[bMx[s], bxdt], writes=[psb[3]])

                d_stage1(0)
                for h in range(8):
                    if h + 1 < 8:
                        d_stage1(h + 1)
                    d_stage2(h)
                if n > 0:
                    em.op("dve", lambda e: e.tensor_tensor(out=t1[:].rearrange("p (h d) -> p h d", h=8), in0=ps[4][:].rearrange("p (h d) -> p h d", h=8),
                                                           in1=_bcast_last(sm[:, 32:40], 64), op=ALU.mult), reads=[psb[4], bsm], writes=[bt1])
                    em.op("dve", lambda e: e.tensor_tensor(out=t1[:], in0=ps[3][:], in1=t1[:], op=ALU.add), reads=[psb[3], bt1], writes=[bt1])
                else:
                    em.op("dve", lambda e: e.tensor_copy(out=t1[:], in_=ps[3][:]), reads=[psb[3]], writes=[bt1])
                em.op("pool", lambda e, xv=xv: e.tensor_tensor(out=t2[:].rearrange("p (h d) -> p h d", h=8), in0=xv, in1=_bcast_last(self.rv[:, oD:oD + 8], 64), op=ALU.mult),
                      reads=[bxtm, bc], writes=[bt2])
                em.op("pool", lambda e: e.tensor_tensor(out=t1[:], in0=t1[:], in1=t2[:], op=ALU.add), reads=[bt1, bt2], writes=[bt1])
                if n < NB - 1:
                    for g in range(2):
                        em.op("pe", lambda e, g=g, n=n: e.matmul(ps[5][:, g * 256:(g + 1) * 256], lhsT=B_tm[:, n, g * 128:(g + 1) * 128], rhs=xdd[:, g * 256:(g + 1) * 256], start=True, stop=True),
                              reads=[bBtm, bxdd], writes=[psb[5]])
                    if n == 0:
                        em.op("dve", lambda e: e.tensor_copy(out=H[:], in_=ps[5][:]), reads=[psb[5]], writes=[bH])
                    else:
                        em.op("pool", lambda e: e.tensor_tensor(out=H[:].rearrange("p (h d) -> p h d", h=8), in0=H[:].rearrange("p (h d) -> p h d", h=8),
                                                                in1=_bcast_last(sm[:, 48:56], 64), op=ALU.mult), reads=[bH, bsm], writes=[bH])
                        em.op("dve", lambda e: e.tensor_tensor(out=H[:], in0=ps[5][:], in1=H[:], op=ALU.add), reads=[psb[5], bH], writes=[bH])
                    em.op("act", lambda e: e.copy(out=Hb[:], in_=H[:]), reads=[bH], writes=[bHb])
                for k in range(8):
                    em.op("pe", lambda e, k=k, c0=c0: e.matmul(ps[6][:], lhsT=self.hT[:, k, c0:c0 + 128], rhs=wz[:, k, :], start=(k == 0), stop=(k == 7)),
                          reads=[bwz, self.b_hT], writes=[psb[6]])
                em.op("act", lambda e: e.activation(out=zs[:], in_=ps[6][:], func=AF.Silu), reads=[psb[6]], writes=[bzs])
                em.op("dve", lambda e: e.tensor_tensor(out=t1[:], in0=t1[:], in1=zs[:], op=ALU.mult), reads=[bt1, bzs], writes=[bt1])
                em.op("act", lambda e: e.activation(out=t2[:], in_=t1[:], func=AF.Square), reads=[bt1], writes=[bt2])
                em.op("dve", lambda e: e.reduce_sum(out=sm[:, 56:57], in_=t2[:], axis=X), reads=[bt2], writes=[bsm])
                em.op("act", lambda e: e.activation(out=sm[:, 57:58], in_=sm[:, 56:57], func=AF.Sqrt, scale=1.0 / 512, bias=self.eps_c[:, 0:1]), reads=[bsm, bc], writes=[bsm])
                em.op("dve", lambda e: e.reciprocal(out=sm[:, 57:58], in_=sm[:, 57:58]), reads=[bsm], writes=[bsm])
                em.op("dve", lambda e: e.scalar_tensor_tensor(out=yn[:], in0=t1[:], scalar=sm[:, 57:58], in1=self.rv[:, oG:oG + 512], op0=ALU.mult, op1=ALU.mult),
                      reads=[bt1, bsm, bc], writes=[byn])
                qn = n % 4
                for c in range(4):
                    em.op("pe", lambda e, c=c: e.transpose(out=psT[:, c * 128:(c + 1) * 128], in_=yn[:, c * 128:(c + 1) * 128], identity=self.ident_bf[:]),
                          reads=[byn, bc], writes=[psb[7]])
                em.op("act", lambda e, qn=qn: e.copy(out=ysg[:, :, qn * 128:(qn + 1) * 128], in_=psT[:, 0:512].rearrange("p (a b) -> p a b", a=4)), reads=[psb[7]], writes=[bysg])
                if qn == 3:
                    t = n // 4
                    em.dma("sp", self.dram["ybr"][1536:2048, t * 512:(t + 1) * 512].rearrange("(c p) s -> p c s", p=128), ysg[:], reads=[bysg], writes=[self.b_ybr])
            em.barrier()


Builder.mixer_D = _mixer_D


def build_program(S=4096):
    b = Builder(S)
    em = b.em
    b.phase_copy_in()
    b.b_ybr = em.buf("ybr")
    b.b_gat = em.buf("gat")
    for l in range(DEPTH):
        with ExitStack() as st:
            b.mix_begin(l, st)
            b.gates(l)
            b.mixer_C(l)
            b.mixer_A(l)
            b.mixer_B(l)
            b.mixer_D(l)
        em.barrier()
        b.phase_merge(l)
        b.phase_xattn(l)
        b.phase_ffn(l)
    toks = b.phase_final()
    em.finish(toks)
    return b


_CACHE = {}


def kernel(**inputs):
    S = 4096
    if "b" not in _CACHE:
        _CACHE["b"] = build_program(S)
    b = _CACHE["b"]
    in_maps = [make_in_map(inputs, c % 4, S) for c in range(8)]
    res = run_bass_kernel_spmd(b.nc, in_maps, core_ids=list(range(8)))
    out = np.stack([np.asarray(res.results[c]["out"]).T for c in range(4)], axis=0)
    return np.ascontiguousarray(out.astype(np.float32))
```

```python
import math
from contextlib import ExitStack

import numpy as np
import concourse.bass as bass
import concourse.mybir as mybir
from concourse.bass_utils import run_bass_kernel_spmd

F32 = mybir.dt.float32
BF16 = mybir.dt.bfloat16
AF = mybir.ActivationFunctionType
ALU = mybir.AluOpType

D = 1024
DC = 8
DEPTH = 2
MEM = 256
EPS = 1e-6
FFN = 2816
FC = 22
IN_W = 10248
O_AQ, O_AK, O_AV = 0, 512, 1024
O_BQ, O_BK, O_BV, O_BG = 1536, 1792, 2048, 2560
O_CQ, O_CK, O_CV = 3072, 3584, 4096
O_DZ, O_DX, O_DDT, O_G = 4608, 5120, 6144, 6152


class Buf:
    __slots__ = ("name", "w", "r")

    def __init__(self, name):
        self.name = name
        self.w = None
        self.r = []


class Em:
    NRING = 24

    def __init__(self, nc):
        self.nc = nc
        self.eng = {"pe": nc.tensor, "act": nc.scalar, "dve": nc.vector, "pool": nc.gpsimd, "sp": nc.sync}
        self.sem = {}
        self.cnt = {}
        for e in ("pe", "act", "dve", "pool"):
            self.sem[e] = nc.alloc_semaphore("s_" + e)
            self.cnt[e] = 0
        self.known = {e: {} for e in self.eng}
        self.ring = {}
        self.ring_use = {}
        self.ring_next = {}
        for q in ("sp", "pool", "act"):
            self.ring[q] = [nc.alloc_semaphore("d_%s_%d" % (q, i)) for i in range(self.NRING)]
            self.ring_use[q] = [0] * self.NRING
            self.ring_next[q] = 0
        self.dma_tokens = []
        self.nbuf = 0

    def buf(self, name=None):
        self.nbuf += 1
        return Buf(name or ("b%d" % self.nbuf))

    def _wait(self, engine, tok):
        if tok is None:
            return
        key, sem, val = tok
        if key == engine and engine == "pe":
            return
        kn = self.known[engine]
        if kn.get(key, 0) >= val:
            return
        self.eng[engine].wait_ge(sem, val)
        kn[key] = val

    def _deps(self, engine, reads, writes):
        for b in reads:
            self._wait(engine, b.w)
        for b in writes:
            self._wait(engine, b.w)
            for t in b.r:
                self._wait(engine, t)

    def _commit(self, tok, reads, writes):
        for b in reads:
            b.r.append(tok)
            if len(b.r) > 12:
                last = {}
                for t in b.r:
                    if t[0] not in last or last[t[0]][2] < t[2]:
                        last[t[0]] = t
                b.r = list(last.values())
        for b in writes:
            b.w = tok
            b.r = []

    def op(self, engine, fn, reads=(), writes=()):
        self._deps(engine, reads, writes)
        inst = fn(self.eng[engine])
        self.cnt[engine] += 1
        inst.then_inc(self.sem[engine], 1)
        tok = (engine, self.sem[engine], self.cnt[engine])
        self._commit(tok, reads, writes)
        return tok

    def dma(self, q, out, in_, reads=(), writes=()):
        i = self.ring_next[q]
        self.ring_next[q] = (i + 1) % self.NRING
        sem = self.ring[q][i]
        key = ("dma", q, i)
        prior = 16 * self.ring_use[q][i]
        if prior:
            self._wait(q, (key, sem, prior))
        self._deps(q, reads, writes)
        self.eng[q].dma_start(out=out, in_=in_).then_inc(sem, 16)
        self.ring_use[q][i] += 1
        tok = (key, sem, 16 * self.ring_use[q][i])
        self._commit(tok, reads, writes)
        self.dma_tokens.append(tok)
        if len(self.dma_tokens) > 3 * self.NRING:
            self.dma_tokens = self.dma_tokens[-3 * self.NRING:]
        return tok

    def barrier(self):
        toks = [(e, self.sem[e], self.cnt[e]) for e in ("pe", "act", "dve", "pool") if self.cnt[e]]
        for q in self.ring:
            for i in range(self.NRING):
                if self.ring_use[q][i]:
                    toks.append((("dma", q, i), self.ring[q][i], 16 * self.ring_use[q][i]))
        for e in self.eng:
            for t in toks:
                if t[0] == e and e == "pe":
                    continue
                self._wait(e, t)

    def finish(self, toks):
        for t in toks:
            self._wait("sp", t)


def _ap(t):
    return t if isinstance(t, bass.AP) else t.ap()


class Ctx:
    def __init__(self, S, depth=DEPTH):
        self.S = S
        self.NT = S // 512
        self.NB = S // 128
        self.depth = depth
        nc = self.nc = bass.Bass("TRN2", target_bir_lowering=False)
        self.em = Em(nc)
        self.es = ExitStack()
        dr = self.dram = {}

        def din(name, shape, dt=F32):
            dr[name] = nc.dram_tensor(name, list(shape), dt, kind="ExternalInput").ap()

        din("xT", [D, S])
        din("memT", [D, MEM])
        din("w_in", [depth, D, IN_W])
        din("w_branch", [depth, 2048, D])
        din("w_mix_out", [depth, D, D])
        din("w_xq", [depth, D, 512])
        din("w_xkv", [depth, D, 1024])
        din("w_xo", [depth, 512, D])
        din("w_ffn_in", [depth, D, 2 * FFN])
        din("w_ffn_out", [depth, FFN, D])
        din("pvec", [128, PV_N])
        din("rvec", [128, RV_N])
        din("cst_f32", [128, CF_N])
        din("rot", [64, 2, S])
        dr["out"] = nc.dram_tensor("out", [D, S], F32, kind="ExternalOutput").ap()
        dr["xs"] = nc.dram_tensor("xs", [D, S], F32, kind="Internal").ap()
        dr["ybr"] = nc.dram_tensor("ybr", [2048, S], BF16, kind="Internal").ap()
        dr["gat"] = nc.dram_tensor("gat", [4096, S], BF16, kind="Internal").ap()

    def sb(self, name, shape, dt):
        return self.es.enter_context(self.nc.sbuf_tensor(name, list(shape), dt))

    _uid = 0

    def tl(self, st, name, shape, dt):
        Ctx._uid += 1
        return st.enter_context(self.nc.sbuf_tensor("%s_%d" % (name, Ctx._uid), list(shape), dt))


def _pv_layout():
    off = {}
    n = 0
    for l in range(DEPTH):
        for nm, w in (("norm_mix_g", 8), ("norm_x_g", 8), ("norm_mem_g", 8), ("norm_ffn_g", 8),
                      ("ret_gn_g", 4), ("diff_subln_g", 1), ("conv_w", 32), ("conv_b", 8)):
            off[(nm, l)] = n
            n += w
    off[("norm_f_g", 0)] = n
    n += 8
    return off, n


PV_OFF, PV_N = _pv_layout()


def _rv_layout():
    off = {}
    n = 0
    for l in range(DEPTH):
        for nm, w in (("dt_bias", 8), ("A_log", 8), ("D", 8), ("ssm_norm_g", 512), ("diff_lambda", 256)):
            off[(nm, l)] = n
            n += w
    return off, n


RV_OFF, RV_N = _rv_layout()

CF_IDENT, CF_TRIU, CF_TRIL, CF_DECT, CF_KEND, CF_QDEC, CF_SL = 0, 128, 256, 384, 896, 900, 1412
CF_N = 1540


class Builder(Ctx):
    def __init__(self, S, depth=DEPTH):
        super().__init__(S, depth)
        nc, em = self.nc, self.em
        self.pv = self.sb("pv", [128, PV_N], F32)
        self.rv = self.sb("rv", [128, RV_N], F32)
        self.cf = self.sb("cf", [128, CF_N], F32)
        self.ones_bf = self.sb("ones_bf", [128, 128], BF16)
        self.ones_f = self.sb("ones_f", [128, 128], F32)
        self.ident_bf = self.sb("ident_bf", [128, 128], BF16)
        self.triU_bf = self.sb("triU_bf", [128, 128], BF16)
        self.triL_bf = self.sb("triL_bf", [128, 128], BF16)
        self.b_const = em.buf("const")
        self.ps = []
        self.psb = []
        for i in range(8):
            self.ps.append(self.es.enter_context(nc.psum_tensor("ps%d" % i, [128, 512], F32)))
            self.psb.append(em.buf("ps%d" % i))
        self.stg = [self.sb("stg%d" % i, [128, 1024], F32) for i in range(2)]
        self.stgb = [em.buf("stg%d" % i) for i in range(2)]
        self.stg_i = 0
        d = self.dram
        bc = self.b_const
        em.dma("sp", self.pv[:], d["pvec"], writes=[bc])
        em.dma("sp", self.rv[:], d["rvec"], writes=[bc])
        em.dma("sp", self.cf[:], d["cst_f32"], writes=[bc])
        em.op("dve", lambda e: e.memset(self.ones_bf[:], 1.0), writes=[bc])
        em.op("dve", lambda e: e.memset(self.ones_f[:], 1.0), writes=[bc])
        self.eps_c = self.sb("eps_c", [128, 1], F32)
        em.op("dve", lambda e: e.memset(self.eps_c[:], EPS), writes=[bc])
        em.op("dve", lambda e: e.tensor_copy(out=self.ident_bf[:], in_=self.cf[:, CF_IDENT:CF_IDENT + 128]), reads=[bc], writes=[bc])
        em.op("dve", lambda e: e.tensor_copy(out=self.triU_bf[:], in_=self.cf[:, CF_TRIU:CF_TRIU + 128]), reads=[bc], writes=[bc])
        em.op("dve", lambda e: e.tensor_copy(out=self.triL_bf[:], in_=self.cf[:, CF_TRIL:CF_TRIL + 128]), reads=[bc], writes=[bc])
        em.barrier()

    def pvcol(self, name, l, j=0):
        o = PV_OFF[(name, l)] + j
        return self.pv[:, o:o + 1]

    def load_w(self, dst, src, nrows, ncols, wbuf):
        em = self.em
        c0 = 0
        while c0 < ncols:
            w = min(1024, ncols - c0)
            i = self.stg_i
            self.stg_i = (i + 1) % 2
            st, sbf = self.stg[i], self.stgb[i]
            em.dma("sp", st[0:nrows, 0:w], src[:, c0:c0 + w], writes=[sbf])
            o, ww, cc = dst, w, c0
            em.op("pool", lambda e, st=st, o=o, ww=ww, cc=cc: e.tensor_copy(out=o[:, cc:cc + ww], in_=st[0:nrows, 0:ww]),
                  reads=[sbf], writes=[wbuf])
            c0 += w

    def norm_tile(self, xt, ht, gname, l, sq, xb, hb, width=512, pi=0):
        em = self.em
        ps, psb = self.ps[pi], self.psb[pi]
        bsq = self.b_sq
        for c in range(DC):
            em.op("act", lambda e, c=c: e.activation(out=sq[:, c % 2, 0:width], in_=xt[:, c, 0:width], func=AF.Square),
                  reads=[xb], writes=[bsq[c % 2]])
            em.op("pe", lambda e, c=c: e.matmul(ps[:, 0:width], lhsT=self.ones_bf[:], rhs=sq[:, c % 2, 0:width],
                                                start=(c == 0), stop=(c == DC - 1)),
                  reads=[bsq[c % 2], self.b_const], writes=[psb])
        rs = self.rstd
        em.op("act", lambda e: e.activation(out=rs[:, 0:width], in_=ps[:, 0:width], func=AF.Sqrt, scale=1.0 / D, bias=self.eps_t[:, 0:1]),
              reads=[psb, self.b_const], writes=[self.b_rstd])
        em.op("dve", lambda e: e.reciprocal(out=rs[:, 0:width], in_=rs[:, 0:width]), reads=[self.b_rstd], writes=[self.b_rstd])
        for c in range(DC):
            em.op("dve", lambda e, c=c: e.scalar_tensor_tensor(out=ht[:, c, 0:width], in0=xt[:, c, 0:width],
                                                                scalar=self.pvcol(gname, l, c), in1=rs[:, 0:width],
                                                                op0=ALU.mult, op1=ALU.mult),
                  reads=[xb, self.b_rstd, self.b_const], writes=[hb])

    def alloc_norm_scratch(self, st):
        em = self.em
        self.sqt = self.tl(st, "sqt", [128, 2, 512], BF16)
        self.rstd = self.tl(st, "rstd", [128, 512], F32)
        self.eps_t = self.tl(st, "eps_t", [128, 1], F32)
        self.b_sq = [em.buf("sq0"), em.buf("sq1")]
        self.b_rstd = em.buf("rstd")
        em.op("dve", lambda e: e.memset(self.eps_t[:], EPS), writes=[self.b_const])

    def xtile_ap(self, name, t, width=512):
        return self.dram[name].rearrange("(c p) s -> p c s", p=128)[:, :, t * width:(t + 1) * width]

    def phase_copy_in(self):
        em = self.em
        b = self.b_xs = em.buf("xs")
        for c in range(DC):
            em.dma("sp", self.dram["xs"][c * 128:(c + 1) * 128, :], self.dram["xT"][c * 128:(c + 1) * 128, :], writes=[b])
        em.barrier()

    def phase_ffn(self, l):
        em, nc = self.em, self.nc
        with ExitStack() as st:
            self.alloc_norm_scratch(st)
            w1 = self.tl(st, "w1", [128, 8, 2 * FFN], BF16)
            w2 = self.tl(st, "w2", [128, FC, D], BF16)
            xt1 = self.tl(st, "xt0", [128, 8, 512], F32)
            xt = [xt1, xt1]
            ht = self.tl(st, "ht", [128, 8, 512], BF16)
            u = self.tl(st, "u", [128, FC, 512], BF16)
            sa = self.tl(st, "sa", [128, 2, 512], BF16)
            bw1 = [em.buf() for _ in range(8)]
            bw2 = [em.buf() for _ in range(FC)]
            bx0 = em.buf()
            bx = [bx0, bx0]
            bh, bu, bsa = em.buf(), em.buf(), [em.buf(), em.buf()]
            for k in range(8):
                self.load_w(w1[:, k, :], self.dram["w_ffn_in"][l, k * 128:(k + 1) * 128, :], 128, 2 * FFN, bw1[k])
            for f in range(FC):
                self.load_w(w2[:, f, :], self.dram["w_ffn_out"][l, f * 128:(f + 1) * 128, :], 128, D, bw2[f])
            for t in range(self.NT):
                cur = 0
                em.dma("sp", xt[cur][:], self.xtile_ap("xs", t), reads=[self.b_xs], writes=[bx[cur]])
                self.norm_tile(xt[cur], ht, "norm_ffn_g", l, self.sqt, bx[cur], bh)
                for f in range(FC):
                    pa, pb = 1 + (f % 2) * 2, 2 + (f % 2) * 2
                    for k in range(8):
                        em.op("pe", lambda e, k=k, f=f, pa=pa: e.matmul(self.ps[pa][:], lhsT=w1[:, k, f * 128:(f + 1) * 128], rhs=ht[:, k, :],
                                                                    start=(k == 0), stop=(k == 7)),
                              reads=[bw1[k], bh], writes=[self.psb[pa]])
                    for k in range(8):
                        em.op("pe", lambda e, k=k, f=f, pb=pb: e.matmul(self.ps[pb][:], lhsT=w1[:, k, FFN + f * 128:FFN + (f + 1) * 128], rhs=ht[:, k, :],
                                                                    start=(k == 0), stop=(k == 7)),
                              reads=[bw1[k], bh], writes=[self.psb[pb]])
                    si = f % 2
                    em.op("act", lambda e, pa=pa, si=si: e.activation(out=sa[:, si, :], in_=self.ps[pa][:], func=AF.Silu),
                          reads=[self.psb[pa]], writes=[bsa[si]])
                    em.op("dve", lambda e, pb=pb, si=si, f=f: e.tensor_tensor(out=u[:, f, :], in0=self.ps[pb][:], in1=sa[:, si, :], op=ALU.mult),
                          reads=[self.psb[pb], bsa[si]], writes=[bu])
                for c in range(DC):
                    po = 5 + (c % 2)
                    for f in range(FC):
                        em.op("pe", lambda e, c=c, f=f, po=po: e.matmul(self.ps[po][:], lhsT=w2[:, f, c * 128:(c + 1) * 128], rhs=u[:, f, :],
                                                                    start=(f == 0), stop=(f == FC - 1)),
                              reads=[bw2[f], bu], writes=[self.psb[po]])
                    em.op("dve", lambda e, c=c, po=po, cur=cur: e.tensor_tensor(out=xt[cur][:, c, :], in0=self.ps[po][:], in1=xt[cur][:, c, :], op=ALU.add),
                          reads=[self.psb[po], bx[cur]], writes=[bx[cur]])
                em.dma("sp", self.xtile_ap("xs", t), xt[cur][:], reads=[bx[cur]], writes=[self.b_xs])
            em.barrier()

    def phase_final(self):
        em, nc = self.em, self.nc
        toks = []
        with ExitStack() as st:
            self.alloc_norm_scratch(st)
            xt = self.tl(st, "xt", [128, 8, 512], F32)
            ot = self.tl(st, "ot", [128, 8, 512], F32)
            bx, bo = em.buf(), em.buf()
            b_out = em.buf("out")
            for t in range(self.NT):
                em.dma("sp", xt[:], self.xtile_ap("xs", t), reads=[self.b_xs], writes=[bx])
                self.norm_tile(xt, ot, "norm_f_g", 0, self.sqt, bx, bo)
                toks.append(em.dma("sp", self.xtile_ap("out", t), ot[:], reads=[bo], writes=[b_out]))
            em.barrier()
        return toks


def _chunked(v, nch):
    return np.ascontiguousarray(np.asarray(v, np.float32).reshape(nch, 128).T)


def make_tables(inp, S):
    pv = np.zeros((128, PV_N), np.float32)
    rv = np.zeros((128, RV_N), np.float32)
    for l in range(DEPTH):
        pv[:, PV_OFF[("norm_mix_g", l)]:][:, :8] = _chunked(inp["norm_mix_g"][l], 8)
        pv[:, PV_OFF[("norm_x_g", l)]:][:, :8] = _chunked(inp["norm_x_g"][l], 8)
        pv[:, PV_OFF[("norm_mem_g", l)]:][:, :8] = _chunked(inp["norm_mem_g"][l], 8)
        pv[:, PV_OFF[("norm_ffn_g", l)]:][:, :8] = _chunked(inp["norm_ffn_g"][l], 8)
        pv[:, PV_OFF[("ret_gn_g", l)]:][:, :4] = _chunked(inp["ret_gn_g"][l], 4)
        pv[:, PV_OFF[("diff_subln_g", l)]:][:, :1] = _chunked(inp["diff_subln_g"][l], 1)
        cw = np.asarray(inp["ssm_conv_w"][l], np.float32)
        o = PV_OFF[("conv_w", l)]
        for c in range(8):
            for k in range(4):
                pv[:, o + c * 4 + k] = cw[k, c * 128:(c + 1) * 128]
        pv[:, PV_OFF[("conv_b", l)]:][:, :8] = _chunked(inp["ssm_conv_b"][l], 8)
        for nm, key, w in (("dt_bias", "ssm_dt_bias", 8), ("A_log", "ssm_A_log", 8), ("D", "ssm_D", 8),
                           ("ssm_norm_g", "ssm_norm_g", 512)):
            o = RV_OFF[(nm, l)]
            rv[:, o:o + w] = np.asarray(inp[key][l], np.float32).reshape(1, w)
        o = RV_OFF[("diff_lambda", l)]
        rv[:, o:o + 256] = np.asarray(inp["diff_lambda"][l], np.float32).reshape(1, 256)
    pv[:, PV_OFF[("norm_f_g", 0)]:][:, :8] = _chunked(inp["norm_f_g"], 8)
    cf = np.zeros((128, CF_N), np.float32)
    idx = np.arange(128)
    cf[:, CF_IDENT:CF_IDENT + 128] = np.eye(128, dtype=np.float32)
    cf[:, CF_TRIU:CF_TRIU + 128] = (idx[:, None] <= idx[None, :])
    cf[:, CF_TRIL:CF_TRIL + 128] = (idx[:, None] >= idx[None, :])
    cf[:, CF_SL:CF_SL + 128] = (idx[:, None] > idx[None, :])
    lg = np.log1p(-np.exp2(-5.0 - np.arange(4, dtype=np.float32))).astype(np.float32)
    for h in range(4):
        rel = (idx[None, :] - idx[:, None]).astype(np.float32)
        dec = np.where(rel >= 0, np.exp(lg[h] * np.maximum(rel, 0.0)), 0.0)
        cf[:, CF_DECT + h * 128:CF_DECT + (h + 1) * 128] = dec
        cf[:, CF_KEND + h] = np.exp((127 - idx) * lg[h])
        cf[:, CF_QDEC + h * 128:CF_QDEC + (h + 1) * 128] = np.exp((idx + 1.0) * lg[h])[None, :]
    half = 32
    inv_freq = (10000.0 ** (-np.arange(half, dtype=np.float32) / half)).astype(np.float32)
    pos = np.arange(S, dtype=np.float32)
    ang = (pos[None, :] * inv_freq[:, None]).astype(np.float32)
    cos, sin = np.cos(ang).astype(np.float32), np.sin(ang).astype(np.float32)
    rot = np.zeros((64, 2, S), np.float32)
    rot[:32, 0] = cos
    rot[32:, 0] = cos
    rot[:32, 1] = -sin
    rot[32:, 1] = sin
    return pv, rv, cf, rot


def make_in_map(inp, b, S):
    pv, rv, cf, rot = make_tables(inp, S)
    f = lambda a: np.ascontiguousarray(np.asarray(a, np.float32))
    return {
        "xT": f(np.asarray(inp["x"][b]).T[:, :S]),
        "memT": f(np.asarray(inp["mem"][b]).T),
        "w_in": f(inp["w_in"]),
        "w_branch": f(np.asarray(inp["w_branch"]).reshape(DEPTH, 2048, D)),
        "w_mix_out": f(inp["w_mix_out"]),
        "w_xq": f(inp["w_xq"]),
        "w_xkv": f(inp["w_xkv"]),
        "w_xo": f(inp["w_xo"]),
        "w_ffn_in": f(inp["w_ffn_in"]),
        "w_ffn_out": f(inp["w_ffn_out"]),
        "pvec": pv, "rvec": rv, "cst_f32": cf, "rot": rot,
    }


def _phase_xattn(self, l):
    em, nc = self.em, self.nc
    ps, psb = self.ps, self.psb
    with ExitStack() as st:
        self.alloc_norm_scratch(st)
        wq = self.tl(st, "wq", [128, 8, 512], BF16)
        wkv = self.tl(st, "wkv", [128, 8, 1024], BF16)
        wo = self.tl(st, "wo", [128, 4, D], BF16)
        mt = self.tl(st, "mt", [128, 8, MEM], F32)
        mh = self.tl(st, "mh", [128, 8, MEM], BF16)
        kT = self.tl(st, "kT", [128, 4, MEM], BF16)
        V = self.tl(st, "V", [128, 2, 512], BF16)
        xt = self.tl(st, "xt", [128, 8, 512], F32)
        ht = self.tl(st, "ht", [128, 8, 512], BF16)
        qh = self.tl(st, "qh", [128, 512], BF16)
        PT = self.tl(st, "PT", [128, 2, 512], BF16)
        rden = self.tl(st, "rden", [128, 512], F32)
        o = self.tl(st, "o", [128, 4, 512], BF16)
        bwq, bwkv, bwo = em.buf(), em.buf(), em.buf()
        bmt, bmh, bk, bv = em.buf(), em.buf(), em.buf(), em.buf()
        bx, bh, bq, bp, brd, bo = em.buf(), em.buf(), em.buf(), [em.buf(), em.buf()], em.buf(), em.buf()
        for k in range(8):
            self.load_w(wq[:, k, :], self.dram["w_xq"][l, k * 128:(k + 1) * 128, :], 128, 512, bwq)
            self.load_w(wkv[:, k, :], self.dram["w_xkv"][l, k * 128:(k + 1) * 128, :], 128, 1024, bwkv)
        for h in range(4):
            self.load_w(wo[:, h, :], self.dram["w_xo"][l, h * 128:(h + 1) * 128, :], 128, D, bwo)
        em.dma("sp", mt[:], self.dram["memT"].rearrange("(c p) s -> p c s", p=128), writes=[bmt])
        self.norm_tile(mt, mh, "norm_mem_g", l, self.sqt, bmt, bmh, width=MEM)
        for h in range(4):
            for k in range(8):
                em.op("pe", lambda e, k=k, h=h: e.matmul(ps[1][:, 0:MEM], lhsT=wkv[:, k, h * 128:(h + 1) * 128], rhs=mh[:, k, :],
                                                         start=(k == 0), stop=(k == 7)), reads=[bwkv, bmh], writes=[psb[1]])
            em.op("act", lambda e, h=h: e.copy(out=kT[:, h, :], in_=ps[1][:, 0:MEM]), reads=[psb[1]], writes=[bk])
        for mb in range(2):
            for k in range(8):
                em.op("pe", lambda e, k=k, mb=mb: e.matmul(ps[2][:], lhsT=mh[:, k, mb * 128:(mb + 1) * 128], rhs=wkv[:, k, 512:1024],
                                                           start=(k == 0), stop=(k == 7)), reads=[bwkv, bmh], writes=[psb[2]])
            em.op("act", lambda e, mb=mb: e.copy(out=V[:, mb, :], in_=ps[2][:]), reads=[psb[2]], writes=[bv])
        sc = 128.0 ** -0.5
        for t in range(self.NT):
            em.dma("sp", xt[:], self.xtile_ap("xs", t), reads=[self.b_xs], writes=[bx])
            self.norm_tile(xt, ht, "norm_x_g", l, self.sqt, bx, bh)
            for h in range(4):
                for k in range(8):
                    em.op("pe", lambda e, k=k, h=h: e.matmul(ps[1][:], lhsT=wq[:, k, h * 128:(h + 1) * 128], rhs=ht[:, k, :],
                                                             start=(k == 0), stop=(k == 7)), reads=[bwq, bh], writes=[psb[1]])
                em.op("act", lambda e: e.copy(out=qh[:], in_=ps[1][:]), reads=[psb[1]], writes=[bq])
                for mb in range(2):
                    em.op("pe", lambda e, h=h, mb=mb: e.matmul(ps[2 + mb][:], lhsT=kT[:, h, mb * 128:(mb + 1) * 128], rhs=qh[:],
                                                               start=True, stop=True), reads=[bk, bq], writes=[psb[2 + mb]])
                    em.op("act", lambda e, mb=mb: e.activation(out=PT[:, mb, :], in_=ps[2 + mb][:], func=AF.Exp, scale=sc),
                          reads=[psb[2 + mb]], writes=[bp[mb]])
                for mb in range(2):
                    em.op("pe", lambda e, h=h, mb=mb: e.matmul(ps[4][:], lhsT=V[:, mb, h * 128:(h + 1) * 128], rhs=PT[:, mb, :],
                                                               start=(mb == 0), stop=(mb == 1)), reads=[bv, bp[mb]], writes=[psb[4]])
                for mb in range(2):
                    em.op("pe", lambda e, mb=mb: e.matmul(ps[5][:], lhsT=self.ones_bf[:], rhs=PT[:, mb, :],
                                                          start=(mb == 0), stop=(mb == 1)), reads=[self.b_const, bp[mb]], writes=[psb[5]])
                em.op("dve", lambda e: e.reciprocal(out=rden[:], in_=ps[5][:]), reads=[psb[5]], writes=[brd])
                em.op("dve", lambda e, h=h: e.tensor_tensor(out=o[:, h, :], in0=ps[4][:], in1=rden[:], op=ALU.mult),
                      reads=[psb[4], brd], writes=[bo])
            for c in range(DC):
                po = 6 + (c % 2)
                for h in range(4):
                    em.op("pe", lambda e, c=c, h=h, po=po: e.matmul(ps[po][:], lhsT=wo[:, h, c * 128:(c + 1) * 128], rhs=o[:, h, :],
                                                                    start=(h == 0), stop=(h == 3)), reads=[bwo, bo], writes=[psb[po]])
                em.op("dve", lambda e, c=c, po=po: e.tensor_tensor(out=xt[:, c, :], in0=ps[po][:], in1=xt[:, c, :], op=ALU.add),
                      reads=[psb[po], bx], writes=[bx])
            em.dma("sp", self.xtile_ap("xs", t), xt[:], reads=[bx], writes=[self.b_xs])
        em.barrier()


Builder.phase_xattn = _phase_xattn


def _mix_begin(self, l, st):
    em, nc = self.em, self.nc
    self.hT = self.tl(st, "hT", [128, 8, self.S], BF16)
    self.b_hT = em.buf("hT")
    with ExitStack() as s2:
        self.alloc_norm_scratch(s2)
        xt = self.tl(s2, "xt", [128, 8, 512], F32)
        bx = em.buf()
        for t in range(self.NT):
            em.dma("sp", xt[:], self.xtile_ap("xs", t), reads=[self.b_xs], writes=[bx])
            self.norm_tile(xt, self.hT[:, :, t * 512:(t + 1) * 512], "norm_mix_g", l, self.sqt, bx, self.b_hT)
        em.barrier()


def _load_wk(self, dst, l, col0, n, wbuf):
    em = self.em
    i = self.stg_i
    self.stg_i = (i + 1) % 2
    stg, sbf = self.stg[i], self.stgb[i]
    src = self.dram["w_in"][l].rearrange("(k p) c -> p k c", p=128)[:, :, col0:col0 + n]
    sv = stg[:, 0:8 * n].rearrange("p (k c) -> p k c", k=8)
    em.dma("sp", sv, src, writes=[sbf])
    em.op("pool", lambda e: e.tensor_copy(out=dst, in_=sv), reads=[sbf], writes=[wbuf])


def _proj_fm(self, w, n, wbuf, pi, evac):
    em = self.em
    for t in range(self.NT):
        p = pi[t % len(pi)]
        for k in range(8):
            em.op("pe", lambda e, k=k, t=t, p=p: e.matmul(self.ps[p][0:n, :], lhsT=w[:, k, 0:n], rhs=self.hT[:, k, t * 512:(t + 1) * 512],
                                                          start=(k == 0), stop=(k == 7)), reads=[wbuf, self.b_hT], writes=[self.psb[p]])
        evac(t, p)


def _proj_tm(self, w, n, wbuf, tok_ap_fn, nblk, pi, evac):
    em = self.em
    for b in range(nblk):
        p = pi[b % len(pi)]
        for k in range(8):
            em.op("pe", lambda e, k=k, b=b, p=p: e.matmul(self.ps[p][:, 0:n], lhsT=tok_ap_fn(k, b), rhs=w[:, k, 0:n],
                                                          start=(k == 0), stop=(k == 7)), reads=[wbuf, self.b_hT], writes=[self.psb[p]])
        evac(b, p)


def _mixer_C(self, l):
    em, nc = self.em, self.nc
    ps, psb = self.ps, self.psb
    S, NB, NT = self.S, self.NB, self.NT
    lam_init = 0.8 - 0.6 * math.exp(-0.3 * l)
    with ExitStack() as st:
        wq = self.tl(st, "wq", [128, 8, 128], BF16)
        wk = self.tl(st, "wk", [128, 8, 128], BF16)
        wv = self.tl(st, "wv", [128, 8, 128], BF16)
        qT = self.tl(st, "qT", [128, S], BF16)
        kT = self.tl(st, "kT", [128, 2, S], BF16)
        V = self.tl(st, "V", [128, NB, 128], BF16)
        PT = self.tl(st, "PT", [128, 4, 512], BF16)
        r1 = self.tl(st, "r1", [128, 512], F32)
        r2 = self.tl(st, "r2", [128, 512], F32)
        t1 = self.tl(st, "t1", [128, 512], F32)
        t2 = self.tl(st, "t2", [128, 512], F32)
        yo = self.tl(st, "yo", [128, 2, 512], BF16)
        lam = self.tl(st, "lam", [128, 8], F32)
        ltmp = self.tl(st, "ltmp", [128, 64], F32)
        bwq, bwk, bwv, bq, bk, bv = [em.buf() for _ in range(6)]
        bp = [em.buf() for _ in range(4)]
        br1, br2, bt1, bt2, blam = [em.buf() for _ in range(5)]
        byo = [em.buf(), em.buf()]
        b_y = self.b_ybr
        o = RV_OFF[("diff_lambda", l)]
        for i in range(2):
            em.op("dve", lambda e, i=i: e.tensor_tensor(out=ltmp[:], in0=self.rv[:, o + 128 * i:o + 128 * i + 64],
                                                         in1=self.rv[:, o + 128 * i + 64:o + 128 * i + 128], op=ALU.mult),
                  reads=[self.b_const], writes=[blam])
            em.op("dve", lambda e, i=i: e.reduce_sum(out=lam[:, i:i + 1], in_=ltmp[:], axis=mybir.AxisListType.X), reads=[blam], writes=[blam])
        em.op("act", lambda e: e.activation(out=lam[:, 2:4], in_=lam[:, 0:2], func=AF.Exp), reads=[blam], writes=[blam])
        em.op("dve", lambda e: e.tensor_tensor(out=lam[:, 4:5], in0=lam[:, 3:4], in1=lam[:, 2:3], op=ALU.subtract), reads=[blam], writes=[blam])
        em.op("dve", lambda e: e.tensor_scalar(out=lam[:, 5:6], in0=lam[:, 4:5], scalar1=-lam_init, scalar2=None, op0=ALU.add), reads=[blam], writes=[blam])
        em.op("dve", lambda e: e.memset(lam[:, 6:7], EPS / (1.0 - lam_init) ** 2), writes=[blam])
        sc = 64.0 ** -0.5
        em.op("pool", lambda e: e.memset(kT[64:128, 0, :], 0.0), writes=[bk])
        em.op("pool", lambda e: e.memset(kT[0:64, 1, :], 0.0), writes=[bk])
        for h in range(4):
            self.load_wk(wq[:], l, O_CQ + h * 128, 128, bwq)
            self.load_wk(wk[:], l, O_CK + h * 128, 128, bwk)
            self.load_wk(wv[:], l, O_CV + h * 128, 128, bwv)
            self.proj_fm(wq, 128, bwq, [6, 7], lambda t, p: em.op("act", lambda e: e.copy(out=qT[:, t * 512:(t + 1) * 512], in_=ps[p][:]), reads=[psb[p]], writes=[bq]))
            def evk(t, p):
                em.op("dve", lambda e: e.tensor_copy(out=kT[0:64, 0, t * 512:(t + 1) * 512], in_=ps[p][0:64, :]), reads=[psb[p]], writes=[bk])
                em.op("dve", lambda e: e.tensor_copy(out=kT[64:128, 1, t * 512:(t + 1) * 512], in_=ps[p][64:128, :]), reads=[psb[p]], writes=[bk])
            self.proj_fm(wk, 128, bwk, [6, 7], evk)
            self.proj_tm(wv, 128, bwv, lambda k, b: self.hT[:, k, b * 128:(b + 1) * 128], NB, [6, 7],
                         lambda b, p: em.op("act", lambda e: e.copy(out=V[:, b, :], in_=ps[p][:, 0:128]), reads=[psb[p]], writes=[bv]))
            LAG = 3
            items = []
            for t in range(NT):
                nj = 4 * t + 4
                for c in range(2):
                    for j in range(nj):
                        items.append((t, c, j, nj))
            SB = [0, 1, 6, 7]

            def stage1(i):
                t, c, j, nj = items[i]
                lo, hi = c * 64, (c + 1) * 64
                q0 = max(0, j - 4 * t) * 128
                s = i % 4
                sp = SB[s]
                em.op("pe", lambda e: e.matmul(ps[sp][:, q0:512], lhsT=kT[:, c, j * 128:(j + 1) * 128], rhs=qT[:, t * 512 + q0:(t + 1) * 512],
                                               start=True, stop=True), reads=[bk, bq], writes=[psb[sp]])
                em.op("act", lambda e: e.activation(out=PT[:, s, q0:512], in_=ps[sp][:, q0:512], func=AF.Exp, scale=sc), reads=[psb[sp]], writes=[bp[s]])
                if j >= 4 * t:
                    em.op("pool", lambda e: e.tensor_tensor(out=PT[:, s, q0:q0 + 128], in0=PT[:, s, q0:q0 + 128], in1=self.triU_bf[:], op=ALU.mult),
                          reads=[bp[s], self.b_const], writes=[bp[s]])

            def stage2(i):
                t, c, j, nj = items[i]
                q0 = max(0, j - 4 * t) * 128
                s = i % 4
                em.op("pe", lambda e: e.matmul(ps[2 + c][:, q0:512], lhsT=V[:, j, :], rhs=PT[:, s, q0:512], start=(j == 0), stop=(j == nj - 1)),
                      reads=[bv, bp[s]], writes=[psb[2 + c]])
                em.op("pe", lambda e: e.matmul(ps[4 + c][:, q0:512], lhsT=self.ones_bf[:], rhs=PT[:, s, q0:512], start=(j == 0), stop=(j == nj - 1)),
                      reads=[self.b_const, bp[s]], writes=[psb[4 + c]])
                if c == 1 and j == nj - 1:
                    epilogue(t)

            def epilogue(t):
                em.op("dve", lambda e: e.reciprocal(out=r1[:], in_=ps[4][:]), reads=[psb[4]], writes=[br1])
                em.op("dve", lambda e: e.reciprocal(out=r2[:], in_=ps[5][:]), reads=[psb[5]], writes=[br2])
                em.op("dve", lambda e: e.tensor_tensor(out=t1[:], in0=ps[2][:], in1=r1[:], op=ALU.mult), reads=[psb[2], br1], writes=[bt1])
                em.op("dve", lambda e: e.tensor_tensor(out=t2[:], in0=ps[3][:], in1=r2[:], op=ALU.mult), reads=[psb[3], br2], writes=[bt2])
                em.op("dve", lambda e: e.scalar_tensor_tensor(out=t1[:], in0=t2[:], scalar=lam[:, 5:6], in1=t1[:], op0=ALU.mult, op1=ALU.add),
                      reads=[bt1, bt2, blam], writes=[bt1])
                em.op("act", lambda e: e.activation(out=t2[:], in_=t1[:], func=AF.Square), reads=[bt1], writes=[bt2])
                em.op("pe", lambda e: e.matmul(ps[4][:], lhsT=self.ones_f[:], rhs=t2[:], start=True, stop=True),
                      reads=[self.b_const, bt2], writes=[psb[4]])
                em.op("act", lambda e: e.activation(out=r1[:], in_=ps[4][:], func=AF.Sqrt, scale=1.0 / (128.0 * (1.0 - lam_init) ** 2), bias=lam[:, 6:7]),
                      reads=[psb[4], blam], writes=[br1])
                em.op("dve", lambda e: e.reciprocal(out=r1[:], in_=r1[:]), reads=[br1], writes=[br1])
                yi = t % 2
                em.op("dve", lambda e: e.scalar_tensor_tensor(out=yo[:, yi, :], in0=t1[:], scalar=self.pvcol("diff_subln_g", l), in1=r1[:],
                                                              op0=ALU.mult, op1=ALU.mult), reads=[bt1, br1, self.b_const], writes=[byo[yi]])
                r0 = 1024 + h * 128
                em.dma("sp", self.dram["ybr"][r0:r0 + 128, t * 512:(t + 1) * 512], yo[:, yi, :], reads=[byo[yi]], writes=[b_y])

            for i in range(len(items) + LAG):
                if i < len(items):
                    stage1(i)
                if i - LAG >= 0:
                    stage2(i - LAG)
        em.barrier()


Builder.mix_begin = _mix_begin
Builder.load_wk = _load_wk
Builder.proj_fm = _proj_fm
Builder.proj_tm = _proj_tm
Builder.mixer_C = _mixer_C


def _gates(self, l):
    em = self.em
    ps, psb = self.ps, self.psb
    with ExitStack() as st:
        wg = [self.tl(st, "wg", [128, 8, 128], BF16) for _ in range(2)]
        bwg = [em.buf(), em.buf()]
        go = [self.tl(st, "go", [128, 512], BF16) for _ in range(2)]
        bgo = [em.buf(), em.buf()]
        b_g = self.b_gat
        n = 0
        for ic in range(32):
            w, bw = wg[ic % 2], bwg[ic % 2]
            self.load_wk(w[:], l, O_G + ic * 128, 128, bw)

            def evac(t, p, ic=ic):
                nonlocal n
                g, bg = go[n % 2], bgo[n % 2]
                n += 1
                em.op("act", lambda e: e.activation(out=g[:], in_=ps[p][:], func=AF.Sigmoid), reads=[psb[p]], writes=[bg])
                em.dma("sp", self.dram["gat"][ic * 128:(ic + 1) * 128, t * 512:(t + 1) * 512], g[:], reads=[bg], writes=[b_g])
            self.proj_fm(w, 128, bw, [0, 1, 2, 3], evac)
        em.barrier()


def _phase_merge(self, l):
    em, nc = self.em, self.nc
    ps, psb = self.ps, self.psb
    with ExitStack() as st:
        wb = self.tl(st, "wb", [128, 16, D], BF16)
        wo = self.tl(st, "wo", [128, 8, D], BF16)
        y = self.tl(st, "y", [128, 16, 512], BF16)
        g = self.tl(st, "g", [128, 32, 512], BF16)
        xt = self.tl(st, "xt", [128, 8, 512], F32)
        mg = self.tl(st, "mg", [128, 8, 512], BF16)
        acc = self.tl(st, "acc", [128, 512], F32)
        tmp = self.tl(st, "tmp", [128, 2, 512], F32)
        bwb = [em.buf() for _ in range(16)]
        bwo = [em.buf() for _ in range(8)]
        by, bg, bx, bmg, bacc = em.buf(), em.buf(), em.buf(), em.buf(), em.buf()
        btmp = [em.buf(), em.buf()]
        for r in range(16):
            self.load_w(wb[:, r, :], self.dram["w_branch"][l, r * 128:(r + 1) * 128, :], 128, D, bwb[r])
        for c in range(8):
            self.load_w(wo[:, c, :], self.dram["w_mix_out"][l, c * 128:(c + 1) * 128, :], 128, D, bwo[c])
        for t in range(self.NT):
            em.dma("sp", y[:], self.dram["ybr"].rearrange("(r p) s -> p r s", p=128)[:, :, t * 512:(t + 1) * 512], reads=[self.b_ybr], writes=[by])
            em.dma("sp", g[:], self.dram["gat"].rearrange("(r p) s -> p r s", p=128)[:, :, t * 512:(t + 1) * 512], reads=[self.b_gat], writes=[bg])
            em.dma("sp", xt[:], self.xtile_ap("xs", t), reads=[self.b_xs], writes=[bx])
            n = 0
            for c in range(8):
                for i in range(4):
                    p = n % 4
                    n += 1
                    for m in range(4):
                        em.op("pe", lambda e, p=p, i=i, m=m, c=c: e.matmul(ps[p][:], lhsT=wb[:, i * 4 + m, c * 128:(c + 1) * 128], rhs=y[:, i * 4 + m, :],
                                                                           start=(m == 0), stop=(m == 3)), reads=[bwb[i * 4 + m], by], writes=[psb[p]])
                    if i == 0:
                        em.op("dve", lambda e, p=p, c=c: e.tensor_tensor(out=acc[:], in0=ps[p][:], in1=g[:, c, :], op=ALU.mult),
                              reads=[psb[p], bg], writes=[bacc])
                    else:
                        ti = i % 2
                        em.op("dve", lambda e, p=p, c=c, i=i, ti=ti: e.tensor_tensor(out=tmp[:, ti, :], in0=ps[p][:], in1=g[:, i * 8 + c, :], op=ALU.mult),
                              reads=[psb[p], bg], writes=[btmp[ti]])
                        if i < 3:
                            em.op("pool", lambda e, ti=ti: e.tensor_tensor(out=acc[:], in0=acc[:], in1=tmp[:, ti, :], op=ALU.add),
                                  reads=[bacc, btmp[ti]], writes=[bacc])
                        else:
                            em.op("pool", lambda e, ti=ti, c=c: e.tensor_tensor(out=mg[:, c, :], in0=acc[:], in1=tmp[:, ti, :], op=ALU.add),
                                  reads=[bacc, btmp[ti]], writes=[bmg])
            for c2 in range(8):
                po = 4 + (c2 % 2)
                for c in range(8):
                    em.op("pe", lambda e, c=c, c2=c2, po=po: e.matmul(ps[po][:], lhsT=wo[:, c, c2 * 128:(c2 + 1) * 128], rhs=mg[:, c, :],
                                                                      start=(c == 0), stop=(c == 7)), reads=[bwo[c], bmg], writes=[psb[po]])
                em.op("dve", lambda e, c2=c2, po=po: e.tensor_tensor(out=xt[:, c2, :], in0=ps[po][:], in1=xt[:, c2, :], op=ALU.add),
                      reads=[psb[po], bx], writes=[bx])
            em.dma("sp", self.xtile_ap("xs", t), xt[:], reads=[bx], writes=[self.b_xs])
        em.barrier()


Builder.gates = _gates
Builder.phase_merge = _phase_merge


def _mixer_A(self, l):
    em, nc = self.em, self.nc
    ps, psb = self.ps, self.psb
    S, NB, NT = self.S, self.NB, self.NT
    with ExitStack() as st:
        wq = self.tl(st, "wq", [128, 8, 128], BF16)
        wk = self.tl(st, "wk", [128, 8, 128], BF16)
        wv = self.tl(st, "wv", [128, 8, 128], BF16)
        qT = self.tl(st, "qT", [128, S], BF16)
        kT = self.tl(st, "kT", [128, S], BF16)
        V = self.tl(st, "V", [128, NB, 128], BF16)
        PT = self.tl(st, "PT", [128, 4, 256], BF16)
        msk = self.tl(st, "msk", [128, 256], BF16)
        accN = self.tl(st, "accN", [128, S], F32)
        accD = self.tl(st, "accD", [128, S], F32)
        yo = self.tl(st, "yo", [64, 2, 512], BF16)
        bwq, bwk, bwv, bq, bk, bv, bm, baN, baD = [em.buf() for _ in range(9)]
        bp = [em.buf() for _ in range(4)]
        byo = [em.buf(), em.buf()]
        em.op("dve", lambda e: e.tensor_copy(out=msk[:, 0:128], in_=self.triU_bf[:]), reads=[self.b_const], writes=[bm])
        em.op("dve", lambda e: e.tensor_copy(out=msk[:, 128:256], in_=self.triL_bf[:]), reads=[self.b_const], writes=[bm])
        sc = 64.0 ** -0.5
        for (wt_, bw_) in ((wq, bwq), (wk, bwk), (wv, bwv)):
            em.op("pool", lambda e, wt_=wt_: e.memset(wt_[:, :, 64:128], 0.0), writes=[bw_])
        for h in range(8):
            self.load_wk(wq[:, :, 0:64], l, O_AQ + h * 64, 64, bwq)
            self.load_wk(wk[:, :, 0:64], l, O_AK + h * 64, 64, bwk)
            self.load_wk(wv[:, :, 0:64], l, O_AV + h * 64, 64, bwv)
            self.proj_fm(wq, 128, bwq, [6, 7], lambda t, p: em.op("act", lambda e: e.copy(out=qT[:, t * 512:(t + 1) * 512], in_=ps[p][:, :]), reads=[psb[p]], writes=[bq]))
            self.proj_fm(wk, 128, bwk, [6, 7], lambda t, p: em.op("dve", lambda e: e.tensor_copy(out=kT[:, t * 512:(t + 1) * 512], in_=ps[p][:, :]), reads=[psb[p]], writes=[bk]))
            for pi_, dil in enumerate((1, 4, 16)):
                L = S // dil
                nbs = L // 128
                assert nbs >= 1 and nbs * 128 * dil == S

                def tok(r, n, cnt=128, dil=dil):
                    s0 = r + dil * n * 128
                    return slice(s0, s0 + dil * (cnt - 1) + 1, dil)
                self.proj_tm(wv, 128, bwv, lambda k, b: self.hT[:, k, tok(b // nbs, b % nbs)], NB, [6, 7],
                             lambda b, p: em.op("act", lambda e: e.copy(out=V[:, b, :], in_=ps[p][:, 0:128]), reads=[psb[p]], writes=[bv]))
                grp = min(4, nbs)
                LAG = 2
                SBF = [0, 1, 6]
                blocks = [(r, n) for r in range(dil) for n in range(nbs)]

                def stage1(i, dil=dil, nbs=nbs, tok=tok):
                    r, n = blocks[i]
                    s = i % 3
                    sp = SBF[s]
                    w = 256 if n > 0 else 128
                    em.op("pe", lambda e: e.matmul(ps[sp][:, 0:128], lhsT=kT[:, tok(r, n)], rhs=qT[:, tok(r, n)], start=True, stop=True),
                          reads=[bk, bq], writes=[psb[sp]])
                    if n > 0:
                        em.op("pe", lambda e: e.matmul(ps[sp][:, 128:256], lhsT=kT[:, tok(r, n - 1)], rhs=qT[:, tok(r, n)], start=True, stop=True),
                              reads=[bk, bq], writes=[psb[sp]])
                    em.op("act", lambda e: e.activation(out=PT[:, s, 0:w], in_=ps[sp][:, 0:w], func=AF.Exp, scale=sc), reads=[psb[sp]], writes=[bp[s]])
                    em.op("pool", lambda e: e.tensor_tensor(out=PT[:, s, 0:w], in0=PT[:, s, 0:w], in1=msk[:, 0:w], op=ALU.mult),
                          reads=[bp[s], bm], writes=[bp[s]])

                def stage2(i, dil=dil, nbs=nbs, tok=tok, pi_=pi_, grp=grp):
                    r, n = blocks[i]
                    s = i % 3
                    gidx = i // grp
                    pn = 2 + (gidx % 2)
                    pd = 4 + (gidx % 2)
                    n0 = (n // grp) * grp
                    qs = (n - n0) * 128
                    pb = r * nbs + n
                    last = (n == 0)
                    em.op("pe", lambda e: e.matmul(ps[pn][:, qs:qs + 128], lhsT=V[:, pb, :], rhs=PT[:, s, 0:128], start=True, stop=last),
                          reads=[bv, bp[s]], writes=[psb[pn]])
                    if n > 0:
                        em.op("pe", lambda e: e.matmul(ps[pn][:, qs:qs + 128], lhsT=V[:, pb - 1, :], rhs=PT[:, s, 128:256], start=False, stop=True),
                              reads=[bv, bp[s]], writes=[psb[pn]])
                    em.op("pe", lambda e: e.matmul(ps[pd][:, qs:qs + 128], lhsT=self.ones_bf[:], rhs=PT[:, s, 0:128], start=True, stop=last),
                          reads=[self.b_const, bp[s]], writes=[psb[pd]])
                    if n > 0:
                        em.op("pe", lambda e: e.matmul(ps[pd][:, qs:qs + 128], lhsT=self.ones_bf[:], rhs=PT[:, s, 128:256], start=False, stop=True),
                              reads=[self.b_const, bp[s]], writes=[psb[pd]])
                    if n == n0 + grp - 1:
                        tsl = tok(r, n0, grp * 128)
                        gw = grp * 128
                        if pi_ == 0:
                            em.op("dve", lambda e: e.tensor_copy(out=accN[:, tsl], in_=ps[pn][:, 0:gw]), reads=[psb[pn]], writes=[baN])
                            em.op("act", lambda e: e.copy(out=accD[:, tsl], in_=ps[pd][:, 0:gw]), reads=[psb[pd]], writes=[baD])
                        else:
                            em.op("dve", lambda e: e.tensor_tensor(out=accN[:, tsl], in0=ps[pn][:, 0:gw], in1=accN[:, tsl], op=ALU.add),
                                  reads=[psb[pn], baN], writes=[baN])
                            em.op("dve", lambda e: e.tensor_tensor(out=accD[:, tsl], in0=ps[pd][:, 0:gw], in1=accD[:, tsl], op=ALU.add),
                                  reads=[psb[pd], baD], writes=[baD])

                for i in range(len(blocks) + LAG):
                    if i < len(blocks):
                        stage1(i)
                    if i - LAG >= 0:
                        stage2(i - LAG)
            for t in range(NT):
                sl = slice(t * 512, (t + 1) * 512)
                yi = t % 2
                em.op("dve", lambda e, sl=sl: e.reciprocal(out=accD[0:64, sl], in_=accD[0:64, sl]), reads=[baD], writes=[baD])
                em.op("dve", lambda e, sl=sl, yi=yi: e.tensor_tensor(out=yo[:, yi, :], in0=accN[0:64, sl], in1=accD[0:64, sl], op=ALU.mult),
                      reads=[baN, baD], writes=[byo[yi]])
                em.dma("sp", self.dram["ybr"][h * 64:(h + 1) * 64, sl], yo[:, yi, :], reads=[byo[yi]], writes=[self.b_ybr])
        em.barrier()


Builder.mixer_A = _mixer_A


def _bcast_mid(ap2d, n):
    a = ap2d.ap
    return bass.AP(ap2d.tensor, ap2d.offset, [list(a[0]), [0, n], list(a[1])])


def _mixer_B(self, l):
    em, nc = self.em, self.nc
    ps, psb = self.ps, self.psb
    S, NB, NT = self.S, self.NB, self.NT
    lg = [math.log1p(-2.0 ** (-5.0 - h)) for h in range(4)]
    with ExitStack() as st:
        wq = self.tl(st, "wq", [128, 8, 64], BF16)
        wqs = self.tl(st, "wqs", [128, 8, 64], BF16)
        wk = self.tl(st, "wk", [128, 8, 64], BF16)
        wks = self.tl(st, "wks", [128, 8, 64], BF16)
        wv = self.tl(st, "wv", [128, 8, 128], BF16)
        wg = self.tl(st, "wg", [128, 8, 128], BF16)
        rot = self.tl(st, "rot", [64, 2, S], F32)
        qr = self.tl(st, "qr", [64, S], BF16)
        qi = self.tl(st, "qi", [64, S], BF16)
        kr = self.tl(st, "kr", [64, S], BF16)
        V = self.tl(st, "V", [128, NB, 128], BF16)
        gs = self.tl(st, "gs", [128, S], BF16)
        ta = self.tl(st, "ta", [64, 512], F32)
        tb = self.tl(st, "tb", [64, 512], F32)
        Sm = self.tl(st, "Sm", [128, 2, 128], BF16)
        kend = self.tl(st, "kend", [128, 2, 64], BF16)
        R = self.tl(st, "R", [64, 128], F32)
        Rb = self.tl(st, "Rb", [64, 128], BF16)
        ot = self.tl(st, "ot", [128, 512], F32)
        cen = self.tl(st, "cen", [128, 512], F32)
        sq = self.tl(st, "sq", [128, 512], F32)
        rs = self.tl(st, "rs", [128, 512], F32)
        yo = self.tl(st, "yo", [128, 2, 512], BF16)
        epst = self.tl(st, "epst", [128, 1], F32)
        (bwq, bwqs, bwk, bwks, bwv, bwg, brot, bqr, bqi, bkr, bv, bgs, bta, btb, bR, bRb, bot, bcen, bsq, brs) = [em.buf() for _ in range(20)]
        bSm = [em.buf(), em.buf()]
        bke = [em.buf(), em.buf()]
        byo = [em.buf(), em.buf()]
        em.dma("sp", rot[:], self.dram["rot"], writes=[brot])
        em.op("dve", lambda e: e.memset(epst[:], EPS), writes=[brs])
        psT = [ps[4].bitcast(BF16)]
        for h in range(4):
            self.load_wk(wq[:], l, O_BQ + h * 64, 64, bwq)
            self.load_wk(wqs[:, :, 0:32], l, O_BQ + h * 64 + 32, 32, bwqs)
            self.load_wk(wqs[:, :, 32:64], l, O_BQ + h * 64, 32, bwqs)
            self.load_wk(wk[:], l, O_BK + h * 64, 64, bwk)
            self.load_wk(wks[:, :, 0:32], l, O_BK + h * 64 + 32, 32, bwks)
            self.load_wk(wks[:, :, 32:64], l, O_BK + h * 64, 32, bwks)
            self.load_wk(wv[:], l, O_BV + h * 128, 128, bwv)
            self.load_wk(wg[:], l, O_BG + h * 128, 128, bwg)
            qdec = self.cf[0:64, CF_QDEC + h * 128:CF_QDEC + (h + 1) * 128]
            for (wa, wb_, ba, bb, dst, bdst) in ((wq, wqs, bwq, bwqs, qr, bqr), (wk, wks, bwk, bwks, kr, bkr)):
                for t in range(NT):
                    sl = slice(t * 512, (t + 1) * 512)
                    for (w_, bw_, p) in ((wa, ba, 6), (wb_, bb, 7)):
                        for k in range(8):
                            em.op("pe", lambda e, k=k, w_=w_, p=p, sl=sl: e.matmul(ps[p][0:64, :], lhsT=w_[:, k, :], rhs=self.hT[:, k, sl],
                                                                                 start=(k == 0), stop=(k == 7)), reads=[bw_, self.b_hT], writes=[psb[p]])
                    em.op("dve", lambda e, sl=sl: e.tensor_tensor(out=ta[:], in0=ps[6][0:64, :], in1=rot[:, 0, sl], op=ALU.mult), reads=[psb[6], brot], writes=[bta])
                    em.op("dve", lambda e, sl=sl: e.tensor_tensor(out=tb[:], in0=ps[7][0:64, :], in1=rot[:, 1, sl], op=ALU.mult), reads=[psb[7], brot], writes=[btb])
                    em.op("pool", lambda e, sl=sl, dst=dst: e.tensor_tensor(out=dst[:, sl], in0=ta[:], in1=tb[:], op=ALU.add), reads=[bta, btb], writes=[bdst])
                    if dst is qr:
                        em.op("pool", lambda e, sl=sl: e.tensor_tensor(out=qi[:, sl].rearrange("p (a b) -> p a b", a=4), in0=qr[:, sl].rearrange("p (a b) -> p a b", a=4),
                                                                       in1=_bcast_mid(qdec, 4), op=ALU.mult), reads=[bqr, self.b_const], writes=[bqi])
            self.proj_tm(wv, 128, bwv, lambda k, b: self.hT[:, k, b * 128:(b + 1) * 128], NB, [6, 7],
                         lambda b, p: em.op("act", lambda e: e.copy(out=V[:, b, :], in_=ps[p][:, 0:128]), reads=[psb[p]], writes=[bv]))
            self.proj_fm(wg, 128, bwg, [6, 7], lambda t, p: em.op("act", lambda e: e.activation(out=gs[:, t * 512:(t + 1) * 512], in_=ps[p][:], func=AF.Silu), reads=[psb[p]], writes=[bgs]))
            decT = self.cf[:, CF_DECT + h * 128:CF_DECT + (h + 1) * 128]
            cdec = math.exp(128.0 * lg[h])
            for n in range(NB):
                c0 = n * 128
                s = n % 2
                qs = (n % 4) * 128
                em.op("pe", lambda e, c0=c0: e.transpose(out=psT[0][:, 0:64], in_=kr[:, c0:c0 + 128], identity=self.ident_bf[0:64, 0:64]),
                      reads=[bkr, self.b_const], writes=[psb[4]])
                em.op("dve", lambda e, s=s: e.tensor_scalar(out=kend[:, s, :], in0=psT[0][:, 0:64], scalar1=self.cf[:, CF_KEND + h:CF_KEND + h + 1], scalar2=0.125,
                                                           op0=ALU.mult, op1=ALU.mult), reads=[psb[4], self.b_const], writes=[bke[s]])
                em.op("pe", lambda e, s=s, c0=c0: e.matmul(ps[s][:, 0:128], lhsT=kr[:, c0:c0 + 128], rhs=qr[:, c0:c0 + 128], start=True, stop=True),
                      reads=[bkr, bqr], writes=[psb[s]])
                em.op("dve", lambda e, s=s: e.scalar_tensor_tensor(out=Sm[:, s, :], in0=ps[s][:, 0:128], scalar=0.125, in1=decT, op0=ALU.mult, op1=ALU.mult),
                      reads=[psb[s], self.b_const], writes=[bSm[s]])
                em.op("pe", lambda e, s=s, n=n, qs=qs: e.matmul(ps[2][:, qs:qs + 128], lhsT=V[:, n, :], rhs=Sm[:, s, :], start=True, stop=(n == 0)),
                      reads=[bv, bSm[s]], writes=[psb[2]])
                if n > 0:
                    em.op("pe", lambda e, c0=c0, qs=qs: e.matmul(ps[2][:, qs:qs + 128], lhsT=Rb[:], rhs=qi[:, c0:c0 + 128], start=False, stop=True),
                          reads=[bRb, bqi], writes=[psb[2]])
                if n < NB - 1:
                    em.op("pe", lambda e, s=s, n=n: e.matmul(ps[3][0:64, 0:128], lhsT=kend[:, s, :], rhs=V[:, n, :], start=True, stop=True),
                          reads=[bke[s], bv], writes=[psb[3]])
                    if n == 0:
                        em.op("dve", lambda e: e.tensor_copy(out=R[:], in_=ps[3][0:64, 0:128]), reads=[psb[3]], writes=[bR])
                    else:
                        em.op("dve", lambda e: e.scalar_tensor_tensor(out=R[:], in0=R[:], scalar=cdec, in1=ps[3][0:64, 0:128], op0=ALU.mult, op1=ALU.add),
                              reads=[psb[3], bR], writes=[bR])
                    em.op("act", lambda e: e.copy(out=Rb[:], in_=R[:]), reads=[bR], writes=[bRb])
                if n % 4 == 3:
                    t = n // 4
                    sl = slice(t * 512, (t + 1) * 512)
                    em.op("act", lambda e: e.copy(out=ot[:], in_=ps[2][:]), reads=[psb[2]], writes=[bot])
                    em.op("pe", lambda e: e.matmul(ps[5][:], lhsT=self.ones_f[:], rhs=ot[:], start=True, stop=True), reads=[self.b_const, bot], writes=[psb[5]])
                    em.op("dve", lambda e: e.scalar_tensor_tensor(out=cen[:], in0=ps[5][:], scalar=-1.0 / 128, in1=ot[:], op0=ALU.mult, op1=ALU.add),
                          reads=[psb[5], bot], writes=[bcen])
                    em.op("act", lambda e: e.activation(out=sq[:], in_=cen[:], func=AF.Square), reads=[bcen], writes=[bsq])
                    em.op("pe", lambda e: e.matmul(ps[5][:], lhsT=self.ones_f[:], rhs=sq[:], start=True, stop=True), reads=[self.b_const, bsq], writes=[psb[5]])
                    em.op("act", lambda e: e.activation(out=rs[:], in_=ps[5][:], func=AF.Sqrt, scale=1.0 / 128, bias=epst[:, 0:1]), reads=[psb[5], brs], writes=[brs])
                    em.op("dve", lambda e: e.reciprocal(out=rs[:], in_=rs[:]), reads=[brs], writes=[brs])
                    em.op("dve", lambda e: e.scalar_tensor_tensor(out=cen[:], in0=cen[:], scalar=self.pvcol("ret_gn_g", l, h), in1=rs[:], op0=ALU.mult, op1=ALU.mult),
                          reads=[bcen, brs, self.b_const], writes=[bcen])
                    yi = t % 2
                    em.op("pool", lambda e, yi=yi, sl=sl: e.tensor_tensor(out=yo[:, yi, :], in0=cen[:], in1=gs[:, sl], op=ALU.mult), reads=[bcen, bgs], writes=[byo[yi]])
                    r0 = 512 + h * 128
                    em.dma("sp", self.dram["ybr"][r0:r0 + 128, sl], yo[:, yi, :], reads=[byo[yi]], writes=[self.b_ybr])
        em.barrier()


Builder.mixer_B = _mixer_B


def _bcast_last(ap2d, n):
    a = ap2d.ap
    return bass.AP(ap2d.tensor, ap2d.offset, [list(a[0]), list(a[1]), [0, n]])


def _mixer_D(self, l):
    em, nc = self.em, self.nc
    ps, psb = self.ps, self.psb
    S, NB, NT = self.S, self.NB, self.NT
    X = mybir.AxisListType.X
    bc = self.b_const
    with ExitStack() as st:
        x_tm = self.tl(st, "x_tm", [128, NB, 512], BF16)
        B_tm = self.tl(st, "B_tm", [128, NB, 256], BF16)
        BT = self.tl(st, "BT", [128, 2, S], BF16)
        CT = self.tl(st, "CT", [128, 2, S], BF16)
        bxtm, bBtm, bBT, bCT = [em.buf() for _ in range(4)]
        psT = ps[7].bitcast(BF16)
        with ExitStack() as s1:
            w = [self.tl(s1, "w", [128, 8, 128], BF16) for _ in range(2)]
            bw = [em.buf(), em.buf()]
            praw = self.tl(s1, "praw", [128, 3 + S], F32)
            cacc = self.tl(s1, "cacc", [128, 512], F32)
            cact = self.tl(s1, "cact", [128, 512], BF16)
            bpraw, bcacc, bcact = [em.buf() for _ in range(3)]
            em.op("dve", lambda e: e.memset(praw[:, 0:3], 0.0), writes=[bpraw])
            for c in range(8):
                ww, bww = w[c % 2], bw[c % 2]
                self.load_wk(ww[:], l, O_DX + c * 128, 128, bww)
                self.proj_fm(ww, 128, bww, [0, 1], lambda t, p: em.op("act", lambda e: e.copy(out=praw[:, 3 + t * 512:3 + (t + 1) * 512], in_=ps[p][:]),
                                                                      reads=[psb[p]], writes=[bpraw]))
                cw = PV_OFF[("conv_w", l)] + c * 4
                for t in range(NT):
                    t0 = t * 512
                    em.op("dve", lambda e, t0=t0: e.tensor_scalar(out=cacc[:], in0=praw[:, t0 + 3:t0 + 515], scalar1=self.pv[:, cw + 3:cw + 4], scalar2=None, op0=ALU.mult),
                          reads=[bpraw, bc], writes=[bcacc])
                    for k in range(3):
                        em.op("dve", lambda e, t0=t0, k=k: e.scalar_tensor_tensor(out=cacc[:], in0=praw[:, t0 + k:t0 + k + 512], scalar=self.pv[:, cw + k:cw + k + 1], in1=cacc[:],
                                                                                op0=ALU.mult, op1=ALU.add), reads=[bpraw, bcacc, bc], writes=[bcacc])
                    if c < 4:
                        dst, bd = cact[:], bcact
                    elif c < 6:
                        dst, bd = BT[:, c - 4, t0:t0 + 512], bBT
                    else:
                        dst, bd = CT[:, c - 6, t0:t0 + 512], bCT
                    em.op("act", lambda e, dst=dst: e.activation(out=dst, in_=cacc[:], func=AF.Silu, bias=self.pvcol("conv_b", l, c), scale=1.0),
                          reads=[bcacc, bc], writes=[bd])
                    if c < 6:
                        for q in range(4):
                            in_ap = cact[:, q * 128:(q + 1) * 128] if c < 4 else BT[:, c - 4, t0 + q * 128:t0 + (q + 1) * 128]
                            em.op("pe", lambda e, q=q, in_ap=in_ap: e.transpose(out=psT[:, q * 128:(q + 1) * 128], in_=in_ap, identity=self.ident_bf[:]),
                                  reads=[bd, bc], writes=[psb[7]])
                        if c < 4:
                            em.op("dve", lambda e, t=t, c=c: e.tensor_copy(out=x_tm[:, 4 * t:4 * t + 4, c * 128:(c + 1) * 128], in_=psT[:, 0:512].rearrange("p (a b) -> p a b", a=4)),
                                  reads=[psb[7]], writes=[bxtm])
                        else:
                            em.op("dve", lambda e, t=t, c=c: e.tensor_copy(out=B_tm[:, 4 * t:4 * t + 4, (c - 4) * 128:(c - 3) * 128], in_=psT[:, 0:512].rearrange("p (a b) -> p a b", a=4)),
                                  reads=[psb[7]], writes=[bBtm])
            em.barrier()
        with ExitStack() as s2:
            wz = self.tl(s2, "wz", [128, 8, 512], BF16)
            wdt = self.tl(s2, "wdt", [128, 8, 8], BF16)
            smA = self.tl(s2, "smA", [128, NB, 64], F32)
            expA = self.tl(s2, "expA", [128, 8], F32)
            xdt = self.tl(s2, "xdt", [128, 512], BF16)
            xdd = self.tl(s2, "xdd", [128, 512], BF16)
            cbm = self.tl(s2, "cbm", [128, 256], BF16)
            lh = self.tl(s2, "lh", [128, 2, 128], F32)
            Lx = self.tl(s2, "Lx", [128, 2, 128], BF16)
            Mx = self.tl(s2, "Mx", [128, 2, 128], BF16)
            H = self.tl(s2, "H", [128, 512], F32)
            Hb = self.tl(s2, "Hb", [128, 512], BF16)
            t1 = self.tl(s2, "t1", [128, 512], F32)
            t2 = self.tl(s2, "t2", [128, 512], F32)
            zs = self.tl(s2, "zs", [128, 512], F32)
            yn = self.tl(s2, "yn", [128, 512], BF16)
            ysg = self.tl(s2, "ysg", [128, 4, 512], BF16)
            (bwz, bwdt, bsm, bexpA, bxdt, bxdd, bcbm, bH, bHb, bt1, bt2, bzs, byn, bysg) = [em.buf() for _ in range(14)]
            blh, bLx, bMx = [em.buf(), em.buf()], [em.buf(), em.buf()], [em.buf(), em.buf()]
            for k in range(8):
                self.load_w(wz[:, k, :], self.dram["w_in"][l, k * 128:(k + 1) * 128, O_DZ:O_DZ + 512], 128, 512, bwz)
            self.load_wk(wdt[:], l, O_DDT, 8, bwdt)
            oA, oB, oD, oG = RV_OFF[("A_log", l)], RV_OFF[("dt_bias", l)], RV_OFF[("D", l)], RV_OFF[("ssm_norm_g", l)]
            em.op("act", lambda e: e.activation(out=expA[:], in_=self.rv[:, oA:oA + 8], func=AF.Exp), reads=[bc], writes=[bexpA])
            triU_f = self.cf[:, CF_TRIU:CF_TRIU + 128]
            SL_f = self.cf[:, CF_SL:CF_SL + 128]
            for n in range(NB):
                for k in range(8):
                    em.op("pe", lambda e, k=k, n=n: e.matmul(ps[6][:, n * 8:(n + 1) * 8], lhsT=self.hT[:, k, n * 128:(n + 1) * 128], rhs=wdt[:, k, :], start=(k == 0), stop=(k == 7)),
                          reads=[bwdt, self.b_hT], writes=[psb[6]])
            em.op("dve", lambda e: e.tensor_tensor(out=smA[:, :, 0:8], in0=ps[6][:, 0:NB * 8].rearrange("p (n c) -> p n c", c=8), in1=_bcast_mid(self.rv[:, oB:oB + 8], NB), op=ALU.add),
                  reads=[psb[6], bc], writes=[bsm])
            em.op("act", lambda e: e.activation(out=smA[:, :, 0:8], in_=smA[:, :, 0:8], func=AF.Exp), reads=[bsm], writes=[bsm])
            em.op("act", lambda e: e.activation(out=smA[:, :, 0:8], in_=smA[:, :, 0:8], func=AF.Ln, bias=self.ones_f[:, 0:1], scale=1.0), reads=[bsm, bc], writes=[bsm])
            em.op("dve", lambda e: e.scalar_tensor_tensor(out=smA[:, :, 8:16], in0=smA[:, :, 0:8], scalar=-1.0, in1=_bcast_mid(expA[:, 0:8], NB), op0=ALU.mult, op1=ALU.mult),
                  reads=[bsm, bexpA], writes=[bsm])
            for n in range(NB):
                em.op("pe", lambda e, n=n: e.matmul(ps[2][:, n * 16:n * 16 + 8], lhsT=triU_f, rhs=smA[:, n, 8:16], start=True, stop=True), reads=[bc, bsm], writes=[psb[2]])
                em.op("pe", lambda e, n=n: e.matmul(ps[2][:, n * 16 + 8:n * 16 + 16], lhsT=self.ones_f[:], rhs=smA[:, n, 8:16], start=True, stop=True), reads=[bc, bsm], writes=[psb[2]])
            em.op("dve", lambda e: e.tensor_copy(out=smA[:, :, 16:32], in_=ps[2][:, 0:NB * 16].rearrange("p (n c) -> p n c", c=16)), reads=[psb[2]], writes=[bsm])
            em.op("dve", lambda e: e.tensor_tensor(out=smA[:, :, 40:48], in0=smA[:, :, 24:32], in1=smA[:, :, 16:24], op=ALU.subtract), reads=[bsm], writes=[bsm])
            em.op("act", lambda e: e.activation(out=smA[:, :, 32:40], in_=smA[:, :, 16:24], func=AF.Exp), reads=[bsm], writes=[bsm])
            em.op("act", lambda e: e.activation(out=smA[:, :, 40:48], in_=smA[:, :, 40:48], func=AF.Exp), reads=[bsm], writes=[bsm])
            em.op("act", lambda e: e.activation(out=smA[:, :, 48:56], in_=smA[:, :, 24:32], func=AF.Exp), reads=[bsm], writes=[bsm])
            bsm2 = em.buf("sm_norm")
            for n in range(NB):
                c0 = n * 128
                sm = smA[:, n, :]
                xv = x_tm[:, n, :].rearrange("p (h d) -> p h d", h=8)
                em.op("dve", lambda e, xv=xv: e.tensor_tensor(out=xdt[:].rearrange("p (h d) -> p h d", h=8), in0=xv, in1=_bcast_last(sm[:, 0:8], 64), op=ALU.mult),
                      reads=[bxtm, bsm], writes=[bxdt])
                em.op("pool", lambda e: e.tensor_tensor(out=xdd[:].rearrange("p (h d) -> p h d", h=8), in0=xdt[:].rearrange("p (h d) -> p h d", h=8),
                                                        in1=_bcast_last(sm[:, 40:48], 64), op=ALU.mult), reads=[bxdt, bsm], writes=[bxdd])
                for g in range(2):
                    em.op("pe", lambda e, g=g, c0=c0: e.matmul(ps[2][:, g * 128:(g + 1) * 128], lhsT=BT[:, g, c0:c0 + 128], rhs=CT[:, g, c0:c0 + 128], start=True, stop=True),
                          reads=[bBT, bCT], writes=[psb[2]])
                em.op("dve", lambda e: e.tensor_tensor(out=cbm[:].rearrange("p (g i) -> p g i", g=2), in0=ps[2][:, 0:256].rearrange("p (g i) -> p g i", g=2),
                                                       in1=_bcast_mid(triU_f, 2), op=ALU.mult), reads=[psb[2], bc], writes=[bcbm])
                if n > 0:
                    for g in range(2):
                        em.op("pe", lambda e, g=g, c0=c0: e.matmul(ps[4][:, g * 256:(g + 1) * 256], lhsT=CT[:, g, c0:c0 + 128], rhs=Hb[:, g * 256:(g + 1) * 256], start=True, stop=True),
                              reads=[bCT, bHb], writes=[psb[4]])
                def d_stage1(h):
                    s = h % 2
                    g = h // 4
                    em.op("dve", lambda e: e.tensor_scalar(out=lh[:, s, :], in0=SL_f, scalar1=sm[:, 8 + h:9 + h], scalar2=None, op0=ALU.mult),
                          reads=[bc, bsm], writes=[blh[s]])
                    em.op("pe", lambda e: e.matmul(ps[s][:, 0:128], lhsT=lh[:, s, :], rhs=triU_f, start=True, stop=True), reads=[blh[s], bc], writes=[psb[s]])
                    em.op("act", lambda e: e.activation(out=Lx[:, s, :], in_=ps[s][:, 0:128], func=AF.Exp), reads=[psb[s]], writes=[bLx[s]])
                    em.op("pool", lambda e: e.tensor_tensor(out=Mx[:, s, :], in0=Lx[:, s, :], in1=cbm[:, g * 128:(g + 1) * 128], op=ALU.mult),
                          reads=[bLx[s], bcbm], writes=[bMx[s]])

                def d_stage2(h):
                    s = h % 2
                    em.op("pe", lambda e: e.matmul(ps[3][:, h * 64:(h + 1) * 64], lhsT=Mx[:, s, :], rhs=xdt[:, h * 64:(h + 1) * 64], start=True, stop=True),
                          reads=[bMx[s], bxdt], writes=[psb[3]])

                d_stage1(0)
                for h in range(8):
                    if h + 1 < 8:
                        d_stage1(h + 1)
                    d_stage2(h)
                if n > 0:
                    em.op("dve", lambda e: e.tensor_tensor(out=t1[:].rearrange("p (h d) -> p h d", h=8), in0=ps[4][:].rearrange("p (h d) -> p h d", h=8),
                                                           in1=_bcast_last(sm[:, 32:40], 64), op=ALU.mult), reads=[psb[4], bsm], writes=[bt1])
                    em.op("dve", lambda e: e.tensor_tensor(out=t1[:], in0=ps[3][:], in1=t1[:], op=ALU.add), reads=[psb[3], bt1], writes=[bt1])
                else:
                    em.op("dve", lambda e: e.tensor_copy(out=t1[:], in_=ps[3][:]), reads=[psb[3]], writes=[bt1])
                em.op("pool", lambda e, xv=xv: e.tensor_tensor(out=t2[:].rearrange("p (h d) -> p h d", h=8), in0=xv, in1=_bcast_last(self.rv[:, oD:oD + 8], 64), op=ALU.mult),
                      reads=[bxtm, bc], writes=[bt2])
                em.op("pool", lambda e: e.tensor_tensor(out=t1[:], in0=t1[:], in1=t2[:], op=ALU.add), reads=[bt1, bt2], writes=[bt1])
                if n < NB - 1:
                    for g in range(2):
                        em.op("pe", lambda e, g=g, n=n: e.matmul(ps[5][:, g * 256:(g + 1) * 256], lhsT=B_tm[:, n, g * 128:(g + 1) * 128], rhs=xdd[:, g * 256:(g + 1) * 256], start=True, stop=True),
                              reads=[bBtm, bxdd], writes=[psb[5]])
                    if n == 0:
                        em.op("dve", lambda e: e.tensor_copy(out=H[:], in_=ps[5][:]), reads=[psb[5]], writes=[bH])
                    else:
                        em.op("pool", lambda e: e.tensor_tensor(out=H[:].rearrange("p (h d) -> p h d", h=8), in0=H[:].rearrange("p (h d) -> p h d", h=8),
                                                                in1=_bcast_last(sm[:, 48:56], 64), op=ALU.mult), reads=[bH, bsm], writes=[bH])
                        em.op("dve", lambda e: e.tensor_tensor(out=H[:], in0=ps[5][:], in1=H[:], op=ALU.add), reads=[psb[5], bH], writes=[bH])
                    em.op("act", lambda e: e.copy(out=Hb[:], in_=H[:]), reads=[bH], writes=[bHb])
                for k in range(8):
                    em.op("pe", lambda e, k=k, c0=c0: e.matmul(ps[6][:], lhsT=self.hT[:, k, c0:c0 + 128], rhs=wz[:, k, :], start=(k == 0), stop=(k == 7)),
                          reads=[bwz, self.b_hT], writes=[psb[6]])
                em.op("act", lambda e: e.activation(out=zs[:], in_=ps[6][:], func=AF.Silu), reads=[psb[6]], writes=[bzs])
                em.op("dve", lambda e: e.tensor_tensor(out=t1[:], in0=t1[:], in1=zs[:], op=ALU.mult), reads=[bt1, bzs], writes=[bt1])
                em.op("act", lambda e: e.activation(out=t2[:], in_=t1[:], func=AF.Square), reads=[bt1], writes=[bt2])
                em.op("dve", lambda e, sm=sm: e.reduce_sum(out=sm[:, 56:57], in_=t2[:], axis=X), reads=[bt2], writes=[bsm2])
                em.op("act", lambda e, sm=sm: e.activation(out=sm[:, 57:58], in_=sm[:, 56:57], func=AF.Sqrt, scale=1.0 / 512, bias=self.eps_c[:, 0:1]), reads=[bsm2, bc], writes=[bsm2])
                em.op("dve", lambda e, sm=sm: e.reciprocal(out=sm[:, 57:58], in_=sm[:, 57:58]), reads=[bsm2], writes=[bsm2])
                em.op("dve", lambda e, sm=sm: e.scalar_tensor_tensor(out=yn[:], in0=t1[:], scalar=sm[:, 57:58], in1=self.rv[:, oG:oG + 512], op0=ALU.mult, op1=ALU.mult),
                      reads=[bt1, bsm2, bc], writes=[byn])
                qn = n % 4
                for c in range(4):
                    em.op("pe", lambda e, c=c: e.transpose(out=psT[:, c * 128:(c + 1) * 128], in_=yn[:, c * 128:(c + 1) * 128], identity=self.ident_bf[:]),
                          reads=[byn, bc], writes=[psb[7]])
                em.op("act", lambda e, qn=qn: e.copy(out=ysg[:, :, qn * 128:(qn + 1) * 128], in_=psT[:, 0:512].rearrange("p (a b) -> p a b", a=4)), reads=[psb[7]], writes=[bysg])
                if qn == 3:
                    t = n // 4
                    em.dma("sp", self.dram["ybr"][1536:2048, t * 512:(t + 1) * 512].rearrange("(c p) s -> p c s", p=128), ysg[:], reads=[bysg], writes=[self.b_ybr])
            em.barrier()


Builder.mixer_D = _mixer_D


def build_program(S=4096):
    b = Builder(S)
    em = b.em
    b.phase_copy_in()
    b.b_ybr = em.buf("ybr")
    b.b_gat = em.buf("gat")
    for l in range(DEPTH):
        with ExitStack() as st:
            b.mix_begin(l, st)
            b.gates(l)
            b.mixer_C(l)
            b.mixer_A(l)
            b.mixer_B(l)
            b.mixer_D(l)
        em.barrier()
        b.phase_merge(l)
        b.phase_xattn(l)
        b.phase_ffn(l)
    toks = b.phase_final()
    em.finish(toks)
    return b


_CACHE = {}


def kernel(**inputs):
    S = 4096
    if "b" not in _CACHE:
        _CACHE["b"] = build_program(S)
    b = _CACHE["b"]
    in_maps = [make_in_map(inputs, c % 4, S) for c in range(8)]
    res = run_bass_kernel_spmd(b.nc, in_maps, core_ids=list(range(8)))
    out = np.stack([np.asarray(res.results[c]["out"]).T for c in range(4)], axis=0)
    return np.ascontiguousarray(out.astype(np.float32))
```
